# Optimizing a Trainium2 kernel written in Bass

```python
import jax
import jax.numpy as jnp
from jax import lax
import numpy as np

D_MODEL = 1024
BATCH = 16
SEQ = 256
DEPTH = 2
DEC_BATCH = 8
DEC_SEQ = 4096
PAST_LEN = 256

GRID_W = 64
ROPE_BASE = 10000.0
NORM_EPS = 1e-6
QBLOCK = 128
NEG_BIG = -1e30
MLA_HEADS = 8
MLA_NOPE = 64
MLA_ROPE = 32
MLA_QK = MLA_NOPE + MLA_ROPE
MLA_V = 64
MLA_Q_RANK = 256
MLA_KV_RANK = 128
MLA_SCALE = MLA_QK ** -0.5
ML_HEADS = 4
ML_DK = 64
ML_DV = 128
ML_CHUNK = 64
SW_HEADS = 8
SW_KV_HEADS = 2
SW_HD = 64
SW_WINDOW = 128
SW_BLOCK = 128
SW_SCALE = SW_HD ** -0.5
RW_HEADS = 8
RW_N = 64
RW_DIM = RW_HEADS * RW_N
RW_W_RANK = 64
RW_A_RANK = 64
RW_G_RANK = 128
RW_DECAY_SCALE = 0.6065306597126334
RW_GN_EPS = 64e-5
D_FF = 4 * D_MODEL
N_BRANCH = 4
N_MOD = 6
IN_WIDTHS = (
    MLA_Q_RANK, MLA_KV_RANK, MLA_ROPE,
    ML_HEADS * ML_DK, ML_HEADS * ML_DK, ML_HEADS * ML_DV,
    2 * ML_HEADS, 2 * ML_HEADS, ML_HEADS * ML_DV,
    SW_HEADS * SW_HD, SW_KV_HEADS * SW_HD, SW_KV_HEADS * SW_HD,
    RW_DIM, RW_DIM, RW_DIM, 2 * RW_W_RANK, 2 * RW_A_RANK, RW_G_RANK,
    N_BRANCH * D_MODEL,
)
D_IN = sum(IN_WIDTHS)

kernel_name = 'hybrid_diffusion_prefix_trunk_step'


def rmsnorm(x, g):
    xf = x.astype(jnp.float32)
    y = xf * lax.rsqrt(jnp.mean(xf * xf, -1, keepdims=True) + NORM_EPS)
    return (y * g.astype(jnp.float32)).astype(x.dtype)


def split_columns(u):
    offsets = []
    acc = 0
    for w_ in IN_WIDTHS[:-1]:
        acc += w_
        offsets.append(acc)
    return jnp.split(u, offsets, axis=-1)


def grid_positions(n_tokens):
    t = jnp.arange(n_tokens, dtype=jnp.int32)
    return (t // GRID_W).astype(jnp.float32), (t % GRID_W).astype(jnp.float32)


def axial_rope(x, pos_r, pos_c):
    R = x.shape[-1]
    q = R // 4
    inv = ROPE_BASE ** (-jnp.arange(q, dtype=jnp.float32) / q)
    ar = pos_r[:, None] * inv
    ac = pos_c[:, None] * inv
    ang = jnp.concatenate([ar, ar, ac, ac], -1)[:, None, :]
    xf = x.astype(jnp.float32)
    rot = jnp.concatenate([-xf[..., q:2 * q], xf[..., :q], -xf[..., 3 * q:], xf[..., 2 * q:3 * q]], -1)
    return (xf * jnp.cos(ang) + rot * jnp.sin(ang)).astype(x.dtype)


def dense_attention(q, k, v, sink, scale):
    B, Sq, H, dq = q.shape
    G = k.shape[2]
    rep = H // G
    dv = v.shape[-1]
    nb = Sq // QBLOCK
    qb = jnp.moveaxis(q.reshape(B, nb, QBLOCK, G, rep, dq), 1, 0)

    def one(qblk):
        s = jnp.einsum('bqgrd,bkgd->bgrqk', qblk, k).astype(jnp.float32) * scale
        if sink is not None:
            s_snk = jnp.broadcast_to(sink.astype(jnp.float32).reshape(1, G, rep, 1, 1), s.shape[:-1] + (1,))
            p = jax.nn.softmax(jnp.concatenate([s, s_snk], -1), -1)[..., :-1]
        else:
            p = jax.nn.softmax(s, -1)
        return jnp.einsum('bgrqk,bkgd->bqgrd', p.astype(v.dtype), v)

    o = lax.map(one, qb)
    return jnp.moveaxis(o, 0, 1).reshape(B, Sq, H, dv)


def window_attention(q, k, v, k_ctx, v_ctx, sink, scale):
    B, S, H, d = q.shape
    G = k.shape[2]
    rep = H // G
    C = k_ctx.shape[1]
    nb = S // SW_BLOCK
    pad = ((0, 0), (SW_BLOCK, SW_BLOCK), (0, 0), (0, 0))
    kp = jnp.pad(k, pad).reshape(B, nb + 2, SW_BLOCK, G, d)
    vp = jnp.pad(v, pad).reshape(B, nb + 2, SW_BLOCK, G, d)
    kwin = jnp.concatenate([kp[:, :-2], kp[:, 1:-1], kp[:, 2:]], axis=2)
    vwin = jnp.concatenate([vp[:, :-2], vp[:, 1:-1], vp[:, 2:]], axis=2)
    qi = jnp.arange(SW_BLOCK)
    kj = jnp.arange(3 * SW_BLOCK)
    rel = kj[None, :] - SW_BLOCK - qi[:, None]
    kpos = jnp.arange(nb)[:, None] * SW_BLOCK - SW_BLOCK + kj[None, :]
    mask = (jnp.abs(rel) <= SW_WINDOW)[None] & ((kpos >= 0) & (kpos < S))[:, None, :]
    qb = q.reshape(B, nb, SW_BLOCK, G, rep, d)
    sink_gr = sink.astype(jnp.float32).reshape(1, G, rep, 1, 1)
    n_loc = 3 * SW_BLOCK

    def one(args):
        qblk, kblk, vblk, mblk = args
        s_loc = jnp.einsum('bqgrd,bkgd->bgrqk', qblk, kblk).astype(jnp.float32) * scale
        s_loc = jnp.where(mblk, s_loc, NEG_BIG)
        s_ctx = jnp.einsum('bqgrd,bcgd->bgrqc', qblk, k_ctx).astype(jnp.float32) * scale
        s_snk = jnp.broadcast_to(sink_gr, s_loc.shape[:-1] + (1,))
        p = jax.nn.softmax(jnp.concatenate([s_loc, s_ctx, s_snk], -1), -1).astype(v.dtype)
        return (jnp.einsum('bgrqk,bkgd->bqgrd', p[..., :n_loc], vblk)
                + jnp.einsum('bgrqc,bcgd->bqgrd', p[..., n_loc:n_loc + C], v_ctx))

    o = lax.map(one, (jnp.moveaxis(qb, 1, 0), jnp.moveaxis(kwin, 1, 0), jnp.moveaxis(vwin, 1, 0), mask))
    return jnp.moveaxis(o, 0, 1).reshape(B, S, H, d)


def mla_keys_values(c_kv, k_rope, w_ukv, k_norm):
    B, S, _ = c_kv.shape
    kv = (c_kv @ w_ukv).reshape(B, S, MLA_HEADS, MLA_NOPE + MLA_V)
    k_nope, v = kv[..., :MLA_NOPE], kv[..., MLA_NOPE:]
    k = jnp.concatenate([k_nope, jnp.broadcast_to(k_rope[:, :, None, :], (B, S, MLA_HEADS, MLA_ROPE))], -1)
    return rmsnorm(k, k_norm), v


def rope_tail(x, pos):
    return jnp.concatenate([x[..., :MLA_NOPE], axial_rope(x[..., MLA_NOPE:], *pos)], -1)


def mlstm_scan(q, k, v, li, lf, C0, n0, m0):
    B, S, H, _ = q.shape
    L = ML_CHUNK
    nc = S // L

    def to_chunks(a):
        a = a.reshape((B, nc, L, H) + a.shape[3:])
        return jnp.moveaxis(a, (1, 3), (0, 2))

    causal = jnp.tril(jnp.ones((L, L), dtype=bool))

    def body(carry, inp):
        C, n, m = carry
        qc, kc, vc, ic, fc = inp
        b = jnp.cumsum(fc, -1)
        D = jnp.where(causal, b[..., :, None] - b[..., None, :] + ic[..., None, :], -jnp.inf)
        inter = b + m[..., None]
        m_t = jnp.maximum(inter, jnp.max(D, -1))
        W = jnp.einsum('bhtd,bhsd->bhts', qc, kc) * jnp.exp(D - m_t[..., None])
        a_in = jnp.exp(inter - m_t)
        num = jnp.einsum('bhts,bhsv->bhtv', W, vc) + a_in[..., None] * jnp.einsum('bhtd,bhdv->bhtv', qc, C)
        den = jnp.sum(W, -1) + a_in * jnp.einsum('bhtd,bhd->bht', qc, n)
        h = num / jnp.maximum(jnp.abs(den), jnp.exp(-m_t))[..., None]
        bL = b[..., -1]
        g = bL[..., None] - b + ic
        m_new = jnp.maximum(bL + m, jnp.max(g, -1))
        wk = jnp.exp(g - m_new[..., None])
        decay = jnp.exp(bL + m - m_new)
        C_new = decay[..., None, None] * C + jnp.einsum('bhs,bhsd,bhsv->bhdv', wk, kc, vc)
        n_new = decay[..., None] * n + jnp.einsum('bhs,bhsd->bhd', wk, kc)
        return (C_new, n_new, m_new), h

    (C, n, m), h = lax.scan(body, (C0, n0, m0), tuple(to_chunks(t) for t in (q, k, v, li, lf)))
    h = jnp.moveaxis(h, (0, 2), (1, 3)).reshape(B, S, H, v.shape[-1])
    return h, C, n, m


def rwkv7_scan(r, w, kt, v, kh, a, S0):
    xs = tuple(jnp.moveaxis(t, 1, 0) for t in (r, w, kt, v, kh, a))

    def step(Sm, inp):
        r_t, w_t, k_t, v_t, kh_t, a_t = inp
        sk = jnp.einsum('bhvk,bhk->bhv', Sm, kh_t)
        Sn = Sm * w_t[:, :, None, :] - sk[..., None] * (a_t * kh_t)[:, :, None, :] + v_t[..., None] * k_t[:, :, None, :]
        return Sn, jnp.einsum('bhvk,bhk->bhv', Sn, r_t)

    Sf, y = lax.scan(step, S0, xs)
    return jnp.moveaxis(y, 0, 1), Sf


def rwkv_mixer(r_r, r_k, r_v, r_w1, r_a1, r_g1, P, l, S0):
    B, S, _ = r_r.shape
    f32 = jnp.float32
    hv = lambda t: t.astype(f32).reshape(B, S, RW_HEADS, RW_N)
    r, k, v = hv(r_r), hv(r_k), hv(r_v)
    kappa = k * P['rwkv_kk'][l].astype(f32).reshape(RW_HEADS, RW_N)
    kh = kappa * lax.rsqrt(jnp.sum(kappa * kappa, -1, keepdims=True) + 1e-12)
    g = (jax.nn.sigmoid(r_g1) @ P['rwkv_g2'][l]).astype(f32)
    ka = P['rwkv_ka'][l].astype(f32).reshape(RW_HEADS, RW_N)
    ys, bonuses, states = [], [], []
    for d in range(2):
        w_pre = jnp.tanh(r_w1[..., d * RW_W_RANK:(d + 1) * RW_W_RANK]) @ P['rwkv_w2'][l, d] + P['rwkv_w0'][l, d]
        w = jnp.exp(-RW_DECAY_SCALE * jax.nn.sigmoid(w_pre.astype(f32))).reshape(B, S, RW_HEADS, RW_N)
        a_pre = r_a1[..., d * RW_A_RANK:(d + 1) * RW_A_RANK] @ P['rwkv_a2'][l, d] + P['rwkv_a0'][l, d]
        a = jax.nn.sigmoid(a_pre.astype(f32)).reshape(B, S, RW_HEADS, RW_N)
        kt = k * (1.0 + (a - 1.0) * ka)
        seqs = (r, w, kt, v, kh, a)
        if d == 1:
            seqs = tuple(jnp.flip(t, 1) for t in seqs)
        y_d, S_d = rwkv7_scan(*seqs, S0[:, d].astype(f32))
        if d == 1:
            y_d = jnp.flip(y_d, 1)
        ys.append(y_d)
        bonuses.append(jnp.sum(r * kt * P['rwkv_u'][l, d].astype(f32).reshape(RW_HEADS, RW_N), -1, keepdims=True) * v)
        states.append(S_d)
    y = ys[0] + ys[1]
    mu = jnp.mean(y, -1, keepdims=True)
    var = jnp.mean(jnp.square(y - mu), -1, keepdims=True)
    yn = (y - mu) * lax.rsqrt(var + RW_GN_EPS)
    yn = yn * P['rwkv_gn_g'][l].astype(f32).reshape(RW_HEADS, RW_N) + P['rwkv_gn_b'][l].astype(f32).reshape(RW_HEADS, RW_N)
    out = ((yn + bonuses[0] + bonuses[1]).reshape(B, S, RW_DIM) * g).astype(r_r.dtype)
    return out, jnp.stack(states, 1)


def token_mixers(h, P, l, ctx, pos):
    B, S, _ = h.shape
    f32 = jnp.float32
    (q_a, kv_a, k_rope, m_q, m_k, m_v, m_i, m_f, m_o,
     s_q, s_k, s_v, r_r, r_k, r_v, r_w1, r_a1, r_g1, gate_pre) = split_columns(h @ P['w_in'][l])

    c_kv = rmsnorm(kv_a, P['mla_kv_a_norm'][l])
    q_lat = rmsnorm(q_a, P['mla_q_a_norm'][l])
    q_mla = rmsnorm((q_lat @ P['mla_w_uq'][l]).reshape(B, S, MLA_HEADS, MLA_QK), P['mla_q_norm'][l])
    k_mla, v_mla = mla_keys_values(c_kv, k_rope, P['mla_w_ukv'][l], P['mla_k_norm'][l])
    if ctx is None:
        y_a = dense_attention(q_mla, k_mla, v_mla, None, MLA_SCALE)
    else:
        q_mla = rope_tail(q_mla, pos)
        k_mla = rope_tail(k_mla, pos)
        k_c, v_c = mla_keys_values(ctx['mla_ckv'], ctx['mla_krope'], P['mla_w_ukv'][l], P['mla_k_norm'][l])
        y_a = dense_attention(q_mla, jnp.concatenate([k_mla, k_c], 1), jnp.concatenate([v_mla, v_c], 1), None, MLA_SCALE)
    y_a = y_a.reshape(B, S, MLA_HEADS * MLA_V)

    mq = m_q.astype(f32).reshape(B, S, ML_HEADS, ML_DK) * (ML_DK ** -0.5)
    mk = m_k.astype(f32).reshape(B, S, ML_HEADS, ML_DK)
    mv = m_v.astype(f32).reshape(B, S, ML_HEADS, ML_DV)
    li = m_i.astype(f32).reshape(B, S, 2, ML_HEADS) + P['mlstm_i_bias'][l].astype(f32)
    lf = jax.nn.log_sigmoid(m_f.astype(f32).reshape(B, S, 2, ML_HEADS) + P['mlstm_f_bias'][l].astype(f32))
    if ctx is None:
        C0 = jnp.zeros((B, 2, ML_HEADS, ML_DK, ML_DV), f32)
        n0 = jnp.zeros((B, 2, ML_HEADS, ML_DK), f32)
        m0 = jnp.zeros((B, 2, ML_HEADS), f32)
    else:
        C0 = ctx['mlstm_C'].astype(f32)
        n0 = ctx['mlstm_n'].astype(f32)
        m0 = ctx['mlstm_m'].astype(f32)
    h_f, C_f, n_f, m_f_ = mlstm_scan(mq, mk, mv, li[:, :, 0], lf[:, :, 0], C0[:, 0], n0[:, 0], m0[:, 0])
    fl = lambda t: jnp.flip(t, 1)
    h_b, C_b, n_b, m_b = mlstm_scan(fl(mq), fl(mk), fl(mv), fl(li[:, :, 1]), fl(lf[:, :, 1]), C0[:, 1], n0[:, 1], m0[:, 1])
    h_ml = rmsnorm(h_f + fl(h_b), P['mlstm_norm'][l]) * jax.nn.sigmoid(m_o.astype(f32)).reshape(B, S, ML_HEADS, ML_DV)
    y_b = h_ml.reshape(B, S, ML_HEADS * ML_DV).astype(h.dtype)

    sq = rmsnorm(s_q.reshape(B, S, SW_HEADS, SW_HD), P['swa_q_norm'][l])
    sk = rmsnorm(s_k.reshape(B, S, SW_KV_HEADS, SW_HD), P['swa_k_norm'][l])
    sv = s_v.reshape(B, S, SW_KV_HEADS, SW_HD)
    if ctx is None:
        y_c = dense_attention(sq, sk, sv, P['swa_sink'][l], SW_SCALE)
    else:
        y_c = window_attention(axial_rope(sq, *pos), axial_rope(sk, *pos), sv,
                               ctx['swa_k'], ctx['swa_v'], P['swa_sink'][l], SW_SCALE)
    y_c = y_c.reshape(B, S, SW_HEADS * SW_HD)

    rw0 = None if ctx is None else ctx['rwkv']
    if rw0 is None:
        rw0 = jnp.zeros((B, 2, RW_HEADS, RW_N, RW_N), f32)
    y_d, S_rw = rwkv_mixer(r_r, r_k, r_v, r_w1, r_a1, r_g1, P, l, rw0)

    gates = jax.nn.sigmoid(gate_pre.astype(f32)).astype(h.dtype).reshape(B, S, N_BRANCH, D_MODEL)
    merged = (gates[:, :, 0] * (y_a @ P['mla_w_o'][l]) + gates[:, :, 1] * (y_b @ P['mlstm_w_o'][l])
              + gates[:, :, 2] * (y_c @ P['swa_w_o'][l]) + gates[:, :, 3] * (y_d @ P['rwkv_w_o'][l]))
    out = merged @ P['w_out'][l]
    new_ctx = {'mla_ckv': c_kv, 'mla_krope': k_rope, 'swa_k': sk, 'swa_v': sv,
               'mlstm_C': jnp.stack([C_f, C_b], 1), 'mlstm_n': jnp.stack([n_f, n_b], 1),
               'mlstm_m': jnp.stack([m_f_, m_b], 1), 'rwkv': S_rw}
    return out, new_ctx


def trunk_layer(x, mod, P, l, ctx, pos):
    shift1, scale1, gate1, shift2, scale2, gate2 = jnp.split(mod.astype(x.dtype), N_MOD, -1)
    h = rmsnorm(x, P['norm1'][l]) * (1.0 + scale1) + shift1
    mix, new_ctx = token_mixers(h, P, l, ctx, pos)
    x = x + gate1 * mix
    h = rmsnorm(x, P['norm2'][l]) * (1.0 + scale2) + shift2
    f = jnp.square(jax.nn.relu(h @ P['mlp_w1'][l])) @ P['mlp_w2'][l]
    return x + gate2 * f, new_ctx


def setup_inputs(seed: int = 0) -> dict:
    key = jax.random.key(seed)
    ks = iter(jax.random.split(key, 64))
    L = DEPTH

    def nrm(shape, scale=1.0):
        return jax.random.normal(next(ks), shape, jnp.float32) * scale

    def gain(shape):
        return 1.0 + nrm(shape, 0.02)

    return {
        'x_prompt': nrm((BATCH, SEQ, D_MODEL)),
        'x_sample': nrm((DEC_BATCH, DEC_SEQ, D_MODEL)),
        'c': nrm((DEC_BATCH, D_MODEL)),
        'cache_mla_ckv': nrm((DEC_BATCH, L, PAST_LEN, MLA_KV_RANK)),
        'cache_mla_krope': nrm((DEC_BATCH, L, PAST_LEN, MLA_ROPE)),
        'cache_swa_k': nrm((DEC_BATCH, L, PAST_LEN, SW_KV_HEADS, SW_HD)),
        'cache_swa_v': nrm((DEC_BATCH, L, PAST_LEN, SW_KV_HEADS, SW_HD)),
        'state_mlstm_C': nrm((DEC_BATCH, L, 2, ML_HEADS, ML_DK, ML_DV), 0.1),
        'state_mlstm_n': nrm((DEC_BATCH, L, 2, ML_HEADS, ML_DK), 0.1),
        'state_mlstm_m': nrm((DEC_BATCH, L, 2, ML_HEADS), 0.5),
        'state_rwkv': nrm((DEC_BATCH, L, 2, RW_HEADS, RW_N, RW_N), 0.1),
        'c_ctx': nrm((D_MODEL,)),
        'ada_w': nrm((L, D_MODEL, N_MOD * D_MODEL), 0.5 * D_MODEL ** -0.5),
        'ada_b': nrm((L, N_MOD * D_MODEL), 0.02),
        'norm1': gain((L, D_MODEL)),
        'norm2': gain((L, D_MODEL)),
        'w_in': nrm((L, D_MODEL, D_IN), D_MODEL ** -0.5),
        'mla_q_a_norm': gain((L, MLA_Q_RANK)),
        'mla_kv_a_norm': gain((L, MLA_KV_RANK)),
        'mla_w_uq': nrm((L, MLA_Q_RANK, MLA_HEADS * MLA_QK), MLA_Q_RANK ** -0.5),
        'mla_w_ukv': nrm((L, MLA_KV_RANK, MLA_HEADS * (MLA_NOPE + MLA_V)), MLA_KV_RANK ** -0.5),
        'mla_q_norm': gain((L, MLA_QK)),
        'mla_k_norm': gain((L, MLA_QK)),
        'mla_w_o': nrm((L, MLA_HEADS * MLA_V, D_MODEL), (MLA_HEADS * MLA_V) ** -0.5),
        'mlstm_i_bias': nrm((L, 2, ML_HEADS), 0.1),
        'mlstm_f_bias': jnp.linspace(3.0, 6.0, ML_HEADS)[None, None, :] + nrm((L, 2, ML_HEADS), 0.1),
        'mlstm_norm': gain((L, ML_DV)),
        'mlstm_w_o': nrm((L, ML_HEADS * ML_DV, D_MODEL), (ML_HEADS * ML_DV) ** -0.5),
        'swa_q_norm': gain((L, SW_HD)),
        'swa_k_norm': gain((L, SW_HD)),
        'swa_sink': nrm((L, SW_HEADS), 0.5),
        'swa_w_o': nrm((L, SW_HEADS * SW_HD, D_MODEL), (SW_HEADS * SW_HD) ** -0.5),
        'rwkv_w0': nrm((L, 2, RW_DIM), 0.5),
        'rwkv_w2': nrm((L, 2, RW_W_RANK, RW_DIM), 0.5 * RW_W_RANK ** -0.5),
        'rwkv_a0': nrm((L, 2, RW_DIM), 0.1),
        'rwkv_a2': nrm((L, 2, RW_A_RANK, RW_DIM), 0.5 * RW_A_RANK ** -0.5),
        'rwkv_g2': nrm((L, RW_G_RANK, RW_DIM), RW_G_RANK ** -0.5),
        'rwkv_kk': 0.85 + nrm((L, RW_DIM), 0.05),
        'rwkv_ka': 1.0 + nrm((L, RW_DIM), 0.05),
        'rwkv_u': nrm((L, 2, RW_DIM), 0.3),
        'rwkv_gn_g': gain((L, RW_DIM)),
        'rwkv_gn_b': nrm((L, RW_DIM), 0.02),
        'rwkv_w_o': nrm((L, RW_DIM, D_MODEL), RW_DIM ** -0.5),
        'w_out': nrm((L, D_MODEL, D_MODEL), D_MODEL ** -0.5),
        'mlp_w1': nrm((L, D_MODEL, D_FF), D_MODEL ** -0.5),
        'mlp_w2': nrm((L, D_FF, D_MODEL), D_FF ** -0.5),
    }


def reference(x_prompt, x_sample, c, cache_mla_ckv, cache_mla_krope, cache_swa_k, cache_swa_v,
              state_mlstm_C, state_mlstm_n, state_mlstm_m, state_rwkv,
              c_ctx, ada_w, ada_b, norm1, norm2, w_in,
              mla_q_a_norm, mla_kv_a_norm, mla_w_uq, mla_w_ukv, mla_q_norm, mla_k_norm, mla_w_o,
              mlstm_i_bias, mlstm_f_bias, mlstm_norm, mlstm_w_o,
              swa_q_norm, swa_k_norm, swa_sink, swa_w_o,
              rwkv_w0, rwkv_w2, rwkv_a0, rwkv_a2, rwkv_g2, rwkv_kk, rwkv_ka, rwkv_u, rwkv_gn_g, rwkv_gn_b, rwkv_w_o,
              w_out, mlp_w1, mlp_w2):
    P = {'norm1': norm1, 'norm2': norm2, 'w_in': w_in,
         'mla_q_a_norm': mla_q_a_norm, 'mla_kv_a_norm': mla_kv_a_norm, 'mla_w_uq': mla_w_uq,
         'mla_w_ukv': mla_w_ukv, 'mla_q_norm': mla_q_norm, 'mla_k_norm': mla_k_norm, 'mla_w_o': mla_w_o,
         'mlstm_i_bias': mlstm_i_bias, 'mlstm_f_bias': mlstm_f_bias, 'mlstm_norm': mlstm_norm, 'mlstm_w_o': mlstm_w_o,
         'swa_q_norm': swa_q_norm, 'swa_k_norm': swa_k_norm, 'swa_sink': swa_sink, 'swa_w_o': swa_w_o,
         'rwkv_w0': rwkv_w0, 'rwkv_w2': rwkv_w2, 'rwkv_a0': rwkv_a0, 'rwkv_a2': rwkv_a2, 'rwkv_g2': rwkv_g2,
         'rwkv_kk': rwkv_kk, 'rwkv_ka': rwkv_ka, 'rwkv_u': rwkv_u, 'rwkv_gn_g': rwkv_gn_g, 'rwkv_gn_b': rwkv_gn_b,
         'rwkv_w_o': rwkv_w_o, 'w_out': w_out, 'mlp_w1': mlp_w1, 'mlp_w2': mlp_w2}

    x = x_prompt
    states = []
    for l in range(DEPTH):
        mod = (jax.nn.silu(c_ctx) @ ada_w[l] + ada_b[l])[None, None, :]
        x, st = trunk_layer(x, mod, P, l, None, None)
        states.append(st)
    y_prompt = x
    new_mla_ckv = jnp.stack([s['mla_ckv'] for s in states], 1)
    new_mla_krope = jnp.stack([s['mla_krope'] for s in states], 1)
    new_swa_k = jnp.stack([s['swa_k'] for s in states], 1)
    new_swa_v = jnp.stack([s['swa_v'] for s in states], 1)
    new_mlstm_C = jnp.stack([s['mlstm_C'] for s in states], 1)
    new_mlstm_n = jnp.stack([s['mlstm_n'] for s in states], 1)
    new_mlstm_m = jnp.stack([s['mlstm_m'] for s in states], 1)
    new_rwkv = jnp.stack([s['rwkv'] for s in states], 1)

    pos = grid_positions(x_sample.shape[1])
    x = x_sample
    for l in range(DEPTH):
        mod = (jax.nn.silu(c) @ ada_w[l] + ada_b[l])[:, None, :]
        ctx = {'mla_ckv': cache_mla_ckv[:, l], 'mla_krope': cache_mla_krope[:, l],
               'swa_k': cache_swa_k[:, l], 'swa_v': cache_swa_v[:, l],
               'mlstm_C': state_mlstm_C[:, l], 'mlstm_n': state_mlstm_n[:, l], 'mlstm_m': state_mlstm_m[:, l],
               'rwkv': state_rwkv[:, l]}
        x, _ = trunk_layer(x, mod, P, l, ctx, pos)
    y_sample = x
    return (y_prompt, y_sample, new_mla_ckv, new_mla_krope, new_swa_k, new_swa_v,
            new_mlstm_C, new_mlstm_n, new_mlstm_m, new_rwkv)
```

```python
import os
import numpy as np
import concourse.bass as bass
import concourse.mybir as mybir
from concourse.bass_utils import run_bass_kernel_spmd
from contextlib import ExitStack

F32 = mybir.dt.float32
F32R = mybir.dt.float32r
FAST_MM = os.environ.get("FAST_MM", "1") == "1"
FDT = F32R if FAST_MM else F32


def fr(ap):
    return ap.bitcast(F32R) if FAST_MM else ap
AF = mybir.ActivationFunctionType
ALU = mybir.AluOpType
AX = mybir.AxisListType

ENGS = ("sync", "scalar", "vector", "gpsimd", "tensor")
SEM_LIMIT = int(os.environ.get("SEM_LIMIT", 30000))
N_DMA_SEMS = 16
import os
STQ = os.environ.get("STQ", "gpsimd")
PENG = os.environ.get("PENG", "vector")

D = 1024
NORM_EPS = 1e-6
CTX = 256
SP = 256
D_IN = 8752
D_FF = 4096
MLA_SCALE = 96 ** -0.5
SW_SCALE = 64 ** -0.5
RW_DECAY = 0.6065306597126334
RW_GN_EPS = 64e-5
C_QA, C_KVA, C_KR = 0, 256, 384
C_MQ, C_MK, C_MV, C_MI, C_MF, C_MO = 416, 672, 928, 1440, 1448, 1456
C_SQ, C_SK, C_SV = 1968, 2480, 2608
C_RR, C_RK, C_RV, C_RW, C_RA, C_RG = 2736, 3248, 3760, 4272, 4400, 4528
C_GATE = 4656
NTOKC = 4656


class Op:
    __slots__ = ("eng", "fn", "deps", "is_dma", "signal", "sem", "cnt", "dsem_prev", "barriered")

    def __init__(self, eng, fn, is_dma):
        self.eng = eng
        self.fn = fn
        self.deps = []
        self.is_dma = is_dma
        self.signal = False
        self.sem = None
        self.cnt = 0
        self.dsem_prev = None
        self.barriered = False


class Prog:
    def __init__(self, nc):
        self.nc = nc
        self.ops = {e: [] for e in ENGS}
        self.lastw = {}
        self.readers = {}
        self.stacks = [ExitStack()]
        self.out_dmas = []
        self.uid = 0
        self.rr = 0
        self.psum_names = set()

    def sb(self, name, shape, dt=F32):
        self.uid += 1
        return self.stacks[-1].enter_context(self.nc.sbuf_tensor(f"{name}_{self.uid}", list(shape), dt))

    def ps(self, name, shape, dt=F32):
        self.uid += 1
        n = 1
        for d in shape[1:]:
            n *= d
        nb = (n * 4 + 2047) // 2048
        t = self.stacks[-1].enter_context(self.nc.psum_tensor(f"{name}_{self.uid}", [128, nb * 512], dt))
        self.psum_names.add(t.name)
        v = t[:shape[0], :n]
        if len(shape) == 3:
            v = v.rearrange("p (a b) -> p a b", a=shape[1])
        elif len(shape) == 4:
            v = v.rearrange("p (a b c) -> p a b c", a=shape[1], b=shape[2])
        return v

    def push(self):
        self.stacks.append(ExitStack())

    def pop(self):
        self.barrier()
        self.stacks.pop().close()

    @staticmethod
    def _key(k):
        if isinstance(k, (str, tuple)):
            return k
        return k.name

    def op(self, eng, fn, reads=(), writes=(), is_dma=False):
        self.nrec = getattr(self, "nrec", 0) + 1
        if self.nrec > int(os.environ.get("KMAXOPS", 10 ** 9)):
            return None
        o = Op(eng, fn, is_dma)
        if os.environ.get("KTRACE") and abs(self.nrec - int(os.environ["KTRACE"])) <= 6:
            print("OP", self.nrec, eng, [self._key(k) for k in writes], flush=True)
        rk = [self._key(k) for k in reads if k is not None and not isinstance(k, (int, float))]
        wk = [self._key(k) for k in writes]
        if eng != "tensor":
            wk = wk + [k for k in rk if k in self.psum_names and k not in wk]
        raw = set()
        deps = set()
        for k in rk:
            w = self.lastw.get(k)
            if w is not None:
                raw.add(w)
                deps.add(w)
        for k in wk:
            w = self.lastw.get(k)
            if w is not None:
                deps.add(w)
            for r in self.readers.get(k, ()):
                deps.add(r)
        for d in deps:
            if d.eng == eng and not d.is_dma and not is_dma:
                if eng == "tensor":
                    continue
            o.deps.append(d)
        for k in rk:
            self.readers.setdefault(k, []).append(o)
        for k in wk:
            self.lastw[k] = o
            self.readers[k] = []
        self.ops[eng].append(o)
        return o

    def barrier(self):
        lasts = [self.ops[e][-1] for e in ENGS if self.ops[e]]
        pend = [o for e in ("sync", "scalar", "gpsimd") for o in self.ops[e] if o.is_dma and not o.barriered]
        for o in pend:
            o.barriered = True
        for e in ENGS:
            b = Op(e, None, False)
            b.deps = lasts + pend
            self.ops[e].append(b)
        self.lastw.clear()
        self.readers.clear()

    def dma(self, out, in_, rk=None, wk=None, eng="sync", final=False, **kw):
        r = [in_] if rk is None else rk
        w = [out] if wk is None else wk
        o = self.op(eng, lambda e: e.dma_start(out=out, in_=in_, **kw), r, w, is_dma=True)
        if final and o is not None:
            self.out_dmas.append(o)
        return o

    def mm(self, out, lhsT, rhs, start=True, stop=True, rk=None, wk=None, fast=False):
        r = [lhsT, rhs] if rk is None else rk
        w = [out] if wk is None else wk
        return self.op("tensor", lambda e: e.matmul(out, lhsT=lhsT, rhs=rhs, start=start, stop=stop), r, w)

    def tr(self, out, in_, ident, rk=None, wk=None):
        r = [in_, ident] if rk is None else rk
        w = [out] if wk is None else wk
        return self.op("tensor", lambda e: e.transpose(out, in_, ident), r, w)

    def act(self, out, in_, func, bias=None, scale=None, accum=None, rk=None, wk=None, extra_r=()):
        kw = {}
        if bias is not None:
            kw["bias"] = bias
        if scale is not None:
            kw["scale"] = scale
        if accum is not None:
            kw["accum_out"] = accum
        r = ([in_, bias, scale] if rk is None else list(rk)) + list(extra_r)
        w = ([out] + ([accum] if accum is not None else [])) if wk is None else wk
        return self.op("scalar", lambda e: e.activation(out=out, in_=in_, func=func, **kw), r, w)

    def tt(self, eng, out, in0, in1, op, rk=None, wk=None):
        r = [in0, in1] if rk is None else rk
        w = [out] if wk is None else wk
        return self.op(eng, lambda e: e.tensor_tensor(out=out, in0=in0, in1=in1, op=op), r, w)

    def ts(self, eng, out, in0, s1, s2=None, op0=ALU.mult, op1=None, rk=None, wk=None):
        r = [in0, s1, s2] if rk is None else rk
        w = [out] if wk is None else wk
        if op1 is None:
            return self.op(eng, lambda e: e.tensor_scalar(out=out, in0=in0, scalar1=s1, scalar2=None, op0=op0), r, w)
        return self.op(eng, lambda e: e.tensor_scalar(out=out, in0=in0, scalar1=s1, scalar2=s2, op0=op0, op1=op1), r, w)

    def stt(self, eng, out, in0, scalar, in1, op0, op1, rk=None, wk=None):
        r = [in0, scalar, in1] if rk is None else rk
        w = [out] if wk is None else wk
        return self.op(eng, lambda e: e.scalar_tensor_tensor(out=out, in0=in0, scalar=scalar, in1=in1, op0=op0, op1=op1), r, w)

    def copy(self, eng, out, in_, rk=None, wk=None):
        r = [in_] if rk is None else rk
        w = [out] if wk is None else wk
        if eng == "scalar":
            return self.op(eng, lambda e: e.copy(out=out, in_=in_), r, w)
        return self.op(eng, lambda e: e.tensor_copy(out=out, in_=in_), r, w)

    def red(self, eng, out, in_, op=ALU.add, rk=None, wk=None):
        r = [in_] if rk is None else rk
        w = [out] if wk is None else wk
        return self.op(eng, lambda e: e.tensor_reduce(out=out, in_=in_, axis=AX.X, op=op), r, w)

    def recip(self, out, in_, rk=None, wk=None):
        r = [in_] if rk is None else rk
        w = [out] if wk is None else wk
        return self.op("vector", lambda e: e.reciprocal(out=out, in_=in_), r, w)

    def memset(self, eng, out, val, wk=None):
        w = [out] if wk is None else wk
        return self.op(eng, lambda e: e.memset(out, val), [], w)

    def evac(self, out, in_, rk=None, wk=None):
        self.rr += 1
        ev = os.environ.get("KEVAC", "alt")
        if ev == "alt":
            ev = "scalar" if self.rr % 2 else "vector"
        return self.copy(ev, out, in_, rk, wk)

    def emit(self):
        nc = self.nc
        if self.out_dmas:
            b = Op("sync", None, False)
            b.deps = list(self.out_dmas)
            self.ops["sync"].append(b)
        for e in ENGS:
            for o in self.ops[e]:
                for d in o.deps:
                    d.signal = True
        with ExitStack() as st:
            def newsem(nm):
                return st.enter_context(nc.semaphore(nm))
            for e in ENGS:
                cur, c, ep = None, 0, 0
                for o in self.ops[e]:
                    if o.is_dma or not o.signal or o.fn is None:
                        continue
                    if cur is None or c >= SEM_LIMIT:
                        cur = newsem(f"s_{e}_{ep}")
                        ep += 1
                        c = 0
                    c += 1
                    o.sem = cur
                    o.cnt = c
            for e in ENGS:
                dl = [o for o in self.ops[e] if o.is_dma]
                if not dl:
                    continue
                dsems = [newsem(f"dq_{e}_{i}") for i in range(N_DMA_SEMS)]
                dcnt = [0] * N_DMA_SEMS
                dlast = [None] * N_DMA_SEMS
                for di, o in enumerate(dl):
                    s_ = di % N_DMA_SEMS
                    if dcnt[s_] + 16 > SEM_LIMIT:
                        dsems[s_] = newsem(f"dq_{e}_{s_}_{di}")
                        dcnt[s_] = 0
                    dcnt[s_] += 16
                    o.sem = dsems[s_]
                    o.cnt = dcnt[s_]
                    o.dsem_prev = dlast[s_]
                    dlast[s_] = o
                    o.signal = True
            if os.environ.get("KSTATS"):
                for e in ENGS:
                    sig = [o for o in self.ops[e] if o.signal and not o.is_dma and o.fn is not None]
                    print("ENG", e, "ops", len(self.ops[e]), "signals", len(sig), "maxcnt", max([o.cnt for o in self.ops[e]] + [0]), flush=True)
            with nc.Block() as block:
                def run(e):
                    def body(eng):
                        seen = {}
                        for o in self.ops[e]:
                            need = {}
                            deps = o.deps
                            if o.is_dma and o.dsem_prev is not None:
                                deps = deps + [o.dsem_prev]
                            for d in deps:
                                if d.fn is None or d.sem is None:
                                    continue
                                nm = d.sem.name
                                if need.get(nm, (None, 0))[1] < d.cnt:
                                    need[nm] = (d.sem, d.cnt)
                            for nm, (s, c) in need.items():
                                if seen.get(nm, 0) >= c:
                                    continue
                                eng.wait_ge(s, c)
                                seen[nm] = c
                            if o.fn is None:
                                continue
                            ins = o.fn(eng)
                            if o.signal:
                                ins.then_inc(o.sem, 16 if o.is_dma else 1)
                    return body
                block.sync(run("sync"))
                block.scalar(run("scalar"))
                block.vector(run("vector"))
                block.gpsimd(run("gpsimd"))
                block.tensor(run("tensor"))
        while self.stacks:
            self.stacks.pop().close()


class Cfg:
    def __init__(self, S_s=4096, n_p=2, depth=2, debug=()):
        self.S_s = S_s
        self.n_p = n_p
        self.depth = depth
        self.debug = tuple(debug)


W_NAMES = ["ada_w", "w_in", "mla_w_uq", "mla_w_ukv", "mla_w_o", "mlstm_w_o", "swa_w_o", "rwkv_w2", "rwkv_a2",
           "rwkv_g2", "rwkv_w_o", "w_out", "mlp_w1", "mlp_w2"]

P1_BLOCKS = [
    (0, 416, "T"), (416, 672, "F"), (672, 928, "TF"), (928, 1440, "T"), (1440, 1456, "T"), (1456, 1968, "T"),
    (1968, 2480, "T"), (2480, 2736, "T"), (2736, 3248, "F"), (3248, 3760, "F"), (3760, 4272, "TF"),
    (4272, 4656, "F"),
] + [(4656 + 512 * i, 4656 + 512 * (i + 1), "G") for i in range(8)]


def build(cfg):
    nc = bass.Bass("TRN2", target_bir_lowering=False)
    if FAST_MM:
        nc.dge_precook = False
    L = cfg.depth
    S_s, n_p = cfg.S_s, cfg.n_p
    TOKP = n_p * SP
    P = Prog(nc)
    I = {}
    O = {}
    SCR = {}

    def din(name, shape):
        I[name] = nc.dram_tensor(name, list(shape), F32, kind="ExternalInput").ap()
        return I[name]

    def dout(name, shape):
        O[name] = nc.dram_tensor(name, list(shape), F32, kind="ExternalOutput").ap()
        return O[name]

    def scr(name, shape, dt=F32):
        kind = "ExternalOutput" if name in cfg.debug else "Internal"
        SCR[name] = nc.dram_tensor(name, list(shape), dt, kind=kind).ap()
        return SCR[name]

    din("xs", [S_s, D]); din("xp", [TOKP, D])
    din("cT", [128, 8, 2])
    din("ckv", [L, CTX, 128]); din("ckr", [L, CTX, 32]); din("cswk", [L, CTX, 128]); din("cswv", [L, CTX, 128])
    din("mC", [L, 2, 4, 64, 128]); din("mn", [L, 2, 4, 64]); din("mm", [L, 2, 4]); din("rw", [L, 2, 8, 64, 64])
    din("ada_w", [L, D, 6 * D]); din("ada_bT", [L, 128, 48]); din("norm1T", [L, 128, 8]); din("norm2T", [L, 128, 8])
    din("w_in", [L, D, D_IN])
    din("mla_q_a_norm", [L, 256]); din("mla_kv_a_norm", [L, 128]); din("mla_w_uq", [L, 256, 768])
    din("mla_w_ukv", [L, 128, 1024]); din("mla_q_norm", [L, 96]); din("mla_k_norm", [L, 96]); din("mla_w_o", [L, 512, D])
    din("mlstm_i_bias", [L, 8]); din("mlstm_f_bias", [L, 8]); din("mlstm_norm", [L, 128]); din("mlstm_w_o", [L, 512, D])
    din("swa_q_norm", [L, 64]); din("swa_k_norm", [L, 64]); din("swa_sink", [L, 8]); din("swa_w_o", [L, 512, D])
    din("rwkv_w0", [L, 2, 512]); din("rwkv_w2", [L, 2, 64, 512]); din("rwkv_a064", [L, 64, 2, 8])
    din("rwkv_a2", [L, 2, 64, 512]); din("rwkv_g2", [L, 128, 512]); din("rwkv_kk64", [L, 64, 8]); din("rwkv_ka64", [L, 64, 8])
    din("rwkv_u64", [L, 64, 2, 8]); din("rwkv_gn_gT", [L, 128, 4]); din("rwkv_gn_bT", [L, 128, 4]); din("rwkv_w_o", [L, 512, D])
    din("w_out", [L, D, D]); din("mlp_w1", [L, D, D_FF]); din("mlp_w2", [L, D_FF, D])
    din("k_ident", [128, 128]); din("k_ones", [128, 128])
    din("k_rope32", [S_s, 2, 32]); din("k_rope64", [S_s, 2, 64]); din("k_tri", [2, 128, 128]); din("k_sel", [2, 128, 128]); din("k_tris", [2, 128, 128])
    dout("y_p", [TOKP, D]); dout("y_s", [S_s, D])
    dout("o_ckv", [n_p, L, SP, 128]); dout("o_ckr", [n_p, L, SP, 32]); dout("o_swk", [n_p, L, SP, 128]); dout("o_swv", [n_p, L, SP, 128])
    dout("o_mC", [n_p, L, 2, 4, 64, 128]); dout("o_mn", [n_p, L, 2, 4, 64]); dout("o_mm", [n_p, L, 2, 4]); dout("o_rw", [n_p, L, 2, 8, 64, 64])

    jobs = [dict(name="s", TOK=S_s, seqs=[(0, S_s)], ctx=True, j=0, x=I["xs"], y=O["y_s"]),
            dict(name="p", TOK=TOKP, seqs=[(i * SP, SP) for i in range(n_p)], ctx=False, j=1, x=I["xp"], y=O["y_p"])]
    for jb in jobs:
        n = jb["name"]
        jb["XT"] = scr(f"XT_{n}", [D, jb["TOK"]])
        jb["UT"] = scr(f"UTOK_{n}", [jb["TOK"], NTOKC])
        jb["UF"] = scr(f"UFEAT_{n}", [D_IN, jb["TOK"]])
        jb["YT"] = scr(f"YT_{n}", [4, 512, jb["TOK"]], FDT)

    ident = P.sb("ident", [128, 128]); ones = P.sb("ones", [128, 128])
    P.dma(ident[:], I["k_ident"][:, :], wk=[ident]); P.dma(ones[:], I["k_ones"][:, :], wk=[ones])
    cT = P.sb("cT", [128, 8, 2]); sT = P.sb("sT", [128, 8, 2])
    P.dma(cT[:], I["cT"][:, :, :])
    P.act(sT[:], cT[:], AF.Silu)
    modT = [P.sb(f"modT{l}", [128, 48, 2]) for l in range(L)]
    A1 = [P.sb(f"A1_{l}", [128, 8, 2]) for l in range(L)]
    A2 = [P.sb(f"A2_{l}", [128, 8, 2]) for l in range(L)]

    def phase0(l):
        P.push()
        wb = [P.sb(f"adaw{i}", [128, 8, 512]) for i in range(2)]
        pm = P.ps("pm", [128, 48, 2])
        bT = P.sb("bT", [128, 48]); n1 = P.sb("n1", [128, 8]); n2 = P.sb("n2", [128, 8])
        P.dma(bT[:], I["ada_bT"][l]); P.dma(n1[:], I["norm1T"][l]); P.dma(n2[:], I["norm2T"][l])
        wv = I["ada_w"][l].rearrange("(k p) c -> p k c", p=128)
        for g in range(12):
            w = wb[g % 2]
            P.dma(w[:], wv[:, :, g * 512:(g + 1) * 512])
            for jj in range(4):
                jc = g * 4 + jj
                for k in range(8):
                    P.mm(pm[:, jc, :], w[:, k, jj * 128:(jj + 1) * 128], sT[:, k, :], start=(k == 0), stop=(k == 7))
        P.tt("vector", modT[l][:], pm[:], bT[:].unsqueeze(2).broadcast_to([128, 48, 2]), ALU.add)
        P.stt("vector", A1[l][:], modT[l][:, 8:16, :], 1.0, n1[:].unsqueeze(2).broadcast_to([128, 8, 2]), ALU.add, ALU.mult)
        P.stt("vector", A2[l][:], modT[l][:, 32:40, :], 1.0, n2[:].unsqueeze(2).broadcast_to([128, 8, 2]), ALU.add, ALU.mult)
        P.pop()

    def rms_rstd_featmajor(xT, sq, pst, rstd, n):
        P.act(sq[:, :, :n], xT[:, :, :n], AF.Square)
        for k in range(8):
            P.mm(pst[:, :n], ones[:], sq[:, k, :n], start=(k == 0), stop=(k == 7))
        P.act(rstd[:, :n], pst[:, :n], AF.Sqrt, bias=NORM_EPS, scale=1.0 / D)
        P.recip(rstd[:, :n], rstd[:, :n])

    def phase1(l, jb, first):
        TOK, j = jb["TOK"], jb["j"]
        ST = 512
        P.push()
        xT = [P.sb(f"xT{i}", [128, 8, ST]) for i in range(2)]
        hT = [P.sb(f"hT{i}", [128, 8, ST], FDT) for i in range(2)]
        sq = P.sb("sq", [128, 8, ST]); rstd = P.sb("rstd", [128, ST])
        wt = [P.sb(f"wt{i}", [128, 8, 512], FDT) for i in range(3)]
        stg = [P.sb(f"stg{i}", [128, 4, 512]) for i in range(3)]
        xin = [P.sb(f"xin{i}", [128, D]) for i in range(2)] if first else None
        pst = P.ps("pst", [128, ST])
        pp = [P.ps(f"pp{i}", [128, 512]) for i in range(6)]
        XTv = jb["XT"].rearrange("(k p) t -> p k t", p=128)
        wv = WR["w_in"][l].rearrange("(k p) c -> p k c", p=128)
        UTv = jb["UT"].rearrange("(tt p) c -> p tt c", p=128)
        ip = 0
        ist = 0
        iw = 0
        for s in range(TOK // ST):
            x_ = xT[s % 2]; h_ = hT[s % 2]
            t0 = s * ST
            if first:
                for tt in range(4):
                    xi = xin[tt % 2]
                    P.dma(xi[:], jb["x"][t0 + tt * 128:t0 + (tt + 1) * 128, :])
                    for kk in range(2):
                        pq = pp[ip % 6]; ip += 1
                        for k4 in range(4):
                            P.tr(pq[:, k4 * 128:(k4 + 1) * 128], xi[:, (kk * 4 + k4) * 128:(kk * 4 + k4 + 1) * 128], ident[:])
                        P.evac(x_[:, kk * 4:(kk + 1) * 4, tt * 128:(tt + 1) * 128], pq[:].rearrange("p (a b) -> p a b", a=4))
                P.dma(XTv[:, :, t0:t0 + ST], x_[:], wk=[(jb["XT"].name, s)], eng=STQ)
            else:
                P.dma(x_[:], XTv[:, :, t0:t0 + ST], rk=[(jb["XT"].name, s)])
            rms_rstd_featmajor(x_, sq, pst, rstd, ST)
            P.tt("vector", sq[:], x_[:], rstd[:].unsqueeze(1).broadcast_to([128, 8, ST]), ALU.mult)
            for k in range(8):
                P.act(h_[:, k, :], sq[:, k, :], AF.Identity, bias=modT[l][:, k, j:j + 1], scale=A1[l][:, k, j:j + 1])
            for (cs, ce, lay) in P1_BLOCKS:
                wd = ce - cs
                w = wt[iw % 3]; iw += 1
                P.dma(w[:, :, :wd], wv[:, :, cs:ce])
                if "T" in lay:
                    sg = stg[ist % 3]; ist += 1
                    for tt in range(4):
                        pq = pp[ip % 6]; ip += 1
                        for k in range(8):
                            P.mm(pq[:, :wd], h_[:, k, tt * 128:(tt + 1) * 128], w[:, k, :wd], start=(k == 0), stop=(k == 7), fast=(wd >= 256))
                        P.evac(sg[:, tt, :wd], pq[:, :wd])
                    P.dma(UTv[:, s * 4:(s + 1) * 4, cs:ce], sg[:, :, :wd], wk=[(jb["UT"].name, s, cs)], eng=STQ)
                if "F" in lay or "G" in lay:
                    sg = stg[ist % 3]; ist += 1
                    nb = wd // 128
                    for cb in range(nb):
                        pq = pp[ip % 6]; ip += 1
                        for k in range(8):
                            P.mm(pq[:, :ST], w[:, k, cb * 128:(cb + 1) * 128], h_[:, k, :], start=(k == 0), stop=(k == 7), fast=True)
                        if lay == "G":
                            P.act(sg[:, cb, :], pq[:, :ST], AF.Sigmoid)
                        else:
                            P.evac(sg[:, cb, :], pq[:, :ST])
                    P.dma(jb["UF"][cs:ce, t0:t0 + ST].rearrange("(cb p) t -> p cb t", p=128), sg[:, :nb, :],
                          wk=[(jb["UF"].name, s, cs)], eng=STQ)
        P.pop()


    def rope(x, cos, sinS, H, q, t1, t2):
        R4 = 4 * q
        t1v = t1[:, :H * R4].rearrange("p (h r) -> p h r", h=H)
        t2v = t2[:, :H * R4].rearrange("p (h a b c) -> p h a b c", h=H, a=2, b=2)
        xv = x.rearrange("p h (a b c) -> p h a b c", a=2, b=2)
        sv = sinS.rearrange("p (a b c) -> p a b c", a=2, b=2)
        P.tt("vector", t1v, x, cos.unsqueeze(1).broadcast_to([128, H, R4]), ALU.mult)
        for b in range(2):
            P.tt(PENG, t2v[:, :, :, b, :], xv[:, :, :, 1 - b, :],
                 sv[:, :, b, :].unsqueeze(1).broadcast_to([128, H, 2, q]), ALU.mult)
        P.tt("vector", x, t1v, t2[:, :H * R4].rearrange("p (h r) -> p h r", h=H), ALU.add)

    def rstd_of(out, ssq, n, eps):
        P.act(out, ssq, AF.Sqrt, bias=eps, scale=1.0 / n)
        P.recip(out, out)

    def bcast_row(name, src_row, n):
        t = P.sb(name, [128, n])
        P.dma(t[:], src_row.partition_broadcast(128))
        return t

    def phaseA(l, jb):
        TOK = jb["TOK"]
        for si, (t_off, S) in enumerate(jb["seqs"]):
            NK = S + (CTX if jb["ctx"] else 0)
            NKT = NK // 128
            n = jb["name"]
            KTs = scr(f"A_KT_{n}{l}_{si}", [8, 96, NK], FDT); VPs = scr(f"A_VP_{n}{l}_{si}", [NK, 8, 65], FDT); QTs = scr(f"A_QT_{n}{l}_{si}", [8, 96, S], FDT)
            P.push()
            gqa = bcast_row("gqa", I["mla_q_a_norm"][l], 256); gkv = bcast_row("gkv", I["mla_kv_a_norm"][l], 128)
            gqn = bcast_row("gqn", I["mla_q_norm"][l], 96); gkn = bcast_row("gkn", I["mla_k_norm"][l], 96)
            wuq = P.sb("wuq", [128, 2, 768]); wukv = P.sb("wukv", [128, 1024])
            P.dma(wuq[:], I["mla_w_uq"][l].rearrange("(k p) c -> p k c", p=128)); P.dma(wukv[:], I["mla_w_ukv"][l])
            ua = [P.sb(f"ua{i}", [128, 416]) for i in range(2)]
            rt = [P.sb(f"rt{i}", [128, 2, 32]) for i in range(2)]
            junk = P.sb("junk", [128, 768]); junk2 = P.sb("junk2", [128, 768])
            st = P.sb("st", [128, 4]); ss16 = P.sb("ss16", [128, 16]); ssr = P.sb("ssr", [128, 1])
            qlat = P.sb("qlat", [128, 256]); ckv = [P.sb(f"ckv{i}", [128, 128]) for i in range(2)]
            qlT = P.sb("qlT", [128, 2, 128]); ckT = P.sb("ckT", [128, 128])
            qf = P.sb("qf", [128, 8, 96]); kvf = P.sb("kvf", [128, 8, 128]); kn = P.sb("kn", [128, 8, 96])
            kr = P.sb("kr", [128, 32])
            vp = [P.sb(f"vp{i}", [128, 8, 65], FDT) for i in range(2)]
            qT = [P.sb(f"qT{i}", [96, 8, 128], FDT) for i in range(2)]; kT = [P.sb(f"kT{i}", [96, 8, 128], FDT) for i in range(2)]
            for v in vp:
                P.copy("vector", v[:, :, 64:65], ones[:, 0:8].unsqueeze(2), wk=[v])
            ptr = P.ps("ptr", [128, 3, 128]); pq1 = P.ps("pq1", [128, 512]); pq2 = P.ps("pq2", [128, 256])
            pk1 = P.ps("pk1", [128, 512]); pk2 = P.ps("pk2", [128, 512])
            pT = [P.ps(f"pT{i}", [96, 4, 128]) for i in range(2)]
            tiles = [("new", i) for i in range(S // 128)] + ([("ctx", i) for i in range(CTX // 128)] if jb["ctx"] else [])
            for it, (kind, i) in enumerate(tiles):
                if it >= int(os.environ.get("KA1T", 999)):
                    break
                u = ua[it % 2]; ck = ckv[it % 2]; v_ = vp[it % 2]; r_ = rt[it % 2]
                if os.environ.get("KSTATS"):
                    print("A1 tile start", jb["name"], si, it, getattr(P, "nrec", 0))
                new = kind == "new"
                rows = slice(t_off + i * 128, t_off + (i + 1) * 128)
                krow = i * 128 if new else S + i * 128
                if new:
                    P.dma(u[:], jb["UT"][rows, 0:416], rk=[(jb["UT"].name, (t_off + i * 128) // 512, 0)])
                    if jb["ctx"]:
                        P.dma(r_[:], I["k_rope32"][i * 128:(i + 1) * 128])
                    P.act(junk[:, :256], u[:, 0:256], AF.Square, accum=st[:, 0:1])
                    P.act(junk[:, :128], u[:, 256:384], AF.Square, accum=st[:, 1:2])
                    P.act(st[:, 2:3], st[:, 0:1], AF.Sqrt, bias=NORM_EPS, scale=1.0 / 256)
                    P.act(st[:, 3:4], st[:, 1:2], AF.Sqrt, bias=NORM_EPS, scale=1.0 / 128)
                    P.recip(st[:, 2:4], st[:, 2:4])
                    P.stt("vector", qlat[:], u[:, 0:256], st[:, 2:3], gqa[:], ALU.mult, ALU.mult)
                    P.stt("vector", ck[:], u[:, 256:384], st[:, 3:4], gkv[:], ALU.mult, ALU.mult)
                    if not jb["ctx"]:
                        P.dma(O["o_ckv"][si, l, i * 128:(i + 1) * 128, :], ck[:], wk=[("o_ckv", si, l, i)], eng=STQ, final=True)
                        P.dma(O["o_ckr"][si, l, i * 128:(i + 1) * 128, :], u[:, 384:416], wk=[("o_ckr", si, l, i)], eng=STQ, final=True)
                    for k in range(2):
                        P.tr(ptr[:, k, :], qlat[:, k * 128:(k + 1) * 128], ident[:])
                    P.tr(ptr[:, 2, :], ck[:], ident[:])
                    P.evac(qlT[:], ptr[:, 0:2, :]); P.evac(ckT[:], ptr[:, 2, :])
                    for k in range(2):
                        P.mm(pq1[:], qlT[:, k, :], wuq[:, k, 0:512], start=(k == 0), stop=(k == 1))
                    for k in range(2):
                        P.mm(pq2[:], qlT[:, k, :], wuq[:, k, 512:768], start=(k == 0), stop=(k == 1))
                    qff = qf[:].rearrange("p h d -> p (h d)")
                    P.evac(qff[:, 0:512], pq1[:]); P.evac(qff[:, 512:768], pq2[:])
                    krope = u[:, 384:416]
                else:
                    P.dma(ck[:], I["ckv"][l, i * 128:(i + 1) * 128, :])
                    P.dma(u[:, 384:416], I["ckr"][l, i * 128:(i + 1) * 128, :])
                    P.tr(ptr[:, 2, :], ck[:], ident[:])
                    P.evac(ckT[:], ptr[:, 2, :])
                    krope = u[:, 384:416]
                P.mm(pk1[:], ckT[:], wukv[:, 0:512]); P.mm(pk2[:], ckT[:], wukv[:, 512:1024])
                kvff = kvf[:].rearrange("p h d -> p (h d)")
                P.evac(kvff[:, 0:512], pk1[:]); P.evac(kvff[:, 512:1024], pk2[:])
                j2 = junk2[:, :512].rearrange("p (h d) -> p h d", h=8)
                P.act(j2, kvf[:, :, 0:64], AF.Square)
                P.red("vector", ss16[:, 8:16], j2)
                P.act(junk[:, :32], krope, AF.Square, accum=ssr[:, 0:1])
                P.ts("vector", ss16[:, 8:16], ss16[:, 8:16], ssr[:, 0:1], None, op0=ALU.add)
                if new:
                    P.act(junk[:].rearrange("p (h d) -> p h d", h=8), qf[:], AF.Square)
                    P.red("vector", ss16[:, 0:8], junk[:].rearrange("p (h d) -> p h d", h=8))
                else:
                    P.memset("vector", ss16[:, 0:8], 1.0)
                rstd_of(ss16[:], ss16[:], 96, NORM_EPS)
                P.tt("vector", kn[:, :, 0:64], kvf[:, :, 0:64], ss16[:, 8:16].unsqueeze(2).broadcast_to([128, 8, 64]), ALU.mult)
                P.tt(PENG, kn[:, :, 0:64], kn[:, :, 0:64], gkn[:, 0:64].unsqueeze(1).broadcast_to([128, 8, 64]), ALU.mult)
                P.tt("vector", kr[:], krope, gkn[:, 64:96], ALU.mult)
                if new and jb["ctx"]:
                    rope(kr[:].unsqueeze(1), r_[:, 0, :], r_[:, 1, :], 1, 8, junk, junk2)
                P.tt("vector", kn[:, :, 64:96], kr[:].unsqueeze(1).broadcast_to([128, 8, 32]),
                     ss16[:, 8:16].unsqueeze(2).broadcast_to([128, 8, 32]), ALU.mult)
                P.copy(PENG, v_[:, :, 0:64], kvf[:, :, 64:128])
                P.dma(VPs[krow:krow + 128], v_[:], wk=[(VPs.name, krow)], eng=STQ)
                kt_ = kT[it % 2]
                for hh in range(2):
                    for h4 in range(4):
                        P.tr(pT[hh][:, h4, :], kn[:, hh * 4 + h4, :], ident[:])
                    P.evac(kt_[:, hh * 4:(hh + 1) * 4, :], pT[hh][:])
                P.dma(KTs[:, :, krow:krow + 128].rearrange("h d t -> d h t"), kt_[:], wk=[(KTs.name, krow)], eng=STQ)
                if new:
                    P.tt("vector", qf[:], qf[:], ss16[:, 0:8].unsqueeze(2).broadcast_to([128, 8, 96]), ALU.mult)
                    P.tt(PENG, qf[:], qf[:], gqn[:].unsqueeze(1).broadcast_to([128, 8, 96]), ALU.mult)
                    if jb["ctx"]:
                        rope(qf[:, :, 64:96], r_[:, 0, :], r_[:, 1, :], 8, 8, junk, junk2)
                    qt_ = qT[it % 2]
                    for hh in range(2):
                        for h4 in range(4):
                            P.tr(pT[hh][:, h4, :], qf[:, hh * 4 + h4, :], ident[:])
                        P.evac(qt_[:, hh * 4:(hh + 1) * 4, :], pT[hh][:])
                    P.dma(QTs[:, :, i * 128:(i + 1) * 128].rearrange("h d t -> d h t"), qt_[:], wk=[(QTs.name, i)], eng=STQ)
            P.pop()
            if os.environ.get("KSTOP", "") == "a1":
                continue
            P.push()
            QC = 256
            KT = [P.sb(f"KT{i}", [96, NK], FDT) for i in range(2)]
            VP = [P.sb(f"VP{i}", [128, NKT, 65], FDT) for i in range(2)]
            QT = [P.sb(f"QT{i}", [96, QC], FDT) for i in range(3)]
            pTs = [P.sb(f"pTs{i}", [128, NKT, QC], FDT) for i in range(2)]
            oT = [P.sb(f"oT{i}", [65, QC], FDT) for i in range(2)]; rec = P.sb("rec", [64, QC]); yT = [P.sb(f"yT{i}", [64, QC], FDT) for i in range(2)]
            sel65 = P.sb("sel65", [65, 64], FDT)
            sel65f = P.sb("sel65f", [65, 64])
            P.memset("vector", sel65f[:], 0.0); P.memset("vector", sel65f[64:65, :], 1.0)
            P.copy("vector", sel65[:], sel65f[:])
            pss = [P.ps(f"pss{i}", [128, 512]) for i in range(4)]
            po = [P.ps(f"po{i}", [65, QC]) for i in range(2)]
            pb = P.ps("pb", [64, QC])
            units = [(h, qc) for h in range(8) for qc in range(S // QC)]
            ik = [0]

            def qk(iu):
                h, qc = units[iu]
                K_ = KT[h % 2]; V_ = VP[h % 2]
                if qc == 0:
                    P.dma(K_[:], KTs[h], rk=[KTs.name + "*"])
                    P.dma(V_[:], VPs[:, h, :].rearrange("(kt p) c -> p kt c", p=128), rk=[VPs.name + "*"])
                Q_ = QT[iu % 3]; p_ = pTs[iu % 2]
                P.dma(Q_[:], QTs[h, :, qc * QC:(qc + 1) * QC], rk=[QTs.name + "*"])
                for kt in range(NKT):
                    ps_ = pss[ik[0] % 4]; ik[0] += 1
                    P.mm(ps_[:, :QC], K_[:, kt * 128:(kt + 1) * 128], Q_[:], fast=True)
                    P.act(p_[:, kt, :], ps_[:, :QC], AF.Exp, scale=MLA_SCALE)

            def pv(iu):
                h, qc = units[iu]
                V_ = VP[h % 2]; p_ = pTs[iu % 2]; po_ = po[iu % 2]; o_ = oT[iu % 2]; yT_ = yT[iu % 2]
                for kt in range(NKT):
                    P.mm(po_[:], V_[:, kt, :], p_[:, kt, :], start=(kt == 0), stop=(kt == NKT - 1), fast=True)
                P.copy("scalar", o_[:], po_[:])
                P.mm(pb[:], sel65[:], o_[:], fast=True)
                P.recip(rec[:], pb[:])
                P.tt("vector", yT_[:], o_[0:64, :], rec[:], ALU.mult)
                c0 = t_off + qc * QC
                P.dma(jb["YT"][0, h * 64:(h + 1) * 64, c0:c0 + QC], yT_[:], wk=[(jb["YT"].name, 0, h, c0)], eng=STQ)

            qk(0)
            for iu in range(len(units)):
                if iu + 1 < len(units):
                    qk(iu + 1)
                pv(iu)
            P.pop()


    def phaseC(l, jb):
        for si, (t_off, S) in enumerate(jb["seqs"]):
            NT = S // 128
            NCT = (CTX // 128) if jb["ctx"] else 0
            NKT = NT + NCT
            P.push()
            gq = bcast_row("gq", I["swa_q_norm"][l], 64); gk = bcast_row("gk", I["swa_k_norm"][l], 64)
            esink = bcast_row("esink", I["swa_sink"][l], 8)
            P.act(esink[:], esink[:], AF.Exp)
            tri = P.sb("tri", [128, 2, 128])
            P.dma(tri[:], I["k_tri"].rearrange("a k q -> k a q"))
            KTa = P.sb("KTa", [128, NKT * 128]); VPa = P.sb("VPa", [128, NKT, 2, 65])
            P.memset("vector", VPa[:, :, :, 64:65], 1.0, wk=[VPa])
            uk = [P.sb(f"uk{i}", [128, 256]) for i in range(2)]
            uq = [P.sb(f"uq{i}", [128, 512]) for i in range(2)]
            rt = [P.sb(f"rt{i}", [128, 2, 64]) for i in range(2)]
            junk = P.sb("junk", [128, 512]); junk2 = P.sb("junk2", [128, 512]); ss = P.sb("ss", [128, 8])
            kk = [P.sb(f"kk{i}", [128, 2, 64]) for i in range(2)]
            qp = P.sb("qp", [128, 4, 2, 64]); QT = [P.sb(f"QT{i}", [128, 4, 128]) for i in range(2)]
            pTa = [P.sb(f"pTa{i}", [128, 5, 4, 128]) for i in range(2)]
            yc = P.sb("yc", [128, 8, 64]); den = P.sb("den", [128, 4, 1]); ycT = [P.sb(f"ycT{i}", [128, 4, 128], FDT) for i in range(2)]
            ptk = P.ps("ptk", [128, 128]); ptq = P.ps("ptq", [128, 4, 128])
            pss = [P.ps(f"pss{i}", [128, 512]) for i in range(3)]
            po = [P.ps(f"po{i}", [128, 4, 65]) for i in range(2)]
            pyt = P.ps("pyt", [128, 4, 128])
            for i in range(NKT):
                new = i < NT
                u = uk[i % 2]; k_ = kk[i % 2]; r_ = rt[i % 2]
                if new:
                    rows = slice(t_off + i * 128, t_off + (i + 1) * 128)
                    P.dma(u[:], jb["UT"][rows, C_SK:C_SK + 256], rk=[(jb["UT"].name, (t_off + i * 128) // 512, 2480)])
                    kv = u[:, 0:128].rearrange("p (g d) -> p g d", g=2)
                    P.act(junk[:, :128].rearrange("p (g d) -> p g d", g=2), kv, AF.Square)
                    P.red("vector", ss[:, 0:2], junk[:, :128].rearrange("p (g d) -> p g d", g=2))
                    rstd_of(ss[:, 0:2], ss[:, 0:2], 64, NORM_EPS)
                    P.tt("vector", k_[:], kv, ss[:, 0:2].unsqueeze(2).broadcast_to([128, 2, 64]), ALU.mult)
                    P.tt("vector", k_[:], k_[:], gk[:].unsqueeze(1).broadcast_to([128, 2, 64]), ALU.mult)
                    if not jb["ctx"]:
                        P.dma(O["o_swk"][si, l, i * 128:(i + 1) * 128, :], k_[:].rearrange("p g d -> p (g d)"), wk=[("o_swk", si, l, i)], eng=STQ, final=True)
                        P.dma(O["o_swv"][si, l, i * 128:(i + 1) * 128, :], u[:, 128:256], wk=[("o_swv", si, l, i)], eng=STQ, final=True)
                    else:
                        P.dma(r_[:], I["k_rope64"][i * 128:(i + 1) * 128])
                        rope(k_[:], r_[:, 0, :], r_[:, 1, :], 2, 16, junk, junk2)
                    ksrc = k_[:].rearrange("p g d -> p (g d)")
                    vsrc = u[:, 128:256]
                else:
                    c = i - NT
                    P.dma(u[:, 0:128], I["cswk"][l, c * 128:(c + 1) * 128, :]); P.dma(u[:, 128:256], I["cswv"][l, c * 128:(c + 1) * 128, :])
                    ksrc = u[:, 0:128]; vsrc = u[:, 128:256]
                P.tr(ptk[:], ksrc, ident[:])
                P.copy("vector", KTa[:, i * 128:(i + 1) * 128], ptk[:])
                P.copy("vector", VPa[:, i, :, 0:64], vsrc.rearrange("p (g d) -> p g d", g=2))
            units = [(b, g) for b in range(NT) for g in range(2)]
            ik = [0]

            def ktiles(b):
                if not jb["ctx"]:
                    return [(kt, None) for kt in range(NT)]
                lst = []
                if b > 0:
                    lst.append((b - 1, 0))
                lst.append((b, None))
                if b < NT - 1:
                    lst.append((b + 1, 1))
                return lst + [(NT + c, None) for c in range(NCT)]

            def qprep(b):
                u = uq[b % 2]; r_ = rt[b % 2]; Q_ = QT[b % 2]
                rows = slice(t_off + b * 128, t_off + (b + 1) * 128)
                P.dma(u[:], jb["UT"][rows, C_SQ:C_SQ + 512], rk=[(jb["UT"].name, (t_off + b * 128) // 512, 1968)])
                qv = u[:].rearrange("p (h d) -> p h d", h=8)
                P.act(junk[:].rearrange("p (h d) -> p h d", h=8), qv, AF.Square)
                P.red("vector", ss[:], junk[:].rearrange("p (h d) -> p h d", h=8))
                rstd_of(ss[:], ss[:], 64, NORM_EPS)
                P.tt("vector", qv, qv, ss[:].unsqueeze(2).broadcast_to([128, 8, 64]), ALU.mult)
                P.tt("vector", qv, qv, gq[:].unsqueeze(1).broadcast_to([128, 8, 64]), ALU.mult)
                if jb["ctx"]:
                    P.dma(r_[:], I["k_rope64"][b * 128:(b + 1) * 128])
                    rope(qv, r_[:, 0, :], r_[:, 1, :], 8, 16, junk, junk2)
                P.copy("vector", qp[:].rearrange("p r g d -> p g r d"), u[:].rearrange("p (g r d) -> p g r d", g=2, r=4))
                for r in range(4):
                    P.tr(ptq[:, r, :], qp[:, r, :, :].rearrange("p g d -> p (g d)"), ident[:])
                P.copy("scalar", Q_[:], ptq[:])

            def qk(iu):
                b, g = units[iu]
                if g == 0:
                    qprep(b)
                Q_ = QT[b % 2]; p_ = pTa[iu % 2]
                pr = slice(g * 64, (g + 1) * 64)
                for j, (kt, mk) in enumerate(ktiles(b)):
                    ps_ = pss[ik[0] % 3]; ik[0] += 1
                    P.mm(ps_[:], KTa[pr, kt * 128:(kt + 1) * 128], Q_[pr, :, :].rearrange("p r q -> p (r q)"))
                    P.act(p_[:, j, :, :].rearrange("p r q -> p (r q)"), ps_[:], AF.Exp, scale=SW_SCALE)
                    if mk is not None:
                        P.tt("vector", p_[:, j, :, :], p_[:, j, :, :], tri[:, mk, :].unsqueeze(1).broadcast_to([128, 4, 128]), ALU.mult)

            def pv(iu):
                b, g = units[iu]
                p_ = pTa[iu % 2]; po_ = po[iu % 2]
                kts = ktiles(b)
                for r in range(4):
                    for j, (kt, mk) in enumerate(kts):
                        P.mm(po_[:, r, :], p_[:, j, r, :], VPa[:, kt, g, :], start=(j == 0), stop=(j == len(kts) - 1))
                P.tt("vector", den[:], po_[:, :, 64:65], esink[:, g * 4:(g + 1) * 4].unsqueeze(2), ALU.add)
                P.recip(den[:], den[:])
                P.tt("vector", yc[:, g * 4:(g + 1) * 4, :], po_[:, :, 0:64], den[:].broadcast_to([128, 4, 64]), ALU.mult)
                if g == 1:
                    yT_ = ycT[b % 2]
                    for c in range(4):
                        P.tr(pyt[:, c, :], yc[:, 2 * c:2 * c + 2, :].rearrange("p h d -> p (h d)"), ident[:])
                    P.copy("scalar", yT_[:], pyt[:])
                    c0 = t_off + b * 128
                    P.dma(jb["YT"][2, :, c0:c0 + 128].rearrange("(c p) t -> p c t", p=128), yT_[:], wk=[(jb["YT"].name, 2, c0)], eng=STQ)

            qk(0)
            for iu in range(len(units)):
                if iu + 1 < len(units):
                    qk(iu + 1)
                pv(iu)
            P.pop()


    def phaseB(l, jb):
        n = jb["name"]
        for si, (t_off, S) in enumerate(jb["seqs"]):
            NC = S // 128
            HS = scr(f"B_HS_{n}{l}_{si}", [2, S, 512])
            P.push()
            tri = P.sb("tri", [128, 2, 128]); neg = P.sb("neg", [128, 2, 128]); sel = P.sb("sel", [128, 2, 128])
            P.dma(tri[:], I["k_tri"].rearrange("a k q -> k a q")); P.dma(sel[:], I["k_sel"].rearrange("a k q -> k a q"))
            P.ts("vector", neg[:], tri[:], -1.0, 1e30, op0=ALU.add, op1=ALU.mult)
            TRI = [tri[:, 1, :], tri[:, 0, :]]
            NEG = [neg[:, 1, :], neg[:, 0, :]]
            NEGts = [neg[:, 0, :], neg[:, 1, :]]
            bias16 = P.sb("bias16", [128, 2, 8])
            P.dma(bias16[:, 0, :], I["mlstm_i_bias"][l].partition_broadcast(128), wk=[bias16])
            P.dma(bias16[:, 1, :], I["mlstm_f_bias"][l].partition_broadcast(128), wk=[bias16])
            Cst = P.sb("Cst", [64, 8, 129]); mprev = P.sb("mprev", [128, 8])
            if jb["ctx"]:
                P.dma(Cst[:, :, 0:128], I["mC"][l].rearrange("d h k v -> k (d h) v"), wk=[Cst])
                P.dma(Cst[:, :, 128:129], I["mn"][l].rearrange("d h (k o) -> k (d h) o", o=1), wk=[Cst], allow_slow_non_contiguous=True)
                P.dma(mprev[:], I["mm"][l].rearrange("d h -> (d h)").partition_broadcast(128))
            else:
                P.memset("vector", Cst[:], 0.0); P.memset("vector", mprev[:], 0.0)
            G = [P.sb(f"G{i}", [128, 2, 8]) for i in range(2)]
            QTd = [P.sb(f"QTd{i}", [64, 2, 4, 128]) for i in range(2)]; KTd = [P.sb(f"KTd{i}", [64, 2, 4, 128]) for i in range(2)]
            Kt = [P.sb(f"Kt{i}", [128, 2, 4, 64]) for i in range(2)]; VPd = [P.sb(f"VPd{i}", [128, 2, 4, 129]) for i in range(2)]
            for v in VPd:
                P.memset("vector", v[:, :, :, 128:129], 1.0, wk=[v])
            sp = P.sb("sp", [128, 8]); b = P.sb("b", [128, 8]); li = P.sb("li", [128, 8]); c = P.sb("c", [128, 8])
            MB = P.sb("MB", [128, 2, 2, 4]); cmax = P.sb("cmax", [128, 8]); bm = P.sb("bm", [128, 8]); ain = P.sb("ain", [128, 8]); en = P.sb("en", [128, 8])
            DG = P.sb("DG", [128, 8, 128]); DG2 = P.sb("DG2", [128, 8, 128]); Rm = P.sb("Rm", [128, 8, 128])
            ET = P.sb("ET", [128, 8, 128]); WT = P.sb("WT", [128, 8, 128])
            tI = P.sb("tI", [128, 8, 129]); numS = P.sb("numS", [128, 8, 129]); dab = P.sb("dab", [128, 8]); hh = P.sb("hh", [128, 8, 128])
            mbl = P.sb("mbl", [128, 2, 2, 4]); wk_ = P.sb("wk", [128, 8]); dec = P.sb("dec", [128, 8]); KW = P.sb("KW", [128, 8, 64])
            pA = [P.ps(f"pA{i}", [128, 4, 128]) for i in range(2)]
            pB = [P.ps(f"pB{i}", [128, 4, 128]) for i in range(2)]
            pC = [P.ps(f"pC{i}", [128, 3, 129]) for i in range(3)]
            pD = P.ps("pD", [128, 2, 8])
            grp3 = [(0, 0, 3), (1, 3, 6), (2, 6, 8)]

            def pc_slot(dh):
                return pC[dh // 3], dh % 3

            for j in range(NC):
                g_ = G[j % 2]; qt = QTd[j % 2]; kt = KTd[j % 2]; ktok = Kt[j % 2]; vp = VPd[j % 2]
                cd = [j, NC - 1 - j]
                for d in range(2):
                    r0 = t_off + cd[d] * 128
                    rows = slice(r0, r0 + 128)
                    sk = r0 // 512
                    P.dma(g_[:, :, d * 4:(d + 1) * 4], jb["UT"][rows, C_MI:C_MI + 16].rearrange("p (a e) -> p a e", a=2)[:, :, d * 4:(d + 1) * 4],
                          rk=[(jb["UT"].name, sk, 1440)], wk=[g_])
                    P.dma(qt[:, d, :, :], jb["UF"][C_MQ:C_MQ + 256, r0:r0 + 128].rearrange("(h p) t -> p h t", p=64), rk=[(jb["UF"].name, sk, 416)], wk=[qt])
                    P.dma(kt[:, d, :, :], jb["UF"][C_MK:C_MK + 256, r0:r0 + 128].rearrange("(h p) t -> p h t", p=64), rk=[(jb["UF"].name, sk, 672)], wk=[kt])
                    P.dma(ktok[:, d, :, :], jb["UT"][rows, C_MK:C_MK + 256].rearrange("p (h e) -> p h e", h=4), rk=[(jb["UT"].name, sk, 672)], wk=[ktok])
                    P.dma(vp[:, d, :, 0:128], jb["UT"][rows, C_MV:C_MV + 512].rearrange("p (h e) -> p h e", h=4), rk=[(jb["UT"].name, sk, 928)], wk=[vp])
                P.act(qt[:], qt[:], AF.Copy, scale=0.125)
                P.tt("vector", g_[:], g_[:], bias16[:], ALU.add)
                P.copy("vector", li[:], g_[:, 0, :])
                P.act(sp[:], g_[:, 1, :], AF.Exp, scale=-1.0)
                P.act(sp[:], sp[:], AF.Ln, bias=1.0)
                for d in range(2):
                    P.mm(pD[:, 0, d * 4:(d + 1) * 4], TRI[d], sp[:, d * 4:(d + 1) * 4])
                P.act(b[:], pD[:, 0, :], AF.Copy, scale=-1.0)
                P.tt("vector", c[:], li[:], b[:], ALU.subtract)
                P.tt("vector", DG[:], ident[:].unsqueeze(1).broadcast_to([128, 8, 128]), c[:].unsqueeze(2).broadcast_to([128, 8, 128]), ALU.mult)
                for dh in range(8):
                    P.mm(pA[dh // 4][:, dh % 4, :], ones[:], DG[:, dh, :])
                for d in range(2):
                    P.tt("vector", Rm[:, d * 4:(d + 1) * 4, :], pA[d][:], NEGts[d].unsqueeze(1).broadcast_to([128, 4, 128]), ALU.add)
                P.red("vector", cmax[:], Rm[:], op=ALU.max)
                v24 = lambda t: t.rearrange("p (d h) -> p d h", d=2)
                mt = MB[:, :, 0, :]
                P.tt("vector", mt, v24(mprev[:]), v24(cmax[:]), ALU.max)
                P.tt("vector", mt, mt, v24(b[:]), ALU.add)
                P.copy("vector", MB[:, :, 1, :], v24(b[:]))
                P.tt("vector", v24(bm[:]), v24(b[:]), mt, ALU.subtract)
                P.tt("vector", ain[:], bm[:], mprev[:], ALU.add)
                P.act(ain[:], ain[:], AF.Exp)
                P.act(v24(en[:]), mt, AF.Exp, scale=-1.0)
                P.tt("vector", DG2[:], ident[:].unsqueeze(1).broadcast_to([128, 8, 128]), bm[:].unsqueeze(2).broadcast_to([128, 8, 128]), ALU.mult)
                for dh in range(8):
                    o_ = pA[dh // 4][:, dh % 4, :]
                    P.mm(o_, ones[:], DG2[:, dh, :], start=True, stop=False)
                    P.mm(o_, DG[:, dh, :], ones[:], start=False, stop=False)
                    P.mm(o_, ident[:], NEG[dh // 4], start=False, stop=True)
                for d in range(2):
                    P.act(ET[:, d * 4:(d + 1) * 4, :], pA[d][:], AF.Exp)
                for dh in range(8):
                    d, h = dh // 4, dh % 4
                    P.mm(pB[d][:, h, :], kt[:, d, h, :], qt[:, d, h, :])
                for d in range(2):
                    P.tt("vector", WT[:, d * 4:(d + 1) * 4, :], ET[:, d * 4:(d + 1) * 4, :], pB[d][:], ALU.mult)
                for dh in range(8):
                    d, h = dh // 4, dh % 4
                    pc, sl = pc_slot(dh)
                    P.mm(pc[:, sl, :], qt[:, d, h, :], Cst[:, dh, :])
                for (bk, lo, hi) in grp3:
                    P.tt("vector", tI[:, lo:hi, :], pC[bk][:, 0:hi - lo, :], ain[:, lo:hi].unsqueeze(2).broadcast_to([128, hi - lo, 129]), ALU.mult)
                for dh in range(8):
                    d, h = dh // 4, dh % 4
                    pc, sl = pc_slot(dh)
                    P.mm(pc[:, sl, :], WT[:, dh, :], vp[:, d, h, :])
                for (bk, lo, hi) in grp3:
                    P.tt("vector", numS[:, lo:hi, :], pC[bk][:, 0:hi - lo, :], tI[:, lo:hi, :], ALU.add)
                P.act(dab[:].unsqueeze(2), numS[:, :, 128:129], AF.Abs)
                P.tt("vector", dab[:], dab[:], en[:], ALU.max)
                P.recip(dab[:], dab[:])
                P.tt("vector", hh[:], numS[:, :, 0:128], dab[:].unsqueeze(2).broadcast_to([128, 8, 128]), ALU.mult)
                for d in range(2):
                    r0 = cd[d] * 128
                    P.dma(HS[d, r0:r0 + 128, :].rearrange("p (h e) -> p h e", h=4), hh[:, d * 4:(d + 1) * 4, :], wk=[(HS.name, d, cd[d])], eng=STQ)
                for d in range(2):
                    P.mm(pD[:, d, :].rearrange("p (a h) -> p a h", a=2).rearrange("p a h -> p (a h)"), sel[:, d, :], MB[:, d, :, :].rearrange("p a h -> p (a h)"))
                P.copy("vector", mbl[:].rearrange("p d a h -> p (d a h)"), pD[:].rearrange("p a e -> p (a e)"))
                P.tt("vector", v24(wk_[:]), mbl[:, :, 1, :], mbl[:, :, 0, :], ALU.subtract)
                P.tt("vector", dec[:], wk_[:], mprev[:], ALU.add)
                P.act(dec[:], dec[:], AF.Exp)
                P.tt("vector", wk_[:], wk_[:], c[:], ALU.add)
                P.act(wk_[:], wk_[:], AF.Exp)
                for d in range(2):
                    P.tt("vector", KW[:, d * 4:(d + 1) * 4, :], ktok[:, d, :, :], wk_[:, d * 4:(d + 1) * 4].unsqueeze(2).broadcast_to([128, 4, 64]), ALU.mult)
                for dh in range(8):
                    d, h = dh // 4, dh % 4
                    pc, sl = pc_slot(dh)
                    P.mm(pc[0:64, sl, :], KW[:, dh, :], vp[:, d, h, :])
                P.tt("vector", Cst[:], Cst[:], dec[0:64, :].unsqueeze(2).broadcast_to([64, 8, 129]), ALU.mult)
                for (bk, lo, hi) in grp3:
                    P.tt("vector", Cst[:, lo:hi, :], Cst[:, lo:hi, :], pC[bk][0:64, 0:hi - lo, :], ALU.add)
                P.copy("vector", v24(mprev[:]), mbl[:, :, 0, :])
            if not jb["ctx"]:
                P.dma(O["o_mC"][si, l].rearrange("d h k v -> k (d h) v"), Cst[:, :, 0:128], wk=[("o_mC", si, l)], eng=STQ, final=True)
                P.dma(O["o_mn"][si, l].rearrange("d h (k o) -> k (d h) o", o=1), Cst[:, :, 128:129], wk=[("o_mn", si, l)], eng=STQ, final=True, allow_slow_non_contiguous=True)
                P.dma(O["o_mm"][si, l].rearrange("d (h o) -> o (d h)", o=1), mprev[0:1, :], wk=[("o_mm", si, l)], eng=STQ, final=True, allow_slow_non_contiguous=True)
            P.pop()
            P.push()
            gm = bcast_row("gm", I["mlstm_norm"][l], 128)
            h0 = [P.sb(f"h0{i}", [128, 4, 128]) for i in range(2)]; h1 = [P.sb(f"h1{i}", [128, 4, 128]) for i in range(2)]
            og = [P.sb(f"og{i}", [128, 512]) for i in range(2)]
            junk = P.sb("junk", [128, 4, 128]); ss = P.sb("ss", [128, 4]); yT = [P.sb(f"yT{i}", [128, 4, 128], FDT) for i in range(2)]
            pyt = P.ps("pyt", [128, 4, 128])
            for i in range(NC):
                a_ = h0[i % 2]; b_ = h1[i % 2]; o_ = og[i % 2]; y_ = yT[i % 2]
                rows = slice(t_off + i * 128, t_off + (i + 1) * 128)
                P.dma(a_[:], HS[0, i * 128:(i + 1) * 128, :].rearrange("p (h e) -> p h e", h=4), rk=[HS.name + "*"])
                P.dma(b_[:], HS[1, i * 128:(i + 1) * 128, :].rearrange("p (h e) -> p h e", h=4), rk=[HS.name + "*"])
                P.dma(o_[:], jb["UT"][rows, C_MO:C_MO + 512], rk=[(jb["UT"].name, (t_off + i * 128) // 512, 1456)])
                P.tt("vector", a_[:], a_[:], b_[:], ALU.add)
                P.act(junk[:], a_[:], AF.Square)
                P.red("vector", ss[:], junk[:])
                rstd_of(ss[:], ss[:], 128, NORM_EPS)
                P.act(o_[:], o_[:], AF.Sigmoid)
                P.tt("vector", a_[:], a_[:], ss[:].unsqueeze(2).broadcast_to([128, 4, 128]), ALU.mult)
                P.tt("vector", a_[:], a_[:], gm[:].unsqueeze(1).broadcast_to([128, 4, 128]), ALU.mult)
                P.tt("vector", a_[:], a_[:], o_[:].rearrange("p (h e) -> p h e", h=4), ALU.mult)
                for h in range(4):
                    P.tr(pyt[:, h, :], a_[:, h, :], ident[:])
                P.copy("scalar", y_[:], pyt[:])
                c0 = t_off + i * 128
                P.dma(jb["YT"][1, :, c0:c0 + 128].rearrange("(c p) t -> p c t", p=128), y_[:], wk=[(jb["YT"].name, 1, c0)], eng=STQ)
            P.pop()


    def phaseD(l, jb):
        n = jb["name"]
        CW = RW_DECAY
        for si, (t_off, S) in enumerate(jb["seqs"]):
            NCH = S // 64
            DF = scr(f"D_F_{n}{l}_{si}", [2, NCH, 2, 64, 4, 4, 64])
            LW = scr(f"D_LW_{n}{l}_{si}", [S, 2, 512])
            BG = scr(f"D_BG_{n}{l}_{si}", [2, 512, S])
            YS = scr(f"D_YS_{n}{l}_{si}", [2, S, 512])
            P.push()
            ST = min(512, S)
            kk = P.sb("kk", [64, 8]); ka = P.sb("ka", [64, 8]); omka = P.sb("omka", [64, 8]); uu = P.sb("uu", [64, 2, 8]); a0 = P.sb("a0", [64, 2, 8])
            P.dma(kk[:], I["rwkv_kk64"][l]); P.dma(ka[:], I["rwkv_ka64"][l]); P.dma(uu[:], I["rwkv_u64"][l]); P.dma(a0[:], I["rwkv_a064"][l])
            P.ts("vector", omka[:], ka[:], -1.0, 1.0, op0=ALU.mult, op1=ALU.add)
            w2 = P.sb("w2", [64, 2, 512]); a2 = P.sb("a2", [64, 2, 512]); g2 = P.sb("g2", [128, 512])
            P.dma(w2[:], I["rwkv_w2"][l].rearrange("d r c -> r d c")); P.dma(a2[:], I["rwkv_a2"][l].rearrange("d r c -> r d c")); P.dma(g2[:], I["rwkv_g2"][l])
            w0row = bcast_row("w0row", I["rwkv_w0"][l].rearrange("d c -> (d c)"), 1024)
            rT = P.sb("rT", [64, 8, ST]); kT = P.sb("kT", [64, 8, ST]); vT = P.sb("vT", [64, 8, ST])
            w1T = P.sb("w1T", [64, 2, ST]); a1T = P.sb("a1T", [64, 2, ST]); g1T = P.sb("g1T", [128, ST])
            kap = P.sb("kap", [64, 8, ST]); kh = P.sb("kh", [64, 8, ST]); tA = P.sb("tA", [64, 8, ST]); tB = P.sb("tB", [64, 8, ST])
            ktt = P.sb("ktt", [64, 8, ST]); rku = P.sb("rku", [64, 8, ST]); lw = [P.sb(f"lw{i}", [128, 2, 512]) for i in range(2)]
            pp = [P.ps(f"pp{i}", [128, 512]) for i in range(4)]
            ip = [0]

            def nps():
                ip[0] += 1
                return pp[ip[0] % 4]

            def store_df(d, arr, tile, s0):
                for cc in range(ST // 64):
                    for hh in range(2):
                        P.dma(DF[d, s0 // 64 + cc, hh, :, arr, :, :], tile[:, hh * 4:(hh + 1) * 4, cc * 64:(cc + 1) * 64], wk=[(DF.name, d, arr, s0, cc, hh)], eng=STQ)

            for s_ in range(S // ST):
                s0 = s_ * ST
                c0 = t_off + s0
                sk = c0 // 512
                UF = jb["UF"]
                P.dma(rT[:], UF[C_RR:C_RR + 512, c0:c0 + ST].rearrange("(h p) t -> p h t", p=64), rk=[(UF.name, sk, 2736)])
                P.dma(kT[:], UF[C_RK:C_RK + 512, c0:c0 + ST].rearrange("(h p) t -> p h t", p=64), rk=[(UF.name, sk, 3248)])
                P.dma(vT[:], UF[C_RV:C_RV + 512, c0:c0 + ST].rearrange("(h p) t -> p h t", p=64), rk=[(UF.name, sk, 3760)])
                P.dma(w1T[:], UF[C_RW:C_RW + 128, c0:c0 + ST].rearrange("(d p) t -> p d t", p=64), rk=[(UF.name, sk, 4272)])
                P.dma(a1T[:], UF[C_RA:C_RA + 128, c0:c0 + ST].rearrange("(d p) t -> p d t", p=64), rk=[(UF.name, sk, 4272)])
                P.dma(g1T[:], UF[C_RG:C_RG + 128, c0:c0 + ST], rk=[(UF.name, sk, 4272)])
                P.act(w1T[:], w1T[:], AF.Tanh)
                P.act(g1T[:], g1T[:], AF.Sigmoid)
                P.tt("vector", kap[:], kT[:], kk[:].unsqueeze(2).broadcast_to([64, 8, ST]), ALU.mult)
                P.act(tA[:], kap[:], AF.Square)
                for h in range(8):
                    ps_ = nps()
                    P.mm(ps_[0:64, :ST], ones[0:64, 0:64], tA[:, h, :])
                    P.act(kh[:, h, :], ps_[0:64, :ST], AF.Sqrt, bias=1e-12)
                P.recip(kh[:], kh[:])
                P.tt("vector", kh[:], kh[:], kap[:], ALU.mult)
                for d in range(2):
                    store_df(d, 0, rT, s0); store_df(d, 1, kh, s0)
                for d in range(2):
                    for h in range(8):
                        ps_ = nps()
                        P.mm(ps_[0:64, :ST], a2[:, d, h * 64:(h + 1) * 64], a1T[:, d, :])
                        P.act(tA[:, h, :], ps_[0:64, :ST], AF.Sigmoid, bias=a0[:, d, h:h + 1])
                    P.tt("vector", tB[:], tA[:], kh[:], ALU.mult)
                    store_df(d, 3, tB, s0)
                    P.tt("vector", tA[:], tA[:], ka[:].unsqueeze(2).broadcast_to([64, 8, ST]), ALU.mult)
                    P.tt("vector", tA[:], tA[:], omka[:].unsqueeze(2).broadcast_to([64, 8, ST]), ALU.add)
                    P.tt("vector", ktt[:], kT[:], tA[:], ALU.mult)
                    store_df(d, 2, ktt, s0)
                    P.tt("vector", tA[:], ktt[:], rT[:], ALU.mult)
                    if d == 0:
                        P.tt("vector", rku[:], tA[:], uu[:, d, :].unsqueeze(2).broadcast_to([64, 8, ST]), ALU.mult)
                    else:
                        P.tt("vector", tA[:], tA[:], uu[:, d, :].unsqueeze(2).broadcast_to([64, 8, ST]), ALU.mult)
                        P.tt("vector", rku[:], rku[:], tA[:], ALU.add)
                for h in range(8):
                    ps_ = nps()
                    P.mm(ps_[0:64, :ST], ones[0:64, 0:64], rku[:, h, :])
                    P.tt("vector", tB[:, h, :], ps_[0:64, :ST], vT[:, h, :], ALU.mult)
                P.dma(BG[0, :, s0:s0 + ST].rearrange("(h p) t -> p h t", p=64), tB[:], wk=[(BG.name, 0, s0)], eng=STQ)
                for h in range(8):
                    ps_ = nps()
                    P.mm(ps_[0:64, :ST], g2[:, h * 64:(h + 1) * 64], g1T[:])
                    P.copy("scalar", kap[:, h, :], ps_[0:64, :ST])
                P.dma(BG[1, :, s0:s0 + ST].rearrange("(h p) t -> p h t", p=64), kap[:], wk=[(BG.name, 1, s0)], eng=STQ)
                for tt in range(ST // 128):
                    lw_ = lw[tt % 2]
                    for d in range(2):
                        ps_ = nps()
                        P.mm(ps_[:], w1T[:, d, tt * 128:(tt + 1) * 128], w2[:, d, :])
                        P.tt("vector", lw_[:, d, :], ps_[:], w0row[:, d * 512:(d + 1) * 512], ALU.add)
                    P.act(lw_[:], lw_[:], AF.Sigmoid)
                    P.dma(LW[s0 + tt * 128:s0 + (tt + 1) * 128], lw_[:], wk=[(LW.name, s0, tt)], eng=STQ)
            P.pop()
            P.push()
            tri = P.sb("tri", [128, 2, 128]); P.dma(tri[:], I["k_tri"].rearrange("a k q -> k a q"))
            trs = P.sb("trs", [128, 2, 128]); P.dma(trs[:], I["k_tris"].rearrange("a k q -> k a q"))
            HP = [slice(0, 64), slice(64, 128)]
            cum = P.sb("cum", [128, 2, 2, 64]); mask4 = P.sb("mask4", [128, 2, 4, 64]); maskT = P.sb("maskT", [128, 2, 64])
            for hh in range(2):
                pr = HP[hh]
                INC = [tri[pr, 1, pr], tri[pr, 0, pr]]; STR = [trs[pr, 1, pr], trs[pr, 0, pr]]
                for d in range(2):
                    P.copy("vector", cum[pr, d, 0, :], INC[d], wk=[cum]); P.copy("vector", cum[pr, d, 1, :], STR[d], wk=[cum])
                    for a_, m_ in enumerate([STR[d], INC[d], STR[d], INC[d]]):
                        P.copy("vector", mask4[pr, d, a_, :], m_, wk=[mask4])
                    P.copy("vector", maskT[pr, d, :], STR[1 - d], wk=[maskT])
            idh = [ident[HP[0], HP[0]], ident[HP[1], HP[1]]]
            id4 = P.sb("id4", [128, 4, 64])
            for hh in range(2):
                for h4 in range(4):
                    P.copy("vector", id4[HP[hh], h4, :], idh[hh], wk=[id4])
            TS = P.sb("TS", [128, 2, 4, 64])
            bG = P.ps("bG", [128, 512]); bX = [P.ps(f"bX{i}", [128, 512]) for i in range(2)]; bY = P.ps("bY", [128, 256])
            bA = P.ps("bA", [128, 512]); bP = P.ps("bP", [128, 256]); bZ = [P.ps(f"bZ{i}", [128, 256]) for i in range(2)]
            s0t = P.sb("s0t", [128, 2, 4, 64])

            def trm(out, in_, hh):
                P.mm(out, in_, idh[hh])

            if jb["ctx"]:
                for hh in range(2):
                    for d in range(2):
                        P.dma(s0t[HP[hh], d], I["rw"][l][d, hh * 4:(hh + 1) * 4].rearrange("h v k -> v h k"), wk=[s0t])
                for d in range(2):
                    for h in range(8):
                        hh, h4 = h // 4, h % 4
                        trm(bX[0][HP[hh], (d * 4 + h4) * 64:(d * 4 + h4 + 1) * 64], s0t[HP[hh], d, h4, :], hh)
                P.copy("vector", TS[:].rearrange("p d h v -> p (d h v)"), bX[0][:])
            else:
                P.memset("vector", TS[:], 0.0)
            Xd = [[P.sb(f"Xd{d}{i}", [128, 4, 4, 64]) for i in range(2)] for d in range(2)]
            Vd = [[P.sb(f"Vd{d}{i}", [128, 4, 64]) for i in range(2)] for d in range(2)]
            LWd = [[P.sb(f"LWd{d}{i}", [128, 256]) for i in range(2)] for d in range(2)]
            EI = P.sb("EI", [128, 4, 64]); EX = P.sb("EX", [128, 4, 64]); EN = P.sb("EN", [128, 4, 64]); gl = P.sb("gl", [128, 4])
            KR = P.sb("KR", [128, 4, 2, 64]); KtM = P.sb("KtM", [128, 4, 64]); BM = P.sb("BM", [128, 4, 64]); KBe = P.sb("KBe", [128, 4, 2, 64])
            AM = P.sb("AM", [128, 4, 4, 64]); N0 = P.sb("N0", [128, 4, 64])
            AB = [P.sb(f"AB{i}", [128, 4, 2, 64]) for i in range(2)]; PI = [P.sb(f"PI{i}", [128, 4, 64]) for i in range(2)]
            RH = P.sb("RH", [128, 4, 64]); Un = P.sb("Un", [128, 4, 64]); Yo = [P.sb(f"Yo{i}", [128, 4, 64]) for i in range(2)]
            KBt = P.sb("KBt", [128, 4, 2, 64])
            HH = [(h // 4, h % 4) for h in range(8)]

            def dpass(j, d):
                c = j if d == 0 else NCH - 1 - j
                X = Xd[d][j % 2]; V = Vd[d][j % 2]; LWc = LWd[d][j % 2]
                r0 = t_off + c * 64
                for hh in range(2):
                    pr = HP[hh]
                    P.dma(X[pr], DF[d, c, hh], rk=[DF.name + "*"], wk=[X])
                    P.dma(V[pr], jb["UT"][r0:r0 + 64, C_RV + hh * 256:C_RV + (hh + 1) * 256].rearrange("p (h e) -> p h e", h=4),
                          rk=[(jb["UT"].name, r0 // 512, 3760)], wk=[V])
                    P.dma(LWc[pr], LW[c * 64:(c + 1) * 64, d, hh * 256:(hh + 1) * 256], rk=[LW.name + "*"], wk=[LWc])
                Rr = X[:, 0, :, :]; Kh = X[:, 1, :, :]; Kt = X[:, 2, :, :]; Bb = X[:, 3, :, :]
                for (hh, h4) in HH:
                    pr = HP[hh]
                    P.mm(bG[pr, h4 * 128:(h4 + 1) * 128], LWc[pr, h4 * 64:(h4 + 1) * 64], cum[pr, d, :, :].rearrange("p a t -> p (a t)"))
                gv = bG[:].rearrange("p (h a t) -> p h a t", h=4, a=2)
                P.act(EI[:], gv[:, :, 0, :], AF.Exp, scale=-CW)
                P.act(EN[:], gv[:, :, 0, :], AF.Exp, scale=CW)
                P.act(EX[:], gv[:, :, 1, :], AF.Exp, scale=-CW)
                last = 63 if d == 0 else 0
                P.copy("vector", gl[:].unsqueeze(2), EI[:, :, last:last + 1])
                P.tt("vector", KR[:, :, 1, :], Rr, EI[:], ALU.mult)
                P.tt("vector", KR[:, :, 0, :], Kh, EX[:], ALU.mult)
                P.tt("vector", KtM[:], Kt, EN[:], ALU.mult)
                P.tt("vector", BM[:], Bb, EN[:], ALU.mult)
                P.tt("vector", KBe[:, :, 0, :], KtM[:], gl[:].unsqueeze(2).broadcast_to([128, 4, 64]), ALU.mult)
                P.tt("vector", KBe[:, :, 1, :], BM[:], gl[:].unsqueeze(2).broadcast_to([128, 4, 64]), ALU.mult)
                for (hh, h4) in HH:
                    pr = HP[hh]
                    o_ = bX[h4 // 2][pr, (h4 % 2) * 256:(h4 % 2 + 1) * 256]
                    rhs = KR[pr, h4, :, :].rearrange("p a t -> p (a t)")
                    P.mm(o_[:, 0:128], KtM[pr, h4, :], rhs)
                    P.mm(o_[:, 128:256], BM[pr, h4, :], rhs)
                for q in range(2):
                    P.tt("vector", AM[:, q * 2:(q + 1) * 2, :, :], bX[q][:].rearrange("p (h a t) -> p h a t", h=2, a=4),
                         mask4[:, d, :, :].unsqueeze(1).broadcast_to([128, 2, 4, 64]), ALU.mult)
                for (hh, h4) in HH:
                    pr = HP[hh]
                    P.mm(bY[pr, h4 * 64:(h4 + 1) * 64], KR[pr, h4, 0, :], BM[pr, h4, :])
                P.tt("vector", N0[:], bY[:].rearrange("p (h s) -> p h s", h=4), maskT[:, d, :].unsqueeze(1).broadcast_to([128, 4, 64]), ALU.mult)
                A_ = lambda pr, h4: AM[pr, h4, 2, :]
                B_ = lambda pr, h4: N0[pr, h4, :]
                Pc = PI[0]
                P.tt("vector", Pc[:], id4[:], AM[:, :, 2, :], ALU.subtract)
                for lv in range(1, 6):
                    ab = AB[lv % 2]
                    for (hh, h4) in HH:
                        pr = HP[hh]
                        o_ = bA[pr, h4 * 128:(h4 + 1) * 128]
                        if lv < 5:
                            P.mm(o_[:, 0:64], B_(pr, h4), A_(pr, h4))
                        P.mm(o_[:, 64:128], A_(pr, h4), B_(pr, h4))
                    src = bA[:].rearrange("p (h a t) -> p h a t", h=4, a=2)
                    if lv < 5:
                        P.copy("scalar", ab[:], src)
                    else:
                        P.copy("scalar", ab[:, :, 1, :], src[:, :, 1, :])
                    A_ = (lambda ab: (lambda pr, h4: ab[pr, h4, 0, :]))(ab)
                    B_ = (lambda ab: (lambda pr, h4: ab[pr, h4, 1, :]))(ab)
                    for (hh, h4) in HH:
                        pr = HP[hh]
                        o_ = bP[pr, h4 * 64:(h4 + 1) * 64]
                        P.mm(o_, B_(pr, h4), Pc[pr, h4, :], start=True, stop=False)
                        P.mm(o_, idh[hh], Pc[pr, h4, :], start=False, stop=True)
                    Pn = PI[lv % 2]
                    P.copy("vector", Pn[:].rearrange("p h t -> p (h t)"), bP[:])
                    Pc = Pn
                for (hh, h4) in HH:
                    pr = HP[hh]
                    o_ = bZ[0][pr, h4 * 64:(h4 + 1) * 64]
                    P.mm(o_, KR[pr, h4, 0, :], TS[pr, d, h4, :], start=True, stop=False)
                    P.mm(o_, AM[pr, h4, 0, :], V[pr, h4, :], start=False, stop=True)
                P.copy("scalar", RH[:].rearrange("p h t -> p (h t)"), bZ[0][:])
                for (hh, h4) in HH:
                    pr = HP[hh]
                    P.mm(bZ[1][pr, h4 * 64:(h4 + 1) * 64], Pc[pr, h4, :], RH[pr, h4, :])
                P.act(Un[:].rearrange("p h t -> p (h t)"), bZ[1][:], AF.Copy, scale=-1.0)
                Y_ = Yo[j % 2]
                for (hh, h4) in HH:
                    pr = HP[hh]
                    o_ = bZ[0][pr, h4 * 64:(h4 + 1) * 64]
                    P.mm(o_, KR[pr, h4, 1, :], TS[pr, d, h4, :], start=True, stop=False)
                    P.mm(o_, AM[pr, h4, 1, :], V[pr, h4, :], start=False, stop=False)
                    P.mm(o_, AM[pr, h4, 3, :], Un[pr, h4, :], start=False, stop=True)
                P.copy("scalar", Y_[:].rearrange("p h t -> p (h t)"), bZ[0][:])
                for hh in range(2):
                    P.dma(YS[d, c * 64:(c + 1) * 64, hh * 256:(hh + 1) * 256], Y_[HP[hh]].rearrange("p h t -> p (h t)"), wk=[(YS.name, d, c, hh)], eng=STQ)
                for (hh, h4) in HH:
                    pr = HP[hh]
                    for a_ in range(2):
                        trm(bX[0][pr, (h4 * 2 + a_) * 64:(h4 * 2 + a_ + 1) * 64], KBe[pr, h4, a_, :], hh)
                P.copy("vector", KBt[:].rearrange("p h a t -> p (h a t)"), bX[0][:])
                for (hh, h4) in HH:
                    pr = HP[hh]
                    o_ = bZ[1][pr, h4 * 64:(h4 + 1) * 64]
                    P.mm(o_, KBt[pr, h4, 0, :], V[pr, h4, :], start=True, stop=False)
                    P.mm(o_, KBt[pr, h4, 1, :], Un[pr, h4, :], start=False, stop=True)
                Td = TS[:, d, :, :]
                P.tt("vector", Td, Td, gl[:].unsqueeze(2).broadcast_to([128, 4, 64]), ALU.mult)
                P.tt("vector", Td, Td, bZ[1][:].rearrange("p (h t) -> p h t", h=4), ALU.add)

            for j in range(NCH):
                for d in range(2):
                    dpass(j, d)
            if not jb["ctx"]:
                for d in range(2):
                    for (hh, h4) in HH:
                        trm(bX[0][HP[hh], (d * 4 + h4) * 64:(d * 4 + h4 + 1) * 64], TS[HP[hh], d, h4, :], hh)
                P.copy("vector", s0t[:].rearrange("p d h k -> p (d h k)"), bX[0][:])
                for hh in range(2):
                    for d in range(2):
                        P.dma(O["o_rw"][si, l][d, hh * 4:(hh + 1) * 4].rearrange("h v k -> v h k"), s0t[HP[hh], d], wk=[("o_rw", si, l, hh, d)], eng=STQ, final=True)
            P.pop()
            P.push()
            gng = P.sb("gng", [128, 4]); gnb = P.sb("gnb", [128, 4])
            P.dma(gng[:], I["rwkv_gn_gT"][l]); P.dma(gnb[:], I["rwkv_gn_bT"][l])
            y0 = [P.sb(f"y0{i}", [128, 8, 64]) for i in range(2)]; y1 = [P.sb(f"y1{i}", [128, 8, 64]) for i in range(2)]
            bg = [P.sb(f"bg{i}", [128, 2, 4, 128]) for i in range(2)]
            junk = P.sb("junk", [128, 8, 64]); st8 = P.sb("st8", [128, 8]); yT = [P.sb(f"yT{i}", [128, 4, 128], FDT) for i in range(2)]
            ytmp = P.sb("ytmp", [128, 4, 128])
            pyt = P.ps("pyt", [128, 4, 128])
            for i in range(S // 128):
                a_ = y0[i % 2]; b_ = y1[i % 2]; g_ = bg[i % 2]; y_ = yT[i % 2]
                P.dma(a_[:], YS[0, i * 128:(i + 1) * 128, :].rearrange("p (h e) -> p h e", h=8), rk=[YS.name + "*"])
                P.dma(b_[:], YS[1, i * 128:(i + 1) * 128, :].rearrange("p (h e) -> p h e", h=8), rk=[YS.name + "*"])
                P.dma(g_[:], BG[:, :, i * 128:(i + 1) * 128].rearrange("a (c p) t -> p a c t", p=128), rk=[BG.name + "*"])
                P.tt("vector", a_[:], a_[:], b_[:], ALU.add)
                P.red("vector", st8[:], a_[:])
                P.ts("vector", st8[:], st8[:], -1.0 / 64, None, op0=ALU.mult)
                P.tt("vector", a_[:], a_[:], st8[:].unsqueeze(2).broadcast_to([128, 8, 64]), ALU.add)
                P.act(junk[:], a_[:], AF.Square)
                P.red("vector", st8[:], junk[:])
                rstd_of(st8[:], st8[:], 64, RW_GN_EPS)
                P.tt("vector", a_[:], a_[:], st8[:].unsqueeze(2).broadcast_to([128, 8, 64]), ALU.mult)
                for c in range(4):
                    P.tr(pyt[:, c, :], a_[:, 2 * c:2 * c + 2, :].rearrange("p h e -> p (h e)"), ident[:])
                for c in range(4):
                    P.act(ytmp[:, c, :], pyt[:, c, :], AF.Identity, bias=gnb[:, c:c + 1], scale=gng[:, c:c + 1])
                P.tt("vector", ytmp[:], ytmp[:], g_[:, 0, :, :], ALU.add)
                P.tt("vector", y_[:], ytmp[:], g_[:, 1, :, :], ALU.mult)
                c0 = t_off + i * 128
                P.dma(jb["YT"][3, :, c0:c0 + 128].rearrange("(c p) t -> p c t", p=128), y_[:], wk=[(jb["YT"].name, 3, c0)], eng=STQ)
            P.pop()


    def phaseM1(l, jb):
        TOK, j = jb["TOK"], jb["j"]
        ST = 512
        P.push()
        Yb = [P.sb(f"Yb{i}", [128, 4, 4, ST], FDT) for i in range(2)]
        Wo = [P.sb(f"Wo{i}", [128, 16, 128], FDT) for i in range(2)]
        Gt = [P.sb(f"Gt{i}", [128, 4, ST]) for i in range(2)]
        mg = P.sb("mg", [128, 8, ST], FDT); tmp = [P.sb(f"tmp{i}", [128, ST]) for i in range(3)]
        wo2 = [P.sb(f"wo2{i}", [128, 8, 128], FDT) for i in range(2)]
        xT = [P.sb(f"xT{i}", [128, 8, ST]) for i in range(2)]
        pp = [P.ps(f"pp{i}", [128, 512]) for i in range(6)]
        XTv = jb["XT"].rearrange("(k p) t -> p k t", p=128)
        wnames = ["mla_w_o", "mlstm_w_o", "swa_w_o", "rwkv_w_o"]
        ip = 0; io = 0
        for s_ in range(TOK // ST):
            t0 = s_ * ST
            Y_ = Yb[s_ % 2]; x_ = xT[s_ % 2]
            for b in range(4):
                P.dma(Y_[:, b, :, :], jb["YT"][b, :, t0:t0 + ST].rearrange("(c p) t -> p c t", p=128), rk=[jb["YT"].name + "*"], wk=[Y_])
            P.dma(x_[:], XTv[:, :, t0:t0 + ST], rk=[(jb["XT"].name, s_)])
            for oc in range(8):
                W_ = Wo[io % 2]; G_ = Gt[io % 2]; io += 1
                for b in range(4):
                    P.dma(W_[:, b * 4:(b + 1) * 4, :], WR[wnames[b]][l][:, oc * 128:(oc + 1) * 128].rearrange("(c p) n -> p c n", p=128), wk=[W_])
                P.dma(G_[:], jb["UF"][C_GATE:C_GATE + 4096, t0:t0 + ST].rearrange("(b o p) t -> p b o t", b=4, o=8)[:, :, oc, :], rk=[jb["UF"].name + "*"])
                for b in range(4):
                    ps_ = pp[ip % 6]; ip += 1
                    for c in range(4):
                        P.mm(ps_[:, :ST], W_[:, b * 4 + c, :], Y_[:, b, c, :], start=(c == 0), stop=(c == 3), fast=True)
                    if b == 0:
                        P.tt("vector", tmp[2][:], ps_[:, :ST], G_[:, b, :], ALU.mult)
                    else:
                        t_ = tmp[b % 2]
                        P.tt("vector", t_[:], ps_[:, :ST], G_[:, b, :], ALU.mult)
                        P.tt(PENG, mg[:, oc, :] if b == 3 else tmp[2][:], tmp[2][:], t_[:], ALU.add)
            for oc in range(8):
                w_ = wo2[oc % 2]
                P.dma(w_[:], WR["w_out"][l][:, oc * 128:(oc + 1) * 128].rearrange("(k p) n -> p k n", p=128))
                ps_ = pp[ip % 6]; ip += 1
                for k in range(8):
                    P.mm(ps_[:, :ST], w_[:, k, :], mg[:, k, :], start=(k == 0), stop=(k == 7), fast=True)
                P.stt("vector", x_[:, oc, :], ps_[:, :ST], modT[l][:, 16 + oc, j:j + 1], x_[:, oc, :], ALU.mult, ALU.add)
            P.dma(XTv[:, :, t0:t0 + ST], x_[:], wk=[(jb["XT"].name, s_)], eng=STQ)
        P.pop()

    def phaseM2(l, jb, last):
        TOK, j = jb["TOK"], jb["j"]
        ST = 512
        P.push()
        x1 = P.sb("x1", [128, 8, ST]); sq = P.sb("sq", [128, 8, ST]); h2 = P.sb("h2", [128, 8, ST], FDT); rstd = P.sb("rstd", [128, ST])
        w1c = [P.sb(f"w1c{i}", [128, 8, 128], FDT) for i in range(2)]
        hid = P.sb("hid", [128, 32, ST], FDT); rl = [P.sb(f"rl{i}", [128, ST]) for i in range(2)]
        w2c = [P.sb(f"w2c{i}", [128, 32, 128], FDT) for i in range(2)]
        ytok = [P.sb(f"ytok{i}", [128, D]) for i in range(2)]
        pst = P.ps("pst", [128, ST])
        pp = [P.ps(f"pp{i}", [128, 512]) for i in range(6)]
        XTv = jb["XT"].rearrange("(k p) t -> p k t", p=128)
        ip = 0; i1 = 0; i2 = 0
        for s_ in range(TOK // ST):
            t0 = s_ * ST
            P.dma(x1[:], XTv[:, :, t0:t0 + ST], rk=[(jb["XT"].name, s_)])
            rms_rstd_featmajor(x1, sq, pst, rstd, ST)
            P.tt("vector", sq[:], x1[:], rstd[:].unsqueeze(1).broadcast_to([128, 8, ST]), ALU.mult)
            for k in range(8):
                P.act(h2[:, k, :], sq[:, k, :], AF.Identity, bias=modT[l][:, 24 + k, j:j + 1], scale=A2[l][:, k, j:j + 1])
            for fc in range(32):
                w_ = w1c[i1 % 2]; i1 += 1
                P.dma(w_[:], WR["mlp_w1"][l][:, fc * 128:(fc + 1) * 128].rearrange("(k p) n -> p k n", p=128))
                ps_ = pp[ip % 6]; ip += 1
                for k in range(8):
                    P.mm(ps_[:, :ST], w_[:, k, :], h2[:, k, :], start=(k == 0), stop=(k == 7), fast=True)
                r_ = rl[fc % 2]
                P.act(r_[:], ps_[:, :ST], AF.Relu)
                P.tt(PENG, hid[:, fc, :], r_[:], r_[:], ALU.mult)
            x2 = sq
            for oc in range(8):
                w_ = w2c[i2 % 2]; i2 += 1
                for q in range(4):
                    P.dma(w_[:, q * 8:(q + 1) * 8, :], WR["mlp_w2"][l][q * 1024:(q + 1) * 1024, oc * 128:(oc + 1) * 128].rearrange("(f p) n -> p f n", p=128), wk=[w_])
                ps_ = pp[ip % 6]; ip += 1
                for fc in range(32):
                    P.mm(ps_[:, :ST], w_[:, fc, :], hid[:, fc, :], start=(fc == 0), stop=(fc == 31), fast=True)
                P.stt("vector", x2[:, oc, :], ps_[:, :ST], modT[l][:, 40 + oc, j:j + 1], x1[:, oc, :], ALU.mult, ALU.add)
            if not last:
                P.dma(XTv[:, :, t0:t0 + ST], x2[:], wk=[(jb["XT"].name, s_)], eng=STQ)
            else:
                for tt in range(ST // 128):
                    yt = ytok[tt % 2]
                    for kk in range(2):
                        ps_ = pp[ip % 6]; ip += 1
                        for k4 in range(4):
                            P.tr(ps_[:, k4 * 128:(k4 + 1) * 128], x2[:, kk * 4 + k4, tt * 128:(tt + 1) * 128], ident[:])
                        P.evac(yt[:, kk * 512:(kk + 1) * 512], ps_[:])
                    P.dma(jb["y"][t0 + tt * 128:t0 + (tt + 1) * 128, :], yt[:], wk=[("y", j, t0, tt)], eng=STQ, final=True)
        P.pop()

    import os
    stop = os.environ.get("KSTOP", "")
    WR = {}
    fast_w = ["w_in", "mla_w_o", "mlstm_w_o", "swa_w_o", "rwkv_w_o", "w_out", "mlp_w1", "mlp_w2"]
    if FAST_MM:
        P.push()
        CH = 2048
        raw = [P.sb(f"wraw{i}", [128, CH]) for i in range(3)]
        rnd = [P.sb(f"wrnd{i}", [128, CH], F32R) for i in range(3)]
        engs = ["gpsimd", "vector", "scalar"]
        iw = 0
        for name in fast_w:
            src = I[name]
            _, Rr, Cc = src.shape
            dst = nc.dram_tensor(name + "_r", [L, Rr, Cc], F32R, kind="Internal").ap()
            WR[name] = dst
            for l in range(L):
                for rb in range(Rr // 128):
                    for c0 in range(0, Cc, CH):
                        w = min(CH, Cc - c0)
                        a_ = raw[iw % 3]; b_ = rnd[iw % 3]
                        P.dma(a_[:, :w], src[l, rb * 128:(rb + 1) * 128, c0:c0 + w])
                        P.copy(engs[iw % 3], b_[:, :w], a_[:, :w])
                        P.dma(dst[l, rb * 128:(rb + 1) * 128, c0:c0 + w], b_[:, :w], wk=[(dst.name, l, rb, c0)], eng=STQ)
                        iw += 1
        P.pop()
    else:
        for name in fast_w:
            WR[name] = I[name]

    only = os.environ.get("KONLY", "")
    for l in range(L):
        phase0(l)
        for jb in jobs:
            phase1(l, jb, first=(l == 0))
        for nm, fn in (("A", phaseA), ("C", phaseC), ("B", phaseB), ("D", phaseD)):
            if only and nm not in only:
                continue
            for jb in jobs:
                fn(l, jb)
        if only and "M" not in only:
            break
        for jb in jobs:
            phaseM1(l, jb)
        for jb in jobs:
            phaseM2(l, jb, last=(l == L - 1))
    print("NREC", getattr(P, "nrec", 0), flush=True)
    P.emit()
    return nc, I, O, SCR


def _fm(v, width=8):
    v = np.asarray(v, np.float32)
    return np.ascontiguousarray(np.swapaxes(v.reshape(v.shape[:-1] + (width, 128)), -1, -2))


def _rope_table(S, R):
    q = R // 4
    t = np.arange(S)
    pr = (t // 64).astype(np.float32); pc = (t % 64).astype(np.float32)
    inv = (10000.0 ** (-np.arange(q, dtype=np.float32) / q)).astype(np.float32)
    ar = pr[:, None] * inv; ac = pc[:, None] * inv
    ang = np.concatenate([ar, ar, ac, ac], -1).astype(np.float32)
    sign = np.concatenate([-np.ones(q), np.ones(q), -np.ones(q), np.ones(q)]).astype(np.float32)
    return np.ascontiguousarray(np.stack([np.cos(ang), np.sin(ang) * sign], 1).astype(np.float32))


def make_in_map(inp, cfg, core):
    L = cfg.depth
    f = lambda a: np.ascontiguousarray(np.asarray(a, np.float32))
    n_p = cfg.n_p
    m = {}
    m["xs"] = f(inp["x_sample"][core])
    m["xp"] = f(inp["x_prompt"][core * n_p:(core + 1) * n_p].reshape(n_p * SP, D))
    cc = np.stack([np.asarray(inp["c"][core]), np.asarray(inp["c_ctx"])], 0)
    m["cT"] = f(cc.reshape(2, 8, 128).transpose(2, 1, 0))
    m["ckv"] = f(inp["cache_mla_ckv"][core]); m["ckr"] = f(inp["cache_mla_krope"][core])
    m["cswk"] = f(np.asarray(inp["cache_swa_k"][core]).reshape(L, CTX, 128))
    m["cswv"] = f(np.asarray(inp["cache_swa_v"][core]).reshape(L, CTX, 128))
    m["mC"] = f(inp["state_mlstm_C"][core]); m["mn"] = f(inp["state_mlstm_n"][core])
    m["mm"] = f(inp["state_mlstm_m"][core]); m["rw"] = f(inp["state_rwkv"][core])
    m["ada_w"] = f(inp["ada_w"]); m["ada_bT"] = _fm(inp["ada_b"], 48)
    m["norm1T"] = _fm(inp["norm1"]); m["norm2T"] = _fm(inp["norm2"])
    for k in ["w_in", "mla_q_a_norm", "mla_kv_a_norm", "mla_w_uq", "mla_w_ukv", "mla_q_norm", "mla_k_norm", "mla_w_o",
              "mlstm_norm", "mlstm_w_o", "swa_q_norm", "swa_k_norm", "swa_sink", "swa_w_o", "rwkv_w2", "rwkv_a2",
              "rwkv_g2", "rwkv_w_o", "w_out", "mlp_w1", "mlp_w2"]:
        m[k] = f(inp[k])
    m["mlstm_i_bias"] = f(np.asarray(inp["mlstm_i_bias"]).reshape(L, 8))
    m["mlstm_f_bias"] = f(np.asarray(inp["mlstm_f_bias"]).reshape(L, 8))
    for k in ["rwkv_gn_g", "rwkv_gn_b"]:
        m[k + "T"] = _fm(inp[k], 4)
    m["rwkv_w0"] = f(inp["rwkv_w0"])
    for k in ["rwkv_kk", "rwkv_ka"]:
        m[k + "64"] = f(np.asarray(inp[k]).reshape(L, 8, 64).transpose(0, 2, 1))
    for k in ["rwkv_a0", "rwkv_u"]:
        m[k + "64"] = f(np.asarray(inp[k]).reshape(L, 2, 8, 64).transpose(0, 3, 1, 2))
    m["k_ident"] = np.eye(128, dtype=np.float32)
    m["k_ones"] = np.ones((128, 128), np.float32)
    kq = np.arange(128)
    m["k_tri"] = np.stack([(kq[:, None] >= kq[None, :]), (kq[:, None] <= kq[None, :])], 0).astype(np.float32)
    m["k_tris"] = np.stack([(kq[:, None] > kq[None, :]), (kq[:, None] < kq[None, :])], 0).astype(np.float32)
    sel = np.zeros((2, 128, 128), np.float32); sel[0, 127, :] = 1.0; sel[1, 0, :] = 1.0
    m["k_sel"] = sel
    m["k_rope32"] = _rope_table(cfg.S_s, 32); m["k_rope64"] = _rope_table(cfg.S_s, 64)
    return m


_CACHE = {}


def kernel(**inputs):
    cfg = Cfg(S_s=4096, n_p=2, depth=2)
    n_cores = 8
    if "nc" not in _CACHE:
        _CACHE["nc"] = build(cfg)
    nc, I, O, SCR = _CACHE["nc"]
    in_maps = []
    for c in range(n_cores):
        m = make_in_map(inputs, cfg, c)
        in_maps.append({k: v for k, v in m.items() if k in I})
    res = run_bass_kernel_spmd(nc, in_maps, core_ids=list(range(n_cores)))
    R = res.results
    L = cfg.depth
    cat = lambda k: np.concatenate([np.asarray(r[k], np.float32) for r in R], 0)
    y_prompt = cat("y_p").reshape(16, SP, D)
    y_sample = np.stack([np.asarray(r["y_s"], np.float32) for r in R], 0)
    return (y_prompt, y_sample,
            cat("o_ckv"), cat("o_ckr"),
            cat("o_swk").reshape(16, L, SP, 2, 64), cat("o_swv").reshape(16, L, SP, 2, 64),
            cat("o_mC"), cat("o_mn"), cat("o_mm"), cat("o_rw"))
```

```python
import os
import numpy as np
import concourse.bass as bass
import concourse.mybir as mybir
from concourse.bass_utils import run_bass_kernel_spmd
from contextlib import ExitStack

F32 = mybir.dt.float32
F32R = mybir.dt.float32r
FAST_MM = os.environ.get("FAST_MM", "1") == "1"
FDT = F32R if FAST_MM else F32


def fr(ap):
    return ap.bitcast(F32R) if FAST_MM else ap
AF = mybir.ActivationFunctionType
ALU = mybir.AluOpType
AX = mybir.AxisListType

ENGS = ("sync", "scalar", "vector", "gpsimd", "tensor")
SEM_LIMIT = int(os.environ.get("SEM_LIMIT", 30000))
N_DMA_SEMS = 16
import os
STQ = os.environ.get("STQ", "gpsimd")
PENG = os.environ.get("PENG", "vector")

D = 1024
NORM_EPS = 1e-6
CTX = 256
SP = 256
D_IN = 8752
D_FF = 4096
MLA_SCALE = 96 ** -0.5
SW_SCALE = 64 ** -0.5
RW_DECAY = 0.6065306597126334
RW_GN_EPS = 64e-5
C_QA, C_KVA, C_KR = 0, 256, 384
C_MQ, C_MK, C_MV, C_MI, C_MF, C_MO = 416, 672, 928, 1440, 1448, 1456
C_SQ, C_SK, C_SV = 1968, 2480, 2608
C_RR, C_RK, C_RV, C_RW, C_RA, C_RG = 2736, 3248, 3760, 4272, 4400, 4528
C_GATE = 4656
NTOKC = 4656


class Op:
    __slots__ = ("eng", "fn", "deps", "is_dma", "signal", "sem", "cnt", "dsem_prev", "barriered")

    def __init__(self, eng, fn, is_dma):
        self.eng = eng
        self.fn = fn
        self.deps = []
        self.is_dma = is_dma
        self.signal = False
        self.sem = None
        self.cnt = 0
        self.dsem_prev = None
        self.barriered = False


class Prog:
    def __init__(self, nc):
        self.nc = nc
        self.ops = {e: [] for e in ENGS}
        self.lastw = {}
        self.readers = {}
        self.stacks = [ExitStack()]
        self.out_dmas = []
        self.uid = 0
        self.rr = 0
        self.psum_names = set()

    def sb(self, name, shape, dt=F32):
        self.uid += 1
        return self.stacks[-1].enter_context(self.nc.sbuf_tensor(f"{name}_{self.uid}", list(shape), dt))

    def ps(self, name, shape, dt=F32):
        self.uid += 1
        n = 1
        for d in shape[1:]:
            n *= d
        nb = (n * 4 + 2047) // 2048
        t = self.stacks[-1].enter_context(self.nc.psum_tensor(f"{name}_{self.uid}", [128, nb * 512], dt))
        self.psum_names.add(t.name)
        v = t[:shape[0], :n]
        if len(shape) == 3:
            v = v.rearrange("p (a b) -> p a b", a=shape[1])
        elif len(shape) == 4:
            v = v.rearrange("p (a b c) -> p a b c", a=shape[1], b=shape[2])
        return v

    def push(self):
        self.stacks.append(ExitStack())

    def pop(self):
        self.barrier()
        self.stacks.pop().close()

    @staticmethod
    def _key(k):
        if isinstance(k, (str, tuple)):
            return k
        return k.name

    def op(self, eng, fn, reads=(), writes=(), is_dma=False):
        self.nrec = getattr(self, "nrec", 0) + 1
        if self.nrec > int(os.environ.get("KMAXOPS", 10 ** 9)):
            return None
        o = Op(eng, fn, is_dma)
        if os.environ.get("KTRACE") and abs(self.nrec - int(os.environ["KTRACE"])) <= 6:
            print("OP", self.nrec, eng, [self._key(k) for k in writes], flush=True)
        rk = [self._key(k) for k in reads if k is not None and not isinstance(k, (int, float))]
        wk = [self._key(k) for k in writes]
        if eng != "tensor":
            wk = wk + [k for k in rk if k in self.psum_names and k not in wk]
        raw = set()
        deps = set()
        for k in rk:
            w = self.lastw.get(k)
            if w is not None:
                raw.add(w)
                deps.add(w)
        for k in wk:
            w = self.lastw.get(k)
            if w is not None:
                deps.add(w)
            for r in self.readers.get(k, ()):
                deps.add(r)
        for d in deps:
            if d.eng == eng and not d.is_dma and not is_dma:
                if eng == "tensor":
                    continue
            o.deps.append(d)
        for k in rk:
            self.readers.setdefault(k, []).append(o)
        for k in wk:
            self.lastw[k] = o
            self.readers[k] = []
        self.ops[eng].append(o)
        return o

    def barrier(self):
        lasts = [self.ops[e][-1] for e in ENGS if self.ops[e]]
        pend = [o for e in ("sync", "scalar", "gpsimd") for o in self.ops[e] if o.is_dma and not o.barriered]
        for o in pend:
            o.barriered = True
        for e in ENGS:
            b = Op(e, None, False)
            b.deps = lasts + pend
            self.ops[e].append(b)
        self.lastw.clear()
        self.readers.clear()

    def dma(self, out, in_, rk=None, wk=None, eng="sync", final=False, **kw):
        r = [in_] if rk is None else rk
        w = [out] if wk is None else wk
        o = self.op(eng, lambda e: e.dma_start(out=out, in_=in_, **kw), r, w, is_dma=True)
        if final and o is not None:
            self.out_dmas.append(o)
        return o

    def mm(self, out, lhsT, rhs, start=True, stop=True, rk=None, wk=None, fast=False):
        r = [lhsT, rhs] if rk is None else rk
        w = [out] if wk is None else wk
        return self.op("tensor", lambda e: e.matmul(out, lhsT=lhsT, rhs=rhs, start=start, stop=stop), r, w)

    def tr(self, out, in_, ident, rk=None, wk=None):
        r = [in_, ident] if rk is None else rk
        w = [out] if wk is None else wk
        return self.op("tensor", lambda e: e.transpose(out, in_, ident), r, w)

    def act(self, out, in_, func, bias=None, scale=None, accum=None, rk=None, wk=None, extra_r=()):
        kw = {}
        if bias is not None:
            kw["bias"] = bias
        if scale is not None:
            kw["scale"] = scale
        if accum is not None:
            kw["accum_out"] = accum
        r = ([in_, bias, scale] if rk is None else list(rk)) + list(extra_r)
        w = ([out] + ([accum] if accum is not None else [])) if wk is None else wk
        return self.op("scalar", lambda e: e.activation(out=out, in_=in_, func=func, **kw), r, w)

    def tt(self, eng, out, in0, in1, op, rk=None, wk=None):
        r = [in0, in1] if rk is None else rk
        w = [out] if wk is None else wk
        return self.op(eng, lambda e: e.tensor_tensor(out=out, in0=in0, in1=in1, op=op), r, w)

    def ts(self, eng, out, in0, s1, s2=None, op0=ALU.mult, op1=None, rk=None, wk=None):
        r = [in0, s1, s2] if rk is None else rk
        w = [out] if wk is None else wk
        if op1 is None:
            return self.op(eng, lambda e: e.tensor_scalar(out=out, in0=in0, scalar1=s1, scalar2=None, op0=op0), r, w)
        return self.op(eng, lambda e: e.tensor_scalar(out=out, in0=in0, scalar1=s1, scalar2=s2, op0=op0, op1=op1), r, w)

    def stt(self, eng, out, in0, scalar, in1, op0, op1, rk=None, wk=None):
        r = [in0, scalar, in1] if rk is None else rk
        w = [out] if wk is None else wk
        return self.op(eng, lambda e: e.scalar_tensor_tensor(out=out, in0=in0, scalar=scalar, in1=in1, op0=op0, op1=op1), r, w)

    def copy(self, eng, out, in_, rk=None, wk=None):
        r = [in_] if rk is None else rk
        w = [out] if wk is None else wk
        if eng == "scalar":
            return self.op(eng, lambda e: e.copy(out=out, in_=in_), r, w)
        return self.op(eng, lambda e: e.tensor_copy(out=out, in_=in_), r, w)

    def red(self, eng, out, in_, op=ALU.add, rk=None, wk=None):
        r = [in_] if rk is None else rk
        w = [out] if wk is None else wk
        return self.op(eng, lambda e: e.tensor_reduce(out=out, in_=in_, axis=AX.X, op=op), r, w)

    def recip(self, out, in_, rk=None, wk=None):
        r = [in_] if rk is None else rk
        w = [out] if wk is None else wk
        return self.op("vector", lambda e: e.reciprocal(out=out, in_=in_), r, w)

    def memset(self, eng, out, val, wk=None):
        w = [out] if wk is None else wk
        return self.op(eng, lambda e: e.memset(out, val), [], w)

    def evac(self, out, in_, rk=None, wk=None):
        self.rr += 1
        ev = os.environ.get("KEVAC", "alt")
        if ev == "alt":
            ev = "scalar" if self.rr % 2 else "vector"
        return self.copy(ev, out, in_, rk, wk)

    def emit(self):
        nc = self.nc
        if self.out_dmas:
            b = Op("sync", None, False)
            b.deps = list(self.out_dmas)
            self.ops["sync"].append(b)
        for e in ENGS:
            for o in self.ops[e]:
                for d in o.deps:
                    d.signal = True
        with ExitStack() as st:
            def newsem(nm):
                return st.enter_context(nc.semaphore(nm))
            for e in ENGS:
                cur, c, ep = None, 0, 0
                for o in self.ops[e]:
                    if o.is_dma or not o.signal or o.fn is None:
                        continue
                    if cur is None or c >= SEM_LIMIT:
                        cur = newsem(f"s_{e}_{ep}")
                        ep += 1
                        c = 0
                    c += 1
                    o.sem = cur
                    o.cnt = c
            for e in ENGS:
                dl = [o for o in self.ops[e] if o.is_dma]
                if not dl:
                    continue
                dsems = [newsem(f"dq_{e}_{i}") for i in range(N_DMA_SEMS)]
                dcnt = [0] * N_DMA_SEMS
                dlast = [None] * N_DMA_SEMS
                for di, o in enumerate(dl):
                    s_ = di % N_DMA_SEMS
                    if dcnt[s_] + 16 > SEM_LIMIT:
                        dsems[s_] = newsem(f"dq_{e}_{s_}_{di}")
                        dcnt[s_] = 0
                    dcnt[s_] += 16
                    o.sem = dsems[s_]
                    o.cnt = dcnt[s_]
                    o.dsem_prev = dlast[s_]
                    dlast[s_] = o
                    o.signal = True
            if os.environ.get("KSTATS"):
                for e in ENGS:
                    sig = [o for o in self.ops[e] if o.signal and not o.is_dma and o.fn is not None]
                    print("ENG", e, "ops", len(self.ops[e]), "signals", len(sig), "maxcnt", max([o.cnt for o in self.ops[e]] + [0]), flush=True)
            with nc.Block() as block:
                def run(e):
                    def body(eng):
                        seen = {}
                        for o in self.ops[e]:
                            need = {}
                            deps = o.deps
                            if o.is_dma and o.dsem_prev is not None:
                                deps = deps + [o.dsem_prev]
                            for d in deps:
                                if d.fn is None or d.sem is None:
                                    continue
                                nm = d.sem.name
                                if need.get(nm, (None, 0))[1] < d.cnt:
                                    need[nm] = (d.sem, d.cnt)
                            for nm, (s, c) in need.items():
                                if seen.get(nm, 0) >= c:
                                    continue
                                eng.wait_ge(s, c)
                                seen[nm] = c
                            if o.fn is None:
                                continue
                            ins = o.fn(eng)
                            if o.signal:
                                ins.then_inc(o.sem, 16 if o.is_dma else 1)
                    return body
                block.sync(run("sync"))
                block.scalar(run("scalar"))
                block.vector(run("vector"))
                block.gpsimd(run("gpsimd"))
                block.tensor(run("tensor"))
        while self.stacks:
            self.stacks.pop().close()


class Cfg:
    def __init__(self, S_s=4096, n_p=2, depth=2, debug=()):
        self.S_s = S_s
        self.n_p = n_p
        self.depth = depth
        self.debug = tuple(debug)


W_NAMES = ["ada_w", "w_in", "mla_w_uq", "mla_w_ukv", "mla_w_o", "mlstm_w_o", "swa_w_o", "rwkv_w2", "rwkv_a2",
           "rwkv_g2", "rwkv_w_o", "w_out", "mlp_w1", "mlp_w2"]

P1_BLOCKS = [
    (0, 416, "T"), (416, 672, "F"), (672, 928, "TF"), (928, 1440, "T"), (1440, 1456, "T"), (1456, 1968, "T"),
    (1968, 2480, "T"), (2480, 2736, "T"), (2736, 3248, "F"), (3248, 3760, "F"), (3760, 4272, "TF"),
    (4272, 4656, "F"),
] + [(4656 + 512 * i, 4656 + 512 * (i + 1), "G") for i in range(8)]


def build(cfg):
    nc = bass.Bass("TRN2", target_bir_lowering=False)
    if FAST_MM:
        nc.dge_precook = False
    L = cfg.depth
    S_s, n_p = cfg.S_s, cfg.n_p
    TOKP = n_p * SP
    P = Prog(nc)
    I = {}
    O = {}
    SCR = {}

    def din(name, shape):
        I[name] = nc.dram_tensor(name, list(shape), F32, kind="ExternalInput").ap()
        return I[name]

    def dout(name, shape):
        O[name] = nc.dram_tensor(name, list(shape), F32, kind="ExternalOutput").ap()
        return O[name]

    def scr(name, shape, dt=F32):
        kind = "ExternalOutput" if name in cfg.debug else "Internal"
        SCR[name] = nc.dram_tensor(name, list(shape), dt, kind=kind).ap()
        return SCR[name]

    din("xs", [S_s, D]); din("xp", [TOKP, D])
    din("cT", [128, 8, 2])
    din("ckv", [L, CTX, 128]); din("ckr", [L, CTX, 32]); din("cswk", [L, CTX, 128]); din("cswv", [L, CTX, 128])
    din("mC", [L, 2, 4, 64, 128]); din("mn", [L, 2, 4, 64]); din("mm", [L, 2, 4]); din("rw", [L, 2, 8, 64, 64])
    din("ada_w", [L, D, 6 * D]); din("ada_bT", [L, 128, 48]); din("norm1T", [L, 128, 8]); din("norm2T", [L, 128, 8])
    din("w_in", [L, D, D_IN])
    din("mla_q_a_norm", [L, 256]); din("mla_kv_a_norm", [L, 128]); din("mla_w_uq", [L, 256, 768])
    din("mla_w_ukv", [L, 128, 1024]); din("mla_q_norm", [L, 96]); din("mla_k_norm", [L, 96]); din("mla_w_o", [L, 512, D])
    din("mlstm_i_bias", [L, 8]); din("mlstm_f_bias", [L, 8]); din("mlstm_norm", [L, 128]); din("mlstm_w_o", [L, 512, D])
    din("swa_q_norm", [L, 64]); din("swa_k_norm", [L, 64]); din("swa_sink", [L, 8]); din("swa_w_o", [L, 512, D])
    din("rwkv_w0", [L, 2, 512]); din("rwkv_w2", [L, 2, 64, 512]); din("rwkv_a064", [L, 64, 2, 8])
    din("rwkv_a2", [L, 2, 64, 512]); din("rwkv_g2", [L, 128, 512]); din("rwkv_kk64", [L, 64, 8]); din("rwkv_ka64", [L, 64, 8])
    din("rwkv_u64", [L, 64, 2, 8]); din("rwkv_gn_gT", [L, 128, 4]); din("rwkv_gn_bT", [L, 128, 4]); din("rwkv_w_o", [L, 512, D])
    din("w_out", [L, D, D]); din("mlp_w1", [L, D, D_FF]); din("mlp_w2", [L, D_FF, D])
    din("k_ident", [128, 128]); din("k_ones", [128, 128])
    din("k_rope32", [S_s, 2, 32]); din("k_rope64", [S_s, 2, 64]); din("k_tri", [2, 128, 128]); din("k_sel", [2, 128, 128]); din("k_tris", [2, 128, 128])
    dout("y_p", [TOKP, D]); dout("y_s", [S_s, D])
    dout("o_ckv", [n_p, L, SP, 128]); dout("o_ckr", [n_p, L, SP, 32]); dout("o_swk", [n_p, L, SP, 128]); dout("o_swv", [n_p, L, SP, 128])
    dout("o_mC", [n_p, L, 2, 4, 64, 128]); dout("o_mn", [n_p, L, 2, 4, 64]); dout("o_mm", [n_p, L, 2, 4]); dout("o_rw", [n_p, L, 2, 8, 64, 64])

    jobs = [dict(name="s", TOK=S_s, seqs=[(0, S_s)], ctx=True, j=0, x=I["xs"], y=O["y_s"]),
            dict(name="p", TOK=TOKP, seqs=[(i * SP, SP) for i in range(n_p)], ctx=False, j=1, x=I["xp"], y=O["y_p"])]
    for jb in jobs:
        n = jb["name"]
        jb["XT"] = scr(f"XT_{n}", [D, jb["TOK"]])
        jb["UT"] = scr(f"UTOK_{n}", [jb["TOK"], NTOKC])
        jb["UF"] = scr(f"UFEAT_{n}", [D_IN, jb["TOK"]])
        jb["YT"] = scr(f"YT_{n}", [4, 512, jb["TOK"]], FDT)

    ident = P.sb("ident", [128, 128]); ones = P.sb("ones", [128, 128])
    P.dma(ident[:], I["k_ident"][:, :], wk=[ident]); P.dma(ones[:], I["k_ones"][:, :], wk=[ones])
    cT = P.sb("cT", [128, 8, 2]); sT = P.sb("sT", [128, 8, 2])
    P.dma(cT[:], I["cT"][:, :, :])
    P.act(sT[:], cT[:], AF.Silu)
    modT = [P.sb(f"modT{l}", [128, 48, 2]) for l in range(L)]
    A1 = [P.sb(f"A1_{l}", [128, 8, 2]) for l in range(L)]
    A2 = [P.sb(f"A2_{l}", [128, 8, 2]) for l in range(L)]

    def phase0(l):
        P.push()
        wb = [P.sb(f"adaw{i}", [128, 8, 512]) for i in range(2)]
        pm = P.ps("pm", [128, 48, 2])
        bT = P.sb("bT", [128, 48]); n1 = P.sb("n1", [128, 8]); n2 = P.sb("n2", [128, 8])
        P.dma(bT[:], I["ada_bT"][l]); P.dma(n1[:], I["norm1T"][l]); P.dma(n2[:], I["norm2T"][l])
        wv = I["ada_w"][l].rearrange("(k p) c -> p k c", p=128)
        for g in range(12):
            w = wb[g % 2]
            P.dma(w[:], wv[:, :, g * 512:(g + 1) * 512])
            for jj in range(4):
                jc = g * 4 + jj
                for k in range(8):
                    P.mm(pm[:, jc, :], w[:, k, jj * 128:(jj + 1) * 128], sT[:, k, :], start=(k == 0), stop=(k == 7))
        P.tt("vector", modT[l][:], pm[:], bT[:].unsqueeze(2).broadcast_to([128, 48, 2]), ALU.add)
        P.stt("vector", A1[l][:], modT[l][:, 8:16, :], 1.0, n1[:].unsqueeze(2).broadcast_to([128, 8, 2]), ALU.add, ALU.mult)
        P.stt("vector", A2[l][:], modT[l][:, 32:40, :], 1.0, n2[:].unsqueeze(2).broadcast_to([128, 8, 2]), ALU.add, ALU.mult)
        P.pop()

    def rms_rstd_featmajor(xT, sq, pst, rstd, n):
        P.act(sq[:, :, :n], xT[:, :, :n], AF.Square)
        for k in range(8):
            P.mm(pst[:, :n], ones[:], sq[:, k, :n], start=(k == 0), stop=(k == 7))
        P.act(rstd[:, :n], pst[:, :n], AF.Sqrt, bias=NORM_EPS, scale=1.0 / D)
        P.recip(rstd[:, :n], rstd[:, :n])

    def phase1(l, jb, first):
        TOK, j = jb["TOK"], jb["j"]
        ST = 512
        P.push()
        xT = [P.sb(f"xT{i}", [128, 8, ST]) for i in range(2)]
        hT = [P.sb(f"hT{i}", [128, 8, ST], FDT) for i in range(2)]
        sq = P.sb("sq", [128, 8, ST]); rstd = P.sb("rstd", [128, ST])
        wt = [P.sb(f"wt{i}", [128, 8, 512], FDT) for i in range(3)]
        stg = [P.sb(f"stg{i}", [128, 4, 512]) for i in range(3)]
        xin = [P.sb(f"xin{i}", [128, D]) for i in range(2)] if first else None
        pst = P.ps("pst", [128, ST])
        pp = [P.ps(f"pp{i}", [128, 512]) for i in range(6)]
        XTv = jb["XT"].rearrange("(k p) t -> p k t", p=128)
        wv = WR["w_in"][l].rearrange("(k p) c -> p k c", p=128)
        UTv = jb["UT"].rearrange("(tt p) c -> p tt c", p=128)
        ip = 0
        ist = 0
        iw = 0
        for s in range(TOK // ST):
            x_ = xT[s % 2]; h_ = hT[s % 2]
            t0 = s * ST
            if first:
                for tt in range(4):
                    xi = xin[tt % 2]
                    P.dma(xi[:], jb["x"][t0 + tt * 128:t0 + (tt + 1) * 128, :])
                    for kk in range(2):
                        pq = pp[ip % 6]; ip += 1
                        for k4 in range(4):
                            P.tr(pq[:, k4 * 128:(k4 + 1) * 128], xi[:, (kk * 4 + k4) * 128:(kk * 4 + k4 + 1) * 128], ident[:])
                        P.evac(x_[:, kk * 4:(kk + 1) * 4, tt * 128:(tt + 1) * 128], pq[:].rearrange("p (a b) -> p a b", a=4))
                P.dma(XTv[:, :, t0:t0 + ST], x_[:], wk=[(jb["XT"].name, s)], eng=STQ)
            else:
                P.dma(x_[:], XTv[:, :, t0:t0 + ST], rk=[(jb["XT"].name, s)])
            rms_rstd_featmajor(x_, sq, pst, rstd, ST)
            P.tt("vector", sq[:], x_[:], rstd[:].unsqueeze(1).broadcast_to([128, 8, ST]), ALU.mult)
            for k in range(8):
                P.act(h_[:, k, :], sq[:, k, :], AF.Identity, bias=modT[l][:, k, j:j + 1], scale=A1[l][:, k, j:j + 1])
            for (cs, ce, lay) in P1_BLOCKS:
                wd = ce - cs
                w = wt[iw % 3]; iw += 1
                P.dma(w[:, :, :wd], wv[:, :, cs:ce])
                if "T" in lay:
                    sg = stg[ist % 3]; ist += 1
                    for tt in range(4):
                        pq = pp[ip % 6]; ip += 1
                        for k in range(8):
                            P.mm(pq[:, :wd], h_[:, k, tt * 128:(tt + 1) * 128], w[:, k, :wd], start=(k == 0), stop=(k == 7), fast=(wd >= 256))
                        P.evac(sg[:, tt, :wd], pq[:, :wd])
                    P.dma(UTv[:, s * 4:(s + 1) * 4, cs:ce], sg[:, :, :wd], wk=[(jb["UT"].name, s, cs)], eng=STQ)
                if "F" in lay or "G" in lay:
                    sg = stg[ist % 3]; ist += 1
                    nb = wd // 128
                    for cb in range(nb):
                        pq = pp[ip % 6]; ip += 1
                        for k in range(8):
                            P.mm(pq[:, :ST], w[:, k, cb * 128:(cb + 1) * 128], h_[:, k, :], start=(k == 0), stop=(k == 7), fast=True)
                        if lay == "G":
                            P.act(sg[:, cb, :], pq[:, :ST], AF.Sigmoid)
                        else:
                            P.evac(sg[:, cb, :], pq[:, :ST])
                    P.dma(jb["UF"][cs:ce, t0:t0 + ST].rearrange("(cb p) t -> p cb t", p=128), sg[:, :nb, :],
                          wk=[(jb["UF"].name, s, cs)], eng=STQ)
        P.pop()


    def rope(x, cos, sinS, H, q, t1, t2):
        R4 = 4 * q
        t1v = t1[:, :H * R4].rearrange("p (h r) -> p h r", h=H)
        t2v = t2[:, :H * R4].rearrange("p (h a b c) -> p h a b c", h=H, a=2, b=2)
        xv = x.rearrange("p h (a b c) -> p h a b c", a=2, b=2)
        sv = sinS.rearrange("p (a b c) -> p a b c", a=2, b=2)
        P.tt("vector", t1v, x, cos.unsqueeze(1).broadcast_to([128, H, R4]), ALU.mult)
        for b in range(2):
            P.tt(PENG, t2v[:, :, :, b, :], xv[:, :, :, 1 - b, :],
                 sv[:, :, b, :].unsqueeze(1).broadcast_to([128, H, 2, q]), ALU.mult)
        P.tt("vector", x, t1v, t2[:, :H * R4].rearrange("p (h r) -> p h r", h=H), ALU.add)

    def rstd_of(out, ssq, n, eps):
        P.act(out, ssq, AF.Sqrt, bias=eps, scale=1.0 / n)
        P.recip(out, out)

    def bcast_row(name, src_row, n):
        t = P.sb(name, [128, n])
        P.dma(t[:], src_row.partition_broadcast(128))
        return t

    def phaseA(l, jb):
        TOK = jb["TOK"]
        for si, (t_off, S) in enumerate(jb["seqs"]):
            NK = S + (CTX if jb["ctx"] else 0)
            NKT = NK // 128
            n = jb["name"]
            KTs = scr(f"A_KT_{n}{l}_{si}", [8, 96, NK], FDT); VPs = scr(f"A_VP_{n}{l}_{si}", [NK, 8, 65], FDT); QTs = scr(f"A_QT_{n}{l}_{si}", [8, 96, S], FDT)
            P.push()
            gqa = bcast_row("gqa", I["mla_q_a_norm"][l], 256); gkv = bcast_row("gkv", I["mla_kv_a_norm"][l], 128)
            gqn = bcast_row("gqn", I["mla_q_norm"][l], 96); gkn = bcast_row("gkn", I["mla_k_norm"][l], 96)
            wuq = P.sb("wuq", [128, 2, 768]); wukv = P.sb("wukv", [128, 1024])
            P.dma(wuq[:], I["mla_w_uq"][l].rearrange("(k p) c -> p k c", p=128)); P.dma(wukv[:], I["mla_w_ukv"][l])
            ua = [P.sb(f"ua{i}", [128, 416]) for i in range(2)]
            rt = [P.sb(f"rt{i}", [128, 2, 32]) for i in range(2)]
            junk = P.sb("junk", [128, 768]); junk2 = P.sb("junk2", [128, 768])
            st = P.sb("st", [128, 4]); ss16 = P.sb("ss16", [128, 16]); ssr = P.sb("ssr", [128, 1])
            qlat = P.sb("qlat", [128, 256]); ckv = [P.sb(f"ckv{i}", [128, 128]) for i in range(2)]
            qlT = P.sb("qlT", [128, 2, 128]); ckT = P.sb("ckT", [128, 128])
            qf = P.sb("qf", [128, 8, 96]); kvf = P.sb("kvf", [128, 8, 128]); kn = P.sb("kn", [128, 8, 96])
            kr = P.sb("kr", [128, 32])
            vp = [P.sb(f"vp{i}", [128, 8, 65], FDT) for i in range(2)]
            qT = [P.sb(f"qT{i}", [96, 8, 128], FDT) for i in range(2)]; kT = [P.sb(f"kT{i}", [96, 8, 128], FDT) for i in range(2)]
            for v in vp:
                P.copy("vector", v[:, :, 64:65], ones[:, 0:8].unsqueeze(2), wk=[v])
            ptr = P.ps("ptr", [128, 3, 128]); pq1 = P.ps("pq1", [128, 512]); pq2 = P.ps("pq2", [128, 256])
            pk1 = P.ps("pk1", [128, 512]); pk2 = P.ps("pk2", [128, 512])
            pT = [P.ps(f"pT{i}", [96, 4, 128]) for i in range(2)]
            tiles = [("new", i) for i in range(S // 128)] + ([("ctx", i) for i in range(CTX // 128)] if jb["ctx"] else [])
            for it, (kind, i) in enumerate(tiles):
                if it >= int(os.environ.get("KA1T", 999)):
                    break
                u = ua[it % 2]; ck = ckv[it % 2]; v_ = vp[it % 2]; r_ = rt[it % 2]
                if os.environ.get("KSTATS"):
                    print("A1 tile start", jb["name"], si, it, getattr(P, "nrec", 0))
                new = kind == "new"
                rows = slice(t_off + i * 128, t_off + (i + 1) * 128)
                krow = i * 128 if new else S + i * 128
                if new:
                    P.dma(u[:], jb["UT"][rows, 0:416], rk=[(jb["UT"].name, (t_off + i * 128) // 512, 0)])
                    if jb["ctx"]:
                        P.dma(r_[:], I["k_rope32"][i * 128:(i + 1) * 128])
                    P.act(junk[:, :256], u[:, 0:256], AF.Square, accum=st[:, 0:1])
                    P.act(junk[:, :128], u[:, 256:384], AF.Square, accum=st[:, 1:2])
                    P.act(st[:, 2:3], st[:, 0:1], AF.Sqrt, bias=NORM_EPS, scale=1.0 / 256)
                    P.act(st[:, 3:4], st[:, 1:2], AF.Sqrt, bias=NORM_EPS, scale=1.0 / 128)
                    P.recip(st[:, 2:4], st[:, 2:4])
                    P.stt("vector", qlat[:], u[:, 0:256], st[:, 2:3], gqa[:], ALU.mult, ALU.mult)
                    P.stt("vector", ck[:], u[:, 256:384], st[:, 3:4], gkv[:], ALU.mult, ALU.mult)
                    if not jb["ctx"]:
                        P.dma(O["o_ckv"][si, l, i * 128:(i + 1) * 128, :], ck[:], wk=[("o_ckv", si, l, i)], eng=STQ, final=True)
                        P.dma(O["o_ckr"][si, l, i * 128:(i + 1) * 128, :], u[:, 384:416], wk=[("o_ckr", si, l, i)], eng=STQ, final=True)
                    for k in range(2):
                        P.tr(ptr[:, k, :], qlat[:, k * 128:(k + 1) * 128], ident[:])
                    P.tr(ptr[:, 2, :], ck[:], ident[:])
                    P.evac(qlT[:], ptr[:, 0:2, :]); P.evac(ckT[:], ptr[:, 2, :])
                    for k in range(2):
                        P.mm(pq1[:], qlT[:, k, :], wuq[:, k, 0:512], start=(k == 0), stop=(k == 1))
                    for k in range(2):
                        P.mm(pq2[:], qlT[:, k, :], wuq[:, k, 512:768], start=(k == 0), stop=(k == 1))
                    qff = qf[:].rearrange("p h d -> p (h d)")
                    P.evac(qff[:, 0:512], pq1[:]); P.evac(qff[:, 512:768], pq2[:])
                    krope = u[:, 384:416]
                else:
                    P.dma(ck[:], I["ckv"][l, i * 128:(i + 1) * 128, :])
                    P.dma(u[:, 384:416], I["ckr"][l, i * 128:(i + 1) * 128, :])
                    P.tr(ptr[:, 2, :], ck[:], ident[:])
                    P.evac(ckT[:], ptr[:, 2, :])
                    krope = u[:, 384:416]
                P.mm(pk1[:], ckT[:], wukv[:, 0:512]); P.mm(pk2[:], ckT[:], wukv[:, 512:1024])
                kvff = kvf[:].rearrange("p h d -> p (h d)")
                P.evac(kvff[:, 0:512], pk1[:]); P.evac(kvff[:, 512:1024], pk2[:])
                j2 = junk2[:, :512].rearrange("p (h d) -> p h d", h=8)
                P.act(j2, kvf[:, :, 0:64], AF.Square)
                P.red("vector", ss16[:, 8:16], j2)
                P.act(junk[:, :32], krope, AF.Square, accum=ssr[:, 0:1])
                P.ts("vector", ss16[:, 8:16], ss16[:, 8:16], ssr[:, 0:1], None, op0=ALU.add)
                if new:
                    P.act(junk[:].rearrange("p (h d) -> p h d", h=8), qf[:], AF.Square)
                    P.red("vector", ss16[:, 0:8], junk[:].rearrange("p (h d) -> p h d", h=8))
                else:
                    P.memset("vector", ss16[:, 0:8], 1.0)
                rstd_of(ss16[:], ss16[:], 96, NORM_EPS)
                P.tt("vector", kn[:, :, 0:64], kvf[:, :, 0:64], ss16[:, 8:16].unsqueeze(2).broadcast_to([128, 8, 64]), ALU.mult)
                P.tt(PENG, kn[:, :, 0:64], kn[:, :, 0:64], gkn[:, 0:64].unsqueeze(1).broadcast_to([128, 8, 64]), ALU.mult)
                P.tt("vector", kr[:], krope, gkn[:, 64:96], ALU.mult)
                if new and jb["ctx"]:
                    rope(kr[:].unsqueeze(1), r_[:, 0, :], r_[:, 1, :], 1, 8, junk, junk2)
                P.tt("vector", kn[:, :, 64:96], kr[:].unsqueeze(1).broadcast_to([128, 8, 32]),
                     ss16[:, 8:16].unsqueeze(2).broadcast_to([128, 8, 32]), ALU.mult)
                P.copy(PENG, v_[:, :, 0:64], kvf[:, :, 64:128])
                P.dma(VPs[krow:krow + 128], v_[:], wk=[(VPs.name, krow)], eng=STQ)
                kt_ = kT[it % 2]
                for hh in range(2):
                    for h4 in range(4):
                        P.tr(pT[hh][:, h4, :], kn[:, hh * 4 + h4, :], ident[:])
                    P.evac(kt_[:, hh * 4:(hh + 1) * 4, :], pT[hh][:])
                P.dma(KTs[:, :, krow:krow + 128].rearrange("h d t -> d h t"), kt_[:], wk=[(KTs.name, krow)], eng=STQ)
                if new:
                    P.tt("vector", qf[:], qf[:], ss16[:, 0:8].unsqueeze(2).broadcast_to([128, 8, 96]), ALU.mult)
                    P.tt(PENG, qf[:], qf[:], gqn[:].unsqueeze(1).broadcast_to([128, 8, 96]), ALU.mult)
                    if jb["ctx"]:
                        rope(qf[:, :, 64:96], r_[:, 0, :], r_[:, 1, :], 8, 8, junk, junk2)
                    qt_ = qT[it % 2]
                    for hh in range(2):
                        for h4 in range(4):
                            P.tr(pT[hh][:, h4, :], qf[:, hh * 4 + h4, :], ident[:])
                        P.evac(qt_[:, hh * 4:(hh + 1) * 4, :], pT[hh][:])
                    P.dma(QTs[:, :, i * 128:(i + 1) * 128].rearrange("h d t -> d h t"), qt_[:], wk=[(QTs.name, i)], eng=STQ)
            P.pop()
            if os.environ.get("KSTOP", "") == "a1":
                continue
            P.push()
            QC = 256
            KT = [P.sb(f"KT{i}", [96, NK], FDT) for i in range(2)]
            VP = [P.sb(f"VP{i}", [128, NKT, 65], FDT) for i in range(2)]
            QT = [P.sb(f"QT{i}", [96, QC], FDT) for i in range(3)]
            pTs = [P.sb(f"pTs{i}", [128, NKT, QC], FDT) for i in range(2)]
            oT = [P.sb(f"oT{i}", [65, QC], FDT) for i in range(2)]; rec = P.sb("rec", [64, QC]); yT = [P.sb(f"yT{i}", [64, QC], FDT) for i in range(2)]
            sel65 = P.sb("sel65", [65, 64], FDT)
            sel65f = P.sb("sel65f", [65, 64])
            P.memset("vector", sel65f[:], 0.0); P.memset("vector", sel65f[64:65, :], 1.0)
            P.copy("vector", sel65[:], sel65f[:])
            pss = [P.ps(f"pss{i}", [128, 512]) for i in range(4)]
            po = [P.ps(f"po{i}", [65, QC]) for i in range(2)]
            pb = P.ps("pb", [64, QC])
            units = [(h, qc) for h in range(8) for qc in range(S // QC)]
            ik = [0]

            def qk(iu):
                h, qc = units[iu]
                K_ = KT[h % 2]; V_ = VP[h % 2]
                if qc == 0:
                    P.dma(K_[:], KTs[h], rk=[KTs.name + "*"])
                    P.dma(V_[:], VPs[:, h, :].rearrange("(kt p) c -> p kt c", p=128), rk=[VPs.name + "*"])
                Q_ = QT[iu % 3]; p_ = pTs[iu % 2]
                P.dma(Q_[:], QTs[h, :, qc * QC:(qc + 1) * QC], rk=[QTs.name + "*"])
                for kt in range(NKT):
                    ps_ = pss[ik[0] % 4]; ik[0] += 1
                    P.mm(ps_[:, :QC], K_[:, kt * 128:(kt + 1) * 128], Q_[:], fast=True)
                    P.act(p_[:, kt, :], ps_[:, :QC], AF.Exp, scale=MLA_SCALE)

            def pv(iu):
                h, qc = units[iu]
                V_ = VP[h % 2]; p_ = pTs[iu % 2]; po_ = po[iu % 2]; o_ = oT[iu % 2]; yT_ = yT[iu % 2]
                for kt in range(NKT):
                    P.mm(po_[:], V_[:, kt, :], p_[:, kt, :], start=(kt == 0), stop=(kt == NKT - 1), fast=True)
                P.copy("scalar", o_[:], po_[:])
                P.mm(pb[:], sel65[:], o_[:], fast=True)
                P.recip(rec[:], pb[:])
                P.tt("vector", yT_[:], o_[0:64, :], rec[:], ALU.mult)
                c0 = t_off + qc * QC
                P.dma(jb["YT"][0, h * 64:(h + 1) * 64, c0:c0 + QC], yT_[:], wk=[(jb["YT"].name, 0, h, c0)], eng=STQ)

            qk(0)
            for iu in range(len(units)):
                if iu + 1 < len(units):
                    qk(iu + 1)
                pv(iu)
            P.pop()


    def phaseC(l, jb):
        for si, (t_off, S) in enumerate(jb["seqs"]):
            NT = S // 128
            NCT = (CTX // 128) if jb["ctx"] else 0
            NKT = NT + NCT
            P.push()
            gq = bcast_row("gq", I["swa_q_norm"][l], 64); gk = bcast_row("gk", I["swa_k_norm"][l], 64)
            esink = bcast_row("esink", I["swa_sink"][l], 8)
            P.act(esink[:], esink[:], AF.Exp)
            tri = P.sb("tri", [128, 2, 128])
            P.dma(tri[:], I["k_tri"].rearrange("a k q -> k a q"))
            KTa = P.sb("KTa", [128, NKT * 128]); VPa = P.sb("VPa", [128, NKT, 2, 65])
            P.memset("vector", VPa[:, :, :, 64:65], 1.0, wk=[VPa])
            uk = [P.sb(f"uk{i}", [128, 256]) for i in range(2)]
            uq = [P.sb(f"uq{i}", [128, 512]) for i in range(2)]
            rt = [P.sb(f"rt{i}", [128, 2, 64]) for i in range(2)]
            junk = P.sb("junk", [128, 512]); junk2 = P.sb("junk2", [128, 512]); ss = P.sb("ss", [128, 8])
            kk = [P.sb(f"kk{i}", [128, 2, 64]) for i in range(2)]
            qp = P.sb("qp", [128, 4, 2, 64]); QT = [P.sb(f"QT{i}", [128, 4, 128]) for i in range(2)]
            pTa = [P.sb(f"pTa{i}", [128, 5, 4, 128]) for i in range(2)]
            yc = P.sb("yc", [128, 8, 64]); den = P.sb("den", [128, 4, 1]); ycT = [P.sb(f"ycT{i}", [128, 4, 128], FDT) for i in range(2)]
            ptk = P.ps("ptk", [128, 128]); ptq = P.ps("ptq", [128, 4, 128])
            pss = [P.ps(f"pss{i}", [128, 512]) for i in range(3)]
            po = [P.ps(f"po{i}", [128, 4, 65]) for i in range(2)]
            pyt = P.ps("pyt", [128, 4, 128])
            for i in range(NKT):
                new = i < NT
                u = uk[i % 2]; k_ = kk[i % 2]; r_ = rt[i % 2]
                if new:
                    rows = slice(t_off + i * 128, t_off + (i + 1) * 128)
                    P.dma(u[:], jb["UT"][rows, C_SK:C_SK + 256], rk=[(jb["UT"].name, (t_off + i * 128) // 512, 2480)])
                    kv = u[:, 0:128].rearrange("p (g d) -> p g d", g=2)
                    P.act(junk[:, :128].rearrange("p (g d) -> p g d", g=2), kv, AF.Square)
                    P.red("vector", ss[:, 0:2], junk[:, :128].rearrange("p (g d) -> p g d", g=2))
                    rstd_of(ss[:, 0:2], ss[:, 0:2], 64, NORM_EPS)
                    P.tt("vector", k_[:], kv, ss[:, 0:2].unsqueeze(2).broadcast_to([128, 2, 64]), ALU.mult)
                    P.tt("vector", k_[:], k_[:], gk[:].unsqueeze(1).broadcast_to([128, 2, 64]), ALU.mult)
                    if not jb["ctx"]:
                        P.dma(O["o_swk"][si, l, i * 128:(i + 1) * 128, :], k_[:].rearrange("p g d -> p (g d)"), wk=[("o_swk", si, l, i)], eng=STQ, final=True)
                        P.dma(O["o_swv"][si, l, i * 128:(i + 1) * 128, :], u[:, 128:256], wk=[("o_swv", si, l, i)], eng=STQ, final=True)
                    else:
                        P.dma(r_[:], I["k_rope64"][i * 128:(i + 1) * 128])
                        rope(k_[:], r_[:, 0, :], r_[:, 1, :], 2, 16, junk, junk2)
                    ksrc = k_[:].rearrange("p g d -> p (g d)")
                    vsrc = u[:, 128:256]
                else:
                    c = i - NT
                    P.dma(u[:, 0:128], I["cswk"][l, c * 128:(c + 1) * 128, :]); P.dma(u[:, 128:256], I["cswv"][l, c * 128:(c + 1) * 128, :])
                    ksrc = u[:, 0:128]; vsrc = u[:, 128:256]
                P.tr(ptk[:], ksrc, ident[:])
                P.copy("vector", KTa[:, i * 128:(i + 1) * 128], ptk[:])
                P.copy("vector", VPa[:, i, :, 0:64], vsrc.rearrange("p (g d) -> p g d", g=2))
            units = [(b, g) for b in range(NT) for g in range(2)]
            ik = [0]

            def ktiles(b):
                if not jb["ctx"]:
                    return [(kt, None) for kt in range(NT)]
                lst = []
                if b > 0:
                    lst.append((b - 1, 0))
                lst.append((b, None))
                if b < NT - 1:
                    lst.append((b + 1, 1))
                return lst + [(NT + c, None) for c in range(NCT)]

            def qprep(b):
                u = uq[b % 2]; r_ = rt[b % 2]; Q_ = QT[b % 2]
                rows = slice(t_off + b * 128, t_off + (b + 1) * 128)
                P.dma(u[:], jb["UT"][rows, C_SQ:C_SQ + 512], rk=[(jb["UT"].name, (t_off + b * 128) // 512, 1968)])
                qv = u[:].rearrange("p (h d) -> p h d", h=8)
                P.act(junk[:].rearrange("p (h d) -> p h d", h=8), qv, AF.Square)
                P.red("vector", ss[:], junk[:].rearrange("p (h d) -> p h d", h=8))
                rstd_of(ss[:], ss[:], 64, NORM_EPS)
                P.tt("vector", qv, qv, ss[:].unsqueeze(2).broadcast_to([128, 8, 64]), ALU.mult)
                P.tt("vector", qv, qv, gq[:].unsqueeze(1).broadcast_to([128, 8, 64]), ALU.mult)
                if jb["ctx"]:
                    P.dma(r_[:], I["k_rope64"][b * 128:(b + 1) * 128])
                    rope(qv, r_[:, 0, :], r_[:, 1, :], 8, 16, junk, junk2)
                P.copy("vector", qp[:].rearrange("p r g d -> p g r d"), u[:].rearrange("p (g r d) -> p g r d", g=2, r=4))
                for r in range(4):
                    P.tr(ptq[:, r, :], qp[:, r, :, :].rearrange("p g d -> p (g d)"), ident[:])
                P.copy("scalar", Q_[:], ptq[:])

            def qk(iu):
                b, g = units[iu]
                if g == 0:
                    qprep(b)
                Q_ = QT[b % 2]; p_ = pTa[iu % 2]
                pr = slice(g * 64, (g + 1) * 64)
                for j, (kt, mk) in enumerate(ktiles(b)):
                    ps_ = pss[ik[0] % 3]; ik[0] += 1
                    P.mm(ps_[:], KTa[pr, kt * 128:(kt + 1) * 128], Q_[pr, :, :].rearrange("p r q -> p (r q)"))
                    P.act(p_[:, j, :, :].rearrange("p r q -> p (r q)"), ps_[:], AF.Exp, scale=SW_SCALE)
                    if mk is not None:
                        P.tt("vector", p_[:, j, :, :], p_[:, j, :, :], tri[:, mk, :].unsqueeze(1).broadcast_to([128, 4, 128]), ALU.mult)

            def pv(iu):
                b, g = units[iu]
                p_ = pTa[iu % 2]; po_ = po[iu % 2]
                kts = ktiles(b)
                for r in range(4):
                    for j, (kt, mk) in enumerate(kts):
                        P.mm(po_[:, r, :], p_[:, j, r, :], VPa[:, kt, g, :], start=(j == 0), stop=(j == len(kts) - 1))
                P.tt("vector", den[:], po_[:, :, 64:65], esink[:, g * 4:(g + 1) * 4].unsqueeze(2), ALU.add)
                P.recip(den[:], den[:])
                P.tt("vector", yc[:, g * 4:(g + 1) * 4, :], po_[:, :, 0:64], den[:].broadcast_to([128, 4, 64]), ALU.mult)
                if g == 1:
                    yT_ = ycT[b % 2]
                    for c in range(4):
                        P.tr(pyt[:, c, :], yc[:, 2 * c:2 * c + 2, :].rearrange("p h d -> p (h d)"), ident[:])
                    P.copy("scalar", yT_[:], pyt[:])
                    c0 = t_off + b * 128
                    P.dma(jb["YT"][2, :, c0:c0 + 128].rearrange("(c p) t -> p c t", p=128), yT_[:], wk=[(jb["YT"].name, 2, c0)], eng=STQ)

            qk(0)
            for iu in range(len(units)):
                if iu + 1 < len(units):
                    qk(iu + 1)
                pv(iu)
            P.pop()


    def phaseB(l, jb):
        n = jb["name"]
        for si, (t_off, S) in enumerate(jb["seqs"]):
            NC = S // 128
            HS = scr(f"B_HS_{n}{l}_{si}", [2, S, 512])
            P.push()
            tri = P.sb("tri", [128, 2, 128]); neg = P.sb("neg", [128, 2, 128]); sel = P.sb("sel", [128, 2, 128])
            P.dma(tri[:], I["k_tri"].rearrange("a k q -> k a q")); P.dma(sel[:], I["k_sel"].rearrange("a k q -> k a q"))
            P.ts("vector", neg[:], tri[:], -1.0, 1e30, op0=ALU.add, op1=ALU.mult)
            TRI = [tri[:, 1, :], tri[:, 0, :]]
            NEG = [neg[:, 1, :], neg[:, 0, :]]
            NEGts = [neg[:, 0, :], neg[:, 1, :]]
            bias16 = P.sb("bias16", [128, 2, 8])
            P.dma(bias16[:, 0, :], I["mlstm_i_bias"][l].partition_broadcast(128), wk=[bias16])
            P.dma(bias16[:, 1, :], I["mlstm_f_bias"][l].partition_broadcast(128), wk=[bias16])
            Cst = P.sb("Cst", [64, 8, 129]); mprev = P.sb("mprev", [128, 8])
            if jb["ctx"]:
                P.dma(Cst[:, :, 0:128], I["mC"][l].rearrange("d h k v -> k (d h) v"), wk=[Cst])
                P.dma(Cst[:, :, 128:129], I["mn"][l].rearrange("d h (k o) -> k (d h) o", o=1), wk=[Cst], allow_slow_non_contiguous=True)
                P.dma(mprev[:], I["mm"][l].rearrange("d h -> (d h)").partition_broadcast(128))
            else:
                P.memset("vector", Cst[:], 0.0); P.memset("vector", mprev[:], 0.0)
            G = [P.sb(f"G{i}", [128, 2, 8]) for i in range(2)]
            QTd = [P.sb(f"QTd{i}", [64, 2, 4, 128]) for i in range(2)]; KTd = [P.sb(f"KTd{i}", [64, 2, 4, 128]) for i in range(2)]
            Kt = [P.sb(f"Kt{i}", [128, 2, 4, 64]) for i in range(2)]; VPd = [P.sb(f"VPd{i}", [128, 2, 4, 129]) for i in range(2)]
            for v in VPd:
                P.memset("vector", v[:, :, :, 128:129], 1.0, wk=[v])
            sp = P.sb("sp", [128, 8]); b = P.sb("b", [128, 8]); li = P.sb("li", [128, 8]); c = P.sb("c", [128, 8])
            MB = P.sb("MB", [128, 2, 2, 4]); cmax = P.sb("cmax", [128, 8]); bm = P.sb("bm", [128, 8]); ain = P.sb("ain", [128, 8]); en = P.sb("en", [128, 8])
            DG = P.sb("DG", [128, 8, 128]); DG2 = P.sb("DG2", [128, 8, 128]); Rm = P.sb("Rm", [128, 8, 128])
            ET = P.sb("ET", [128, 8, 128]); WT = P.sb("WT", [128, 8, 128])
            tI = P.sb("tI", [128, 8, 129]); numS = P.sb("numS", [128, 8, 129]); dab = P.sb("dab", [128, 8]); hh = P.sb("hh", [128, 8, 128])
            mbl = P.sb("mbl", [128, 2, 2, 4]); wk_ = P.sb("wk", [128, 8]); dec = P.sb("dec", [128, 8]); KW = P.sb("KW", [128, 8, 64])
            pA = [P.ps(f"pA{i}", [128, 4, 128]) for i in range(2)]
            pB = [P.ps(f"pB{i}", [128, 4, 128]) for i in range(2)]
            pC = [P.ps(f"pC{i}", [128, 3, 129]) for i in range(3)]
            pD = P.ps("pD", [128, 2, 8])
            grp3 = [(0, 0, 3), (1, 3, 6), (2, 6, 8)]

            def pc_slot(dh):
                return pC[dh // 3], dh % 3

            for j in range(NC):
                g_ = G[j % 2]; qt = QTd[j % 2]; kt = KTd[j % 2]; ktok = Kt[j % 2]; vp = VPd[j % 2]
                cd = [j, NC - 1 - j]
                for d in range(2):
                    r0 = t_off + cd[d] * 128
                    rows = slice(r0, r0 + 128)
                    sk = r0 // 512
                    P.dma(g_[:, :, d * 4:(d + 1) * 4], jb["UT"][rows, C_MI:C_MI + 16].rearrange("p (a e) -> p a e", a=2)[:, :, d * 4:(d + 1) * 4],
                          rk=[(jb["UT"].name, sk, 1440)], wk=[g_])
                    P.dma(qt[:, d, :, :], jb["UF"][C_MQ:C_MQ + 256, r0:r0 + 128].rearrange("(h p) t -> p h t", p=64), rk=[(jb["UF"].name, sk, 416)], wk=[qt])
                    P.dma(kt[:, d, :, :], jb["UF"][C_MK:C_MK + 256, r0:r0 + 128].rearrange("(h p) t -> p h t", p=64), rk=[(jb["UF"].name, sk, 672)], wk=[kt])
                    P.dma(ktok[:, d, :, :], jb["UT"][rows, C_MK:C_MK + 256].rearrange("p (h e) -> p h e", h=4), rk=[(jb["UT"].name, sk, 672)], wk=[ktok])
                    P.dma(vp[:, d, :, 0:128], jb["UT"][rows, C_MV:C_MV + 512].rearrange("p (h e) -> p h e", h=4), rk=[(jb["UT"].name, sk, 928)], wk=[vp])
                P.act(qt[:], qt[:], AF.Copy, scale=0.125)
                P.tt("vector", g_[:], g_[:], bias16[:], ALU.add)
                P.copy("vector", li[:], g_[:, 0, :])
                P.act(sp[:], g_[:, 1, :], AF.Exp, scale=-1.0)
                P.act(sp[:], sp[:], AF.Ln, bias=1.0)
                for d in range(2):
                    P.mm(pD[:, 0, d * 4:(d + 1) * 4], TRI[d], sp[:, d * 4:(d + 1) * 4])
                P.act(b[:], pD[:, 0, :], AF.Copy, scale=-1.0)
                P.tt("vector", c[:], li[:], b[:], ALU.subtract)
                P.tt("vector", DG[:], ident[:].unsqueeze(1).broadcast_to([128, 8, 128]), c[:].unsqueeze(2).broadcast_to([128, 8, 128]), ALU.mult)
                for dh in range(8):
                    P.mm(pA[dh // 4][:, dh % 4, :], ones[:], DG[:, dh, :])
                for d in range(2):
                    P.tt("vector", Rm[:, d * 4:(d + 1) * 4, :], pA[d][:], NEGts[d].unsqueeze(1).broadcast_to([128, 4, 128]), ALU.add)
                P.red("vector", cmax[:], Rm[:], op=ALU.max)
                v24 = lambda t: t.rearrange("p (d h) -> p d h", d=2)
                mt = MB[:, :, 0, :]
                P.tt("vector", mt, v24(mprev[:]), v24(cmax[:]), ALU.max)
                P.tt("vector", mt, mt, v24(b[:]), ALU.add)
                P.copy("vector", MB[:, :, 1, :], v24(b[:]))
                P.tt("vector", v24(bm[:]), v24(b[:]), mt, ALU.subtract)
                P.tt("vector", ain[:], bm[:], mprev[:], ALU.add)
                P.act(ain[:], ain[:], AF.Exp)
                P.act(v24(en[:]), mt, AF.Exp, scale=-1.0)
                P.tt("vector", DG2[:], ident[:].unsqueeze(1).broadcast_to([128, 8, 128]), bm[:].unsqueeze(2).broadcast_to([128, 8, 128]), ALU.mult)
                for dh in range(8):
                    o_ = pA[dh // 4][:, dh % 4, :]
                    P.mm(o_, ones[:], DG2[:, dh, :], start=True, stop=False)
                    P.mm(o_, DG[:, dh, :], ones[:], start=False, stop=False)
                    P.mm(o_, ident[:], NEG[dh // 4], start=False, stop=True)
                for d in range(2):
                    P.act(ET[:, d * 4:(d + 1) * 4, :], pA[d][:], AF.Exp)
                for dh in range(8):
                    d, h = dh // 4, dh % 4
                    P.mm(pB[d][:, h, :], kt[:, d, h, :], qt[:, d, h, :])
                for d in range(2):
                    P.tt("vector", WT[:, d * 4:(d + 1) * 4, :], ET[:, d * 4:(d + 1) * 4, :], pB[d][:], ALU.mult)
                for dh in range(8):
                    d, h = dh // 4, dh % 4
                    pc, sl = pc_slot(dh)
                    P.mm(pc[:, sl, :], qt[:, d, h, :], Cst[:, dh, :])
                for (bk, lo, hi) in grp3:
                    P.tt("vector", tI[:, lo:hi, :], pC[bk][:, 0:hi - lo, :], ain[:, lo:hi].unsqueeze(2).broadcast_to([128, hi - lo, 129]), ALU.mult)
                for dh in range(8):
                    d, h = dh // 4, dh % 4
                    pc, sl = pc_slot(dh)
                    P.mm(pc[:, sl, :], WT[:, dh, :], vp[:, d, h, :])
                for (bk, lo, hi) in grp3:
                    P.tt("vector", numS[:, lo:hi, :], pC[bk][:, 0:hi - lo, :], tI[:, lo:hi, :], ALU.add)
                P.act(dab[:].unsqueeze(2), numS[:, :, 128:129], AF.Abs)
                P.tt("vector", dab[:], dab[:], en[:], ALU.max)
                P.recip(dab[:], dab[:])
                P.tt("vector", hh[:], numS[:, :, 0:128], dab[:].unsqueeze(2).broadcast_to([128, 8, 128]), ALU.mult)
                for d in range(2):
                    r0 = cd[d] * 128
                    P.dma(HS[d, r0:r0 + 128, :].rearrange("p (h e) -> p h e", h=4), hh[:, d * 4:(d + 1) * 4, :], wk=[(HS.name, d, cd[d])], eng=STQ)
                for d in range(2):
                    P.mm(pD[:, d, :].rearrange("p (a h) -> p a h", a=2).rearrange("p a h -> p (a h)"), sel[:, d, :], MB[:, d, :, :].rearrange("p a h -> p (a h)"))
                P.copy("vector", mbl[:].rearrange("p d a h -> p (d a h)"), pD[:].rearrange("p a e -> p (a e)"))
                P.tt("vector", v24(wk_[:]), mbl[:, :, 1, :], mbl[:, :, 0, :], ALU.subtract)
                P.tt("vector", dec[:], wk_[:], mprev[:], ALU.add)
                P.act(dec[:], dec[:], AF.Exp)
                P.tt("vector", wk_[:], wk_[:], c[:], ALU.add)
                P.act(wk_[:], wk_[:], AF.Exp)
                for d in range(2):
                    P.tt("vector", KW[:, d * 4:(d + 1) * 4, :], ktok[:, d, :, :], wk_[:, d * 4:(d + 1) * 4].unsqueeze(2).broadcast_to([128, 4, 64]), ALU.mult)
                for dh in range(8):
                    d, h = dh // 4, dh % 4
                    pc, sl = pc_slot(dh)
                    P.mm(pc[0:64, sl, :], KW[:, dh, :], vp[:, d, h, :])
                P.tt("vector", Cst[:], Cst[:], dec[0:64, :].unsqueeze(2).broadcast_to([64, 8, 129]), ALU.mult)
                for (bk, lo, hi) in grp3:
                    P.tt("vector", Cst[:, lo:hi, :], Cst[:, lo:hi, :], pC[bk][0:64, 0:hi - lo, :], ALU.add)
                P.copy("vector", v24(mprev[:]), mbl[:, :, 0, :])
            if not jb["ctx"]:
                P.dma(O["o_mC"][si, l].rearrange("d h k v -> k (d h) v"), Cst[:, :, 0:128], wk=[("o_mC", si, l)], eng=STQ, final=True)
                P.dma(O["o_mn"][si, l].rearrange("d h (k o) -> k (d h) o", o=1), Cst[:, :, 128:129], wk=[("o_mn", si, l)], eng=STQ, final=True, allow_slow_non_contiguous=True)
                P.dma(O["o_mm"][si, l].rearrange("d (h o) -> o (d h)", o=1), mprev[0:1, :], wk=[("o_mm", si, l)], eng=STQ, final=True, allow_slow_non_contiguous=True)
            P.pop()
            P.push()
            gm = bcast_row("gm", I["mlstm_norm"][l], 128)
            h0 = [P.sb(f"h0{i}", [128, 4, 128]) for i in range(2)]; h1 = [P.sb(f"h1{i}", [128, 4, 128]) for i in range(2)]
            og = [P.sb(f"og{i}", [128, 512]) for i in range(2)]
            junk = P.sb("junk", [128, 4, 128]); ss = P.sb("ss", [128, 4]); yT = [P.sb(f"yT{i}", [128, 4, 128], FDT) for i in range(2)]
            pyt = P.ps("pyt", [128, 4, 128])
            for i in range(NC):
                a_ = h0[i % 2]; b_ = h1[i % 2]; o_ = og[i % 2]; y_ = yT[i % 2]
                rows = slice(t_off + i * 128, t_off + (i + 1) * 128)
                P.dma(a_[:], HS[0, i * 128:(i + 1) * 128, :].rearrange("p (h e) -> p h e", h=4), rk=[HS.name + "*"])
                P.dma(b_[:], HS[1, i * 128:(i + 1) * 128, :].rearrange("p (h e) -> p h e", h=4), rk=[HS.name + "*"])
                P.dma(o_[:], jb["UT"][rows, C_MO:C_MO + 512], rk=[(jb["UT"].name, (t_off + i * 128) // 512, 1456)])
                P.tt("vector", a_[:], a_[:], b_[:], ALU.add)
                P.act(junk[:], a_[:], AF.Square)
                P.red("vector", ss[:], junk[:])
                rstd_of(ss[:], ss[:], 128, NORM_EPS)
                P.act(o_[:], o_[:], AF.Sigmoid)
                P.tt("vector", a_[:], a_[:], ss[:].unsqueeze(2).broadcast_to([128, 4, 128]), ALU.mult)
                P.tt("vector", a_[:], a_[:], gm[:].unsqueeze(1).broadcast_to([128, 4, 128]), ALU.mult)
                P.tt("vector", a_[:], a_[:], o_[:].rearrange("p (h e) -> p h e", h=4), ALU.mult)
                for h in range(4):
                    P.tr(pyt[:, h, :], a_[:, h, :], ident[:])
                P.copy("scalar", y_[:], pyt[:])
                c0 = t_off + i * 128
                P.dma(jb["YT"][1, :, c0:c0 + 128].rearrange("(c p) t -> p c t", p=128), y_[:], wk=[(jb["YT"].name, 1, c0)], eng=STQ)
            P.pop()


    def phaseD(l, jb):
        n = jb["name"]
        CW = RW_DECAY
        for si, (t_off, S) in enumerate(jb["seqs"]):
            NCH = S // 64
            DF = scr(f"D_F_{n}{l}_{si}", [2, NCH, 2, 64, 4, 4, 64])
            LW = scr(f"D_LW_{n}{l}_{si}", [S, 2, 512])
            BG = scr(f"D_BG_{n}{l}_{si}", [2, 512, S])
            YS = scr(f"D_YS_{n}{l}_{si}", [2, S, 512])
            P.push()
            ST = min(512, S)
            kk = P.sb("kk", [64, 8]); ka = P.sb("ka", [64, 8]); omka = P.sb("omka", [64, 8]); uu = P.sb("uu", [64, 2, 8]); a0 = P.sb("a0", [64, 2, 8])
            P.dma(kk[:], I["rwkv_kk64"][l]); P.dma(ka[:], I["rwkv_ka64"][l]); P.dma(uu[:], I["rwkv_u64"][l]); P.dma(a0[:], I["rwkv_a064"][l])
            P.ts("vector", omka[:], ka[:], -1.0, 1.0, op0=ALU.mult, op1=ALU.add)
            w2 = P.sb("w2", [64, 2, 512]); a2 = P.sb("a2", [64, 2, 512]); g2 = P.sb("g2", [128, 512])
            P.dma(w2[:], I["rwkv_w2"][l].rearrange("d r c -> r d c")); P.dma(a2[:], I["rwkv_a2"][l].rearrange("d r c -> r d c")); P.dma(g2[:], I["rwkv_g2"][l])
            w0row = bcast_row("w0row", I["rwkv_w0"][l].rearrange("d c -> (d c)"), 1024)
            rT = P.sb("rT", [64, 8, ST]); kT = P.sb("kT", [64, 8, ST]); vT = P.sb("vT", [64, 8, ST])
            w1T = P.sb("w1T", [64, 2, ST]); a1T = P.sb("a1T", [64, 2, ST]); g1T = P.sb("g1T", [128, ST])
            kap = P.sb("kap", [64, 8, ST]); kh = P.sb("kh", [64, 8, ST]); tA = P.sb("tA", [64, 8, ST]); tB = P.sb("tB", [64, 8, ST])
            ktt = P.sb("ktt", [64, 8, ST]); rku = P.sb("rku", [64, 8, ST]); lw = [P.sb(f"lw{i}", [128, 2, 512]) for i in range(2)]
            pp = [P.ps(f"pp{i}", [128, 512]) for i in range(4)]
            ip = [0]

            def nps():
                ip[0] += 1
                return pp[ip[0] % 4]

            def store_df(d, arr, tile, s0):
                for cc in range(ST // 64):
                    for hh in range(2):
                        P.dma(DF[d, s0 // 64 + cc, hh, :, arr, :, :], tile[:, hh * 4:(hh + 1) * 4, cc * 64:(cc + 1) * 64], wk=[(DF.name, d, arr, s0, cc, hh)], eng=STQ)

            for s_ in range(S // ST):
                s0 = s_ * ST
                c0 = t_off + s0
                sk = c0 // 512
                UF = jb["UF"]
                P.dma(rT[:], UF[C_RR:C_RR + 512, c0:c0 + ST].rearrange("(h p) t -> p h t", p=64), rk=[(UF.name, sk, 2736)])
                P.dma(kT[:], UF[C_RK:C_RK + 512, c0:c0 + ST].rearrange("(h p) t -> p h t", p=64), rk=[(UF.name, sk, 3248)])
                P.dma(vT[:], UF[C_RV:C_RV + 512, c0:c0 + ST].rearrange("(h p) t -> p h t", p=64), rk=[(UF.name, sk, 3760)])
                P.dma(w1T[:], UF[C_RW:C_RW + 128, c0:c0 + ST].rearrange("(d p) t -> p d t", p=64), rk=[(UF.name, sk, 4272)])
                P.dma(a1T[:], UF[C_RA:C_RA + 128, c0:c0 + ST].rearrange("(d p) t -> p d t", p=64), rk=[(UF.name, sk, 4272)])
                P.dma(g1T[:], UF[C_RG:C_RG + 128, c0:c0 + ST], rk=[(UF.name, sk, 4272)])
                P.act(w1T[:], w1T[:], AF.Tanh)
                P.act(g1T[:], g1T[:], AF.Sigmoid)
                P.tt("vector", kap[:], kT[:], kk[:].unsqueeze(2).broadcast_to([64, 8, ST]), ALU.mult)
                P.act(tA[:], kap[:], AF.Square)
                for h in range(8):
                    ps_ = nps()
                    P.mm(ps_[0:64, :ST], ones[0:64, 0:64], tA[:, h, :])
                    P.act(kh[:, h, :], ps_[0:64, :ST], AF.Sqrt, bias=1e-12)
                P.recip(kh[:], kh[:])
                P.tt("vector", kh[:], kh[:], kap[:], ALU.mult)
                for d in range(2):
                    store_df(d, 0, rT, s0); store_df(d, 1, kh, s0)
                for d in range(2):
                    for h in range(8):
                        ps_ = nps()
                        P.mm(ps_[0:64, :ST], a2[:, d, h * 64:(h + 1) * 64], a1T[:, d, :])
                        P.act(tA[:, h, :], ps_[0:64, :ST], AF.Sigmoid, bias=a0[:, d, h:h + 1])
                    P.tt("vector", tB[:], tA[:], kh[:], ALU.mult)
                    store_df(d, 3, tB, s0)
                    P.tt("vector", tA[:], tA[:], ka[:].unsqueeze(2).broadcast_to([64, 8, ST]), ALU.mult)
                    P.tt("vector", tA[:], tA[:], omka[:].unsqueeze(2).broadcast_to([64, 8, ST]), ALU.add)
                    P.tt("vector", ktt[:], kT[:], tA[:], ALU.mult)
                    store_df(d, 2, ktt, s0)
                    P.tt("vector", tA[:], ktt[:], rT[:], ALU.mult)
                    if d == 0:
                        P.tt("vector", rku[:], tA[:], uu[:, d, :].unsqueeze(2).broadcast_to([64, 8, ST]), ALU.mult)
                    else:
                        P.tt("vector", tA[:], tA[:], uu[:, d, :].unsqueeze(2).broadcast_to([64, 8, ST]), ALU.mult)
                        P.tt("vector", rku[:], rku[:], tA[:], ALU.add)
                for h in range(8):
                    ps_ = nps()
                    P.mm(ps_[0:64, :ST], ones[0:64, 0:64], rku[:, h, :])
                    P.tt("vector", tB[:, h, :], ps_[0:64, :ST], vT[:, h, :], ALU.mult)
                P.dma(BG[0, :, s0:s0 + ST].rearrange("(h p) t -> p h t", p=64), tB[:], wk=[(BG.name, 0, s0)], eng=STQ)
                for h in range(8):
                    ps_ = nps()
                    P.mm(ps_[0:64, :ST], g2[:, h * 64:(h + 1) * 64], g1T[:])
                    P.copy("scalar", kap[:, h, :], ps_[0:64, :ST])
                P.dma(BG[1, :, s0:s0 + ST].rearrange("(h p) t -> p h t", p=64), kap[:], wk=[(BG.name, 1, s0)], eng=STQ)
                for tt in range(ST // 128):
                    lw_ = lw[tt % 2]
                    for d in range(2):
                        ps_ = nps()
                        P.mm(ps_[:], w1T[:, d, tt * 128:(tt + 1) * 128], w2[:, d, :])
                        P.tt("vector", lw_[:, d, :], ps_[:], w0row[:, d * 512:(d + 1) * 512], ALU.add)
                    P.act(lw_[:], lw_[:], AF.Sigmoid)
                    P.dma(LW[s0 + tt * 128:s0 + (tt + 1) * 128], lw_[:], wk=[(LW.name, s0, tt)], eng=STQ)
            P.pop()
            P.push()
            tri = P.sb("tri", [128, 2, 128]); P.dma(tri[:], I["k_tri"].rearrange("a k q -> k a q"))
            trs = P.sb("trs", [128, 2, 128]); P.dma(trs[:], I["k_tris"].rearrange("a k q -> k a q"))
            HP = [slice(0, 64), slice(64, 128)]
            cum = P.sb("cum", [128, 2, 2, 64]); mask4 = P.sb("mask4", [128, 2, 4, 64]); maskT = P.sb("maskT", [128, 2, 64])
            for hh in range(2):
                pr = HP[hh]
                INC = [tri[pr, 1, pr], tri[pr, 0, pr]]; STR = [trs[pr, 1, pr], trs[pr, 0, pr]]
                for d in range(2):
                    P.copy("vector", cum[pr, d, 0, :], INC[d], wk=[cum]); P.copy("vector", cum[pr, d, 1, :], STR[d], wk=[cum])
                    for a_, m_ in enumerate([STR[d], INC[d], STR[d], INC[d]]):
                        P.copy("vector", mask4[pr, d, a_, :], m_, wk=[mask4])
                    P.copy("vector", maskT[pr, d, :], STR[1 - d], wk=[maskT])
            idh = [ident[HP[0], HP[0]], ident[HP[1], HP[1]]]
            id4 = P.sb("id4", [128, 4, 64])
            for hh in range(2):
                for h4 in range(4):
                    P.copy("vector", id4[HP[hh], h4, :], idh[hh], wk=[id4])
            TS = P.sb("TS", [128, 2, 4, 64])
            bG = P.ps("bG", [128, 512]); bX = [P.ps(f"bX{i}", [128, 512]) for i in range(2)]; bY = P.ps("bY", [128, 256])
            bA = P.ps("bA", [128, 512]); bP = P.ps("bP", [128, 256]); bZ = [P.ps(f"bZ{i}", [128, 256]) for i in range(2)]
            s0t = P.sb("s0t", [128, 2, 4, 64])

            def trm(out, in_, hh):
                P.mm(out, in_, idh[hh])

            if jb["ctx"]:
                for hh in range(2):
                    for d in range(2):
                        P.dma(s0t[HP[hh], d], I["rw"][l][d, hh * 4:(hh + 1) * 4].rearrange("h v k -> v h k"), wk=[s0t])
                for d in range(2):
                    for h in range(8):
                        hh, h4 = h // 4, h % 4
                        trm(bX[0][HP[hh], (d * 4 + h4) * 64:(d * 4 + h4 + 1) * 64], s0t[HP[hh], d, h4, :], hh)
                P.copy("vector", TS[:].rearrange("p d h v -> p (d h v)"), bX[0][:])
            else:
                P.memset("vector", TS[:], 0.0)
            Xd = [[P.sb(f"Xd{d}{i}", [128, 4, 4, 64]) for i in range(2)] for d in range(2)]
            Vd = [[P.sb(f"Vd{d}{i}", [128, 4, 64]) for i in range(2)] for d in range(2)]
            LWd = [[P.sb(f"LWd{d}{i}", [128, 256]) for i in range(2)] for d in range(2)]
            EI = P.sb("EI", [128, 4, 64]); EX = P.sb("EX", [128, 4, 64]); EN = P.sb("EN", [128, 4, 64]); gl = P.sb("gl", [128, 4])
            KR = P.sb("KR", [128, 4, 2, 64]); KtM = P.sb("KtM", [128, 4, 64]); BM = P.sb("BM", [128, 4, 64]); KBe = P.sb("KBe", [128, 4, 2, 64])
            AM = P.sb("AM", [128, 4, 4, 64]); N0 = P.sb("N0", [128, 4, 64])
            AB = [P.sb(f"AB{i}", [128, 4, 2, 64]) for i in range(2)]; PI = [P.sb(f"PI{i}", [128, 4, 64]) for i in range(2)]
            RH = P.sb("RH", [128, 4, 64]); Un = P.sb("Un", [128, 4, 64]); Yo = [P.sb(f"Yo{i}", [128, 4, 64]) for i in range(2)]
            KBt = P.sb("KBt", [128, 4, 2, 64])
            HH = [(h // 4, h % 4) for h in range(8)]

            def dpass(j, d):
                c = j if d == 0 else NCH - 1 - j
                X = Xd[d][j % 2]; V = Vd[d][j % 2]; LWc = LWd[d][j % 2]
                r0 = t_off + c * 64
                for hh in range(2):
                    pr = HP[hh]
                    P.dma(X[pr], DF[d, c, hh], rk=[DF.name + "*"], wk=[X])
                    P.dma(V[pr], jb["UT"][r0:r0 + 64, C_RV + hh * 256:C_RV + (hh + 1) * 256].rearrange("p (h e) -> p h e", h=4),
                          rk=[(jb["UT"].name, r0 // 512, 3760)], wk=[V])
                    P.dma(LWc[pr], LW[c * 64:(c + 1) * 64, d, hh * 256:(hh + 1) * 256], rk=[LW.name + "*"], wk=[LWc])
                Rr = X[:, 0, :, :]; Kh = X[:, 1, :, :]; Kt = X[:, 2, :, :]; Bb = X[:, 3, :, :]
                for (hh, h4) in HH:
                    pr = HP[hh]
                    P.mm(bG[pr, h4 * 128:(h4 + 1) * 128], LWc[pr, h4 * 64:(h4 + 1) * 64], cum[pr, d, :, :].rearrange("p a t -> p (a t)"))
                gv = bG[:].rearrange("p (h a t) -> p h a t", h=4, a=2)
                P.act(EI[:], gv[:, :, 0, :], AF.Exp, scale=-CW)
                P.act(EN[:], gv[:, :, 0, :], AF.Exp, scale=CW)
                P.act(EX[:], gv[:, :, 1, :], AF.Exp, scale=-CW)
                last = 63 if d == 0 else 0
                P.copy("vector", gl[:].unsqueeze(2), EI[:, :, last:last + 1])
                P.tt("vector", KR[:, :, 1, :], Rr, EI[:], ALU.mult)
                P.tt("vector", KR[:, :, 0, :], Kh, EX[:], ALU.mult)
                P.tt("vector", KtM[:], Kt, EN[:], ALU.mult)
                P.tt("vector", BM[:], Bb, EN[:], ALU.mult)
                P.tt("vector", KBe[:, :, 0, :], KtM[:], gl[:].unsqueeze(2).broadcast_to([128, 4, 64]), ALU.mult)
                P.tt("vector", KBe[:, :, 1, :], BM[:], gl[:].unsqueeze(2).broadcast_to([128, 4, 64]), ALU.mult)
                for (hh, h4) in HH:
                    pr = HP[hh]
                    o_ = bX[h4 // 2][pr, (h4 % 2) * 256:(h4 % 2 + 1) * 256]
                    rhs = KR[pr, h4, :, :].rearrange("p a t -> p (a t)")
                    P.mm(o_[:, 0:128], KtM[pr, h4, :], rhs)
                    P.mm(o_[:, 128:256], BM[pr, h4, :], rhs)
                for q in range(2):
                    P.tt("vector", AM[:, q * 2:(q + 1) * 2, :, :], bX[q][:].rearrange("p (h a t) -> p h a t", h=2, a=4),
                         mask4[:, d, :, :].unsqueeze(1).broadcast_to([128, 2, 4, 64]), ALU.mult)
                for (hh, h4) in HH:
                    pr = HP[hh]
                    P.mm(bY[pr, h4 * 64:(h4 + 1) * 64], KR[pr, h4, 0, :], BM[pr, h4, :])
                P.tt("vector", N0[:], bY[:].rearrange("p (h s) -> p h s", h=4), maskT[:, d, :].unsqueeze(1).broadcast_to([128, 4, 64]), ALU.mult)
                A_ = lambda pr, h4: AM[pr, h4, 2, :]
                B_ = lambda pr, h4: N0[pr, h4, :]
                Pc = PI[0]
                P.tt("vector", Pc[:], id4[:], AM[:, :, 2, :], ALU.subtract)
                for lv in range(1, 6):
                    ab = AB[lv % 2]
                    for (hh, h4) in HH:
                        pr = HP[hh]
                        o_ = bA[pr, h4 * 128:(h4 + 1) * 128]
                        if lv < 5:
                            P.mm(o_[:, 0:64], B_(pr, h4), A_(pr, h4))
                        P.mm(o_[:, 64:128], A_(pr, h4), B_(pr, h4))
                    src = bA[:].rearrange("p (h a t) -> p h a t", h=4, a=2)
                    if lv < 5:
                        P.copy("scalar", ab[:], src)
                    else:
                        P.copy("scalar", ab[:, :, 1, :], src[:, :, 1, :])
                    A_ = (lambda ab: (lambda pr, h4: ab[pr, h4, 0, :]))(ab)
                    B_ = (lambda ab: (lambda pr, h4: ab[pr, h4, 1, :]))(ab)
                    for (hh, h4) in HH:
                        pr = HP[hh]
                        o_ = bP[pr, h4 * 64:(h4 + 1) * 64]
                        P.mm(o_, B_(pr, h4), Pc[pr, h4, :], start=True, stop=False)
                        P.mm(o_, idh[hh], Pc[pr, h4, :], start=False, stop=True)
                    Pn = PI[lv % 2]
                    P.copy("vector", Pn[:].rearrange("p h t -> p (h t)"), bP[:])
                    Pc = Pn
                for (hh, h4) in HH:
                    pr = HP[hh]
                    o_ = bZ[0][pr, h4 * 64:(h4 + 1) * 64]
                    P.mm(o_, KR[pr, h4, 0, :], TS[pr, d, h4, :], start=True, stop=False)
                    P.mm(o_, AM[pr, h4, 0, :], V[pr, h4, :], start=False, stop=True)
                P.copy("scalar", RH[:].rearrange("p h t -> p (h t)"), bZ[0][:])
                for (hh, h4) in HH:
                    pr = HP[hh]
                    P.mm(bZ[1][pr, h4 * 64:(h4 + 1) * 64], Pc[pr, h4, :], RH[pr, h4, :])
                P.act(Un[:].rearrange("p h t -> p (h t)"), bZ[1][:], AF.Copy, scale=-1.0)
                Y_ = Yo[j % 2]
                for (hh, h4) in HH:
                    pr = HP[hh]
                    o_ = bZ[0][pr, h4 * 64:(h4 + 1) * 64]
                    P.mm(o_, KR[pr, h4, 1, :], TS[pr, d, h4, :], start=True, stop=False)
                    P.mm(o_, AM[pr, h4, 1, :], V[pr, h4, :], start=False, stop=False)
                    P.mm(o_, AM[pr, h4, 3, :], Un[pr, h4, :], start=False, stop=True)
                P.copy("scalar", Y_[:].rearrange("p h t -> p (h t)"), bZ[0][:])
                for hh in range(2):
                    P.dma(YS[d, c * 64:(c + 1) * 64, hh * 256:(hh + 1) * 256], Y_[HP[hh]].rearrange("p h t -> p (h t)"), wk=[(YS.name, d, c, hh)], eng=STQ)
                for (hh, h4) in HH:
                    pr = HP[hh]
                    for a_ in range(2):
                        trm(bX[0][pr, (h4 * 2 + a_) * 64:(h4 * 2 + a_ + 1) * 64], KBe[pr, h4, a_, :], hh)
                P.copy("vector", KBt[:].rearrange("p h a t -> p (h a t)"), bX[0][:])
                for (hh, h4) in HH:
                    pr = HP[hh]
                    o_ = bZ[1][pr, h4 * 64:(h4 + 1) * 64]
                    P.mm(o_, KBt[pr, h4, 0, :], V[pr, h4, :], start=True, stop=False)
                    P.mm(o_, KBt[pr, h4, 1, :], Un[pr, h4, :], start=False, stop=True)
                Td = TS[:, d, :, :]
                P.tt("vector", Td, Td, gl[:].unsqueeze(2).broadcast_to([128, 4, 64]), ALU.mult)
                P.tt("vector", Td, Td, bZ[1][:].rearrange("p (h t) -> p h t", h=4), ALU.add)

            for j in range(NCH):
                for d in range(2):
                    dpass(j, d)
            if not jb["ctx"]:
                for d in range(2):
                    for (hh, h4) in HH:
                        trm(bX[0][HP[hh], (d * 4 + h4) * 64:(d * 4 + h4 + 1) * 64], TS[HP[hh], d, h4, :], hh)
                P.copy("vector", s0t[:].rearrange("p d h k -> p (d h k)"), bX[0][:])
                for hh in range(2):
                    for d in range(2):
                        P.dma(O["o_rw"][si, l][d, hh * 4:(hh + 1) * 4].rearrange("h v k -> v h k"), s0t[HP[hh], d], wk=[("o_rw", si, l, hh, d)], eng=STQ, final=True)
            P.pop()
            P.push()
            gng = P.sb("gng", [128, 4]); gnb = P.sb("gnb", [128, 4])
            P.dma(gng[:], I["rwkv_gn_gT"][l]); P.dma(gnb[:], I["rwkv_gn_bT"][l])
            y0 = [P.sb(f"y0{i}", [128, 8, 64]) for i in range(2)]; y1 = [P.sb(f"y1{i}", [128, 8, 64]) for i in range(2)]
            bg = [P.sb(f"bg{i}", [128, 2, 4, 128]) for i in range(2)]
            junk = P.sb("junk", [128, 8, 64]); st8 = P.sb("st8", [128, 8]); yT = [P.sb(f"yT{i}", [128, 4, 128], FDT) for i in range(2)]
            ytmp = P.sb("ytmp", [128, 4, 128])
            pyt = P.ps("pyt", [128, 4, 128])
            for i in range(S // 128):
                a_ = y0[i % 2]; b_ = y1[i % 2]; g_ = bg[i % 2]; y_ = yT[i % 2]
                P.dma(a_[:], YS[0, i * 128:(i + 1) * 128, :].rearrange("p (h e) -> p h e", h=8), rk=[YS.name + "*"])
                P.dma(b_[:], YS[1, i * 128:(i + 1) * 128, :].rearrange("p (h e) -> p h e", h=8), rk=[YS.name + "*"])
                P.dma(g_[:], BG[:, :, i * 128:(i + 1) * 128].rearrange("a (c p) t -> p a c t", p=128), rk=[BG.name + "*"])
                P.tt("vector", a_[:], a_[:], b_[:], ALU.add)
                P.red("vector", st8[:], a_[:])
                P.ts("vector", st8[:], st8[:], -1.0 / 64, None, op0=ALU.mult)
                P.tt("vector", a_[:], a_[:], st8[:].unsqueeze(2).broadcast_to([128, 8, 64]), ALU.add)
                P.act(junk[:], a_[:], AF.Square)
                P.red("vector", st8[:], junk[:])
                rstd_of(st8[:], st8[:], 64, RW_GN_EPS)
                P.tt("vector", a_[:], a_[:], st8[:].unsqueeze(2).broadcast_to([128, 8, 64]), ALU.mult)
                for c in range(4):
                    P.tr(pyt[:, c, :], a_[:, 2 * c:2 * c + 2, :].rearrange("p h e -> p (h e)"), ident[:])
                for c in range(4):
                    P.act(ytmp[:, c, :], pyt[:, c, :], AF.Identity, bias=gnb[:, c:c + 1], scale=gng[:, c:c + 1])
                P.tt("vector", ytmp[:], ytmp[:], g_[:, 0, :, :], ALU.add)
                P.tt("vector", y_[:], ytmp[:], g_[:, 1, :, :], ALU.mult)
                c0 = t_off + i * 128
                P.dma(jb["YT"][3, :, c0:c0 + 128].rearrange("(c p) t -> p c t", p=128), y_[:], wk=[(jb["YT"].name, 3, c0)], eng=STQ)
            P.pop()


    def phaseM1(l, jb):
        TOK, j = jb["TOK"], jb["j"]
        ST = 512
        P.push()
        Yb = P.sb("Yb", [128, 4, 4, ST], FDT)
        Wo = P.sb("Wo", [128, 4, 4, D], FDT)
        wo2 = P.sb("wo2", [128, 8, D], FDT)
        Gt = [P.sb(f"Gt{i}", [128, 4, ST]) for i in range(2)]
        mg = P.sb("mg", [128, 8, ST], FDT); tmp = [P.sb(f"tmp{i}", [128, ST]) for i in range(3)]
        xT = P.sb("xT", [128, 8, ST])
        pp = [P.ps(f"pp{i}", [128, 512]) for i in range(6)]
        XTv = jb["XT"].rearrange("(k p) t -> p k t", p=128)
        wnames = ["mla_w_o", "mlstm_w_o", "swa_w_o", "rwkv_w_o"]
        for b in range(4):
            P.dma(Wo[:, b, :, :], WR[wnames[b]][l].rearrange("(c p) n -> p c n", p=128), wk=[Wo])
        for k2 in range(2):
            P.dma(wo2[:, k2 * 4:(k2 + 1) * 4, :], WR["w_out"][l][k2 * 512:(k2 + 1) * 512, :].rearrange("(k p) n -> p k n", p=128), wk=[wo2])
        ip = 0; io = 0
        for s_ in range(TOK // ST):
            t0 = s_ * ST
            Y_ = Yb; x_ = xT
            for b in range(4):
                P.dma(Y_[:, b, :, :], jb["YT"][b, :, t0:t0 + ST].rearrange("(c p) t -> p c t", p=128), rk=[jb["YT"].name + "*"], wk=[Y_])
            P.dma(x_[:], XTv[:, :, t0:t0 + ST], rk=[(jb["XT"].name, s_)])
            for oc in range(8):
                G_ = Gt[io % 2]; io += 1
                P.dma(G_[:], jb["UF"][C_GATE:C_GATE + 4096, t0:t0 + ST].rearrange("(b o p) t -> p b o t", b=4, o=8)[:, :, oc, :], rk=[jb["UF"].name + "*"])
                for b in range(4):
                    ps_ = pp[ip % 6]; ip += 1
                    for c in range(4):
                        P.mm(ps_[:, :ST], Wo[:, b, c, oc * 128:(oc + 1) * 128], Y_[:, b, c, :], start=(c == 0), stop=(c == 3), fast=True)
                    if b == 0:
                        P.tt("vector", tmp[2][:], ps_[:, :ST], G_[:, b, :], ALU.mult)
                    else:
                        t_ = tmp[b % 2]
                        P.tt("vector", t_[:], ps_[:, :ST], G_[:, b, :], ALU.mult)
                        P.tt(PENG, mg[:, oc, :] if b == 3 else tmp[2][:], tmp[2][:], t_[:], ALU.add)
            for oc in range(8):
                ps_ = pp[ip % 6]; ip += 1
                for k in range(8):
                    P.mm(ps_[:, :ST], wo2[:, k, oc * 128:(oc + 1) * 128], mg[:, k, :], start=(k == 0), stop=(k == 7), fast=True)
                P.stt("vector", x_[:, oc, :], ps_[:, :ST], modT[l][:, 16 + oc, j:j + 1], x_[:, oc, :], ALU.mult, ALU.add)
            P.dma(XTv[:, :, t0:t0 + ST], x_[:], wk=[(jb["XT"].name, s_)], eng=STQ)
        P.pop()

    def phaseM2(l, jb, last):
        TOK, j = jb["TOK"], jb["j"]
        ST = 512
        P.push()
        x1 = P.sb("x1", [128, 8, ST]); sq = P.sb("sq", [128, 8, ST]); h2 = P.sb("h2", [128, 8, ST], FDT); rstd = P.sb("rstd", [128, ST])
        w1c = [P.sb(f"w1c{i}", [128, 8, 256], FDT) for i in range(2)]
        hid = P.sb("hid", [128, 32, ST], FDT); rl = [P.sb(f"rl{i}", [128, ST]) for i in range(2)]
        w2c = [P.sb(f"w2c{i}", [128, 32, 128], FDT) for i in range(2)]
        ytok = [P.sb(f"ytok{i}", [128, D]) for i in range(2)]
        pst = P.ps("pst", [128, ST])
        pp = [P.ps(f"pp{i}", [128, 512]) for i in range(6)]
        XTv = jb["XT"].rearrange("(k p) t -> p k t", p=128)
        ip = 0; i1 = 0; i2 = 0
        for s_ in range(TOK // ST):
            t0 = s_ * ST
            P.dma(x1[:], XTv[:, :, t0:t0 + ST], rk=[(jb["XT"].name, s_)])
            rms_rstd_featmajor(x1, sq, pst, rstd, ST)
            P.tt("vector", sq[:], x1[:], rstd[:].unsqueeze(1).broadcast_to([128, 8, ST]), ALU.mult)
            for k in range(8):
                P.act(h2[:, k, :], sq[:, k, :], AF.Identity, bias=modT[l][:, 24 + k, j:j + 1], scale=A2[l][:, k, j:j + 1])
            for fc in range(32):
                if fc % 2 == 0:
                    w_ = w1c[i1 % 2]; i1 += 1
                    P.dma(w_[:], WR["mlp_w1"][l][:, fc * 128:(fc + 2) * 128].rearrange("(k p) n -> p k n", p=128))
                ps_ = pp[ip % 6]; ip += 1
                for k in range(8):
                    P.mm(ps_[:, :ST], w_[:, k, (fc % 2) * 128:(fc % 2 + 1) * 128], h2[:, k, :], start=(k == 0), stop=(k == 7), fast=True)
                r_ = rl[fc % 2]
                P.act(r_[:], ps_[:, :ST], AF.Relu)
                P.tt(PENG, hid[:, fc, :], r_[:], r_[:], ALU.mult)
            x2 = sq
            for oc in range(8):
                w_ = w2c[i2 % 2]; i2 += 1
                for q in range(4):
                    P.dma(w_[:, q * 8:(q + 1) * 8, :], WR["mlp_w2"][l][q * 1024:(q + 1) * 1024, oc * 128:(oc + 1) * 128].rearrange("(f p) n -> p f n", p=128), wk=[w_])
                ps_ = pp[ip % 6]; ip += 1
                for fc in range(32):
                    P.mm(ps_[:, :ST], w_[:, fc, :], hid[:, fc, :], start=(fc == 0), stop=(fc == 31), fast=True)
                P.stt("vector", x2[:, oc, :], ps_[:, :ST], modT[l][:, 40 + oc, j:j + 1], x1[:, oc, :], ALU.mult, ALU.add)
            if not last:
                P.dma(XTv[:, :, t0:t0 + ST], x2[:], wk=[(jb["XT"].name, s_)], eng=STQ)
            else:
                for tt in range(ST // 128):
                    yt = ytok[tt % 2]
                    for kk in range(2):
                        ps_ = pp[ip % 6]; ip += 1
                        for k4 in range(4):
                            P.tr(ps_[:, k4 * 128:(k4 + 1) * 128], x2[:, kk * 4 + k4, tt * 128:(tt + 1) * 128], ident[:])
                        P.evac(yt[:, kk * 512:(kk + 1) * 512], ps_[:])
                    P.dma(jb["y"][t0 + tt * 128:t0 + (tt + 1) * 128, :], yt[:], wk=[("y", j, t0, tt)], eng=STQ, final=True)
        P.pop()

    import os
    stop = os.environ.get("KSTOP", "")
    WR = {}
    fast_w = ["w_in", "mla_w_o", "mlstm_w_o", "swa_w_o", "rwkv_w_o", "w_out", "mlp_w1", "mlp_w2"]
    if FAST_MM:
        P.push()
        CH = 2048
        raw = [P.sb(f"wraw{i}", [128, CH]) for i in range(3)]
        rnd = [P.sb(f"wrnd{i}", [128, CH], F32R) for i in range(3)]
        engs = ["gpsimd", "vector", "scalar"]
        iw = 0
        for name in fast_w:
            src = I[name]
            _, Rr, Cc = src.shape
            dst = nc.dram_tensor(name + "_r", [L, Rr, Cc], F32R, kind="Internal").ap()
            WR[name] = dst
            for l in range(L):
                for rb in range(Rr // 128):
                    for c0 in range(0, Cc, CH):
                        w = min(CH, Cc - c0)
                        a_ = raw[iw % 3]; b_ = rnd[iw % 3]
                        P.dma(a_[:, :w], src[l, rb * 128:(rb + 1) * 128, c0:c0 + w])
                        P.copy(engs[iw % 3], b_[:, :w], a_[:, :w])
                        P.dma(dst[l, rb * 128:(rb + 1) * 128, c0:c0 + w], b_[:, :w], wk=[(dst.name, l, rb, c0)], eng=STQ)
                        iw += 1
        P.pop()
    else:
        for name in fast_w:
            WR[name] = I[name]

    only = os.environ.get("KONLY", "")
    for l in range(L):
        phase0(l)
        for jb in jobs:
            phase1(l, jb, first=(l == 0))
        for nm, fn in (("A", phaseA), ("C", phaseC), ("B", phaseB), ("D", phaseD)):
            if only and nm not in only:
                continue
            for jb in jobs:
                fn(l, jb)
        if only and "M" not in only:
            break
        for jb in jobs:
            phaseM1(l, jb)
        for jb in jobs:
            phaseM2(l, jb, last=(l == L - 1))
    print("NREC", getattr(P, "nrec", 0), flush=True)
    P.emit()
    return nc, I, O, SCR


def _fm(v, width=8):
    v = np.asarray(v, np.float32)
    return np.ascontiguousarray(np.swapaxes(v.reshape(v.shape[:-1] + (width, 128)), -1, -2))


def _rope_table(S, R):
    q = R // 4
    t = np.arange(S)
    pr = (t // 64).astype(np.float32); pc = (t % 64).astype(np.float32)
    inv = (10000.0 ** (-np.arange(q, dtype=np.float32) / q)).astype(np.float32)
    ar = pr[:, None] * inv; ac = pc[:, None] * inv
    ang = np.concatenate([ar, ar, ac, ac], -1).astype(np.float32)
    sign = np.concatenate([-np.ones(q), np.ones(q), -np.ones(q), np.ones(q)]).astype(np.float32)
    return np.ascontiguousarray(np.stack([np.cos(ang), np.sin(ang) * sign], 1).astype(np.float32))


def make_in_map(inp, cfg, core):
    L = cfg.depth
    f = lambda a: np.ascontiguousarray(np.asarray(a, np.float32))
    n_p = cfg.n_p
    m = {}
    m["xs"] = f(inp["x_sample"][core])
    m["xp"] = f(inp["x_prompt"][core * n_p:(core + 1) * n_p].reshape(n_p * SP, D))
    cc = np.stack([np.asarray(inp["c"][core]), np.asarray(inp["c_ctx"])], 0)
    m["cT"] = f(cc.reshape(2, 8, 128).transpose(2, 1, 0))
    m["ckv"] = f(inp["cache_mla_ckv"][core]); m["ckr"] = f(inp["cache_mla_krope"][core])
    m["cswk"] = f(np.asarray(inp["cache_swa_k"][core]).reshape(L, CTX, 128))
    m["cswv"] = f(np.asarray(inp["cache_swa_v"][core]).reshape(L, CTX, 128))
    m["mC"] = f(inp["state_mlstm_C"][core]); m["mn"] = f(inp["state_mlstm_n"][core])
    m["mm"] = f(inp["state_mlstm_m"][core]); m["rw"] = f(inp["state_rwkv"][core])
    m["ada_w"] = f(inp["ada_w"]); m["ada_bT"] = _fm(inp["ada_b"], 48)
    m["norm1T"] = _fm(inp["norm1"]); m["norm2T"] = _fm(inp["norm2"])
    for k in ["w_in", "mla_q_a_norm", "mla_kv_a_norm", "mla_w_uq", "mla_w_ukv", "mla_q_norm", "mla_k_norm", "mla_w_o",
              "mlstm_norm", "mlstm_w_o", "swa_q_norm", "swa_k_norm", "swa_sink", "swa_w_o", "rwkv_w2", "rwkv_a2",
              "rwkv_g2", "rwkv_w_o", "w_out", "mlp_w1", "mlp_w2"]:
        m[k] = f(inp[k])
    m["mlstm_i_bias"] = f(np.asarray(inp["mlstm_i_bias"]).reshape(L, 8))
    m["mlstm_f_bias"] = f(np.asarray(inp["mlstm_f_bias"]).reshape(L, 8))
    for k in ["rwkv_gn_g", "rwkv_gn_b"]:
        m[k + "T"] = _fm(inp[k], 4)
    m["rwkv_w0"] = f(inp["rwkv_w0"])
    for k in ["rwkv_kk", "rwkv_ka"]:
        m[k + "64"] = f(np.asarray(inp[k]).reshape(L, 8, 64).transpose(0, 2, 1))
    for k in ["rwkv_a0", "rwkv_u"]:
        m[k + "64"] = f(np.asarray(inp[k]).reshape(L, 2, 8, 64).transpose(0, 3, 1, 2))
    m["k_ident"] = np.eye(128, dtype=np.float32)
    m["k_ones"] = np.ones((128, 128), np.float32)
    kq = np.arange(128)
    m["k_tri"] = np.stack([(kq[:, None] >= kq[None, :]), (kq[:, None] <= kq[None, :])], 0).astype(np.float32)
    m["k_tris"] = np.stack([(kq[:, None] > kq[None, :]), (kq[:, None] < kq[None, :])], 0).astype(np.float32)
    sel = np.zeros((2, 128, 128), np.float32); sel[0, 127, :] = 1.0; sel[1, 0, :] = 1.0
    m["k_sel"] = sel
    m["k_rope32"] = _rope_table(cfg.S_s, 32); m["k_rope64"] = _rope_table(cfg.S_s, 64)
    return m


_CACHE = {}


def kernel(**inputs):
    cfg = Cfg(S_s=4096, n_p=2, depth=2)
    n_cores = 8
    if "nc" not in _CACHE:
        _CACHE["nc"] = build(cfg)
    nc, I, O, SCR = _CACHE["nc"]
    in_maps = []
    for c in range(n_cores):
        m = make_in_map(inputs, cfg, c)
        in_maps.append({k: v for k, v in m.items() if k in I})
    res = run_bass_kernel_spmd(nc, in_maps, core_ids=list(range(n_cores)))
    R = res.results
    L = cfg.depth
    cat = lambda k: np.concatenate([np.asarray(r[k], np.float32) for r in R], 0)
    y_prompt = cat("y_p").reshape(16, SP, D)
    y_sample = np.stack([np.asarray(r["y_s"], np.float32) for r in R], 0)
    return (y_prompt, y_sample,
            cat("o_ckv"), cat("o_ckr"),
            cat("o_swk").reshape(16, L, SP, 2, 64), cat("o_swv").reshape(16, L, SP, 2, 64),
            cat("o_mC"), cat("o_mn"), cat("o_mm"), cat("o_rw"))
```

```python
import os
import numpy as np
import concourse.bass as bass
import concourse.mybir as mybir
from concourse.bass_utils import run_bass_kernel_spmd
from contextlib import ExitStack

F32 = mybir.dt.float32
F32R = mybir.dt.float32r
FAST_MM = os.environ.get("FAST_MM", "1") == "1"
FDT = F32R if FAST_MM else F32


def fr(ap):
    return ap.bitcast(F32R) if FAST_MM else ap
AF = mybir.ActivationFunctionType
ALU = mybir.AluOpType
AX = mybir.AxisListType

ENGS = ("sync", "scalar", "vector", "gpsimd", "tensor")
SEM_LIMIT = int(os.environ.get("SEM_LIMIT", 30000))
N_DMA_SEMS = 16
import os
STQ = os.environ.get("STQ", "gpsimd")
PENG = os.environ.get("PENG", "vector")

D = 1024
NORM_EPS = 1e-6
CTX = 256
SP = 256
D_IN = 8752
D_FF = 4096
MLA_SCALE = 96 ** -0.5
SW_SCALE = 64 ** -0.5
RW_DECAY = 0.6065306597126334
RW_GN_EPS = 64e-5
C_QA, C_KVA, C_KR = 0, 256, 384
C_MQ, C_MK, C_MV, C_MI, C_MF, C_MO = 416, 672, 928, 1440, 1448, 1456
C_SQ, C_SK, C_SV = 1968, 2480, 2608
C_RR, C_RK, C_RV, C_RW, C_RA, C_RG = 2736, 3248, 3760, 4272, 4400, 4528
C_GATE = 4656
NTOKC = 4656


class Op:
    __slots__ = ("eng", "fn", "deps", "is_dma", "signal", "sem", "cnt", "dsem_prev", "barriered")

    def __init__(self, eng, fn, is_dma):
        self.eng = eng
        self.fn = fn
        self.deps = []
        self.is_dma = is_dma
        self.signal = False
        self.sem = None
        self.cnt = 0
        self.dsem_prev = None
        self.barriered = False


class Prog:
    def __init__(self, nc):
        self.nc = nc
        self.ops = {e: [] for e in ENGS}
        self.lastw = {}
        self.readers = {}
        self.stacks = [ExitStack()]
        self.out_dmas = []
        self.uid = 0
        self.rr = 0
        self.psum_names = set()

    def sb(self, name, shape, dt=F32):
        self.uid += 1
        return self.stacks[-1].enter_context(self.nc.sbuf_tensor(f"{name}_{self.uid}", list(shape), dt))

    def ps(self, name, shape, dt=F32):
        self.uid += 1
        n = 1
        for d in shape[1:]:
            n *= d
        nb = (n * 4 + 2047) // 2048
        t = self.stacks[-1].enter_context(self.nc.psum_tensor(f"{name}_{self.uid}", [128, nb * 512], dt))
        self.psum_names.add(t.name)
        v = t[:shape[0], :n]
        if len(shape) == 3:
            v = v.rearrange("p (a b) -> p a b", a=shape[1])
        elif len(shape) == 4:
            v = v.rearrange("p (a b c) -> p a b c", a=shape[1], b=shape[2])
        return v

    def push(self):
        self.stacks.append(ExitStack())

    def pop(self):
        self.barrier()
        self.stacks.pop().close()

    @staticmethod
    def _key(k):
        if isinstance(k, (str, tuple)):
            return k
        return k.name

    def op(self, eng, fn, reads=(), writes=(), is_dma=False):
        self.nrec = getattr(self, "nrec", 0) + 1
        if self.nrec > int(os.environ.get("KMAXOPS", 10 ** 9)):
            return None
        o = Op(eng, fn, is_dma)
        if os.environ.get("KTRACE") and abs(self.nrec - int(os.environ["KTRACE"])) <= 6:
            print("OP", self.nrec, eng, [self._key(k) for k in writes], flush=True)
        rk = [self._key(k) for k in reads if k is not None and not isinstance(k, (int, float))]
        wk = [self._key(k) for k in writes]
        if eng != "tensor":
            wk = wk + [k for k in rk if k in self.psum_names and k not in wk]
        raw = set()
        deps = set()
        for k in rk:
            w = self.lastw.get(k)
            if w is not None:
                raw.add(w)
                deps.add(w)
        for k in wk:
            w = self.lastw.get(k)
            if w is not None:
                deps.add(w)
            for r in self.readers.get(k, ()):
                deps.add(r)
        for d in deps:
            if d.eng == eng and not d.is_dma and not is_dma:
                if eng == "tensor":
                    continue
            o.deps.append(d)
        for k in rk:
            self.readers.setdefault(k, []).append(o)
        for k in wk:
            self.lastw[k] = o
            self.readers[k] = []
        self.ops[eng].append(o)
        return o

    def barrier(self):
        lasts = [self.ops[e][-1] for e in ENGS if self.ops[e]]
        pend = [o for e in ("sync", "scalar", "gpsimd") for o in self.ops[e] if o.is_dma and not o.barriered]
        for o in pend:
            o.barriered = True
        for e in ENGS:
            b = Op(e, None, False)
            b.deps = lasts + pend
            self.ops[e].append(b)
        self.lastw.clear()
        self.readers.clear()

    def dma(self, out, in_, rk=None, wk=None, eng="sync", final=False, **kw):
        r = [in_] if rk is None else rk
        w = [out] if wk is None else wk
        o = self.op(eng, lambda e: e.dma_start(out=out, in_=in_, **kw), r, w, is_dma=True)
        if final and o is not None:
            self.out_dmas.append(o)
        return o

    def mm(self, out, lhsT, rhs, start=True, stop=True, rk=None, wk=None, fast=False):
        r = [lhsT, rhs] if rk is None else rk
        w = [out] if wk is None else wk
        return self.op("tensor", lambda e: e.matmul(out, lhsT=lhsT, rhs=rhs, start=start, stop=stop), r, w)

    def tr(self, out, in_, ident, rk=None, wk=None):
        r = [in_, ident] if rk is None else rk
        w = [out] if wk is None else wk
        return self.op("tensor", lambda e: e.transpose(out, in_, ident), r, w)

    def act(self, out, in_, func, bias=None, scale=None, accum=None, rk=None, wk=None, extra_r=()):
        kw = {}
        if bias is not None:
            kw["bias"] = bias
        if scale is not None:
            kw["scale"] = scale
        if accum is not None:
            kw["accum_out"] = accum
        r = ([in_, bias, scale] if rk is None else list(rk)) + list(extra_r)
        w = ([out] + ([accum] if accum is not None else [])) if wk is None else wk
        return self.op("scalar", lambda e: e.activation(out=out, in_=in_, func=func, **kw), r, w)

    def tt(self, eng, out, in0, in1, op, rk=None, wk=None):
        r = [in0, in1] if rk is None else rk
        w = [out] if wk is None else wk
        return self.op(eng, lambda e: e.tensor_tensor(out=out, in0=in0, in1=in1, op=op), r, w)

    def ts(self, eng, out, in0, s1, s2=None, op0=ALU.mult, op1=None, rk=None, wk=None):
        r = [in0, s1, s2] if rk is None else rk
        w = [out] if wk is None else wk
        if op1 is None:
            return self.op(eng, lambda e: e.tensor_scalar(out=out, in0=in0, scalar1=s1, scalar2=None, op0=op0), r, w)
        return self.op(eng, lambda e: e.tensor_scalar(out=out, in0=in0, scalar1=s1, scalar2=s2, op0=op0, op1=op1), r, w)

    def stt(self, eng, out, in0, scalar, in1, op0, op1, rk=None, wk=None):
        r = [in0, scalar, in1] if rk is None else rk
        w = [out] if wk is None else wk
        return self.op(eng, lambda e: e.scalar_tensor_tensor(out=out, in0=in0, scalar=scalar, in1=in1, op0=op0, op1=op1), r, w)

    def copy(self, eng, out, in_, rk=None, wk=None):
        r = [in_] if rk is None else rk
        w = [out] if wk is None else wk
        if eng == "scalar":
            return self.op(eng, lambda e: e.copy(out=out, in_=in_), r, w)
        return self.op(eng, lambda e: e.tensor_copy(out=out, in_=in_), r, w)

    def red(self, eng, out, in_, op=ALU.add, rk=None, wk=None):
        r = [in_] if rk is None else rk
        w = [out] if wk is None else wk
        return self.op(eng, lambda e: e.tensor_reduce(out=out, in_=in_, axis=AX.X, op=op), r, w)

    def recip(self, out, in_, rk=None, wk=None):
        r = [in_] if rk is None else rk
        w = [out] if wk is None else wk
        return self.op("vector", lambda e: e.reciprocal(out=out, in_=in_), r, w)

    def memset(self, eng, out, val, wk=None):
        w = [out] if wk is None else wk
        return self.op(eng, lambda e: e.memset(out, val), [], w)

    def evac(self, out, in_, rk=None, wk=None):
        self.rr += 1
        ev = os.environ.get("KEVAC", "alt")
        if ev == "alt":
            ev = "scalar" if self.rr % 2 else "vector"
        return self.copy(ev, out, in_, rk, wk)

    def emit(self):
        nc = self.nc
        if self.out_dmas:
            b = Op("sync", None, False)
            b.deps = list(self.out_dmas)
            self.ops["sync"].append(b)
        for e in ENGS:
            for o in self.ops[e]:
                for d in o.deps:
                    d.signal = True
        with ExitStack() as st:
            def newsem(nm):
                return st.enter_context(nc.semaphore(nm))
            for e in ENGS:
                cur, c, ep = None, 0, 0
                for o in self.ops[e]:
                    if o.is_dma or not o.signal or o.fn is None:
                        continue
                    if cur is None or c >= SEM_LIMIT:
                        cur = newsem(f"s_{e}_{ep}")
                        ep += 1
                        c = 0
                    c += 1
                    o.sem = cur
                    o.cnt = c
            for e in ENGS:
                dl = [o for o in self.ops[e] if o.is_dma]
                if not dl:
                    continue
                dsems = [newsem(f"dq_{e}_{i}") for i in range(N_DMA_SEMS)]
                dcnt = [0] * N_DMA_SEMS
                dlast = [None] * N_DMA_SEMS
                for di, o in enumerate(dl):
                    s_ = di % N_DMA_SEMS
                    if dcnt[s_] + 16 > SEM_LIMIT:
                        dsems[s_] = newsem(f"dq_{e}_{s_}_{di}")
                        dcnt[s_] = 0
                    dcnt[s_] += 16
                    o.sem = dsems[s_]
                    o.cnt = dcnt[s_]
                    o.dsem_prev = dlast[s_]
                    dlast[s_] = o
                    o.signal = True
            if os.environ.get("KSTATS"):
                for e in ENGS:
                    sig = [o for o in self.ops[e] if o.signal and not o.is_dma and o.fn is not None]
                    print("ENG", e, "ops", len(self.ops[e]), "signals", len(sig), "maxcnt", max([o.cnt for o in self.ops[e]] + [0]), flush=True)
            with nc.Block() as block:
                def run(e):
                    def body(eng):
                        seen = {}
                        for o in self.ops[e]:
                            need = {}
                            deps = o.deps
                            if o.is_dma and o.dsem_prev is not None:
                                deps = deps + [o.dsem_prev]
                            for d in deps:
                                if d.fn is None or d.sem is None:
                                    continue
                                nm = d.sem.name
                                if need.get(nm, (None, 0))[1] < d.cnt:
                                    need[nm] = (d.sem, d.cnt)
                            for nm, (s, c) in need.items():
                                if seen.get(nm, 0) >= c:
                                    continue
                                eng.wait_ge(s, c)
                                seen[nm] = c
                            if o.fn is None:
                                continue
                            ins = o.fn(eng)
                            if o.signal:
                                ins.then_inc(o.sem, 16 if o.is_dma else 1)
                    return body
                block.sync(run("sync"))
                block.scalar(run("scalar"))
                block.vector(run("vector"))
                block.gpsimd(run("gpsimd"))
                block.tensor(run("tensor"))
        while self.stacks:
            self.stacks.pop().close()


class Cfg:
    def __init__(self, S_s=4096, n_p=2, depth=2, debug=()):
        self.S_s = S_s
        self.n_p = n_p
        self.depth = depth
        self.debug = tuple(debug)


W_NAMES = ["ada_w", "w_in", "mla_w_uq", "mla_w_ukv", "mla_w_o", "mlstm_w_o", "swa_w_o", "rwkv_w2", "rwkv_a2",
           "rwkv_g2", "rwkv_w_o", "w_out", "mlp_w1", "mlp_w2"]

P1_BLOCKS = [
    (0, 416, "T"), (416, 672, "F"), (672, 928, "TF"), (928, 1440, "T"), (1440, 1456, "T"), (1456, 1968, "T"),
    (1968, 2480, "T"), (2480, 2736, "T"), (2736, 3248, "F"), (3248, 3760, "F"), (3760, 4272, "TF"),
    (4272, 4656, "F"),
] + [(4656 + 512 * i, 4656 + 512 * (i + 1), "G") for i in range(8)]


def build(cfg):
    nc = bass.Bass("TRN2", target_bir_lowering=False)
    if FAST_MM:
        nc.dge_precook = False
    L = cfg.depth
    S_s, n_p = cfg.S_s, cfg.n_p
    TOKP = n_p * SP
    P = Prog(nc)
    I = {}
    O = {}
    SCR = {}

    def din(name, shape):
        I[name] = nc.dram_tensor(name, list(shape), F32, kind="ExternalInput").ap()
        return I[name]

    def dout(name, shape):
        O[name] = nc.dram_tensor(name, list(shape), F32, kind="ExternalOutput").ap()
        return O[name]

    def scr(name, shape, dt=F32):
        kind = "ExternalOutput" if name in cfg.debug else "Internal"
        SCR[name] = nc.dram_tensor(name, list(shape), dt, kind=kind).ap()
        return SCR[name]

    din("xs", [S_s, D]); din("xp", [TOKP, D])
    din("cT", [128, 8, 2])
    din("ckv", [L, CTX, 128]); din("ckr", [L, CTX, 32]); din("cswk", [L, CTX, 128]); din("cswv", [L, CTX, 128])
    din("mC", [L, 2, 4, 64, 128]); din("mn", [L, 2, 4, 64]); din("mm", [L, 2, 4]); din("rw", [L, 2, 8, 64, 64])
    din("ada_w", [L, D, 6 * D]); din("ada_bT", [L, 128, 48]); din("norm1T", [L, 128, 8]); din("norm2T", [L, 128, 8])
    din("w_in", [L, D, D_IN])
    din("mla_q_a_norm", [L, 256]); din("mla_kv_a_norm", [L, 128]); din("mla_w_uq", [L, 256, 768])
    din("mla_w_ukv", [L, 128, 1024]); din("mla_q_norm", [L, 96]); din("mla_k_norm", [L, 96]); din("mla_w_o", [L, 512, D])
    din("mlstm_i_bias", [L, 8]); din("mlstm_f_bias", [L, 8]); din("mlstm_norm", [L, 128]); din("mlstm_w_o", [L, 512, D])
    din("swa_q_norm", [L, 64]); din("swa_k_norm", [L, 64]); din("swa_sink", [L, 8]); din("swa_w_o", [L, 512, D])
    din("rwkv_w0", [L, 2, 512]); din("rwkv_w2", [L, 2, 64, 512]); din("rwkv_a064", [L, 64, 2, 8])
    din("rwkv_a2", [L, 2, 64, 512]); din("rwkv_g2", [L, 128, 512]); din("rwkv_kk64", [L, 64, 8]); din("rwkv_ka64", [L, 64, 8])
    din("rwkv_u64", [L, 64, 2, 8]); din("rwkv_gn_gT", [L, 128, 4]); din("rwkv_gn_bT", [L, 128, 4]); din("rwkv_w_o", [L, 512, D])
    din("w_out", [L, D, D]); din("mlp_w1", [L, D, D_FF]); din("mlp_w2", [L, D_FF, D])
    din("k_ident", [128, 128]); din("k_ones", [128, 128])
    din("k_rope32", [S_s, 2, 32]); din("k_rope64", [S_s, 2, 64]); din("k_tri", [2, 128, 128]); din("k_sel", [2, 128, 128]); din("k_tris", [2, 128, 128])
    dout("y_p", [TOKP, D]); dout("y_s", [S_s, D])
    dout("o_ckv", [n_p, L, SP, 128]); dout("o_ckr", [n_p, L, SP, 32]); dout("o_swk", [n_p, L, SP, 128]); dout("o_swv", [n_p, L, SP, 128])
    dout("o_mC", [n_p, L, 2, 4, 64, 128]); dout("o_mn", [n_p, L, 2, 4, 64]); dout("o_mm", [n_p, L, 2, 4]); dout("o_rw", [n_p, L, 2, 8, 64, 64])

    jobs = [dict(name="s", TOK=S_s, seqs=[(0, S_s)], ctx=True, j=0, x=I["xs"], y=O["y_s"]),
            dict(name="p", TOK=TOKP, seqs=[(i * SP, SP) for i in range(n_p)], ctx=False, j=1, x=I["xp"], y=O["y_p"])]
    for jb in jobs:
        n = jb["name"]
        jb["XT"] = scr(f"XT_{n}", [D, jb["TOK"]])
        jb["UT"] = scr(f"UTOK_{n}", [jb["TOK"], NTOKC])
        jb["UF"] = scr(f"UFEAT_{n}", [D_IN, jb["TOK"]])
        jb["YT"] = scr(f"YT_{n}", [4, 512, jb["TOK"]], FDT)

    ident = P.sb("ident", [128, 128]); ones = P.sb("ones", [128, 128])
    P.dma(ident[:], I["k_ident"][:, :], wk=[ident]); P.dma(ones[:], I["k_ones"][:, :], wk=[ones])
    cT = P.sb("cT", [128, 8, 2]); sT = P.sb("sT", [128, 8, 2])
    P.dma(cT[:], I["cT"][:, :, :])
    P.act(sT[:], cT[:], AF.Silu)
    modT = [P.sb(f"modT{l}", [128, 48, 2]) for l in range(L)]
    A1 = [P.sb(f"A1_{l}", [128, 8, 2]) for l in range(L)]
    A2 = [P.sb(f"A2_{l}", [128, 8, 2]) for l in range(L)]

    def phase0(l):
        P.push()
        wb = [P.sb(f"adaw{i}", [128, 8, 512]) for i in range(2)]
        pm = P.ps("pm", [128, 48, 2])
        bT = P.sb("bT", [128, 48]); n1 = P.sb("n1", [128, 8]); n2 = P.sb("n2", [128, 8])
        P.dma(bT[:], I["ada_bT"][l]); P.dma(n1[:], I["norm1T"][l]); P.dma(n2[:], I["norm2T"][l])
        wv = I["ada_w"][l].rearrange("(k p) c -> p k c", p=128)
        for g in range(12):
            w = wb[g % 2]
            P.dma(w[:], wv[:, :, g * 512:(g + 1) * 512])
            for jj in range(4):
                jc = g * 4 + jj
                for k in range(8):
                    P.mm(pm[:, jc, :], w[:, k, jj * 128:(jj + 1) * 128], sT[:, k, :], start=(k == 0), stop=(k == 7))
        P.tt("vector", modT[l][:], pm[:], bT[:].unsqueeze(2).broadcast_to([128, 48, 2]), ALU.add)
        P.stt("vector", A1[l][:], modT[l][:, 8:16, :], 1.0, n1[:].unsqueeze(2).broadcast_to([128, 8, 2]), ALU.add, ALU.mult)
        P.stt("vector", A2[l][:], modT[l][:, 32:40, :], 1.0, n2[:].unsqueeze(2).broadcast_to([128, 8, 2]), ALU.add, ALU.mult)
        P.pop()

    def rms_rstd_featmajor(xT, sq, pst, rstd, n):
        P.act(sq[:, :, :n], xT[:, :, :n], AF.Square)
        for k in range(8):
            P.mm(pst[:, :n], ones[:], sq[:, k, :n], start=(k == 0), stop=(k == 7))
        P.act(rstd[:, :n], pst[:, :n], AF.Sqrt, bias=NORM_EPS, scale=1.0 / D)
        P.recip(rstd[:, :n], rstd[:, :n])

    def phase1(l, jb, first):
        TOK, j = jb["TOK"], jb["j"]
        ST = 512
        P.push()
        xT = [P.sb(f"xT{i}", [128, 8, ST]) for i in range(2)]
        hT = [P.sb(f"hT{i}", [128, 8, ST], FDT) for i in range(2)]
        sq = P.sb("sq", [128, 8, ST]); rstd = P.sb("rstd", [128, ST])
        wt = [P.sb(f"wt{i}", [128, 8, 512], FDT) for i in range(3)]
        stg = [P.sb(f"stg{i}", [128, 4, 512]) for i in range(3)]
        xin = [P.sb(f"xin{i}", [128, D]) for i in range(2)] if first else None
        pst = P.ps("pst", [128, ST])
        pp = [P.ps(f"pp{i}", [128, 512]) for i in range(6)]
        XTv = jb["XT"].rearrange("(k p) t -> p k t", p=128)
        wv = WR["w_in"][l].rearrange("(k p) c -> p k c", p=128)
        UTv = jb["UT"].rearrange("(tt p) c -> p tt c", p=128)
        ip = 0
        ist = 0
        iw = 0
        for s in range(TOK // ST):
            x_ = xT[s % 2]; h_ = hT[s % 2]
            t0 = s * ST
            if first:
                for tt in range(4):
                    xi = xin[tt % 2]
                    P.dma(xi[:], jb["x"][t0 + tt * 128:t0 + (tt + 1) * 128, :])
                    for kk in range(2):
                        pq = pp[ip % 6]; ip += 1
                        for k4 in range(4):
                            P.tr(pq[:, k4 * 128:(k4 + 1) * 128], xi[:, (kk * 4 + k4) * 128:(kk * 4 + k4 + 1) * 128], ident[:])
                        P.evac(x_[:, kk * 4:(kk + 1) * 4, tt * 128:(tt + 1) * 128], pq[:].rearrange("p (a b) -> p a b", a=4))
                P.dma(XTv[:, :, t0:t0 + ST], x_[:], wk=[(jb["XT"].name, s)], eng=STQ)
            else:
                P.dma(x_[:], XTv[:, :, t0:t0 + ST], rk=[(jb["XT"].name, s)])
            rms_rstd_featmajor(x_, sq, pst, rstd, ST)
            P.tt("vector", sq[:], x_[:], rstd[:].unsqueeze(1).broadcast_to([128, 8, ST]), ALU.mult)
            for k in range(8):
                P.act(h_[:, k, :], sq[:, k, :], AF.Identity, bias=modT[l][:, k, j:j + 1], scale=A1[l][:, k, j:j + 1])
            for (cs, ce, lay) in P1_BLOCKS:
                wd = ce - cs
                w = wt[iw % 3]; iw += 1
                P.dma(w[:, :, :wd], wv[:, :, cs:ce])
                if "T" in lay:
                    sg = stg[ist % 3]; ist += 1
                    for tt in range(4):
                        pq = pp[ip % 6]; ip += 1
                        for k in range(8):
                            P.mm(pq[:, :wd], h_[:, k, tt * 128:(tt + 1) * 128], w[:, k, :wd], start=(k == 0), stop=(k == 7), fast=(wd >= 256))
                        P.evac(sg[:, tt, :wd], pq[:, :wd])
                    P.dma(UTv[:, s * 4:(s + 1) * 4, cs:ce], sg[:, :, :wd], wk=[(jb["UT"].name, s, cs)], eng=STQ)
                if "F" in lay or "G" in lay:
                    sg = stg[ist % 3]; ist += 1
                    nb = wd // 128
                    for cb in range(nb):
                        pq = pp[ip % 6]; ip += 1
                        for k in range(8):
                            P.mm(pq[:, :ST], w[:, k, cb * 128:(cb + 1) * 128], h_[:, k, :], start=(k == 0), stop=(k == 7), fast=True)
                        if lay == "G":
                            P.act(sg[:, cb, :], pq[:, :ST], AF.Sigmoid)
                        else:
                            P.evac(sg[:, cb, :], pq[:, :ST])
                    P.dma(jb["UF"][cs:ce, t0:t0 + ST].rearrange("(cb p) t -> p cb t", p=128), sg[:, :nb, :],
                          wk=[(jb["UF"].name, s, cs)], eng=STQ)
        P.pop()


    def rope(x, cos, sinS, H, q, t1, t2):
        R4 = 4 * q
        t1v = t1[:, :H * R4].rearrange("p (h r) -> p h r", h=H)
        t2v = t2[:, :H * R4].rearrange("p (h a b c) -> p h a b c", h=H, a=2, b=2)
        xv = x.rearrange("p h (a b c) -> p h a b c", a=2, b=2)
        sv = sinS.rearrange("p (a b c) -> p a b c", a=2, b=2)
        P.tt("vector", t1v, x, cos.unsqueeze(1).broadcast_to([128, H, R4]), ALU.mult)
        for b in range(2):
            P.tt(PENG, t2v[:, :, :, b, :], xv[:, :, :, 1 - b, :],
                 sv[:, :, b, :].unsqueeze(1).broadcast_to([128, H, 2, q]), ALU.mult)
        P.tt("vector", x, t1v, t2[:, :H * R4].rearrange("p (h r) -> p h r", h=H), ALU.add)

    def rstd_of(out, ssq, n, eps):
        P.act(out, ssq, AF.Sqrt, bias=eps, scale=1.0 / n)
        P.recip(out, out)

    def bcast_row(name, src_row, n):
        t = P.sb(name, [128, n])
        P.dma(t[:], src_row.partition_broadcast(128))
        return t

    def phaseA(l, jb):
        TOK = jb["TOK"]
        for si, (t_off, S) in enumerate(jb["seqs"]):
            NK = S + (CTX if jb["ctx"] else 0)
            NKT = NK // 128
            n = jb["name"]
            KTs = scr(f"A_KT_{n}{l}_{si}", [8, 96, NK], FDT); VPs = scr(f"A_VP_{n}{l}_{si}", [NK, 8, 65], FDT); QTs = scr(f"A_QT_{n}{l}_{si}", [8, 96, S], FDT)
            P.push()
            gqa = bcast_row("gqa", I["mla_q_a_norm"][l], 256); gkv = bcast_row("gkv", I["mla_kv_a_norm"][l], 128)
            gqn = bcast_row("gqn", I["mla_q_norm"][l], 96); gkn = bcast_row("gkn", I["mla_k_norm"][l], 96)
            wuq = P.sb("wuq", [128, 2, 768]); wukv = P.sb("wukv", [128, 1024])
            P.dma(wuq[:], I["mla_w_uq"][l].rearrange("(k p) c -> p k c", p=128)); P.dma(wukv[:], I["mla_w_ukv"][l])
            ua = [P.sb(f"ua{i}", [128, 416]) for i in range(2)]
            rt = [P.sb(f"rt{i}", [128, 2, 32]) for i in range(2)]
            junk = P.sb("junk", [128, 768]); junk2 = P.sb("junk2", [128, 768])
            st = P.sb("st", [128, 4]); ss16 = P.sb("ss16", [128, 16]); ssr = P.sb("ssr", [128, 1])
            qlat = P.sb("qlat", [128, 256]); ckv = [P.sb(f"ckv{i}", [128, 128]) for i in range(2)]
            qlT = P.sb("qlT", [128, 2, 128]); ckT = P.sb("ckT", [128, 128])
            qf = P.sb("qf", [128, 8, 96]); kvf = P.sb("kvf", [128, 8, 128]); kn = P.sb("kn", [128, 8, 96])
            kr = P.sb("kr", [128, 32])
            vp = [P.sb(f"vp{i}", [128, 8, 65], FDT) for i in range(2)]
            qT = [P.sb(f"qT{i}", [96, 8, 128], FDT) for i in range(2)]; kT = [P.sb(f"kT{i}", [96, 8, 128], FDT) for i in range(2)]
            for v in vp:
                P.copy("vector", v[:, :, 64:65], ones[:, 0:8].unsqueeze(2), wk=[v])
            ptr = P.ps("ptr", [128, 3, 128]); pq1 = P.ps("pq1", [128, 512]); pq2 = P.ps("pq2", [128, 256])
            pk1 = P.ps("pk1", [128, 512]); pk2 = P.ps("pk2", [128, 512])
            pT = [P.ps(f"pT{i}", [96, 4, 128]) for i in range(2)]
            tiles = [("new", i) for i in range(S // 128)] + ([("ctx", i) for i in range(CTX // 128)] if jb["ctx"] else [])
            for it, (kind, i) in enumerate(tiles):
                if it >= int(os.environ.get("KA1T", 999)):
                    break
                u = ua[it % 2]; ck = ckv[it % 2]; v_ = vp[it % 2]; r_ = rt[it % 2]
                if os.environ.get("KSTATS"):
                    print("A1 tile start", jb["name"], si, it, getattr(P, "nrec", 0))
                new = kind == "new"
                rows = slice(t_off + i * 128, t_off + (i + 1) * 128)
                krow = i * 128 if new else S + i * 128
                if new:
                    P.dma(u[:], jb["UT"][rows, 0:416], rk=[(jb["UT"].name, (t_off + i * 128) // 512, 0)])
                    if jb["ctx"]:
                        P.dma(r_[:], I["k_rope32"][i * 128:(i + 1) * 128])
                    P.act(junk[:, :256], u[:, 0:256], AF.Square, accum=st[:, 0:1])
                    P.act(junk[:, :128], u[:, 256:384], AF.Square, accum=st[:, 1:2])
                    P.act(st[:, 2:3], st[:, 0:1], AF.Sqrt, bias=NORM_EPS, scale=1.0 / 256)
                    P.act(st[:, 3:4], st[:, 1:2], AF.Sqrt, bias=NORM_EPS, scale=1.0 / 128)
                    P.recip(st[:, 2:4], st[:, 2:4])
                    P.stt("vector", qlat[:], u[:, 0:256], st[:, 2:3], gqa[:], ALU.mult, ALU.mult)
                    P.stt("vector", ck[:], u[:, 256:384], st[:, 3:4], gkv[:], ALU.mult, ALU.mult)
                    if not jb["ctx"]:
                        P.dma(O["o_ckv"][si, l, i * 128:(i + 1) * 128, :], ck[:], wk=[("o_ckv", si, l, i)], eng=STQ, final=True)
                        P.dma(O["o_ckr"][si, l, i * 128:(i + 1) * 128, :], u[:, 384:416], wk=[("o_ckr", si, l, i)], eng=STQ, final=True)
                    for k in range(2):
                        P.tr(ptr[:, k, :], qlat[:, k * 128:(k + 1) * 128], ident[:])
                    P.tr(ptr[:, 2, :], ck[:], ident[:])
                    P.evac(qlT[:], ptr[:, 0:2, :]); P.evac(ckT[:], ptr[:, 2, :])
                    for k in range(2):
                        P.mm(pq1[:], qlT[:, k, :], wuq[:, k, 0:512], start=(k == 0), stop=(k == 1))
                    for k in range(2):
                        P.mm(pq2[:], qlT[:, k, :], wuq[:, k, 512:768], start=(k == 0), stop=(k == 1))
                    qff = qf[:].rearrange("p h d -> p (h d)")
                    P.evac(qff[:, 0:512], pq1[:]); P.evac(qff[:, 512:768], pq2[:])
                    krope = u[:, 384:416]
                else:
                    P.dma(ck[:], I["ckv"][l, i * 128:(i + 1) * 128, :])
                    P.dma(u[:, 384:416], I["ckr"][l, i * 128:(i + 1) * 128, :])
                    P.tr(ptr[:, 2, :], ck[:], ident[:])
                    P.evac(ckT[:], ptr[:, 2, :])
                    krope = u[:, 384:416]
                P.mm(pk1[:], ckT[:], wukv[:, 0:512]); P.mm(pk2[:], ckT[:], wukv[:, 512:1024])
                kvff = kvf[:].rearrange("p h d -> p (h d)")
                P.evac(kvff[:, 0:512], pk1[:]); P.evac(kvff[:, 512:1024], pk2[:])
                j2 = junk2[:, :512].rearrange("p (h d) -> p h d", h=8)
                P.act(j2, kvf[:, :, 0:64], AF.Square)
                P.red("vector", ss16[:, 8:16], j2)
                P.act(junk[:, :32], krope, AF.Square, accum=ssr[:, 0:1])
                P.ts("vector", ss16[:, 8:16], ss16[:, 8:16], ssr[:, 0:1], None, op0=ALU.add)
                if new:
                    P.act(junk[:].rearrange("p (h d) -> p h d", h=8), qf[:], AF.Square)
                    P.red("vector", ss16[:, 0:8], junk[:].rearrange("p (h d) -> p h d", h=8))
                else:
                    P.memset("vector", ss16[:, 0:8], 1.0)
                rstd_of(ss16[:], ss16[:], 96, NORM_EPS)
                P.tt("vector", kn[:, :, 0:64], kvf[:, :, 0:64], ss16[:, 8:16].unsqueeze(2).broadcast_to([128, 8, 64]), ALU.mult)
                P.tt(PENG, kn[:, :, 0:64], kn[:, :, 0:64], gkn[:, 0:64].unsqueeze(1).broadcast_to([128, 8, 64]), ALU.mult)
                P.tt("vector", kr[:], krope, gkn[:, 64:96], ALU.mult)
                if new and jb["ctx"]:
                    rope(kr[:].unsqueeze(1), r_[:, 0, :], r_[:, 1, :], 1, 8, junk, junk2)
                P.tt("vector", kn[:, :, 64:96], kr[:].unsqueeze(1).broadcast_to([128, 8, 32]),
                     ss16[:, 8:16].unsqueeze(2).broadcast_to([128, 8, 32]), ALU.mult)
                P.copy(PENG, v_[:, :, 0:64], kvf[:, :, 64:128])
                P.dma(VPs[krow:krow + 128], v_[:], wk=[(VPs.name, krow)], eng=STQ)
                kt_ = kT[it % 2]
                for hh in range(2):
                    for h4 in range(4):
                        P.tr(pT[hh][:, h4, :], kn[:, hh * 4 + h4, :], ident[:])
                    P.evac(kt_[:, hh * 4:(hh + 1) * 4, :], pT[hh][:])
                P.dma(KTs[:, :, krow:krow + 128].rearrange("h d t -> d h t"), kt_[:], wk=[(KTs.name, krow)], eng=STQ)
                if new:
                    P.tt("vector", qf[:], qf[:], ss16[:, 0:8].unsqueeze(2).broadcast_to([128, 8, 96]), ALU.mult)
                    P.tt(PENG, qf[:], qf[:], gqn[:].unsqueeze(1).broadcast_to([128, 8, 96]), ALU.mult)
                    if jb["ctx"]:
                        rope(qf[:, :, 64:96], r_[:, 0, :], r_[:, 1, :], 8, 8, junk, junk2)
                    qt_ = qT[it % 2]
                    for hh in range(2):
                        for h4 in range(4):
                            P.tr(pT[hh][:, h4, :], qf[:, hh * 4 + h4, :], ident[:])
                        P.evac(qt_[:, hh * 4:(hh + 1) * 4, :], pT[hh][:])
                    P.dma(QTs[:, :, i * 128:(i + 1) * 128].rearrange("h d t -> d h t"), qt_[:], wk=[(QTs.name, i)], eng=STQ)
            P.pop()
            if os.environ.get("KSTOP", "") == "a1":
                continue
            P.push()
            QC = 256
            KT = [P.sb(f"KT{i}", [96, NK], FDT) for i in range(2)]
            VP = [P.sb(f"VP{i}", [128, NKT, 65], FDT) for i in range(2)]
            QT = [P.sb(f"QT{i}", [96, QC], FDT) for i in range(3)]
            pTs = [P.sb(f"pTs{i}", [128, NKT, QC], FDT) for i in range(2)]
            oT = [P.sb(f"oT{i}", [65, QC], FDT) for i in range(2)]; rec = P.sb("rec", [64, QC]); yT = [P.sb(f"yT{i}", [64, QC], FDT) for i in range(2)]
            sel65 = P.sb("sel65", [65, 64], FDT)
            sel65f = P.sb("sel65f", [65, 64])
            P.memset("vector", sel65f[:], 0.0); P.memset("vector", sel65f[64:65, :], 1.0)
            P.copy("vector", sel65[:], sel65f[:])
            pss = [P.ps(f"pss{i}", [128, 512]) for i in range(4)]
            po = [P.ps(f"po{i}", [65, QC]) for i in range(2)]
            pb = P.ps("pb", [64, QC])
            units = [(h, qc) for h in range(8) for qc in range(S // QC)]
            ik = [0]

            def qk(iu):
                h, qc = units[iu]
                K_ = KT[h % 2]; V_ = VP[h % 2]
                if qc == 0:
                    P.dma(K_[:], KTs[h], rk=[KTs.name + "*"])
                    P.dma(V_[:], VPs[:, h, :].rearrange("(kt p) c -> p kt c", p=128), rk=[VPs.name + "*"])
                Q_ = QT[iu % 3]; p_ = pTs[iu % 2]
                P.dma(Q_[:], QTs[h, :, qc * QC:(qc + 1) * QC], rk=[QTs.name + "*"])
                for kt in range(NKT):
                    ps_ = pss[ik[0] % 4]; ik[0] += 1
                    P.mm(ps_[:, :QC], K_[:, kt * 128:(kt + 1) * 128], Q_[:], fast=True)
                    P.act(p_[:, kt, :], ps_[:, :QC], AF.Exp, scale=MLA_SCALE)

            def pv(iu):
                h, qc = units[iu]
                V_ = VP[h % 2]; p_ = pTs[iu % 2]; po_ = po[iu % 2]; o_ = oT[iu % 2]; yT_ = yT[iu % 2]
                for kt in range(NKT):
                    P.mm(po_[:], V_[:, kt, :], p_[:, kt, :], start=(kt == 0), stop=(kt == NKT - 1), fast=True)
                P.copy("scalar", o_[:], po_[:])
                P.mm(pb[:], sel65[:], o_[:], fast=True)
                P.recip(rec[:], pb[:])
                P.tt("vector", yT_[:], o_[0:64, :], rec[:], ALU.mult)
                c0 = t_off + qc * QC
                P.dma(jb["YT"][0, h * 64:(h + 1) * 64, c0:c0 + QC], yT_[:], wk=[(jb["YT"].name, 0, h, c0)], eng=STQ)

            qk(0)
            for iu in range(len(units)):
                if iu + 1 < len(units):
                    qk(iu + 1)
                pv(iu)
            P.pop()


    def phaseC(l, jb):
        for si, (t_off, S) in enumerate(jb["seqs"]):
            NT = S // 128
            NCT = (CTX // 128) if jb["ctx"] else 0
            NKT = NT + NCT
            P.push()
            gq = bcast_row("gq", I["swa_q_norm"][l], 64); gk = bcast_row("gk", I["swa_k_norm"][l], 64)
            esink = bcast_row("esink", I["swa_sink"][l], 8)
            P.act(esink[:], esink[:], AF.Exp)
            tri = P.sb("tri", [128, 2, 128])
            P.dma(tri[:], I["k_tri"].rearrange("a k q -> k a q"))
            KTa = P.sb("KTa", [128, NKT * 128]); VPa = P.sb("VPa", [128, NKT, 2, 65])
            P.memset("vector", VPa[:, :, :, 64:65], 1.0, wk=[VPa])
            uk = [P.sb(f"uk{i}", [128, 256]) for i in range(2)]
            uq = [P.sb(f"uq{i}", [128, 512]) for i in range(2)]
            rt = [P.sb(f"rt{i}", [128, 2, 64]) for i in range(2)]
            junk = P.sb("junk", [128, 512]); junk2 = P.sb("junk2", [128, 512]); ss = P.sb("ss", [128, 8])
            kk = [P.sb(f"kk{i}", [128, 2, 64]) for i in range(2)]
            qp = P.sb("qp", [128, 4, 2, 64]); QT = [P.sb(f"QT{i}", [128, 4, 128]) for i in range(2)]
            pTa = [P.sb(f"pTa{i}", [128, 5, 4, 128]) for i in range(2)]
            yc = P.sb("yc", [128, 8, 64]); den = P.sb("den", [128, 4, 1]); ycT = [P.sb(f"ycT{i}", [128, 4, 128], FDT) for i in range(2)]
            ptk = P.ps("ptk", [128, 128]); ptq = P.ps("ptq", [128, 4, 128])
            pss = [P.ps(f"pss{i}", [128, 512]) for i in range(3)]
            po = [P.ps(f"po{i}", [128, 4, 65]) for i in range(2)]
            pyt = P.ps("pyt", [128, 4, 128])
            for i in range(NKT):
                new = i < NT
                u = uk[i % 2]; k_ = kk[i % 2]; r_ = rt[i % 2]
                if new:
                    rows = slice(t_off + i * 128, t_off + (i + 1) * 128)
                    P.dma(u[:], jb["UT"][rows, C_SK:C_SK + 256], rk=[(jb["UT"].name, (t_off + i * 128) // 512, 2480)])
                    kv = u[:, 0:128].rearrange("p (g d) -> p g d", g=2)
                    P.act(junk[:, :128].rearrange("p (g d) -> p g d", g=2), kv, AF.Square)
                    P.red("vector", ss[:, 0:2], junk[:, :128].rearrange("p (g d) -> p g d", g=2))
                    rstd_of(ss[:, 0:2], ss[:, 0:2], 64, NORM_EPS)
                    P.tt("vector", k_[:], kv, ss[:, 0:2].unsqueeze(2).broadcast_to([128, 2, 64]), ALU.mult)
                    P.tt("vector", k_[:], k_[:], gk[:].unsqueeze(1).broadcast_to([128, 2, 64]), ALU.mult)
                    if not jb["ctx"]:
                        P.dma(O["o_swk"][si, l, i * 128:(i + 1) * 128, :], k_[:].rearrange("p g d -> p (g d)"), wk=[("o_swk", si, l, i)], eng=STQ, final=True)
                        P.dma(O["o_swv"][si, l, i * 128:(i + 1) * 128, :], u[:, 128:256], wk=[("o_swv", si, l, i)], eng=STQ, final=True)
                    else:
                        P.dma(r_[:], I["k_rope64"][i * 128:(i + 1) * 128])
                        rope(k_[:], r_[:, 0, :], r_[:, 1, :], 2, 16, junk, junk2)
                    ksrc = k_[:].rearrange("p g d -> p (g d)")
                    vsrc = u[:, 128:256]
                else:
                    c = i - NT
                    P.dma(u[:, 0:128], I["cswk"][l, c * 128:(c + 1) * 128, :]); P.dma(u[:, 128:256], I["cswv"][l, c * 128:(c + 1) * 128, :])
                    ksrc = u[:, 0:128]; vsrc = u[:, 128:256]
                P.tr(ptk[:], ksrc, ident[:])
                P.copy("vector", KTa[:, i * 128:(i + 1) * 128], ptk[:])
                P.copy("vector", VPa[:, i, :, 0:64], vsrc.rearrange("p (g d) -> p g d", g=2))
            units = [(b, g) for b in range(NT) for g in range(2)]
            ik = [0]

            def ktiles(b):
                if not jb["ctx"]:
                    return [(kt, None) for kt in range(NT)]
                lst = []
                if b > 0:
                    lst.append((b - 1, 0))
                lst.append((b, None))
                if b < NT - 1:
                    lst.append((b + 1, 1))
                return lst + [(NT + c, None) for c in range(NCT)]

            def qprep(b):
                u = uq[b % 2]; r_ = rt[b % 2]; Q_ = QT[b % 2]
                rows = slice(t_off + b * 128, t_off + (b + 1) * 128)
                P.dma(u[:], jb["UT"][rows, C_SQ:C_SQ + 512], rk=[(jb["UT"].name, (t_off + b * 128) // 512, 1968)])
                qv = u[:].rearrange("p (h d) -> p h d", h=8)
                P.act(junk[:].rearrange("p (h d) -> p h d", h=8), qv, AF.Square)
                P.red("vector", ss[:], junk[:].rearrange("p (h d) -> p h d", h=8))
                rstd_of(ss[:], ss[:], 64, NORM_EPS)
                P.tt("vector", qv, qv, ss[:].unsqueeze(2).broadcast_to([128, 8, 64]), ALU.mult)
                P.tt("vector", qv, qv, gq[:].unsqueeze(1).broadcast_to([128, 8, 64]), ALU.mult)
                if jb["ctx"]:
                    P.dma(r_[:], I["k_rope64"][b * 128:(b + 1) * 128])
                    rope(qv, r_[:, 0, :], r_[:, 1, :], 8, 16, junk, junk2)
                P.copy("vector", qp[:].rearrange("p r g d -> p g r d"), u[:].rearrange("p (g r d) -> p g r d", g=2, r=4))
                for r in range(4):
                    P.tr(ptq[:, r, :], qp[:, r, :, :].rearrange("p g d -> p (g d)"), ident[:])
                P.copy("scalar", Q_[:], ptq[:])

            def qk(iu):
                b, g = units[iu]
                if g == 0:
                    qprep(b)
                Q_ = QT[b % 2]; p_ = pTa[iu % 2]
                pr = slice(g * 64, (g + 1) * 64)
                for j, (kt, mk) in enumerate(ktiles(b)):
                    ps_ = pss[ik[0] % 3]; ik[0] += 1
                    P.mm(ps_[:], KTa[pr, kt * 128:(kt + 1) * 128], Q_[pr, :, :].rearrange("p r q -> p (r q)"))
                    P.act(p_[:, j, :, :].rearrange("p r q -> p (r q)"), ps_[:], AF.Exp, scale=SW_SCALE)
                    if mk is not None:
                        P.tt("vector", p_[:, j, :, :], p_[:, j, :, :], tri[:, mk, :].unsqueeze(1).broadcast_to([128, 4, 128]), ALU.mult)

            def pv(iu):
                b, g = units[iu]
                p_ = pTa[iu % 2]; po_ = po[iu % 2]
                kts = ktiles(b)
                for r in range(4):
                    for j, (kt, mk) in enumerate(kts):
                        P.mm(po_[:, r, :], p_[:, j, r, :], VPa[:, kt, g, :], start=(j == 0), stop=(j == len(kts) - 1))
                P.tt("vector", den[:], po_[:, :, 64:65], esink[:, g * 4:(g + 1) * 4].unsqueeze(2), ALU.add)
                P.recip(den[:], den[:])
                P.tt("vector", yc[:, g * 4:(g + 1) * 4, :], po_[:, :, 0:64], den[:].broadcast_to([128, 4, 64]), ALU.mult)
                if g == 1:
                    yT_ = ycT[b % 2]
                    for c in range(4):
                        P.tr(pyt[:, c, :], yc[:, 2 * c:2 * c + 2, :].rearrange("p h d -> p (h d)"), ident[:])
                    P.copy("scalar", yT_[:], pyt[:])
                    c0 = t_off + b * 128
                    P.dma(jb["YT"][2, :, c0:c0 + 128].rearrange("(c p) t -> p c t", p=128), yT_[:], wk=[(jb["YT"].name, 2, c0)], eng=STQ)

            qk(0)
            for iu in range(len(units)):
                if iu + 1 < len(units):
                    qk(iu + 1)
                pv(iu)
            P.pop()


    def phaseB(l, jb):
        n = jb["name"]
        for si, (t_off, S) in enumerate(jb["seqs"]):
            NC = S // 128
            HS = scr(f"B_HS_{n}{l}_{si}", [2, S, 512])
            P.push()
            tri = P.sb("tri", [128, 2, 128]); neg = P.sb("neg", [128, 2, 128]); sel = P.sb("sel", [128, 2, 128])
            P.dma(tri[:], I["k_tri"].rearrange("a k q -> k a q")); P.dma(sel[:], I["k_sel"].rearrange("a k q -> k a q"))
            P.ts("vector", neg[:], tri[:], -1.0, 1e30, op0=ALU.add, op1=ALU.mult)
            TRI = [tri[:, 1, :], tri[:, 0, :]]
            NEG = [neg[:, 1, :], neg[:, 0, :]]
            NEGts = [neg[:, 0, :], neg[:, 1, :]]
            bias16 = P.sb("bias16", [128, 2, 8])
            P.dma(bias16[:, 0, :], I["mlstm_i_bias"][l].partition_broadcast(128), wk=[bias16])
            P.dma(bias16[:, 1, :], I["mlstm_f_bias"][l].partition_broadcast(128), wk=[bias16])
            Cst = P.sb("Cst", [64, 8, 129]); mprev = P.sb("mprev", [128, 8])
            if jb["ctx"]:
                P.dma(Cst[:, :, 0:128], I["mC"][l].rearrange("d h k v -> k (d h) v"), wk=[Cst])
                P.dma(Cst[:, :, 128:129], I["mn"][l].rearrange("d h (k o) -> k (d h) o", o=1), wk=[Cst], allow_slow_non_contiguous=True)
                P.dma(mprev[:], I["mm"][l].rearrange("d h -> (d h)").partition_broadcast(128))
            else:
                P.memset("vector", Cst[:], 0.0); P.memset("vector", mprev[:], 0.0)
            G = [P.sb(f"G{i}", [128, 2, 8]) for i in range(2)]
            QTd = [P.sb(f"QTd{i}", [64, 2, 4, 128]) for i in range(2)]; KTd = [P.sb(f"KTd{i}", [64, 2, 4, 128]) for i in range(2)]
            Kt = [P.sb(f"Kt{i}", [128, 2, 4, 64]) for i in range(2)]; VPd = [P.sb(f"VPd{i}", [128, 2, 4, 129]) for i in range(2)]
            for v in VPd:
                P.memset("vector", v[:, :, :, 128:129], 1.0, wk=[v])
            sp = P.sb("sp", [128, 8]); b = P.sb("b", [128, 8]); li = P.sb("li", [128, 8]); c = P.sb("c", [128, 8])
            MB = P.sb("MB", [128, 2, 2, 4]); cmax = P.sb("cmax", [128, 8]); bm = P.sb("bm", [128, 8]); ain = P.sb("ain", [128, 8]); en = P.sb("en", [128, 8])
            DG = P.sb("DG", [128, 8, 128]); DG2 = P.sb("DG2", [128, 8, 128]); Rm = P.sb("Rm", [128, 8, 128])
            ET = P.sb("ET", [128, 8, 128]); WT = P.sb("WT", [128, 8, 128])
            tI = P.sb("tI", [128, 8, 129]); numS = P.sb("numS", [128, 8, 129]); dab = P.sb("dab", [128, 8]); hh = P.sb("hh", [128, 8, 128])
            mbl = P.sb("mbl", [128, 2, 2, 4]); wk_ = P.sb("wk", [128, 8]); dec = P.sb("dec", [128, 8]); KW = P.sb("KW", [128, 8, 64])
            pA = [P.ps(f"pA{i}", [128, 4, 128]) for i in range(2)]
            pB = [P.ps(f"pB{i}", [128, 4, 128]) for i in range(2)]
            pC = [P.ps(f"pC{i}", [128, 3, 129]) for i in range(3)]
            pD = P.ps("pD", [128, 2, 8])
            grp3 = [(0, 0, 3), (1, 3, 6), (2, 6, 8)]

            def pc_slot(dh):
                return pC[dh // 3], dh % 3

            for j in range(NC):
                g_ = G[j % 2]; qt = QTd[j % 2]; kt = KTd[j % 2]; ktok = Kt[j % 2]; vp = VPd[j % 2]
                cd = [j, NC - 1 - j]
                for d in range(2):
                    r0 = t_off + cd[d] * 128
                    rows = slice(r0, r0 + 128)
                    sk = r0 // 512
                    P.dma(g_[:, :, d * 4:(d + 1) * 4], jb["UT"][rows, C_MI:C_MI + 16].rearrange("p (a e) -> p a e", a=2)[:, :, d * 4:(d + 1) * 4],
                          rk=[(jb["UT"].name, sk, 1440)], wk=[g_])
                    P.dma(qt[:, d, :, :], jb["UF"][C_MQ:C_MQ + 256, r0:r0 + 128].rearrange("(h p) t -> p h t", p=64), rk=[(jb["UF"].name, sk, 416)], wk=[qt])
                    P.dma(kt[:, d, :, :], jb["UF"][C_MK:C_MK + 256, r0:r0 + 128].rearrange("(h p) t -> p h t", p=64), rk=[(jb["UF"].name, sk, 672)], wk=[kt])
                    P.dma(ktok[:, d, :, :], jb["UT"][rows, C_MK:C_MK + 256].rearrange("p (h e) -> p h e", h=4), rk=[(jb["UT"].name, sk, 672)], wk=[ktok])
                    P.dma(vp[:, d, :, 0:128], jb["UT"][rows, C_MV:C_MV + 512].rearrange("p (h e) -> p h e", h=4), rk=[(jb["UT"].name, sk, 928)], wk=[vp])
                P.act(qt[:], qt[:], AF.Copy, scale=0.125)
                P.tt("vector", g_[:], g_[:], bias16[:], ALU.add)
                P.copy("vector", li[:], g_[:, 0, :])
                P.act(sp[:], g_[:, 1, :], AF.Exp, scale=-1.0)
                P.act(sp[:], sp[:], AF.Ln, bias=1.0)
                for d in range(2):
                    P.mm(pD[:, 0, d * 4:(d + 1) * 4], TRI[d], sp[:, d * 4:(d + 1) * 4])
                P.act(b[:], pD[:, 0, :], AF.Copy, scale=-1.0)
                P.tt("vector", c[:], li[:], b[:], ALU.subtract)
                P.tt("vector", DG[:], ident[:].unsqueeze(1).broadcast_to([128, 8, 128]), c[:].unsqueeze(2).broadcast_to([128, 8, 128]), ALU.mult)
                for dh in range(8):
                    P.mm(pA[dh // 4][:, dh % 4, :], ones[:], DG[:, dh, :])
                for d in range(2):
                    P.tt("vector", Rm[:, d * 4:(d + 1) * 4, :], pA[d][:], NEGts[d].unsqueeze(1).broadcast_to([128, 4, 128]), ALU.add)
                P.red("vector", cmax[:], Rm[:], op=ALU.max)
                v24 = lambda t: t.rearrange("p (d h) -> p d h", d=2)
                mt = MB[:, :, 0, :]
                P.tt("vector", mt, v24(mprev[:]), v24(cmax[:]), ALU.max)
                P.tt("vector", mt, mt, v24(b[:]), ALU.add)
                P.copy("vector", MB[:, :, 1, :], v24(b[:]))
                P.tt("vector", v24(bm[:]), v24(b[:]), mt, ALU.subtract)
                P.tt("vector", ain[:], bm[:], mprev[:], ALU.add)
                P.act(ain[:], ain[:], AF.Exp)
                P.act(v24(en[:]), mt, AF.Exp, scale=-1.0)
                P.tt("vector", DG2[:], ident[:].unsqueeze(1).broadcast_to([128, 8, 128]), bm[:].unsqueeze(2).broadcast_to([128, 8, 128]), ALU.mult)
                for dh in range(8):
                    o_ = pA[dh // 4][:, dh % 4, :]
                    P.mm(o_, ones[:], DG2[:, dh, :], start=True, stop=False)
                    P.mm(o_, DG[:, dh, :], ones[:], start=False, stop=False)
                    P.mm(o_, ident[:], NEG[dh // 4], start=False, stop=True)
                for d in range(2):
                    P.act(ET[:, d * 4:(d + 1) * 4, :], pA[d][:], AF.Exp)
                for dh in range(8):
                    d, h = dh // 4, dh % 4
                    P.mm(pB[d][:, h, :], kt[:, d, h, :], qt[:, d, h, :])
                for d in range(2):
                    P.tt("vector", WT[:, d * 4:(d + 1) * 4, :], ET[:, d * 4:(d + 1) * 4, :], pB[d][:], ALU.mult)
                for dh in range(8):
                    d, h = dh // 4, dh % 4
                    pc, sl = pc_slot(dh)
                    P.mm(pc[:, sl, :], qt[:, d, h, :], Cst[:, dh, :])
                for (bk, lo, hi) in grp3:
                    P.tt("vector", tI[:, lo:hi, :], pC[bk][:, 0:hi - lo, :], ain[:, lo:hi].unsqueeze(2).broadcast_to([128, hi - lo, 129]), ALU.mult)
                for dh in range(8):
                    d, h = dh // 4, dh % 4
                    pc, sl = pc_slot(dh)
                    P.mm(pc[:, sl, :], WT[:, dh, :], vp[:, d, h, :])
                for (bk, lo, hi) in grp3:
                    P.tt("vector", numS[:, lo:hi, :], pC[bk][:, 0:hi - lo, :], tI[:, lo:hi, :], ALU.add)
                P.act(dab[:].unsqueeze(2), numS[:, :, 128:129], AF.Abs)
                P.tt("vector", dab[:], dab[:], en[:], ALU.max)
                P.recip(dab[:], dab[:])
                P.tt("vector", hh[:], numS[:, :, 0:128], dab[:].unsqueeze(2).broadcast_to([128, 8, 128]), ALU.mult)
                for d in range(2):
                    r0 = cd[d] * 128
                    P.dma(HS[d, r0:r0 + 128, :].rearrange("p (h e) -> p h e", h=4), hh[:, d * 4:(d + 1) * 4, :], wk=[(HS.name, d, cd[d])], eng=STQ)
                for d in range(2):
                    P.mm(pD[:, d, :].rearrange("p (a h) -> p a h", a=2).rearrange("p a h -> p (a h)"), sel[:, d, :], MB[:, d, :, :].rearrange("p a h -> p (a h)"))
                P.copy("vector", mbl[:].rearrange("p d a h -> p (d a h)"), pD[:].rearrange("p a e -> p (a e)"))
                P.tt("vector", v24(wk_[:]), mbl[:, :, 1, :], mbl[:, :, 0, :], ALU.subtract)
                P.tt("vector", dec[:], wk_[:], mprev[:], ALU.add)
                P.act(dec[:], dec[:], AF.Exp)
                P.tt("vector", wk_[:], wk_[:], c[:], ALU.add)
                P.act(wk_[:], wk_[:], AF.Exp)
                for d in range(2):
                    P.tt("vector", KW[:, d * 4:(d + 1) * 4, :], ktok[:, d, :, :], wk_[:, d * 4:(d + 1) * 4].unsqueeze(2).broadcast_to([128, 4, 64]), ALU.mult)
                for dh in range(8):
                    d, h = dh // 4, dh % 4
                    pc, sl = pc_slot(dh)
                    P.mm(pc[0:64, sl, :], KW[:, dh, :], vp[:, d, h, :])
                P.tt("vector", Cst[:], Cst[:], dec[0:64, :].unsqueeze(2).broadcast_to([64, 8, 129]), ALU.mult)
                for (bk, lo, hi) in grp3:
                    P.tt("vector", Cst[:, lo:hi, :], Cst[:, lo:hi, :], pC[bk][0:64, 0:hi - lo, :], ALU.add)
                P.copy("vector", v24(mprev[:]), mbl[:, :, 0, :])
            if not jb["ctx"]:
                P.dma(O["o_mC"][si, l].rearrange("d h k v -> k (d h) v"), Cst[:, :, 0:128], wk=[("o_mC", si, l)], eng=STQ, final=True)
                P.dma(O["o_mn"][si, l].rearrange("d h (k o) -> k (d h) o", o=1), Cst[:, :, 128:129], wk=[("o_mn", si, l)], eng=STQ, final=True, allow_slow_non_contiguous=True)
                P.dma(O["o_mm"][si, l].rearrange("d (h o) -> o (d h)", o=1), mprev[0:1, :], wk=[("o_mm", si, l)], eng=STQ, final=True, allow_slow_non_contiguous=True)
            P.pop()
            P.push()
            gm = bcast_row("gm", I["mlstm_norm"][l], 128)
            h0 = [P.sb(f"h0{i}", [128, 4, 128]) for i in range(2)]; h1 = [P.sb(f"h1{i}", [128, 4, 128]) for i in range(2)]
            og = [P.sb(f"og{i}", [128, 512]) for i in range(2)]
            junk = P.sb("junk", [128, 4, 128]); ss = P.sb("ss", [128, 4]); yT = [P.sb(f"yT{i}", [128, 4, 128], FDT) for i in range(2)]
            pyt = P.ps("pyt", [128, 4, 128])
            for i in range(NC):
                a_ = h0[i % 2]; b_ = h1[i % 2]; o_ = og[i % 2]; y_ = yT[i % 2]
                rows = slice(t_off + i * 128, t_off + (i + 1) * 128)
                P.dma(a_[:], HS[0, i * 128:(i + 1) * 128, :].rearrange("p (h e) -> p h e", h=4), rk=[HS.name + "*"])
                P.dma(b_[:], HS[1, i * 128:(i + 1) * 128, :].rearrange("p (h e) -> p h e", h=4), rk=[HS.name + "*"])
                P.dma(o_[:], jb["UT"][rows, C_MO:C_MO + 512], rk=[(jb["UT"].name, (t_off + i * 128) // 512, 1456)])
                P.tt("vector", a_[:], a_[:], b_[:], ALU.add)
                P.act(junk[:], a_[:], AF.Square)
                P.red("vector", ss[:], junk[:])
                rstd_of(ss[:], ss[:], 128, NORM_EPS)
                P.act(o_[:], o_[:], AF.Sigmoid)
                P.tt("vector", a_[:], a_[:], ss[:].unsqueeze(2).broadcast_to([128, 4, 128]), ALU.mult)
                P.tt("vector", a_[:], a_[:], gm[:].unsqueeze(1).broadcast_to([128, 4, 128]), ALU.mult)
                P.tt("vector", a_[:], a_[:], o_[:].rearrange("p (h e) -> p h e", h=4), ALU.mult)
                for h in range(4):
                    P.tr(pyt[:, h, :], a_[:, h, :], ident[:])
                P.copy("scalar", y_[:], pyt[:])
                c0 = t_off + i * 128
                P.dma(jb["YT"][1, :, c0:c0 + 128].rearrange("(c p) t -> p c t", p=128), y_[:], wk=[(jb["YT"].name, 1, c0)], eng=STQ)
            P.pop()


    def phaseD(l, jb):
        n = jb["name"]
        CW = RW_DECAY
        for si, (t_off, S) in enumerate(jb["seqs"]):
            NCH = S // 64
            DF = scr(f"D_F_{n}{l}_{si}", [2, NCH, 2, 64, 4, 4, 64])
            LW = scr(f"D_LW_{n}{l}_{si}", [S, 2, 512])
            BG = scr(f"D_BG_{n}{l}_{si}", [2, 512, S])
            YS = scr(f"D_YS_{n}{l}_{si}", [2, S, 512])
            P.push()
            ST = min(512, S)
            kk = P.sb("kk", [64, 8]); ka = P.sb("ka", [64, 8]); omka = P.sb("omka", [64, 8]); uu = P.sb("uu", [64, 2, 8]); a0 = P.sb("a0", [64, 2, 8])
            P.dma(kk[:], I["rwkv_kk64"][l]); P.dma(ka[:], I["rwkv_ka64"][l]); P.dma(uu[:], I["rwkv_u64"][l]); P.dma(a0[:], I["rwkv_a064"][l])
            P.ts("vector", omka[:], ka[:], -1.0, 1.0, op0=ALU.mult, op1=ALU.add)
            w2 = P.sb("w2", [64, 2, 512]); a2 = P.sb("a2", [64, 2, 512]); g2 = P.sb("g2", [128, 512])
            P.dma(w2[:], I["rwkv_w2"][l].rearrange("d r c -> r d c")); P.dma(a2[:], I["rwkv_a2"][l].rearrange("d r c -> r d c")); P.dma(g2[:], I["rwkv_g2"][l])
            w0row = bcast_row("w0row", I["rwkv_w0"][l].rearrange("d c -> (d c)"), 1024)
            rT = P.sb("rT", [64, 8, ST]); kT = P.sb("kT", [64, 8, ST]); vT = P.sb("vT", [64, 8, ST])
            w1T = P.sb("w1T", [64, 2, ST]); a1T = P.sb("a1T", [64, 2, ST]); g1T = P.sb("g1T", [128, ST])
            kap = P.sb("kap", [64, 8, ST]); kh = P.sb("kh", [64, 8, ST]); tA = P.sb("tA", [64, 8, ST]); tB = P.sb("tB", [64, 8, ST])
            ktt = P.sb("ktt", [64, 8, ST]); rku = P.sb("rku", [64, 8, ST]); lw = [P.sb(f"lw{i}", [128, 2, 512]) for i in range(2)]
            pp = [P.ps(f"pp{i}", [128, 512]) for i in range(4)]
            ip = [0]

            def nps():
                ip[0] += 1
                return pp[ip[0] % 4]

            def store_df(d, arr, tile, s0):
                for cc in range(ST // 64):
                    for hh in range(2):
                        P.dma(DF[d, s0 // 64 + cc, hh, :, arr, :, :], tile[:, hh * 4:(hh + 1) * 4, cc * 64:(cc + 1) * 64], wk=[(DF.name, d, arr, s0, cc, hh)], eng=STQ)

            for s_ in range(S // ST):
                s0 = s_ * ST
                c0 = t_off + s0
                sk = c0 // 512
                UF = jb["UF"]
                P.dma(rT[:], UF[C_RR:C_RR + 512, c0:c0 + ST].rearrange("(h p) t -> p h t", p=64), rk=[(UF.name, sk, 2736)])
                P.dma(kT[:], UF[C_RK:C_RK + 512, c0:c0 + ST].rearrange("(h p) t -> p h t", p=64), rk=[(UF.name, sk, 3248)])
                P.dma(vT[:], UF[C_RV:C_RV + 512, c0:c0 + ST].rearrange("(h p) t -> p h t", p=64), rk=[(UF.name, sk, 3760)])
                P.dma(w1T[:], UF[C_RW:C_RW + 128, c0:c0 + ST].rearrange("(d p) t -> p d t", p=64), rk=[(UF.name, sk, 4272)])
                P.dma(a1T[:], UF[C_RA:C_RA + 128, c0:c0 + ST].rearrange("(d p) t -> p d t", p=64), rk=[(UF.name, sk, 4272)])
                P.dma(g1T[:], UF[C_RG:C_RG + 128, c0:c0 + ST], rk=[(UF.name, sk, 4272)])
                P.act(w1T[:], w1T[:], AF.Tanh)
                P.act(g1T[:], g1T[:], AF.Sigmoid)
                P.tt("vector", kap[:], kT[:], kk[:].unsqueeze(2).broadcast_to([64, 8, ST]), ALU.mult)
                P.act(tA[:], kap[:], AF.Square)
                for h in range(8):
                    ps_ = nps()
                    P.mm(ps_[0:64, :ST], ones[0:64, 0:64], tA[:, h, :])
                    P.act(kh[:, h, :], ps_[0:64, :ST], AF.Sqrt, bias=1e-12)
                P.recip(kh[:], kh[:])
                P.tt("vector", kh[:], kh[:], kap[:], ALU.mult)
                for d in range(2):
                    store_df(d, 0, rT, s0); store_df(d, 1, kh, s0)
                for d in range(2):
                    for h in range(8):
                        ps_ = nps()
                        P.mm(ps_[0:64, :ST], a2[:, d, h * 64:(h + 1) * 64], a1T[:, d, :])
                        P.act(tA[:, h, :], ps_[0:64, :ST], AF.Sigmoid, bias=a0[:, d, h:h + 1])
                    P.tt("vector", tB[:], tA[:], kh[:], ALU.mult)
                    store_df(d, 3, tB, s0)
                    P.tt("vector", tA[:], tA[:], ka[:].unsqueeze(2).broadcast_to([64, 8, ST]), ALU.mult)
                    P.tt("vector", tA[:], tA[:], omka[:].unsqueeze(2).broadcast_to([64, 8, ST]), ALU.add)
                    P.tt("vector", ktt[:], kT[:], tA[:], ALU.mult)
                    store_df(d, 2, ktt, s0)
                    P.tt("vector", tA[:], ktt[:], rT[:], ALU.mult)
                    if d == 0:
                        P.tt("vector", rku[:], tA[:], uu[:, d, :].unsqueeze(2).broadcast_to([64, 8, ST]), ALU.mult)
                    else:
                        P.tt("vector", tA[:], tA[:], uu[:, d, :].unsqueeze(2).broadcast_to([64, 8, ST]), ALU.mult)
                        P.tt("vector", rku[:], rku[:], tA[:], ALU.add)
                for h in range(8):
                    ps_ = nps()
                    P.mm(ps_[0:64, :ST], ones[0:64, 0:64], rku[:, h, :])
                    P.tt("vector", tB[:, h, :], ps_[0:64, :ST], vT[:, h, :], ALU.mult)
                P.dma(BG[0, :, s0:s0 + ST].rearrange("(h p) t -> p h t", p=64), tB[:], wk=[(BG.name, 0, s0)], eng=STQ)
                for h in range(8):
                    ps_ = nps()
                    P.mm(ps_[0:64, :ST], g2[:, h * 64:(h + 1) * 64], g1T[:])
                    P.copy("scalar", kap[:, h, :], ps_[0:64, :ST])
                P.dma(BG[1, :, s0:s0 + ST].rearrange("(h p) t -> p h t", p=64), kap[:], wk=[(BG.name, 1, s0)], eng=STQ)
                for tt in range(ST // 128):
                    lw_ = lw[tt % 2]
                    for d in range(2):
                        ps_ = nps()
                        P.mm(ps_[:], w1T[:, d, tt * 128:(tt + 1) * 128], w2[:, d, :])
                        P.tt("vector", lw_[:, d, :], ps_[:], w0row[:, d * 512:(d + 1) * 512], ALU.add)
                    P.act(lw_[:], lw_[:], AF.Sigmoid)
                    P.dma(LW[s0 + tt * 128:s0 + (tt + 1) * 128], lw_[:], wk=[(LW.name, s0, tt)], eng=STQ)
            P.pop()
            P.push()
            tri = P.sb("tri", [128, 2, 128]); P.dma(tri[:], I["k_tri"].rearrange("a k q -> k a q"))
            trs = P.sb("trs", [128, 2, 128]); P.dma(trs[:], I["k_tris"].rearrange("a k q -> k a q"))
            HP = [slice(0, 64), slice(64, 128)]
            cum = P.sb("cum", [128, 2, 2, 64]); mask4 = P.sb("mask4", [128, 2, 4, 64]); maskT = P.sb("maskT", [128, 2, 64])
            for hh in range(2):
                pr = HP[hh]
                INC = [tri[pr, 1, pr], tri[pr, 0, pr]]; STR = [trs[pr, 1, pr], trs[pr, 0, pr]]
                for d in range(2):
                    P.copy("vector", cum[pr, d, 0, :], INC[d], wk=[cum]); P.copy("vector", cum[pr, d, 1, :], STR[d], wk=[cum])
                    for a_, m_ in enumerate([STR[d], INC[d], STR[d], INC[d]]):
                        P.copy("vector", mask4[pr, d, a_, :], m_, wk=[mask4])
                    P.copy("vector", maskT[pr, d, :], STR[1 - d], wk=[maskT])
            idh = [ident[HP[0], HP[0]], ident[HP[1], HP[1]]]
            id4 = P.sb("id4", [128, 4, 64])
            for hh in range(2):
                for h4 in range(4):
                    P.copy("vector", id4[HP[hh], h4, :], idh[hh], wk=[id4])
            TS = P.sb("TS", [128, 2, 4, 64])
            bG = P.ps("bG", [128, 512]); bX = [P.ps(f"bX{i}", [128, 512]) for i in range(2)]; bY = P.ps("bY", [128, 256])
            bA = P.ps("bA", [128, 512]); bP = P.ps("bP", [128, 256]); bZ = [P.ps(f"bZ{i}", [128, 256]) for i in range(2)]
            s0t = P.sb("s0t", [128, 2, 4, 64])

            def trm(out, in_, hh):
                P.mm(out, in_, idh[hh])

            if jb["ctx"]:
                for hh in range(2):
                    for d in range(2):
                        P.dma(s0t[HP[hh], d], I["rw"][l][d, hh * 4:(hh + 1) * 4].rearrange("h v k -> v h k"), wk=[s0t])
                for d in range(2):
                    for h in range(8):
                        hh, h4 = h // 4, h % 4
                        trm(bX[0][HP[hh], (d * 4 + h4) * 64:(d * 4 + h4 + 1) * 64], s0t[HP[hh], d, h4, :], hh)
                P.copy("vector", TS[:].rearrange("p d h v -> p (d h v)"), bX[0][:])
            else:
                P.memset("vector", TS[:], 0.0)
            Xd = [[P.sb(f"Xd{d}{i}", [128, 4, 4, 64]) for i in range(2)] for d in range(2)]
            Vd = [[P.sb(f"Vd{d}{i}", [128, 4, 64]) for i in range(2)] for d in range(2)]
            LWd = [[P.sb(f"LWd{d}{i}", [128, 256]) for i in range(2)] for d in range(2)]
            EI = P.sb("EI", [128, 4, 64]); EX = P.sb("EX", [128, 4, 64]); EN = P.sb("EN", [128, 4, 64]); gl = P.sb("gl", [128, 4])
            KR = P.sb("KR", [128, 4, 2, 64]); KtM = P.sb("KtM", [128, 4, 64]); BM = P.sb("BM", [128, 4, 64]); KBe = P.sb("KBe", [128, 4, 2, 64])
            AM = P.sb("AM", [128, 4, 4, 64]); N0 = P.sb("N0", [128, 4, 64])
            AB = [P.sb(f"AB{i}", [128, 4, 2, 64]) for i in range(2)]; PI = [P.sb(f"PI{i}", [128, 4, 64]) for i in range(2)]
            BI = [P.sb(f"BI{i}", [128, 4, 64]) for i in range(2)]
            RH = P.sb("RH", [128, 4, 64]); Un = P.sb("Un", [128, 4, 64]); Yo = [P.sb(f"Yo{i}", [128, 4, 64]) for i in range(2)]
            KBt = P.sb("KBt", [128, 4, 2, 64])
            HH = [(h // 4, h % 4) for h in range(8)]

            def dpass(j, d):
                c = j if d == 0 else NCH - 1 - j
                X = Xd[d][j % 2]; V = Vd[d][j % 2]; LWc = LWd[d][j % 2]
                r0 = t_off + c * 64
                for hh in range(2):
                    pr = HP[hh]
                    P.dma(X[pr], DF[d, c, hh], rk=[DF.name + "*"], wk=[X])
                    P.dma(V[pr], jb["UT"][r0:r0 + 64, C_RV + hh * 256:C_RV + (hh + 1) * 256].rearrange("p (h e) -> p h e", h=4),
                          rk=[(jb["UT"].name, r0 // 512, 3760)], wk=[V])
                    P.dma(LWc[pr], LW[c * 64:(c + 1) * 64, d, hh * 256:(hh + 1) * 256], rk=[LW.name + "*"], wk=[LWc])
                Rr = X[:, 0, :, :]; Kh = X[:, 1, :, :]; Kt = X[:, 2, :, :]; Bb = X[:, 3, :, :]
                for (hh, h4) in HH:
                    pr = HP[hh]
                    P.mm(bG[pr, h4 * 128:(h4 + 1) * 128], LWc[pr, h4 * 64:(h4 + 1) * 64], cum[pr, d, :, :].rearrange("p a t -> p (a t)"))
                gv = bG[:].rearrange("p (h a t) -> p h a t", h=4, a=2)
                P.act(EI[:], gv[:, :, 0, :], AF.Exp, scale=-CW)
                P.act(EN[:], gv[:, :, 0, :], AF.Exp, scale=CW)
                P.act(EX[:], gv[:, :, 1, :], AF.Exp, scale=-CW)
                last = 63 if d == 0 else 0
                P.copy("vector", gl[:].unsqueeze(2), EI[:, :, last:last + 1])
                P.tt("vector", KR[:, :, 1, :], Rr, EI[:], ALU.mult)
                P.tt("vector", KR[:, :, 0, :], Kh, EX[:], ALU.mult)
                P.tt("vector", KtM[:], Kt, EN[:], ALU.mult)
                P.tt("vector", BM[:], Bb, EN[:], ALU.mult)
                P.tt("vector", KBe[:, :, 0, :], KtM[:], gl[:].unsqueeze(2).broadcast_to([128, 4, 64]), ALU.mult)
                P.tt("vector", KBe[:, :, 1, :], BM[:], gl[:].unsqueeze(2).broadcast_to([128, 4, 64]), ALU.mult)
                for (hh, h4) in HH:
                    pr = HP[hh]
                    o_ = bX[h4 // 2][pr, (h4 % 2) * 256:(h4 % 2 + 1) * 256]
                    rhs = KR[pr, h4, :, :].rearrange("p a t -> p (a t)")
                    P.mm(o_[:, 0:128], KtM[pr, h4, :], rhs)
                    P.mm(o_[:, 128:256], BM[pr, h4, :], rhs)
                for q in range(2):
                    P.tt("vector", AM[:, q * 2:(q + 1) * 2, :, :], bX[q][:].rearrange("p (h a t) -> p h a t", h=2, a=4),
                         mask4[:, d, :, :].unsqueeze(1).broadcast_to([128, 2, 4, 64]), ALU.mult)
                for (hh, h4) in HH:
                    pr = HP[hh]
                    P.mm(bY[pr, h4 * 64:(h4 + 1) * 64], KR[pr, h4, 0, :], BM[pr, h4, :])
                P.tt("vector", N0[:], bY[:].rearrange("p (h s) -> p h s", h=4), maskT[:, d, :].unsqueeze(1).broadcast_to([128, 4, 64]), ALU.mult)
                A_ = lambda pr, h4: AM[pr, h4, 2, :]
                B_ = lambda pr, h4: N0[pr, h4, :]
                Pc = PI[0]
                P.tt("vector", Pc[:], id4[:], AM[:, :, 2, :], ALU.subtract)
                for lv in range(1, 6):
                    ab = AB[lv % 2]
                    for (hh, h4) in HH:
                        pr = HP[hh]
                        o_ = bA[pr, h4 * 128:(h4 + 1) * 128]
                        if lv < 5:
                            P.mm(o_[:, 0:64], B_(pr, h4), A_(pr, h4))
                        P.mm(o_[:, 64:128], A_(pr, h4), B_(pr, h4))
                    src = bA[:].rearrange("p (h a t) -> p h a t", h=4, a=2)
                    bi = BI[lv % 2]
                    P.tt("vector", bi[:], src[:, :, 1, :], id4[:], ALU.add)
                    if lv < 5:
                        P.copy("scalar", ab[:], src)
                    A_ = (lambda ab: (lambda pr, h4: ab[pr, h4, 0, :]))(ab)
                    B_ = (lambda ab: (lambda pr, h4: ab[pr, h4, 1, :]))(ab)
                    for (hh, h4) in HH:
                        pr = HP[hh]
                        P.mm(bP[pr, h4 * 64:(h4 + 1) * 64], bi[pr, h4, :], Pc[pr, h4, :])
                    Pn = PI[lv % 2]
                    P.copy("vector", Pn[:].rearrange("p h t -> p (h t)"), bP[:])
                    Pc = Pn
                for (hh, h4) in HH:
                    pr = HP[hh]
                    o_ = bZ[0][pr, h4 * 64:(h4 + 1) * 64]
                    P.mm(o_, KR[pr, h4, 0, :], TS[pr, d, h4, :], start=True, stop=False)
                    P.mm(o_, AM[pr, h4, 0, :], V[pr, h4, :], start=False, stop=True)
                P.copy("scalar", RH[:].rearrange("p h t -> p (h t)"), bZ[0][:])
                for (hh, h4) in HH:
                    pr = HP[hh]
                    P.mm(bZ[1][pr, h4 * 64:(h4 + 1) * 64], Pc[pr, h4, :], RH[pr, h4, :])
                P.act(Un[:].rearrange("p h t -> p (h t)"), bZ[1][:], AF.Copy, scale=-1.0)
                Y_ = Yo[j % 2]
                for (hh, h4) in HH:
                    pr = HP[hh]
                    o_ = bZ[0][pr, h4 * 64:(h4 + 1) * 64]
                    P.mm(o_, KR[pr, h4, 1, :], TS[pr, d, h4, :], start=True, stop=False)
                    P.mm(o_, AM[pr, h4, 1, :], V[pr, h4, :], start=False, stop=False)
                    P.mm(o_, AM[pr, h4, 3, :], Un[pr, h4, :], start=False, stop=True)
                P.copy("scalar", Y_[:].rearrange("p h t -> p (h t)"), bZ[0][:])
                for hh in range(2):
                    P.dma(YS[d, c * 64:(c + 1) * 64, hh * 256:(hh + 1) * 256], Y_[HP[hh]].rearrange("p h t -> p (h t)"), wk=[(YS.name, d, c, hh)], eng=STQ)
                for (hh, h4) in HH:
                    pr = HP[hh]
                    for a_ in range(2):
                        trm(bX[0][pr, (h4 * 2 + a_) * 64:(h4 * 2 + a_ + 1) * 64], KBe[pr, h4, a_, :], hh)
                P.copy("vector", KBt[:].rearrange("p h a t -> p (h a t)"), bX[0][:])
                for (hh, h4) in HH:
                    pr = HP[hh]
                    o_ = bZ[1][pr, h4 * 64:(h4 + 1) * 64]
                    P.mm(o_, KBt[pr, h4, 0, :], V[pr, h4, :], start=True, stop=False)
                    P.mm(o_, KBt[pr, h4, 1, :], Un[pr, h4, :], start=False, stop=True)
                Td = TS[:, d, :, :]
                P.tt("vector", Td, Td, gl[:].unsqueeze(2).broadcast_to([128, 4, 64]), ALU.mult)
                P.tt("vector", Td, Td, bZ[1][:].rearrange("p (h t) -> p h t", h=4), ALU.add)

            for j in range(NCH):
                for d in range(2):
                    dpass(j, d)
            if not jb["ctx"]:
                for d in range(2):
                    for (hh, h4) in HH:
                        trm(bX[0][HP[hh], (d * 4 + h4) * 64:(d * 4 + h4 + 1) * 64], TS[HP[hh], d, h4, :], hh)
                P.copy("vector", s0t[:].rearrange("p d h k -> p (d h k)"), bX[0][:])
                for hh in range(2):
                    for d in range(2):
                        P.dma(O["o_rw"][si, l][d, hh * 4:(hh + 1) * 4].rearrange("h v k -> v h k"), s0t[HP[hh], d], wk=[("o_rw", si, l, hh, d)], eng=STQ, final=True)
            P.pop()
            P.push()
            gng = P.sb("gng", [128, 4]); gnb = P.sb("gnb", [128, 4])
            P.dma(gng[:], I["rwkv_gn_gT"][l]); P.dma(gnb[:], I["rwkv_gn_bT"][l])
            y0 = [P.sb(f"y0{i}", [128, 8, 64]) for i in range(2)]; y1 = [P.sb(f"y1{i}", [128, 8, 64]) for i in range(2)]
            bg = [P.sb(f"bg{i}", [128, 2, 4, 128]) for i in range(2)]
            junk = P.sb("junk", [128, 8, 64]); st8 = P.sb("st8", [128, 8]); yT = [P.sb(f"yT{i}", [128, 4, 128], FDT) for i in range(2)]
            ytmp = P.sb("ytmp", [128, 4, 128])
            pyt = P.ps("pyt", [128, 4, 128])
            for i in range(S // 128):
                a_ = y0[i % 2]; b_ = y1[i % 2]; g_ = bg[i % 2]; y_ = yT[i % 2]
                P.dma(a_[:], YS[0, i * 128:(i + 1) * 128, :].rearrange("p (h e) -> p h e", h=8), rk=[YS.name + "*"])
                P.dma(b_[:], YS[1, i * 128:(i + 1) * 128, :].rearrange("p (h e) -> p h e", h=8), rk=[YS.name + "*"])
                P.dma(g_[:], BG[:, :, i * 128:(i + 1) * 128].rearrange("a (c p) t -> p a c t", p=128), rk=[BG.name + "*"])
                P.tt("vector", a_[:], a_[:], b_[:], ALU.add)
                P.red("vector", st8[:], a_[:])
                P.ts("vector", st8[:], st8[:], -1.0 / 64, None, op0=ALU.mult)
                P.tt("vector", a_[:], a_[:], st8[:].unsqueeze(2).broadcast_to([128, 8, 64]), ALU.add)
                P.act(junk[:], a_[:], AF.Square)
                P.red("vector", st8[:], junk[:])
                rstd_of(st8[:], st8[:], 64, RW_GN_EPS)
                P.tt("vector", a_[:], a_[:], st8[:].unsqueeze(2).broadcast_to([128, 8, 64]), ALU.mult)
                for c in range(4):
                    P.tr(pyt[:, c, :], a_[:, 2 * c:2 * c + 2, :].rearrange("p h e -> p (h e)"), ident[:])
                for c in range(4):
                    P.act(ytmp[:, c, :], pyt[:, c, :], AF.Identity, bias=gnb[:, c:c + 1], scale=gng[:, c:c + 1])
                P.tt("vector", ytmp[:], ytmp[:], g_[:, 0, :, :], ALU.add)
                P.tt("vector", y_[:], ytmp[:], g_[:, 1, :, :], ALU.mult)
                c0 = t_off + i * 128
                P.dma(jb["YT"][3, :, c0:c0 + 128].rearrange("(c p) t -> p c t", p=128), y_[:], wk=[(jb["YT"].name, 3, c0)], eng=STQ)
            P.pop()


    def phaseM1(l, jb):
        TOK, j = jb["TOK"], jb["j"]
        ST = 512
        P.push()
        Yb = P.sb("Yb", [128, 4, 4, ST], FDT)
        Wo = P.sb("Wo", [128, 4, 4, D], FDT)
        wo2 = P.sb("wo2", [128, 8, D], FDT)
        Gt = [P.sb(f"Gt{i}", [128, 4, ST]) for i in range(2)]
        mg = P.sb("mg", [128, 8, ST], FDT); tmp = [P.sb(f"tmp{i}", [128, ST]) for i in range(3)]
        xT = P.sb("xT", [128, 8, ST])
        pp = [P.ps(f"pp{i}", [128, 512]) for i in range(6)]
        XTv = jb["XT"].rearrange("(k p) t -> p k t", p=128)
        wnames = ["mla_w_o", "mlstm_w_o", "swa_w_o", "rwkv_w_o"]
        for b in range(4):
            P.dma(Wo[:, b, :, :], WR[wnames[b]][l].rearrange("(c p) n -> p c n", p=128), wk=[Wo])
        for k2 in range(2):
            P.dma(wo2[:, k2 * 4:(k2 + 1) * 4, :], WR["w_out"][l][k2 * 512:(k2 + 1) * 512, :].rearrange("(k p) n -> p k n", p=128), wk=[wo2])
        ip = 0; io = 0
        for s_ in range(TOK // ST):
            t0 = s_ * ST
            Y_ = Yb; x_ = xT
            for b in range(4):
                P.dma(Y_[:, b, :, :], jb["YT"][b, :, t0:t0 + ST].rearrange("(c p) t -> p c t", p=128), rk=[jb["YT"].name + "*"], wk=[Y_])
            P.dma(x_[:], XTv[:, :, t0:t0 + ST], rk=[(jb["XT"].name, s_)])
            for oc in range(8):
                G_ = Gt[io % 2]; io += 1
                P.dma(G_[:], jb["UF"][C_GATE:C_GATE + 4096, t0:t0 + ST].rearrange("(b o p) t -> p b o t", b=4, o=8)[:, :, oc, :], rk=[jb["UF"].name + "*"])
                for b in range(4):
                    ps_ = pp[ip % 6]; ip += 1
                    for c in range(4):
                        P.mm(ps_[:, :ST], Wo[:, b, c, oc * 128:(oc + 1) * 128], Y_[:, b, c, :], start=(c == 0), stop=(c == 3), fast=True)
                    if b == 0:
                        P.tt("vector", tmp[2][:], ps_[:, :ST], G_[:, b, :], ALU.mult)
                    else:
                        t_ = tmp[b % 2]
                        P.tt("vector", t_[:], ps_[:, :ST], G_[:, b, :], ALU.mult)
                        P.tt(PENG, mg[:, oc, :] if b == 3 else tmp[2][:], tmp[2][:], t_[:], ALU.add)
            for oc in range(8):
                ps_ = pp[ip % 6]; ip += 1
                for k in range(8):
                    P.mm(ps_[:, :ST], wo2[:, k, oc * 128:(oc + 1) * 128], mg[:, k, :], start=(k == 0), stop=(k == 7), fast=True)
                P.stt("vector", x_[:, oc, :], ps_[:, :ST], modT[l][:, 16 + oc, j:j + 1], x_[:, oc, :], ALU.mult, ALU.add)
            P.dma(XTv[:, :, t0:t0 + ST], x_[:], wk=[(jb["XT"].name, s_)], eng=STQ)
        P.pop()

    def phaseM2(l, jb, last):
        TOK, j = jb["TOK"], jb["j"]
        ST = 512
        P.push()
        x1 = P.sb("x1", [128, 8, ST]); sq = P.sb("sq", [128, 8, ST]); h2 = P.sb("h2", [128, 8, ST], FDT); rstd = P.sb("rstd", [128, ST])
        w1c = [P.sb(f"w1c{i}", [128, 8, 256], FDT) for i in range(2)]
        hid = P.sb("hid", [128, 32, ST], FDT); rl = [P.sb(f"rl{i}", [128, ST]) for i in range(2)]
        w2c = [P.sb(f"w2c{i}", [128, 32, 128], FDT) for i in range(2)]
        ytok = [P.sb(f"ytok{i}", [128, D]) for i in range(2)]
        pst = P.ps("pst", [128, ST])
        pp = [P.ps(f"pp{i}", [128, 512]) for i in range(6)]
        XTv = jb["XT"].rearrange("(k p) t -> p k t", p=128)
        ip = 0; i1 = 0; i2 = 0
        for s_ in range(TOK // ST):
            t0 = s_ * ST
            P.dma(x1[:], XTv[:, :, t0:t0 + ST], rk=[(jb["XT"].name, s_)])
            rms_rstd_featmajor(x1, sq, pst, rstd, ST)
            P.tt("vector", sq[:], x1[:], rstd[:].unsqueeze(1).broadcast_to([128, 8, ST]), ALU.mult)
            for k in range(8):
                P.act(h2[:, k, :], sq[:, k, :], AF.Identity, bias=modT[l][:, 24 + k, j:j + 1], scale=A2[l][:, k, j:j + 1])
            for fc in range(32):
                if fc % 2 == 0:
                    w_ = w1c[i1 % 2]; i1 += 1
                    P.dma(w_[:], WR["mlp_w1"][l][:, fc * 128:(fc + 2) * 128].rearrange("(k p) n -> p k n", p=128))
                ps_ = pp[ip % 6]; ip += 1
                for k in range(8):
                    P.mm(ps_[:, :ST], w_[:, k, (fc % 2) * 128:(fc % 2 + 1) * 128], h2[:, k, :], start=(k == 0), stop=(k == 7), fast=True)
                r_ = rl[fc % 2]
                P.act(r_[:], ps_[:, :ST], AF.Relu)
                P.tt(PENG, hid[:, fc, :], r_[:], r_[:], ALU.mult)
            x2 = sq
            for oc in range(8):
                w_ = w2c[i2 % 2]; i2 += 1
                for q in range(4):
                    P.dma(w_[:, q * 8:(q + 1) * 8, :], WR["mlp_w2"][l][q * 1024:(q + 1) * 1024, oc * 128:(oc + 1) * 128].rearrange("(f p) n -> p f n", p=128), wk=[w_])
                ps_ = pp[ip % 6]; ip += 1
                for fc in range(32):
                    P.mm(ps_[:, :ST], w_[:, fc, :], hid[:, fc, :], start=(fc == 0), stop=(fc == 31), fast=True)
                P.stt("vector", x2[:, oc, :], ps_[:, :ST], modT[l][:, 40 + oc, j:j + 1], x1[:, oc, :], ALU.mult, ALU.add)
            if not last:
                P.dma(XTv[:, :, t0:t0 + ST], x2[:], wk=[(jb["XT"].name, s_)], eng=STQ)
            else:
                for tt in range(ST // 128):
                    yt = ytok[tt % 2]
                    for kk in range(2):
                        ps_ = pp[ip % 6]; ip += 1
                        for k4 in range(4):
                            P.tr(ps_[:, k4 * 128:(k4 + 1) * 128], x2[:, kk * 4 + k4, tt * 128:(tt + 1) * 128], ident[:])
                        P.evac(yt[:, kk * 512:(kk + 1) * 512], ps_[:])
                    P.dma(jb["y"][t0 + tt * 128:t0 + (tt + 1) * 128, :], yt[:], wk=[("y", j, t0, tt)], eng=STQ, final=True)
        P.pop()

    import os
    stop = os.environ.get("KSTOP", "")
    WR = {}
    fast_w = ["w_in", "mla_w_o", "mlstm_w_o", "swa_w_o", "rwkv_w_o", "w_out", "mlp_w1", "mlp_w2"]
    if FAST_MM:
        P.push()
        CH = 2048
        raw = [P.sb(f"wraw{i}", [128, CH]) for i in range(3)]
        rnd = [P.sb(f"wrnd{i}", [128, CH], F32R) for i in range(3)]
        engs = ["gpsimd", "vector", "scalar"]
        iw = 0
        for name in fast_w:
            src = I[name]
            _, Rr, Cc = src.shape
            dst = nc.dram_tensor(name + "_r", [L, Rr, Cc], F32R, kind="Internal").ap()
            WR[name] = dst
            for l in range(L):
                for rb in range(Rr // 128):
                    for c0 in range(0, Cc, CH):
                        w = min(CH, Cc - c0)
                        a_ = raw[iw % 3]; b_ = rnd[iw % 3]
                        P.dma(a_[:, :w], src[l, rb * 128:(rb + 1) * 128, c0:c0 + w])
                        P.copy(engs[iw % 3], b_[:, :w], a_[:, :w])
                        P.dma(dst[l, rb * 128:(rb + 1) * 128, c0:c0 + w], b_[:, :w], wk=[(dst.name, l, rb, c0)], eng=STQ)
                        iw += 1
        P.pop()
    else:
        for name in fast_w:
            WR[name] = I[name]

    only = os.environ.get("KONLY", "")
    for l in range(L):
        phase0(l)
        for jb in jobs:
            phase1(l, jb, first=(l == 0))
        for nm, fn in (("A", phaseA), ("C", phaseC), ("B", phaseB), ("D", phaseD)):
            if only and nm not in only:
                continue
            for jb in jobs:
                fn(l, jb)
        if only and "M" not in only:
            break
        for jb in jobs:
            phaseM1(l, jb)
        for jb in jobs:
            phaseM2(l, jb, last=(l == L - 1))
    print("NREC", getattr(P, "nrec", 0), flush=True)
    P.emit()
    return nc, I, O, SCR


def _fm(v, width=8):
    v = np.asarray(v, np.float32)
    return np.ascontiguousarray(np.swapaxes(v.reshape(v.shape[:-1] + (width, 128)), -1, -2))


def _rope_table(S, R):
    q = R // 4
    t = np.arange(S)
    pr = (t // 64).astype(np.float32); pc = (t % 64).astype(np.float32)
    inv = (10000.0 ** (-np.arange(q, dtype=np.float32) / q)).astype(np.float32)
    ar = pr[:, None] * inv; ac = pc[:, None] * inv
    ang = np.concatenate([ar, ar, ac, ac], -1).astype(np.float32)
    sign = np.concatenate([-np.ones(q), np.ones(q), -np.ones(q), np.ones(q)]).astype(np.float32)
    return np.ascontiguousarray(np.stack([np.cos(ang), np.sin(ang) * sign], 1).astype(np.float32))


def make_in_map(inp, cfg, core):
    L = cfg.depth
    f = lambda a: np.ascontiguousarray(np.asarray(a, np.float32))
    n_p = cfg.n_p
    m = {}
    m["xs"] = f(inp["x_sample"][core])
    m["xp"] = f(inp["x_prompt"][core * n_p:(core + 1) * n_p].reshape(n_p * SP, D))
    cc = np.stack([np.asarray(inp["c"][core]), np.asarray(inp["c_ctx"])], 0)
    m["cT"] = f(cc.reshape(2, 8, 128).transpose(2, 1, 0))
    m["ckv"] = f(inp["cache_mla_ckv"][core]); m["ckr"] = f(inp["cache_mla_krope"][core])
    m["cswk"] = f(np.asarray(inp["cache_swa_k"][core]).reshape(L, CTX, 128))
    m["cswv"] = f(np.asarray(inp["cache_swa_v"][core]).reshape(L, CTX, 128))
    m["mC"] = f(inp["state_mlstm_C"][core]); m["mn"] = f(inp["state_mlstm_n"][core])
    m["mm"] = f(inp["state_mlstm_m"][core]); m["rw"] = f(inp["state_rwkv"][core])
    m["ada_w"] = f(inp["ada_w"]); m["ada_bT"] = _fm(inp["ada_b"], 48)
    m["norm1T"] = _fm(inp["norm1"]); m["norm2T"] = _fm(inp["norm2"])
    for k in ["w_in", "mla_q_a_norm", "mla_kv_a_norm", "mla_w_uq", "mla_w_ukv", "mla_q_norm", "mla_k_norm", "mla_w_o",
              "mlstm_norm", "mlstm_w_o", "swa_q_norm", "swa_k_norm", "swa_sink", "swa_w_o", "rwkv_w2", "rwkv_a2",
              "rwkv_g2", "rwkv_w_o", "w_out", "mlp_w1", "mlp_w2"]:
        m[k] = f(inp[k])
    m["mlstm_i_bias"] = f(np.asarray(inp["mlstm_i_bias"]).reshape(L, 8))
    m["mlstm_f_bias"] = f(np.asarray(inp["mlstm_f_bias"]).reshape(L, 8))
    for k in ["rwkv_gn_g", "rwkv_gn_b"]:
        m[k + "T"] = _fm(inp[k], 4)
    m["rwkv_w0"] = f(inp["rwkv_w0"])
    for k in ["rwkv_kk", "rwkv_ka"]:
        m[k + "64"] = f(np.asarray(inp[k]).reshape(L, 8, 64).transpose(0, 2, 1))
    for k in ["rwkv_a0", "rwkv_u"]:
        m[k + "64"] = f(np.asarray(inp[k]).reshape(L, 2, 8, 64).transpose(0, 3, 1, 2))
    m["k_ident"] = np.eye(128, dtype=np.float32)
    m["k_ones"] = np.ones((128, 128), np.float32)
    kq = np.arange(128)
    m["k_tri"] = np.stack([(kq[:, None] >= kq[None, :]), (kq[:, None] <= kq[None, :])], 0).astype(np.float32)
    m["k_tris"] = np.stack([(kq[:, None] > kq[None, :]), (kq[:, None] < kq[None, :])], 0).astype(np.float32)
    sel = np.zeros((2, 128, 128), np.float32); sel[0, 127, :] = 1.0; sel[1, 0, :] = 1.0
    m["k_sel"] = sel
    m["k_rope32"] = _rope_table(cfg.S_s, 32); m["k_rope64"] = _rope_table(cfg.S_s, 64)
    return m


_CACHE = {}


def kernel(**inputs):
    cfg = Cfg(S_s=4096, n_p=2, depth=2)
    n_cores = 8
    if "nc" not in _CACHE:
        _CACHE["nc"] = build(cfg)
    nc, I, O, SCR = _CACHE["nc"]
    in_maps = []
    for c in range(n_cores):
        m = make_in_map(inputs, cfg, c)
        in_maps.append({k: v for k, v in m.items() if k in I})
    res = run_bass_kernel_spmd(nc, in_maps, core_ids=list(range(n_cores)))
    R = res.results
    L = cfg.depth
    cat = lambda k: np.concatenate([np.asarray(r[k], np.float32) for r in R], 0)
    y_prompt = cat("y_p").reshape(16, SP, D)
    y_sample = np.stack([np.asarray(r["y_s"], np.float32) for r in R], 0)
    return (y_prompt, y_sample,
            cat("o_ckv"), cat("o_ckr"),
            cat("o_swk").reshape(16, L, SP, 2, 64), cat("o_swv").reshape(16, L, SP, 2, 64),
            cat("o_mC"), cat("o_mn"), cat("o_mm"), cat("o_rw"))
```

```python
import os
import numpy as np
import concourse.bass as bass
import concourse.mybir as mybir
from concourse.bass_utils import run_bass_kernel_spmd
from contextlib import ExitStack

F32 = mybir.dt.float32
F32R = mybir.dt.float32r
FAST_MM = os.environ.get("FAST_MM", "1") == "1"
FDT = F32R if FAST_MM else F32


def fr(ap):
    return ap.bitcast(F32R) if FAST_MM else ap
AF = mybir.ActivationFunctionType
ALU = mybir.AluOpType
AX = mybir.AxisListType

ENGS = ("sync", "scalar", "vector", "gpsimd", "tensor")
SEM_LIMIT = int(os.environ.get("SEM_LIMIT", 30000))
N_DMA_SEMS = 16
import os
STQ = os.environ.get("STQ", "gpsimd")
PENG = os.environ.get("PENG", "vector")

D = 1024
NORM_EPS = 1e-6
CTX = 256
SP = 256
D_IN = 8752
D_FF = 4096
MLA_SCALE = 96 ** -0.5
SW_SCALE = 64 ** -0.5
RW_DECAY = 0.6065306597126334
RW_GN_EPS = 64e-5
C_QA, C_KVA, C_KR = 0, 256, 384
C_MQ, C_MK, C_MV, C_MI, C_MF, C_MO = 416, 672, 928, 1440, 1448, 1456
C_SQ, C_SK, C_SV = 1968, 2480, 2608
C_RR, C_RK, C_RV, C_RW, C_RA, C_RG = 2736, 3248, 3760, 4272, 4400, 4528
C_GATE = 4656
NTOKC = 4656


class Op:
    __slots__ = ("eng", "fn", "deps", "is_dma", "signal", "sem", "cnt", "dsem_prev", "barriered")

    def __init__(self, eng, fn, is_dma):
        self.eng = eng
        self.fn = fn
        self.deps = []
        self.is_dma = is_dma
        self.signal = False
        self.sem = None
        self.cnt = 0
        self.dsem_prev = None
        self.barriered = False


class Prog:
    def __init__(self, nc):
        self.nc = nc
        self.ops = {e: [] for e in ENGS}
        self.lastw = {}
        self.readers = {}
        self.stacks = [ExitStack()]
        self.out_dmas = []
        self.uid = 0
        self.rr = 0
        self.psum_names = set()

    def sb(self, name, shape, dt=F32):
        self.uid += 1
        return self.stacks[-1].enter_context(self.nc.sbuf_tensor(f"{name}_{self.uid}", list(shape), dt))

    def ps(self, name, shape, dt=F32):
        self.uid += 1
        n = 1
        for d in shape[1:]:
            n *= d
        nb = (n * 4 + 2047) // 2048
        t = self.stacks[-1].enter_context(self.nc.psum_tensor(f"{name}_{self.uid}", [128, nb * 512], dt))
        self.psum_names.add(t.name)
        v = t[:shape[0], :n]
        if len(shape) == 3:
            v = v.rearrange("p (a b) -> p a b", a=shape[1])
        elif len(shape) == 4:
            v = v.rearrange("p (a b c) -> p a b c", a=shape[1], b=shape[2])
        return v

    def push(self):
        self.stacks.append(ExitStack())

    def pop(self):
        self.barrier()
        self.stacks.pop().close()

    @staticmethod
    def _key(k):
        if isinstance(k, (str, tuple)):
            return k
        return k.name

    def op(self, eng, fn, reads=(), writes=(), is_dma=False):
        self.nrec = getattr(self, "nrec", 0) + 1
        if self.nrec > int(os.environ.get("KMAXOPS", 10 ** 9)):
            return None
        o = Op(eng, fn, is_dma)
        if os.environ.get("KTRACE") and abs(self.nrec - int(os.environ["KTRACE"])) <= 6:
            print("OP", self.nrec, eng, [self._key(k) for k in writes], flush=True)
        rk = [self._key(k) for k in reads if k is not None and not isinstance(k, (int, float))]
        wk = [self._key(k) for k in writes]
        if eng != "tensor":
            wk = wk + [k for k in rk if k in self.psum_names and k not in wk]
        raw = set()
        deps = set()
        for k in rk:
            w = self.lastw.get(k)
            if w is not None:
                raw.add(w)
                deps.add(w)
        for k in wk:
            w = self.lastw.get(k)
            if w is not None:
                deps.add(w)
            for r in self.readers.get(k, ()):
                deps.add(r)
        for d in deps:
            if d.eng == eng and not d.is_dma and not is_dma:
                if eng == "tensor":
                    continue
            o.deps.append(d)
        for k in rk:
            self.readers.setdefault(k, []).append(o)
        for k in wk:
            self.lastw[k] = o
            self.readers[k] = []
        self.ops[eng].append(o)
        return o

    def barrier(self):
        lasts = [self.ops[e][-1] for e in ENGS if self.ops[e]]
        pend = [o for e in ("sync", "scalar", "gpsimd") for o in self.ops[e] if o.is_dma and not o.barriered]
        for o in pend:
            o.barriered = True
        for e in ENGS:
            b = Op(e, None, False)
            b.deps = lasts + pend
            self.ops[e].append(b)
        self.lastw.clear()
        self.readers.clear()

    def dma(self, out, in_, rk=None, wk=None, eng="sync", final=False, **kw):
        r = [in_] if rk is None else rk
        w = [out] if wk is None else wk
        o = self.op(eng, lambda e: e.dma_start(out=out, in_=in_, **kw), r, w, is_dma=True)
        if final and o is not None:
            self.out_dmas.append(o)
        return o

    def mm(self, out, lhsT, rhs, start=True, stop=True, rk=None, wk=None, fast=False):
        r = [lhsT, rhs] if rk is None else rk
        w = [out] if wk is None else wk
        return self.op("tensor", lambda e: e.matmul(out, lhsT=lhsT, rhs=rhs, start=start, stop=stop), r, w)

    def tr(self, out, in_, ident, rk=None, wk=None):
        r = [in_, ident] if rk is None else rk
        w = [out] if wk is None else wk
        return self.op("tensor", lambda e: e.transpose(out, in_, ident), r, w)

    def act(self, out, in_, func, bias=None, scale=None, accum=None, rk=None, wk=None, extra_r=()):
        kw = {}
        if bias is not None:
            kw["bias"] = bias
        if scale is not None:
            kw["scale"] = scale
        if accum is not None:
            kw["accum_out"] = accum
        r = ([in_, bias, scale] if rk is None else list(rk)) + list(extra_r)
        w = ([out] + ([accum] if accum is not None else [])) if wk is None else wk
        return self.op("scalar", lambda e: e.activation(out=out, in_=in_, func=func, **kw), r, w)

    def tt(self, eng, out, in0, in1, op, rk=None, wk=None):
        r = [in0, in1] if rk is None else rk
        w = [out] if wk is None else wk
        return self.op(eng, lambda e: e.tensor_tensor(out=out, in0=in0, in1=in1, op=op), r, w)

    def ts(self, eng, out, in0, s1, s2=None, op0=ALU.mult, op1=None, rk=None, wk=None):
        r = [in0, s1, s2] if rk is None else rk
        w = [out] if wk is None else wk
        if op1 is None:
            return self.op(eng, lambda e: e.tensor_scalar(out=out, in0=in0, scalar1=s1, scalar2=None, op0=op0), r, w)
        return self.op(eng, lambda e: e.tensor_scalar(out=out, in0=in0, scalar1=s1, scalar2=s2, op0=op0, op1=op1), r, w)

    def stt(self, eng, out, in0, scalar, in1, op0, op1, rk=None, wk=None):
        r = [in0, scalar, in1] if rk is None else rk
        w = [out] if wk is None else wk
        return self.op(eng, lambda e: e.scalar_tensor_tensor(out=out, in0=in0, scalar=scalar, in1=in1, op0=op0, op1=op1), r, w)

    def copy(self, eng, out, in_, rk=None, wk=None):
        r = [in_] if rk is None else rk
        w = [out] if wk is None else wk
        if eng == "scalar":
            return self.op(eng, lambda e: e.copy(out=out, in_=in_), r, w)
        return self.op(eng, lambda e: e.tensor_copy(out=out, in_=in_), r, w)

    def red(self, eng, out, in_, op=ALU.add, rk=None, wk=None):
        r = [in_] if rk is None else rk
        w = [out] if wk is None else wk
        return self.op(eng, lambda e: e.tensor_reduce(out=out, in_=in_, axis=AX.X, op=op), r, w)

    def recip(self, out, in_, rk=None, wk=None):
        r = [in_] if rk is None else rk
        w = [out] if wk is None else wk
        return self.op("vector", lambda e: e.reciprocal(out=out, in_=in_), r, w)

    def memset(self, eng, out, val, wk=None):
        w = [out] if wk is None else wk
        return self.op(eng, lambda e: e.memset(out, val), [], w)

    def evac(self, out, in_, rk=None, wk=None):
        self.rr += 1
        ev = os.environ.get("KEVAC", "alt")
        if ev == "alt":
            ev = "scalar" if self.rr % 2 else "vector"
        return self.copy(ev, out, in_, rk, wk)

    def emit(self):
        nc = self.nc
        if self.out_dmas:
            b = Op("sync", None, False)
            b.deps = list(self.out_dmas)
            self.ops["sync"].append(b)
        for e in ENGS:
            for o in self.ops[e]:
                for d in o.deps:
                    d.signal = True
        with ExitStack() as st:
            def newsem(nm):
                return st.enter_context(nc.semaphore(nm))
            for e in ENGS:
                cur, c, ep = None, 0, 0
                for o in self.ops[e]:
                    if o.is_dma or not o.signal or o.fn is None:
                        continue
                    if cur is None or c >= SEM_LIMIT:
                        cur = newsem(f"s_{e}_{ep}")
                        ep += 1
                        c = 0
                    c += 1
                    o.sem = cur
                    o.cnt = c
            for e in ENGS:
                dl = [o for o in self.ops[e] if o.is_dma]
                if not dl:
                    continue
                dsems = [newsem(f"dq_{e}_{i}") for i in range(N_DMA_SEMS)]
                dcnt = [0] * N_DMA_SEMS
                dlast = [None] * N_DMA_SEMS
                for di, o in enumerate(dl):
                    s_ = di % N_DMA_SEMS
                    if dcnt[s_] + 16 > SEM_LIMIT:
                        dsems[s_] = newsem(f"dq_{e}_{s_}_{di}")
                        dcnt[s_] = 0
                    dcnt[s_] += 16
                    o.sem = dsems[s_]
                    o.cnt = dcnt[s_]
                    o.dsem_prev = dlast[s_]
                    dlast[s_] = o
                    o.signal = True
            if os.environ.get("KSTATS"):
                for e in ENGS:
                    sig = [o for o in self.ops[e] if o.signal and not o.is_dma and o.fn is not None]
                    print("ENG", e, "ops", len(self.ops[e]), "signals", len(sig), "maxcnt", max([o.cnt for o in self.ops[e]] + [0]), flush=True)
            with nc.Block() as block:
                def run(e):
                    def body(eng):
                        seen = {}
                        for o in self.ops[e]:
                            need = {}
                            deps = o.deps
                            if o.is_dma and o.dsem_prev is not None:
                                deps = deps + [o.dsem_prev]
                            for d in deps:
                                if d.fn is None or d.sem is None:
                                    continue
                                nm = d.sem.name
                                if need.get(nm, (None, 0))[1] < d.cnt:
                                    need[nm] = (d.sem, d.cnt)
                            for nm, (s, c) in need.items():
                                if seen.get(nm, 0) >= c:
                                    continue
                                eng.wait_ge(s, c)
                                seen[nm] = c
                            if o.fn is None:
                                continue
                            ins = o.fn(eng)
                            if o.signal:
                                ins.then_inc(o.sem, 16 if o.is_dma else 1)
                    return body
                block.sync(run("sync"))
                block.scalar(run("scalar"))
                block.vector(run("vector"))
                block.gpsimd(run("gpsimd"))
                block.tensor(run("tensor"))
        while self.stacks:
            self.stacks.pop().close()


class Cfg:
    def __init__(self, S_s=4096, n_p=2, depth=2, debug=()):
        self.S_s = S_s
        self.n_p = n_p
        self.depth = depth
        self.debug = tuple(debug)


W_NAMES = ["ada_w", "w_in", "mla_w_uq", "mla_w_ukv", "mla_w_o", "mlstm_w_o", "swa_w_o", "rwkv_w2", "rwkv_a2",
           "rwkv_g2", "rwkv_w_o", "w_out", "mlp_w1", "mlp_w2"]

P1_BLOCKS = [
    (0, 416, "T"), (416, 672, "F"), (672, 928, "TF"), (928, 1440, "T"), (1440, 1456, "T"), (1456, 1968, "T"),
    (1968, 2480, "T"), (2480, 2736, "T"), (2736, 3248, "F"), (3248, 3760, "F"), (3760, 4272, "TF"),
    (4272, 4656, "F"),
] + [(4656 + 512 * i, 4656 + 512 * (i + 1), "G") for i in range(8)]


def build(cfg):
    nc = bass.Bass("TRN2", target_bir_lowering=False)
    if FAST_MM:
        nc.dge_precook = False
    L = cfg.depth
    S_s, n_p = cfg.S_s, cfg.n_p
    TOKP = n_p * SP
    P = Prog(nc)
    I = {}
    O = {}
    SCR = {}

    def din(name, shape):
        I[name] = nc.dram_tensor(name, list(shape), F32, kind="ExternalInput").ap()
        return I[name]

    def dout(name, shape):
        O[name] = nc.dram_tensor(name, list(shape), F32, kind="ExternalOutput").ap()
        return O[name]

    def scr(name, shape, dt=F32):
        kind = "ExternalOutput" if name in cfg.debug else "Internal"
        SCR[name] = nc.dram_tensor(name, list(shape), dt, kind=kind).ap()
        return SCR[name]

    din("xs", [S_s, D]); din("xp", [TOKP, D])
    din("cT", [128, 8, 2])
    din("ckv", [L, CTX, 128]); din("ckr", [L, CTX, 32]); din("cswk", [L, CTX, 128]); din("cswv", [L, CTX, 128])
    din("mC", [L, 2, 4, 64, 128]); din("mn", [L, 2, 4, 64]); din("mm", [L, 2, 4]); din("rw", [L, 2, 8, 64, 64])
    din("ada_w", [L, D, 6 * D]); din("ada_bT", [L, 128, 48]); din("norm1T", [L, 128, 8]); din("norm2T", [L, 128, 8])
    din("w_in", [L, D, D_IN])
    din("mla_q_a_norm", [L, 256]); din("mla_kv_a_norm", [L, 128]); din("mla_w_uq", [L, 256, 768])
    din("mla_w_ukv", [L, 128, 1024]); din("mla_q_norm", [L, 96]); din("mla_k_norm", [L, 96]); din("mla_w_o", [L, 512, D])
    din("mlstm_i_bias", [L, 8]); din("mlstm_f_bias", [L, 8]); din("mlstm_norm", [L, 128]); din("mlstm_w_o", [L, 512, D])
    din("swa_q_norm", [L, 64]); din("swa_k_norm", [L, 64]); din("swa_sink", [L, 8]); din("swa_w_o", [L, 512, D])
    din("rwkv_w0", [L, 2, 512]); din("rwkv_w2", [L, 2, 64, 512]); din("rwkv_a064", [L, 64, 2, 8])
    din("rwkv_a2", [L, 2, 64, 512]); din("rwkv_g2", [L, 128, 512]); din("rwkv_kk64", [L, 64, 8]); din("rwkv_ka64", [L, 64, 8])
    din("rwkv_u64", [L, 64, 2, 8]); din("rwkv_gn_gT", [L, 128, 4]); din("rwkv_gn_bT", [L, 128, 4]); din("rwkv_w_o", [L, 512, D])
    din("w_out", [L, D, D]); din("mlp_w1", [L, D, D_FF]); din("mlp_w2", [L, D_FF, D])
    din("k_ident", [128, 128]); din("k_ones", [128, 128])
    din("k_rope32", [S_s, 2, 32]); din("k_rope64", [S_s, 2, 64]); din("k_tri", [2, 128, 128]); din("k_sel", [2, 128, 128]); din("k_tris", [2, 128, 128])
    dout("y_p", [TOKP, D]); dout("y_s", [S_s, D])
    dout("o_ckv", [n_p, L, SP, 128]); dout("o_ckr", [n_p, L, SP, 32]); dout("o_swk", [n_p, L, SP, 128]); dout("o_swv", [n_p, L, SP, 128])
    dout("o_mC", [n_p, L, 2, 4, 64, 128]); dout("o_mn", [n_p, L, 2, 4, 64]); dout("o_mm", [n_p, L, 2, 4]); dout("o_rw", [n_p, L, 2, 8, 64, 64])

    jobs = [dict(name="s", TOK=S_s, seqs=[(0, S_s)], ctx=True, j=0, x=I["xs"], y=O["y_s"]),
            dict(name="p", TOK=TOKP, seqs=[(i * SP, SP) for i in range(n_p)], ctx=False, j=1, x=I["xp"], y=O["y_p"])]
    for jb in jobs:
        n = jb["name"]
        jb["XT"] = scr(f"XT_{n}", [D, jb["TOK"]])
        jb["UT"] = scr(f"UTOK_{n}", [jb["TOK"], NTOKC])
        jb["UF"] = scr(f"UFEAT_{n}", [D_IN, jb["TOK"]])
        jb["YT"] = scr(f"YT_{n}", [4, 512, jb["TOK"]], FDT)

    ident = P.sb("ident", [128, 128]); ones = P.sb("ones", [128, 128])
    P.dma(ident[:], I["k_ident"][:, :], wk=[ident]); P.dma(ones[:], I["k_ones"][:, :], wk=[ones])
    cT = P.sb("cT", [128, 8, 2]); sT = P.sb("sT", [128, 8, 2])
    P.dma(cT[:], I["cT"][:, :, :])
    P.act(sT[:], cT[:], AF.Silu)
    modT = [P.sb(f"modT{l}", [128, 48, 2]) for l in range(L)]
    A1 = [P.sb(f"A1_{l}", [128, 8, 2]) for l in range(L)]
    A2 = [P.sb(f"A2_{l}", [128, 8, 2]) for l in range(L)]

    def phase0(l):
        P.push()
        wb = [P.sb(f"adaw{i}", [128, 8, 512]) for i in range(2)]
        pm = P.ps("pm", [128, 48, 2])
        bT = P.sb("bT", [128, 48]); n1 = P.sb("n1", [128, 8]); n2 = P.sb("n2", [128, 8])
        P.dma(bT[:], I["ada_bT"][l]); P.dma(n1[:], I["norm1T"][l]); P.dma(n2[:], I["norm2T"][l])
        wv = I["ada_w"][l].rearrange("(k p) c -> p k c", p=128)
        for g in range(12):
            w = wb[g % 2]
            P.dma(w[:], wv[:, :, g * 512:(g + 1) * 512])
            for jj in range(4):
                jc = g * 4 + jj
                for k in range(8):
                    P.mm(pm[:, jc, :], w[:, k, jj * 128:(jj + 1) * 128], sT[:, k, :], start=(k == 0), stop=(k == 7))
        P.tt("vector", modT[l][:], pm[:], bT[:].unsqueeze(2).broadcast_to([128, 48, 2]), ALU.add)
        P.stt("vector", A1[l][:], modT[l][:, 8:16, :], 1.0, n1[:].unsqueeze(2).broadcast_to([128, 8, 2]), ALU.add, ALU.mult)
        P.stt("vector", A2[l][:], modT[l][:, 32:40, :], 1.0, n2[:].unsqueeze(2).broadcast_to([128, 8, 2]), ALU.add, ALU.mult)
        P.pop()

    def rms_rstd_featmajor(xT, sq, pst, rstd, n):
        P.act(sq[:, :, :n], xT[:, :, :n], AF.Square)
        for k in range(8):
            P.mm(pst[:, :n], ones[:], sq[:, k, :n], start=(k == 0), stop=(k == 7))
        P.act(rstd[:, :n], pst[:, :n], AF.Sqrt, bias=NORM_EPS, scale=1.0 / D)
        P.recip(rstd[:, :n], rstd[:, :n])

    def phase1(l, jb, first):
        TOK, j = jb["TOK"], jb["j"]
        ST = 512
        P.push()
        xT = [P.sb(f"xT{i}", [128, 8, ST]) for i in range(2)]
        hT = [P.sb(f"hT{i}", [128, 8, ST], FDT) for i in range(2)]
        sq = P.sb("sq", [128, 8, ST]); rstd = P.sb("rstd", [128, ST])
        wt = [P.sb(f"wt{i}", [128, 8, 512], FDT) for i in range(3)]
        stg = [P.sb(f"stg{i}", [128, 4, 512]) for i in range(3)]
        xin = [P.sb(f"xin{i}", [128, D]) for i in range(2)] if first else None
        pst = P.ps("pst", [128, ST])
        pp = [P.ps(f"pp{i}", [128, 512]) for i in range(6)]
        XTv = jb["XT"].rearrange("(k p) t -> p k t", p=128)
        wv = WR["w_in"][l].rearrange("(k p) c -> p k c", p=128)
        UTv = jb["UT"].rearrange("(tt p) c -> p tt c", p=128)
        ip = 0
        ist = 0
        iw = 0
        for s in range(TOK // ST):
            x_ = xT[s % 2]; h_ = hT[s % 2]
            t0 = s * ST
            if first:
                for tt in range(4):
                    xi = xin[tt % 2]
                    P.dma(xi[:], jb["x"][t0 + tt * 128:t0 + (tt + 1) * 128, :])
                    for kk in range(2):
                        pq = pp[ip % 6]; ip += 1
                        for k4 in range(4):
                            P.tr(pq[:, k4 * 128:(k4 + 1) * 128], xi[:, (kk * 4 + k4) * 128:(kk * 4 + k4 + 1) * 128], ident[:])
                        P.evac(x_[:, kk * 4:(kk + 1) * 4, tt * 128:(tt + 1) * 128], pq[:].rearrange("p (a b) -> p a b", a=4))
                P.dma(XTv[:, :, t0:t0 + ST], x_[:], wk=[(jb["XT"].name, s)], eng=STQ)
            else:
                P.dma(x_[:], XTv[:, :, t0:t0 + ST], rk=[(jb["XT"].name, s)])
            rms_rstd_featmajor(x_, sq, pst, rstd, ST)
            P.tt("vector", sq[:], x_[:], rstd[:].unsqueeze(1).broadcast_to([128, 8, ST]), ALU.mult)
            for k in range(8):
                P.act(h_[:, k, :], sq[:, k, :], AF.Identity, bias=modT[l][:, k, j:j + 1], scale=A1[l][:, k, j:j + 1])
            for (cs, ce, lay) in P1_BLOCKS:
                wd = ce - cs
                w = wt[iw % 3]; iw += 1
                P.dma(w[:, :, :wd], wv[:, :, cs:ce])
                if "T" in lay:
                    sg = stg[ist % 3]; ist += 1
                    for tt in range(4):
                        pq = pp[ip % 6]; ip += 1
                        for k in range(8):
                            P.mm(pq[:, :wd], h_[:, k, tt * 128:(tt + 1) * 128], w[:, k, :wd], start=(k == 0), stop=(k == 7), fast=(wd >= 256))
                        P.evac(sg[:, tt, :wd], pq[:, :wd])
                    P.dma(UTv[:, s * 4:(s + 1) * 4, cs:ce], sg[:, :, :wd], wk=[(jb["UT"].name, s, cs)], eng=STQ)
                if "F" in lay or "G" in lay:
                    sg = stg[ist % 3]; ist += 1
                    nb = wd // 128
                    for cb in range(nb):
                        pq = pp[ip % 6]; ip += 1
                        for k in range(8):
                            P.mm(pq[:, :ST], w[:, k, cb * 128:(cb + 1) * 128], h_[:, k, :], start=(k == 0), stop=(k == 7), fast=True)
                        if lay == "G":
                            P.act(sg[:, cb, :], pq[:, :ST], AF.Sigmoid)
                        else:
                            P.evac(sg[:, cb, :], pq[:, :ST])
                    P.dma(jb["UF"][cs:ce, t0:t0 + ST].rearrange("(cb p) t -> p cb t", p=128), sg[:, :nb, :],
                          wk=[(jb["UF"].name, s, cs)], eng=STQ)
        P.pop()


    def rope(x, cos, sinS, H, q, t1, t2):
        R4 = 4 * q
        t1v = t1[:, :H * R4].rearrange("p (h r) -> p h r", h=H)
        t2v = t2[:, :H * R4].rearrange("p (h a b c) -> p h a b c", h=H, a=2, b=2)
        xv = x.rearrange("p h (a b c) -> p h a b c", a=2, b=2)
        sv = sinS.rearrange("p (a b c) -> p a b c", a=2, b=2)
        P.tt("vector", t1v, x, cos.unsqueeze(1).broadcast_to([128, H, R4]), ALU.mult)
        for b in range(2):
            P.tt(PENG, t2v[:, :, :, b, :], xv[:, :, :, 1 - b, :],
                 sv[:, :, b, :].unsqueeze(1).broadcast_to([128, H, 2, q]), ALU.mult)
        P.tt("vector", x, t1v, t2[:, :H * R4].rearrange("p (h r) -> p h r", h=H), ALU.add)

    def rstd_of(out, ssq, n, eps):
        P.act(out, ssq, AF.Sqrt, bias=eps, scale=1.0 / n)
        P.recip(out, out)

    def bcast_row(name, src_row, n):
        t = P.sb(name, [128, n])
        P.dma(t[:], src_row.partition_broadcast(128))
        return t

    def phaseA(l, jb):
        TOK = jb["TOK"]
        for si, (t_off, S) in enumerate(jb["seqs"]):
            NK = S + (CTX if jb["ctx"] else 0)
            NKT = NK // 128
            n = jb["name"]
            KTs = scr(f"A_KT_{n}{l}_{si}", [8, 96, NK], FDT); VPs = scr(f"A_VP_{n}{l}_{si}", [NK, 8, 65], FDT); QTs = scr(f"A_QT_{n}{l}_{si}", [8, 96, S], FDT)
            P.push()
            gqa = bcast_row("gqa", I["mla_q_a_norm"][l], 256); gkv = bcast_row("gkv", I["mla_kv_a_norm"][l], 128)
            gqn = bcast_row("gqn", I["mla_q_norm"][l], 96); gkn = bcast_row("gkn", I["mla_k_norm"][l], 96)
            wuq = P.sb("wuq", [128, 2, 768]); wukv = P.sb("wukv", [128, 1024])
            P.dma(wuq[:], I["mla_w_uq"][l].rearrange("(k p) c -> p k c", p=128)); P.dma(wukv[:], I["mla_w_ukv"][l])
            ua = [P.sb(f"ua{i}", [128, 416]) for i in range(2)]
            rt = [P.sb(f"rt{i}", [128, 2, 32]) for i in range(2)]
            junk = P.sb("junk", [128, 768]); junk2 = P.sb("junk2", [128, 768])
            st = P.sb("st", [128, 4]); ss16 = P.sb("ss16", [128, 16]); ssr = P.sb("ssr", [128, 1])
            qlat = P.sb("qlat", [128, 256]); ckv = [P.sb(f"ckv{i}", [128, 128]) for i in range(2)]
            qlT = P.sb("qlT", [128, 2, 128]); ckT = P.sb("ckT", [128, 128])
            qf = P.sb("qf", [128, 8, 96]); kvf = P.sb("kvf", [128, 8, 128]); kn = P.sb("kn", [128, 8, 96])
            kr = P.sb("kr", [128, 32])
            vp = [P.sb(f"vp{i}", [128, 8, 65], FDT) for i in range(2)]
            qT = [P.sb(f"qT{i}", [96, 8, 128], FDT) for i in range(2)]; kT = [P.sb(f"kT{i}", [96, 8, 128], FDT) for i in range(2)]
            for v in vp:
                P.copy("vector", v[:, :, 64:65], ones[:, 0:8].unsqueeze(2), wk=[v])
            ptr = P.ps("ptr", [128, 3, 128]); pq1 = P.ps("pq1", [128, 512]); pq2 = P.ps("pq2", [128, 256])
            pk1 = P.ps("pk1", [128, 512]); pk2 = P.ps("pk2", [128, 512])
            pT = [P.ps(f"pT{i}", [96, 4, 128]) for i in range(2)]
            tiles = [("new", i) for i in range(S // 128)] + ([("ctx", i) for i in range(CTX // 128)] if jb["ctx"] else [])
            for it, (kind, i) in enumerate(tiles):
                if it >= int(os.environ.get("KA1T", 999)):
                    break
                u = ua[it % 2]; ck = ckv[it % 2]; v_ = vp[it % 2]; r_ = rt[it % 2]
                if os.environ.get("KSTATS"):
                    print("A1 tile start", jb["name"], si, it, getattr(P, "nrec", 0))
                new = kind == "new"
                rows = slice(t_off + i * 128, t_off + (i + 1) * 128)
                krow = i * 128 if new else S + i * 128
                if new:
                    P.dma(u[:], jb["UT"][rows, 0:416], rk=[(jb["UT"].name, (t_off + i * 128) // 512, 0)])
                    if jb["ctx"]:
                        P.dma(r_[:], I["k_rope32"][i * 128:(i + 1) * 128])
                    P.act(junk[:, :256], u[:, 0:256], AF.Square, accum=st[:, 0:1])
                    P.act(junk[:, :128], u[:, 256:384], AF.Square, accum=st[:, 1:2])
                    P.act(st[:, 2:3], st[:, 0:1], AF.Sqrt, bias=NORM_EPS, scale=1.0 / 256)
                    P.act(st[:, 3:4], st[:, 1:2], AF.Sqrt, bias=NORM_EPS, scale=1.0 / 128)
                    P.recip(st[:, 2:4], st[:, 2:4])
                    P.stt("vector", qlat[:], u[:, 0:256], st[:, 2:3], gqa[:], ALU.mult, ALU.mult)
                    P.stt("vector", ck[:], u[:, 256:384], st[:, 3:4], gkv[:], ALU.mult, ALU.mult)
                    if not jb["ctx"]:
                        P.dma(O["o_ckv"][si, l, i * 128:(i + 1) * 128, :], ck[:], wk=[("o_ckv", si, l, i)], eng=STQ, final=True)
                        P.dma(O["o_ckr"][si, l, i * 128:(i + 1) * 128, :], u[:, 384:416], wk=[("o_ckr", si, l, i)], eng=STQ, final=True)
                    for k in range(2):
                        P.tr(ptr[:, k, :], qlat[:, k * 128:(k + 1) * 128], ident[:])
                    P.tr(ptr[:, 2, :], ck[:], ident[:])
                    P.evac(qlT[:], ptr[:, 0:2, :]); P.evac(ckT[:], ptr[:, 2, :])
                    for k in range(2):
                        P.mm(pq1[:], qlT[:, k, :], wuq[:, k, 0:512], start=(k == 0), stop=(k == 1))
                    for k in range(2):
                        P.mm(pq2[:], qlT[:, k, :], wuq[:, k, 512:768], start=(k == 0), stop=(k == 1))
                    qff = qf[:].rearrange("p h d -> p (h d)")
                    P.evac(qff[:, 0:512], pq1[:]); P.evac(qff[:, 512:768], pq2[:])
                    krope = u[:, 384:416]
                else:
                    P.dma(ck[:], I["ckv"][l, i * 128:(i + 1) * 128, :])
                    P.dma(u[:, 384:416], I["ckr"][l, i * 128:(i + 1) * 128, :])
                    P.tr(ptr[:, 2, :], ck[:], ident[:])
                    P.evac(ckT[:], ptr[:, 2, :])
                    krope = u[:, 384:416]
                P.mm(pk1[:], ckT[:], wukv[:, 0:512]); P.mm(pk2[:], ckT[:], wukv[:, 512:1024])
                kvff = kvf[:].rearrange("p h d -> p (h d)")
                P.evac(kvff[:, 0:512], pk1[:]); P.evac(kvff[:, 512:1024], pk2[:])
                j2 = junk2[:, :512].rearrange("p (h d) -> p h d", h=8)
                P.act(j2, kvf[:, :, 0:64], AF.Square)
                P.red("vector", ss16[:, 8:16], j2)
                P.act(junk[:, :32], krope, AF.Square, accum=ssr[:, 0:1])
                P.ts("vector", ss16[:, 8:16], ss16[:, 8:16], ssr[:, 0:1], None, op0=ALU.add)
                if new:
                    P.act(junk[:].rearrange("p (h d) -> p h d", h=8), qf[:], AF.Square)
                    P.red("vector", ss16[:, 0:8], junk[:].rearrange("p (h d) -> p h d", h=8))
                else:
                    P.memset("vector", ss16[:, 0:8], 1.0)
                rstd_of(ss16[:], ss16[:], 96, NORM_EPS)
                P.tt("vector", kn[:, :, 0:64], kvf[:, :, 0:64], ss16[:, 8:16].unsqueeze(2).broadcast_to([128, 8, 64]), ALU.mult)
                P.tt(PENG, kn[:, :, 0:64], kn[:, :, 0:64], gkn[:, 0:64].unsqueeze(1).broadcast_to([128, 8, 64]), ALU.mult)
                P.tt("vector", kr[:], krope, gkn[:, 64:96], ALU.mult)
                if new and jb["ctx"]:
                    rope(kr[:].unsqueeze(1), r_[:, 0, :], r_[:, 1, :], 1, 8, junk, junk2)
                P.tt("vector", kn[:, :, 64:96], kr[:].unsqueeze(1).broadcast_to([128, 8, 32]),
                     ss16[:, 8:16].unsqueeze(2).broadcast_to([128, 8, 32]), ALU.mult)
                P.copy(PENG, v_[:, :, 0:64], kvf[:, :, 64:128])
                P.dma(VPs[krow:krow + 128], v_[:], wk=[(VPs.name, krow)], eng=STQ)
                kt_ = kT[it % 2]
                for hh in range(2):
                    for h4 in range(4):
                        P.tr(pT[hh][:, h4, :], kn[:, hh * 4 + h4, :], ident[:])
                    P.evac(kt_[:, hh * 4:(hh + 1) * 4, :], pT[hh][:])
                P.dma(KTs[:, :, krow:krow + 128].rearrange("h d t -> d h t"), kt_[:], wk=[(KTs.name, krow)], eng=STQ)
                if new:
                    P.tt("vector", qf[:], qf[:], ss16[:, 0:8].unsqueeze(2).broadcast_to([128, 8, 96]), ALU.mult)
                    P.tt(PENG, qf[:], qf[:], gqn[:].unsqueeze(1).broadcast_to([128, 8, 96]), ALU.mult)
                    if jb["ctx"]:
                        rope(qf[:, :, 64:96], r_[:, 0, :], r_[:, 1, :], 8, 8, junk, junk2)
                    qt_ = qT[it % 2]
                    for hh in range(2):
                        for h4 in range(4):
                            P.tr(pT[hh][:, h4, :], qf[:, hh * 4 + h4, :], ident[:])
                        P.evac(qt_[:, hh * 4:(hh + 1) * 4, :], pT[hh][:])
                    P.dma(QTs[:, :, i * 128:(i + 1) * 128].rearrange("h d t -> d h t"), qt_[:], wk=[(QTs.name, i)], eng=STQ)
            P.pop()
            if os.environ.get("KSTOP", "") == "a1":
                continue
            P.push()
            QC = 256
            KT = [P.sb(f"KT{i}", [96, NK], FDT) for i in range(2)]
            VP = [P.sb(f"VP{i}", [128, NKT, 65], FDT) for i in range(2)]
            QT = [P.sb(f"QT{i}", [96, QC], FDT) for i in range(3)]
            pTs = [P.sb(f"pTs{i}", [128, NKT, QC], FDT) for i in range(2)]
            oT = [P.sb(f"oT{i}", [65, QC], FDT) for i in range(2)]; rec = P.sb("rec", [64, QC]); yT = [P.sb(f"yT{i}", [64, QC], FDT) for i in range(2)]
            sel65 = P.sb("sel65", [65, 64], FDT)
            sel65f = P.sb("sel65f", [65, 64])
            P.memset("vector", sel65f[:], 0.0); P.memset("vector", sel65f[64:65, :], 1.0)
            P.copy("vector", sel65[:], sel65f[:])
            pss = [P.ps(f"pss{i}", [128, 512]) for i in range(4)]
            po = [P.ps(f"po{i}", [65, QC]) for i in range(2)]
            pb = P.ps("pb", [64, QC])
            units = [(h, qc) for h in range(8) for qc in range(S // QC)]
            ik = [0]

            def qk(iu):
                h, qc = units[iu]
                K_ = KT[h % 2]; V_ = VP[h % 2]
                if qc == 0:
                    P.dma(K_[:], KTs[h], rk=[KTs.name + "*"])
                    P.dma(V_[:], VPs[:, h, :].rearrange("(kt p) c -> p kt c", p=128), rk=[VPs.name + "*"])
                Q_ = QT[iu % 3]; p_ = pTs[iu % 2]
                P.dma(Q_[:], QTs[h, :, qc * QC:(qc + 1) * QC], rk=[QTs.name + "*"])
                for kt in range(NKT):
                    ps_ = pss[ik[0] % 4]; ik[0] += 1
                    P.mm(ps_[:, :QC], K_[:, kt * 128:(kt + 1) * 128], Q_[:], fast=True)
                    P.act(p_[:, kt, :], ps_[:, :QC], AF.Exp, scale=MLA_SCALE)

            def pv(iu):
                h, qc = units[iu]
                V_ = VP[h % 2]; p_ = pTs[iu % 2]; po_ = po[iu % 2]; o_ = oT[iu % 2]; yT_ = yT[iu % 2]
                for kt in range(NKT):
                    P.mm(po_[:], V_[:, kt, :], p_[:, kt, :], start=(kt == 0), stop=(kt == NKT - 1), fast=True)
                P.copy("scalar", o_[:], po_[:])
                P.mm(pb[:], sel65[:], o_[:], fast=True)
                P.recip(rec[:], pb[:])
                P.tt("vector", yT_[:], o_[0:64, :], rec[:], ALU.mult)
                c0 = t_off + qc * QC
                P.dma(jb["YT"][0, h * 64:(h + 1) * 64, c0:c0 + QC], yT_[:], wk=[(jb["YT"].name, 0, h, c0)], eng=STQ)

            qk(0)
            for iu in range(len(units)):
                if iu + 1 < len(units):
                    qk(iu + 1)
                pv(iu)
            P.pop()


    def phaseC(l, jb):
        for si, (t_off, S) in enumerate(jb["seqs"]):
            NT = S // 128
            NCT = (CTX // 128) if jb["ctx"] else 0
            NKT = NT + NCT
            P.push()
            gq = bcast_row("gq", I["swa_q_norm"][l], 64); gk = bcast_row("gk", I["swa_k_norm"][l], 64)
            esink = bcast_row("esink", I["swa_sink"][l], 8)
            P.act(esink[:], esink[:], AF.Exp)
            tri = P.sb("tri", [128, 2, 128])
            P.dma(tri[:], I["k_tri"].rearrange("a k q -> k a q"))
            KTa = P.sb("KTa", [128, NKT * 128]); VPa = P.sb("VPa", [128, NKT, 2, 65])
            P.memset("vector", VPa[:, :, :, 64:65], 1.0, wk=[VPa])
            uk = [P.sb(f"uk{i}", [128, 256]) for i in range(2)]
            uq = [P.sb(f"uq{i}", [128, 512]) for i in range(2)]
            rt = [P.sb(f"rt{i}", [128, 2, 64]) for i in range(2)]
            junk = P.sb("junk", [128, 512]); junk2 = P.sb("junk2", [128, 512]); ss = P.sb("ss", [128, 8])
            kk = [P.sb(f"kk{i}", [128, 2, 64]) for i in range(2)]
            qp = P.sb("qp", [128, 4, 2, 64]); QT = [P.sb(f"QT{i}", [128, 4, 128]) for i in range(2)]
            pTa = [P.sb(f"pTa{i}", [128, 5, 4, 128]) for i in range(2)]
            yc = P.sb("yc", [128, 8, 64]); den = P.sb("den", [128, 4, 1]); ycT = [P.sb(f"ycT{i}", [128, 4, 128], FDT) for i in range(2)]
            ptk = P.ps("ptk", [128, 128]); ptq = P.ps("ptq", [128, 4, 128])
            pss = [P.ps(f"pss{i}", [128, 512]) for i in range(3)]
            po = [P.ps(f"po{i}", [128, 4, 65]) for i in range(2)]
            pyt = P.ps("pyt", [128, 4, 128])
            for i in range(NKT):
                new = i < NT
                u = uk[i % 2]; k_ = kk[i % 2]; r_ = rt[i % 2]
                if new:
                    rows = slice(t_off + i * 128, t_off + (i + 1) * 128)
                    P.dma(u[:], jb["UT"][rows, C_SK:C_SK + 256], rk=[(jb["UT"].name, (t_off + i * 128) // 512, 2480)])
                    kv = u[:, 0:128].rearrange("p (g d) -> p g d", g=2)
                    P.act(junk[:, :128].rearrange("p (g d) -> p g d", g=2), kv, AF.Square)
                    P.red("vector", ss[:, 0:2], junk[:, :128].rearrange("p (g d) -> p g d", g=2))
                    rstd_of(ss[:, 0:2], ss[:, 0:2], 64, NORM_EPS)
                    P.tt("vector", k_[:], kv, ss[:, 0:2].unsqueeze(2).broadcast_to([128, 2, 64]), ALU.mult)
                    P.tt("vector", k_[:], k_[:], gk[:].unsqueeze(1).broadcast_to([128, 2, 64]), ALU.mult)
                    if not jb["ctx"]:
                        P.dma(O["o_swk"][si, l, i * 128:(i + 1) * 128, :], k_[:].rearrange("p g d -> p (g d)"), wk=[("o_swk", si, l, i)], eng=STQ, final=True)
                        P.dma(O["o_swv"][si, l, i * 128:(i + 1) * 128, :], u[:, 128:256], wk=[("o_swv", si, l, i)], eng=STQ, final=True)
                    else:
                        P.dma(r_[:], I["k_rope64"][i * 128:(i + 1) * 128])
                        rope(k_[:], r_[:, 0, :], r_[:, 1, :], 2, 16, junk, junk2)
                    ksrc = k_[:].rearrange("p g d -> p (g d)")
                    vsrc = u[:, 128:256]
                else:
                    c = i - NT
                    P.dma(u[:, 0:128], I["cswk"][l, c * 128:(c + 1) * 128, :]); P.dma(u[:, 128:256], I["cswv"][l, c * 128:(c + 1) * 128, :])
                    ksrc = u[:, 0:128]; vsrc = u[:, 128:256]
                P.tr(ptk[:], ksrc, ident[:])
                P.copy("vector", KTa[:, i * 128:(i + 1) * 128], ptk[:])
                P.copy("vector", VPa[:, i, :, 0:64], vsrc.rearrange("p (g d) -> p g d", g=2))
            units = [(b, g) for b in range(NT) for g in range(2)]
            ik = [0]

            def ktiles(b):
                if not jb["ctx"]:
                    return [(kt, None) for kt in range(NT)]
                lst = []
                if b > 0:
                    lst.append((b - 1, 0))
                lst.append((b, None))
                if b < NT - 1:
                    lst.append((b + 1, 1))
                return lst + [(NT + c, None) for c in range(NCT)]

            def qprep(b):
                u = uq[b % 2]; r_ = rt[b % 2]; Q_ = QT[b % 2]
                rows = slice(t_off + b * 128, t_off + (b + 1) * 128)
                P.dma(u[:], jb["UT"][rows, C_SQ:C_SQ + 512], rk=[(jb["UT"].name, (t_off + b * 128) // 512, 1968)])
                qv = u[:].rearrange("p (h d) -> p h d", h=8)
                P.act(junk[:].rearrange("p (h d) -> p h d", h=8), qv, AF.Square)
                P.red("vector", ss[:], junk[:].rearrange("p (h d) -> p h d", h=8))
                rstd_of(ss[:], ss[:], 64, NORM_EPS)
                P.tt("vector", qv, qv, ss[:].unsqueeze(2).broadcast_to([128, 8, 64]), ALU.mult)
                P.tt("vector", qv, qv, gq[:].unsqueeze(1).broadcast_to([128, 8, 64]), ALU.mult)
                if jb["ctx"]:
                    P.dma(r_[:], I["k_rope64"][b * 128:(b + 1) * 128])
                    rope(qv, r_[:, 0, :], r_[:, 1, :], 8, 16, junk, junk2)
                P.copy("vector", qp[:].rearrange("p r g d -> p g r d"), u[:].rearrange("p (g r d) -> p g r d", g=2, r=4))
                for r in range(4):
                    P.tr(ptq[:, r, :], qp[:, r, :, :].rearrange("p g d -> p (g d)"), ident[:])
                P.copy("scalar", Q_[:], ptq[:])

            def qk(iu):
                b, g = units[iu]
                if g == 0:
                    qprep(b)
                Q_ = QT[b % 2]; p_ = pTa[iu % 2]
                pr = slice(g * 64, (g + 1) * 64)
                for j, (kt, mk) in enumerate(ktiles(b)):
                    ps_ = pss[ik[0] % 3]; ik[0] += 1
                    P.mm(ps_[:], KTa[pr, kt * 128:(kt + 1) * 128], Q_[pr, :, :].rearrange("p r q -> p (r q)"))
                    P.act(p_[:, j, :, :].rearrange("p r q -> p (r q)"), ps_[:], AF.Exp, scale=SW_SCALE)
                    if mk is not None:
                        P.tt("vector", p_[:, j, :, :], p_[:, j, :, :], tri[:, mk, :].unsqueeze(1).broadcast_to([128, 4, 128]), ALU.mult)

            def pv(iu):
                b, g = units[iu]
                p_ = pTa[iu % 2]; po_ = po[iu % 2]
                kts = ktiles(b)
                for r in range(4):
                    for j, (kt, mk) in enumerate(kts):
                        P.mm(po_[:, r, :], p_[:, j, r, :], VPa[:, kt, g, :], start=(j == 0), stop=(j == len(kts) - 1))
                P.tt("vector", den[:], po_[:, :, 64:65], esink[:, g * 4:(g + 1) * 4].unsqueeze(2), ALU.add)
                P.recip(den[:], den[:])
                P.tt("vector", yc[:, g * 4:(g + 1) * 4, :], po_[:, :, 0:64], den[:].broadcast_to([128, 4, 64]), ALU.mult)
                if g == 1:
                    yT_ = ycT[b % 2]
                    for c in range(4):
                        P.tr(pyt[:, c, :], yc[:, 2 * c:2 * c + 2, :].rearrange("p h d -> p (h d)"), ident[:])
                    P.copy("scalar", yT_[:], pyt[:])
                    c0 = t_off + b * 128
                    P.dma(jb["YT"][2, :, c0:c0 + 128].rearrange("(c p) t -> p c t", p=128), yT_[:], wk=[(jb["YT"].name, 2, c0)], eng=STQ)

            qk(0)
            for iu in range(len(units)):
                if iu + 1 < len(units):
                    qk(iu + 1)
                pv(iu)
            P.pop()


    def phaseB(l, jb):
        n = jb["name"]
        for si, (t_off, S) in enumerate(jb["seqs"]):
            NC = S // 128
            HS = scr(f"B_HS_{n}{l}_{si}", [2, S, 512])
            P.push()
            tri = P.sb("tri", [128, 2, 128]); neg = P.sb("neg", [128, 2, 128]); sel = P.sb("sel", [128, 2, 128])
            P.dma(tri[:], I["k_tri"].rearrange("a k q -> k a q")); P.dma(sel[:], I["k_sel"].rearrange("a k q -> k a q"))
            P.ts("vector", neg[:], tri[:], -1.0, 1e30, op0=ALU.add, op1=ALU.mult)
            TRI = [tri[:, 1, :], tri[:, 0, :]]
            NEG = [neg[:, 1, :], neg[:, 0, :]]
            NEGts = [neg[:, 0, :], neg[:, 1, :]]
            bias16 = P.sb("bias16", [128, 2, 8])
            P.dma(bias16[:, 0, :], I["mlstm_i_bias"][l].partition_broadcast(128), wk=[bias16])
            P.dma(bias16[:, 1, :], I["mlstm_f_bias"][l].partition_broadcast(128), wk=[bias16])
            Cst = P.sb("Cst", [64, 8, 129]); mprev = P.sb("mprev", [128, 8])
            if jb["ctx"]:
                P.dma(Cst[:, :, 0:128], I["mC"][l].rearrange("d h k v -> k (d h) v"), wk=[Cst])
                P.dma(Cst[:, :, 128:129], I["mn"][l].rearrange("d h (k o) -> k (d h) o", o=1), wk=[Cst], allow_slow_non_contiguous=True)
                P.dma(mprev[:], I["mm"][l].rearrange("d h -> (d h)").partition_broadcast(128))
            else:
                P.memset("vector", Cst[:], 0.0); P.memset("vector", mprev[:], 0.0)
            G = [P.sb(f"G{i}", [128, 2, 8]) for i in range(2)]
            QTd = [P.sb(f"QTd{i}", [64, 2, 4, 128]) for i in range(2)]; KTd = [P.sb(f"KTd{i}", [64, 2, 4, 128]) for i in range(2)]
            Kt = [P.sb(f"Kt{i}", [128, 2, 4, 64]) for i in range(2)]; VPd = [P.sb(f"VPd{i}", [128, 2, 4, 129]) for i in range(2)]
            for v in VPd:
                P.memset("vector", v[:, :, :, 128:129], 1.0, wk=[v])
            sp = P.sb("sp", [128, 8]); b = P.sb("b", [128, 8]); li = P.sb("li", [128, 8]); c = P.sb("c", [128, 8])
            MB = P.sb("MB", [128, 2, 2, 4]); cmax = P.sb("cmax", [128, 8]); bm = P.sb("bm", [128, 8]); ain = P.sb("ain", [128, 8]); en = P.sb("en", [128, 8])
            DG = P.sb("DG", [128, 8, 128]); DG2 = P.sb("DG2", [128, 8, 128]); Rm = P.sb("Rm", [128, 8, 128])
            ET = P.sb("ET", [128, 8, 128]); WT = P.sb("WT", [128, 8, 128])
            tI = P.sb("tI", [128, 8, 129]); numS = P.sb("numS", [128, 8, 129]); dab = P.sb("dab", [128, 8]); hh = P.sb("hh", [128, 8, 128])
            mbl = P.sb("mbl", [128, 2, 2, 4]); wk_ = P.sb("wk", [128, 8]); dec = P.sb("dec", [128, 8]); KW = P.sb("KW", [128, 8, 64])
            pA = [P.ps(f"pA{i}", [128, 4, 128]) for i in range(2)]
            pB = [P.ps(f"pB{i}", [128, 4, 128]) for i in range(2)]
            pC = [P.ps(f"pC{i}", [128, 3, 129]) for i in range(3)]
            pD = P.ps("pD", [128, 2, 8])
            grp3 = [(0, 0, 3), (1, 3, 6), (2, 6, 8)]

            def pc_slot(dh):
                return pC[dh // 3], dh % 3

            for j in range(NC):
                g_ = G[j % 2]; qt = QTd[j % 2]; kt = KTd[j % 2]; ktok = Kt[j % 2]; vp = VPd[j % 2]
                cd = [j, NC - 1 - j]
                for d in range(2):
                    r0 = t_off + cd[d] * 128
                    rows = slice(r0, r0 + 128)
                    sk = r0 // 512
                    P.dma(g_[:, :, d * 4:(d + 1) * 4], jb["UT"][rows, C_MI:C_MI + 16].rearrange("p (a e) -> p a e", a=2)[:, :, d * 4:(d + 1) * 4],
                          rk=[(jb["UT"].name, sk, 1440)], wk=[g_])
                    P.dma(qt[:, d, :, :], jb["UF"][C_MQ:C_MQ + 256, r0:r0 + 128].rearrange("(h p) t -> p h t", p=64), rk=[(jb["UF"].name, sk, 416)], wk=[qt])
                    P.dma(kt[:, d, :, :], jb["UF"][C_MK:C_MK + 256, r0:r0 + 128].rearrange("(h p) t -> p h t", p=64), rk=[(jb["UF"].name, sk, 672)], wk=[kt])
                    P.dma(ktok[:, d, :, :], jb["UT"][rows, C_MK:C_MK + 256].rearrange("p (h e) -> p h e", h=4), rk=[(jb["UT"].name, sk, 672)], wk=[ktok])
                    P.dma(vp[:, d, :, 0:128], jb["UT"][rows, C_MV:C_MV + 512].rearrange("p (h e) -> p h e", h=4), rk=[(jb["UT"].name, sk, 928)], wk=[vp])
                P.act(qt[:], qt[:], AF.Copy, scale=0.125)
                P.tt("vector", g_[:], g_[:], bias16[:], ALU.add)
                P.copy("vector", li[:], g_[:, 0, :])
                P.act(sp[:], g_[:, 1, :], AF.Exp, scale=-1.0)
                P.act(sp[:], sp[:], AF.Ln, bias=1.0)
                for d in range(2):
                    P.mm(pD[:, 0, d * 4:(d + 1) * 4], TRI[d], sp[:, d * 4:(d + 1) * 4])
                P.act(b[:], pD[:, 0, :], AF.Copy, scale=-1.0)
                P.tt("vector", c[:], li[:], b[:], ALU.subtract)
                P.tt("vector", DG[:], ident[:].unsqueeze(1).broadcast_to([128, 8, 128]), c[:].unsqueeze(2).broadcast_to([128, 8, 128]), ALU.mult)
                for dh in range(8):
                    P.mm(pA[dh // 4][:, dh % 4, :], ones[:], DG[:, dh, :])
                for d in range(2):
                    P.tt("vector", Rm[:, d * 4:(d + 1) * 4, :], pA[d][:], NEGts[d].unsqueeze(1).broadcast_to([128, 4, 128]), ALU.add)
                P.red("vector", cmax[:], Rm[:], op=ALU.max)
                v24 = lambda t: t.rearrange("p (d h) -> p d h", d=2)
                mt = MB[:, :, 0, :]
                P.tt("vector", mt, v24(mprev[:]), v24(cmax[:]), ALU.max)
                P.tt("vector", mt, mt, v24(b[:]), ALU.add)
                P.copy("vector", MB[:, :, 1, :], v24(b[:]))
                P.tt("vector", v24(bm[:]), v24(b[:]), mt, ALU.subtract)
                P.tt("vector", ain[:], bm[:], mprev[:], ALU.add)
                P.act(ain[:], ain[:], AF.Exp)
                P.act(v24(en[:]), mt, AF.Exp, scale=-1.0)
                P.tt("vector", DG2[:], ident[:].unsqueeze(1).broadcast_to([128, 8, 128]), bm[:].unsqueeze(2).broadcast_to([128, 8, 128]), ALU.mult)
                for dh in range(8):
                    o_ = pA[dh // 4][:, dh % 4, :]
                    P.mm(o_, ones[:], DG2[:, dh, :], start=True, stop=False)
                    P.mm(o_, DG[:, dh, :], ones[:], start=False, stop=False)
                    P.mm(o_, ident[:], NEG[dh // 4], start=False, stop=True)
                for d in range(2):
                    P.act(ET[:, d * 4:(d + 1) * 4, :], pA[d][:], AF.Exp)
                for dh in range(8):
                    d, h = dh // 4, dh % 4
                    P.mm(pB[d][:, h, :], kt[:, d, h, :], qt[:, d, h, :])
                for d in range(2):
                    P.tt("vector", WT[:, d * 4:(d + 1) * 4, :], ET[:, d * 4:(d + 1) * 4, :], pB[d][:], ALU.mult)
                for dh in range(8):
                    d, h = dh // 4, dh % 4
                    pc, sl = pc_slot(dh)
                    P.mm(pc[:, sl, :], qt[:, d, h, :], Cst[:, dh, :])
                for (bk, lo, hi) in grp3:
                    P.tt("vector", tI[:, lo:hi, :], pC[bk][:, 0:hi - lo, :], ain[:, lo:hi].unsqueeze(2).broadcast_to([128, hi - lo, 129]), ALU.mult)
                for dh in range(8):
                    d, h = dh // 4, dh % 4
                    pc, sl = pc_slot(dh)
                    P.mm(pc[:, sl, :], WT[:, dh, :], vp[:, d, h, :])
                for (bk, lo, hi) in grp3:
                    P.tt("vector", numS[:, lo:hi, :], pC[bk][:, 0:hi - lo, :], tI[:, lo:hi, :], ALU.add)
                P.act(dab[:].unsqueeze(2), numS[:, :, 128:129], AF.Abs)
                P.tt("vector", dab[:], dab[:], en[:], ALU.max)
                P.recip(dab[:], dab[:])
                P.tt("vector", hh[:], numS[:, :, 0:128], dab[:].unsqueeze(2).broadcast_to([128, 8, 128]), ALU.mult)
                for d in range(2):
                    r0 = cd[d] * 128
                    P.dma(HS[d, r0:r0 + 128, :].rearrange("p (h e) -> p h e", h=4), hh[:, d * 4:(d + 1) * 4, :], wk=[(HS.name, d, cd[d])], eng=STQ)
                for d in range(2):
                    P.mm(pD[:, d, :].rearrange("p (a h) -> p a h", a=2).rearrange("p a h -> p (a h)"), sel[:, d, :], MB[:, d, :, :].rearrange("p a h -> p (a h)"))
                P.copy("vector", mbl[:].rearrange("p d a h -> p (d a h)"), pD[:].rearrange("p a e -> p (a e)"))
                P.tt("vector", v24(wk_[:]), mbl[:, :, 1, :], mbl[:, :, 0, :], ALU.subtract)
                P.tt("vector", dec[:], wk_[:], mprev[:], ALU.add)
                P.act(dec[:], dec[:], AF.Exp)
                P.tt("vector", wk_[:], wk_[:], c[:], ALU.add)
                P.act(wk_[:], wk_[:], AF.Exp)
                for d in range(2):
                    P.tt("vector", KW[:, d * 4:(d + 1) * 4, :], ktok[:, d, :, :], wk_[:, d * 4:(d + 1) * 4].unsqueeze(2).broadcast_to([128, 4, 64]), ALU.mult)
                for dh in range(8):
                    d, h = dh // 4, dh % 4
                    pc, sl = pc_slot(dh)
                    P.mm(pc[0:64, sl, :], KW[:, dh, :], vp[:, d, h, :])
                P.tt("vector", Cst[:], Cst[:], dec[0:64, :].unsqueeze(2).broadcast_to([64, 8, 129]), ALU.mult)
                for (bk, lo, hi) in grp3:
                    P.tt("vector", Cst[:, lo:hi, :], Cst[:, lo:hi, :], pC[bk][0:64, 0:hi - lo, :], ALU.add)
                P.copy("vector", v24(mprev[:]), mbl[:, :, 0, :])
            if not jb["ctx"]:
                P.dma(O["o_mC"][si, l].rearrange("d h k v -> k (d h) v"), Cst[:, :, 0:128], wk=[("o_mC", si, l)], eng=STQ, final=True)
                P.dma(O["o_mn"][si, l].rearrange("d h (k o) -> k (d h) o", o=1), Cst[:, :, 128:129], wk=[("o_mn", si, l)], eng=STQ, final=True, allow_slow_non_contiguous=True)
                P.dma(O["o_mm"][si, l].rearrange("d (h o) -> o (d h)", o=1), mprev[0:1, :], wk=[("o_mm", si, l)], eng=STQ, final=True, allow_slow_non_contiguous=True)
            P.pop()
            P.push()
            gm = bcast_row("gm", I["mlstm_norm"][l], 128)
            h0 = [P.sb(f"h0{i}", [128, 4, 128]) for i in range(2)]; h1 = [P.sb(f"h1{i}", [128, 4, 128]) for i in range(2)]
            og = [P.sb(f"og{i}", [128, 512]) for i in range(2)]
            junk = P.sb("junk", [128, 4, 128]); ss = P.sb("ss", [128, 4]); yT = [P.sb(f"yT{i}", [128, 4, 128], FDT) for i in range(2)]
            pyt = P.ps("pyt", [128, 4, 128])
            for i in range(NC):
                a_ = h0[i % 2]; b_ = h1[i % 2]; o_ = og[i % 2]; y_ = yT[i % 2]
                rows = slice(t_off + i * 128, t_off + (i + 1) * 128)
                P.dma(a_[:], HS[0, i * 128:(i + 1) * 128, :].rearrange("p (h e) -> p h e", h=4), rk=[HS.name + "*"])
                P.dma(b_[:], HS[1, i * 128:(i + 1) * 128, :].rearrange("p (h e) -> p h e", h=4), rk=[HS.name + "*"])
                P.dma(o_[:], jb["UT"][rows, C_MO:C_MO + 512], rk=[(jb["UT"].name, (t_off + i * 128) // 512, 1456)])
                P.tt("vector", a_[:], a_[:], b_[:], ALU.add)
                P.act(junk[:], a_[:], AF.Square)
                P.red("vector", ss[:], junk[:])
                rstd_of(ss[:], ss[:], 128, NORM_EPS)
                P.act(o_[:], o_[:], AF.Sigmoid)
                P.tt("vector", a_[:], a_[:], ss[:].unsqueeze(2).broadcast_to([128, 4, 128]), ALU.mult)
                P.tt("vector", a_[:], a_[:], gm[:].unsqueeze(1).broadcast_to([128, 4, 128]), ALU.mult)
                P.tt("vector", a_[:], a_[:], o_[:].rearrange("p (h e) -> p h e", h=4), ALU.mult)
                for h in range(4):
                    P.tr(pyt[:, h, :], a_[:, h, :], ident[:])
                P.copy("scalar", y_[:], pyt[:])
                c0 = t_off + i * 128
                P.dma(jb["YT"][1, :, c0:c0 + 128].rearrange("(c p) t -> p c t", p=128), y_[:], wk=[(jb["YT"].name, 1, c0)], eng=STQ)
            P.pop()


    def phaseD(l, jb):
        n = jb["name"]
        CW = RW_DECAY
        for si, (t_off, S) in enumerate(jb["seqs"]):
            NCH = S // 64
            DF = scr(f"D_F_{n}{l}_{si}", [2, NCH, 2, 64, 4, 4, 64])
            LW = scr(f"D_LW_{n}{l}_{si}", [S, 2, 512])
            BG = scr(f"D_BG_{n}{l}_{si}", [2, 512, S])
            YS = scr(f"D_YS_{n}{l}_{si}", [2, S, 512])
            P.push()
            ST = min(512, S)
            kk = P.sb("kk", [64, 8]); ka = P.sb("ka", [64, 8]); omka = P.sb("omka", [64, 8]); uu = P.sb("uu", [64, 2, 8]); a0 = P.sb("a0", [64, 2, 8])
            P.dma(kk[:], I["rwkv_kk64"][l]); P.dma(ka[:], I["rwkv_ka64"][l]); P.dma(uu[:], I["rwkv_u64"][l]); P.dma(a0[:], I["rwkv_a064"][l])
            P.ts("vector", omka[:], ka[:], -1.0, 1.0, op0=ALU.mult, op1=ALU.add)
            w2 = P.sb("w2", [64, 2, 512]); a2 = P.sb("a2", [64, 2, 512]); g2 = P.sb("g2", [128, 512])
            P.dma(w2[:], I["rwkv_w2"][l].rearrange("d r c -> r d c")); P.dma(a2[:], I["rwkv_a2"][l].rearrange("d r c -> r d c")); P.dma(g2[:], I["rwkv_g2"][l])
            w0row = bcast_row("w0row", I["rwkv_w0"][l].rearrange("d c -> (d c)"), 1024)
            rT = P.sb("rT", [64, 8, ST]); kT = P.sb("kT", [64, 8, ST]); vT = P.sb("vT", [64, 8, ST])
            w1T = P.sb("w1T", [64, 2, ST]); a1T = P.sb("a1T", [64, 2, ST]); g1T = P.sb("g1T", [128, ST])
            kap = P.sb("kap", [64, 8, ST]); kh = P.sb("kh", [64, 8, ST]); tA = P.sb("tA", [64, 8, ST]); tB = P.sb("tB", [64, 8, ST])
            ktt = P.sb("ktt", [64, 8, ST]); rku = P.sb("rku", [64, 8, ST]); lw = [P.sb(f"lw{i}", [128, 2, 512]) for i in range(2)]
            pp = [P.ps(f"pp{i}", [128, 512]) for i in range(4)]
            ip = [0]

            def nps():
                ip[0] += 1
                return pp[ip[0] % 4]

            def store_df(d, arr, tile, s0):
                for cc in range(ST // 64):
                    for hh in range(2):
                        P.dma(DF[d, s0 // 64 + cc, hh, :, arr, :, :], tile[:, hh * 4:(hh + 1) * 4, cc * 64:(cc + 1) * 64], wk=[(DF.name, d, arr, s0, cc, hh)], eng=STQ)

            for s_ in range(S // ST):
                s0 = s_ * ST
                c0 = t_off + s0
                sk = c0 // 512
                UF = jb["UF"]
                P.dma(rT[:], UF[C_RR:C_RR + 512, c0:c0 + ST].rearrange("(h p) t -> p h t", p=64), rk=[(UF.name, sk, 2736)])
                P.dma(kT[:], UF[C_RK:C_RK + 512, c0:c0 + ST].rearrange("(h p) t -> p h t", p=64), rk=[(UF.name, sk, 3248)])
                P.dma(vT[:], UF[C_RV:C_RV + 512, c0:c0 + ST].rearrange("(h p) t -> p h t", p=64), rk=[(UF.name, sk, 3760)])
                P.dma(w1T[:], UF[C_RW:C_RW + 128, c0:c0 + ST].rearrange("(d p) t -> p d t", p=64), rk=[(UF.name, sk, 4272)])
                P.dma(a1T[:], UF[C_RA:C_RA + 128, c0:c0 + ST].rearrange("(d p) t -> p d t", p=64), rk=[(UF.name, sk, 4272)])
                P.dma(g1T[:], UF[C_RG:C_RG + 128, c0:c0 + ST], rk=[(UF.name, sk, 4272)])
                P.act(w1T[:], w1T[:], AF.Tanh)
                P.act(g1T[:], g1T[:], AF.Sigmoid)
                P.tt("vector", kap[:], kT[:], kk[:].unsqueeze(2).broadcast_to([64, 8, ST]), ALU.mult)
                P.act(tA[:], kap[:], AF.Square)
                for h in range(8):
                    ps_ = nps()
                    P.mm(ps_[0:64, :ST], ones[0:64, 0:64], tA[:, h, :])
                    P.act(kh[:, h, :], ps_[0:64, :ST], AF.Sqrt, bias=1e-12)
                P.recip(kh[:], kh[:])
                P.tt("vector", kh[:], kh[:], kap[:], ALU.mult)
                for d in range(2):
                    store_df(d, 0, rT, s0); store_df(d, 1, kh, s0)
                for d in range(2):
                    for h in range(8):
                        ps_ = nps()
                        P.mm(ps_[0:64, :ST], a2[:, d, h * 64:(h + 1) * 64], a1T[:, d, :])
                        P.act(tA[:, h, :], ps_[0:64, :ST], AF.Sigmoid, bias=a0[:, d, h:h + 1])
                    P.tt("vector", tB[:], tA[:], kh[:], ALU.mult)
                    store_df(d, 3, tB, s0)
                    P.tt("vector", tA[:], tA[:], ka[:].unsqueeze(2).broadcast_to([64, 8, ST]), ALU.mult)
                    P.tt("vector", tA[:], tA[:], omka[:].unsqueeze(2).broadcast_to([64, 8, ST]), ALU.add)
                    P.tt("vector", ktt[:], kT[:], tA[:], ALU.mult)
                    store_df(d, 2, ktt, s0)
                    P.tt("vector", tA[:], ktt[:], rT[:], ALU.mult)
                    if d == 0:
                        P.tt("vector", rku[:], tA[:], uu[:, d, :].unsqueeze(2).broadcast_to([64, 8, ST]), ALU.mult)
                    else:
                        P.tt("vector", tA[:], tA[:], uu[:, d, :].unsqueeze(2).broadcast_to([64, 8, ST]), ALU.mult)
                        P.tt("vector", rku[:], rku[:], tA[:], ALU.add)
                for h in range(8):
                    ps_ = nps()
                    P.mm(ps_[0:64, :ST], ones[0:64, 0:64], rku[:, h, :])
                    P.tt("vector", tB[:, h, :], ps_[0:64, :ST], vT[:, h, :], ALU.mult)
                P.dma(BG[0, :, s0:s0 + ST].rearrange("(h p) t -> p h t", p=64), tB[:], wk=[(BG.name, 0, s0)], eng=STQ)
                for h in range(8):
                    ps_ = nps()
                    P.mm(ps_[0:64, :ST], g2[:, h * 64:(h + 1) * 64], g1T[:])
                    P.copy("scalar", kap[:, h, :], ps_[0:64, :ST])
                P.dma(BG[1, :, s0:s0 + ST].rearrange("(h p) t -> p h t", p=64), kap[:], wk=[(BG.name, 1, s0)], eng=STQ)
                for tt in range(ST // 128):
                    lw_ = lw[tt % 2]
                    for d in range(2):
                        ps_ = nps()
                        P.mm(ps_[:], w1T[:, d, tt * 128:(tt + 1) * 128], w2[:, d, :])
                        P.tt("vector", lw_[:, d, :], ps_[:], w0row[:, d * 512:(d + 1) * 512], ALU.add)
                    P.act(lw_[:], lw_[:], AF.Sigmoid)
                    P.dma(LW[s0 + tt * 128:s0 + (tt + 1) * 128], lw_[:], wk=[(LW.name, s0, tt)], eng=STQ)
            P.pop()
            P.push()
            tri = P.sb("tri", [128, 2, 128]); P.dma(tri[:], I["k_tri"].rearrange("a k q -> k a q"))
            trs = P.sb("trs", [128, 2, 128]); P.dma(trs[:], I["k_tris"].rearrange("a k q -> k a q"))
            HP = [slice(0, 64), slice(64, 128)]
            cum = P.sb("cum", [128, 2, 2, 64]); mask4 = P.sb("mask4", [128, 2, 4, 64]); maskT = P.sb("maskT", [128, 2, 64])
            for hh in range(2):
                pr = HP[hh]
                INC = [tri[pr, 1, pr], tri[pr, 0, pr]]; STR = [trs[pr, 1, pr], trs[pr, 0, pr]]
                for d in range(2):
                    P.copy("vector", cum[pr, d, 0, :], INC[d], wk=[cum]); P.copy("vector", cum[pr, d, 1, :], STR[d], wk=[cum])
                    for a_, m_ in enumerate([STR[d], INC[d], STR[d], INC[d]]):
                        P.copy("vector", mask4[pr, d, a_, :], m_, wk=[mask4])
                    P.copy("vector", maskT[pr, d, :], STR[1 - d], wk=[maskT])
            idh = [ident[HP[0], HP[0]], ident[HP[1], HP[1]]]
            id4 = P.sb("id4", [128, 4, 64])
            for hh in range(2):
                for h4 in range(4):
                    P.copy("vector", id4[HP[hh], h4, :], idh[hh], wk=[id4])
            TS = P.sb("TS", [128, 2, 4, 64])
            bG = P.ps("bG", [128, 512]); bX = [P.ps(f"bX{i}", [128, 512]) for i in range(2)]; bY = P.ps("bY", [128, 256])
            bA = P.ps("bA", [128, 512]); bP = P.ps("bP", [128, 256]); bZ = [P.ps(f"bZ{i}", [128, 256]) for i in range(2)]
            s0t = P.sb("s0t", [128, 2, 4, 64])

            def trm(out, in_, hh):
                P.mm(out, in_, idh[hh])

            if jb["ctx"]:
                for hh in range(2):
                    for d in range(2):
                        P.dma(s0t[HP[hh], d], I["rw"][l][d, hh * 4:(hh + 1) * 4].rearrange("h v k -> v h k"), wk=[s0t])
                for d in range(2):
                    for h in range(8):
                        hh, h4 = h // 4, h % 4
                        trm(bX[0][HP[hh], (d * 4 + h4) * 64:(d * 4 + h4 + 1) * 64], s0t[HP[hh], d, h4, :], hh)
                P.copy("vector", TS[:].rearrange("p d h v -> p (d h v)"), bX[0][:])
            else:
                P.memset("vector", TS[:], 0.0)
            Xd = [[P.sb(f"Xd{d}{i}", [128, 4, 4, 64]) for i in range(2)] for d in range(2)]
            Vd = [[P.sb(f"Vd{d}{i}", [128, 4, 64]) for i in range(2)] for d in range(2)]
            LWd = [[P.sb(f"LWd{d}{i}", [128, 256]) for i in range(2)] for d in range(2)]
            EI = P.sb("EI", [128, 4, 64]); EX = P.sb("EX", [128, 4, 64]); EN = P.sb("EN", [128, 4, 64]); gl = P.sb("gl", [128, 4])
            KR = P.sb("KR", [128, 4, 2, 64]); KtM = P.sb("KtM", [128, 4, 64]); BM = P.sb("BM", [128, 4, 64]); KBe = P.sb("KBe", [128, 4, 2, 64])
            AM = P.sb("AM", [128, 4, 4, 64]); N0 = P.sb("N0", [128, 4, 64])
            AB = [P.sb(f"AB{i}", [128, 4, 2, 64]) for i in range(2)]; PI = [P.sb(f"PI{i}", [128, 4, 64]) for i in range(2)]
            BI = [P.sb(f"BI{i}", [128, 4, 64]) for i in range(2)]
            RH = P.sb("RH", [128, 4, 64]); Un = P.sb("Un", [128, 4, 64]); Yo = [P.sb(f"Yo{i}", [128, 4, 64]) for i in range(2)]
            KBt = P.sb("KBt", [128, 4, 2, 64])
            HH = [(h // 4, h % 4) for h in range(8)]

            def dpass(j, d):
                c = j if d == 0 else NCH - 1 - j
                X = Xd[d][j % 2]; V = Vd[d][j % 2]; LWc = LWd[d][j % 2]
                r0 = t_off + c * 64
                for hh in range(2):
                    pr = HP[hh]
                    P.dma(X[pr], DF[d, c, hh], rk=[DF.name + "*"], wk=[X])
                    P.dma(V[pr], jb["UT"][r0:r0 + 64, C_RV + hh * 256:C_RV + (hh + 1) * 256].rearrange("p (h e) -> p h e", h=4),
                          rk=[(jb["UT"].name, r0 // 512, 3760)], wk=[V])
                    P.dma(LWc[pr], LW[c * 64:(c + 1) * 64, d, hh * 256:(hh + 1) * 256], rk=[LW.name + "*"], wk=[LWc])
                Rr = X[:, 0, :, :]; Kh = X[:, 1, :, :]; Kt = X[:, 2, :, :]; Bb = X[:, 3, :, :]
                for (hh, h4) in HH:
                    pr = HP[hh]
                    P.mm(bG[pr, h4 * 128:(h4 + 1) * 128], LWc[pr, h4 * 64:(h4 + 1) * 64], cum[pr, d, :, :].rearrange("p a t -> p (a t)"))
                gv = bG[:].rearrange("p (h a t) -> p h a t", h=4, a=2)
                P.act(EI[:], gv[:, :, 0, :], AF.Exp, scale=-CW)
                P.act(EN[:], gv[:, :, 0, :], AF.Exp, scale=CW)
                P.act(EX[:], gv[:, :, 1, :], AF.Exp, scale=-CW)
                last = 63 if d == 0 else 0
                P.copy("vector", gl[:].unsqueeze(2), EI[:, :, last:last + 1])
                P.tt("vector", KR[:, :, 1, :], Rr, EI[:], ALU.mult)
                P.tt("vector", KR[:, :, 0, :], Kh, EX[:], ALU.mult)
                P.tt("vector", KtM[:], Kt, EN[:], ALU.mult)
                P.tt("vector", BM[:], Bb, EN[:], ALU.mult)
                P.tt("vector", KBe[:, :, 0, :], KtM[:], gl[:].unsqueeze(2).broadcast_to([128, 4, 64]), ALU.mult)
                P.tt("vector", KBe[:, :, 1, :], BM[:], gl[:].unsqueeze(2).broadcast_to([128, 4, 64]), ALU.mult)
                for (hh, h4) in HH:
                    pr = HP[hh]
                    o_ = bX[h4 // 2][pr, (h4 % 2) * 256:(h4 % 2 + 1) * 256]
                    rhs = KR[pr, h4, :, :].rearrange("p a t -> p (a t)")
                    P.mm(o_[:, 0:128], KtM[pr, h4, :], rhs)
                    P.mm(o_[:, 128:256], BM[pr, h4, :], rhs)
                for q in range(2):
                    P.tt("vector", AM[:, q * 2:(q + 1) * 2, :, :], bX[q][:].rearrange("p (h a t) -> p h a t", h=2, a=4),
                         mask4[:, d, :, :].unsqueeze(1).broadcast_to([128, 2, 4, 64]), ALU.mult)
                for (hh, h4) in HH:
                    pr = HP[hh]
                    P.mm(bY[pr, h4 * 64:(h4 + 1) * 64], KR[pr, h4, 0, :], BM[pr, h4, :])
                P.tt("vector", N0[:], bY[:].rearrange("p (h s) -> p h s", h=4), maskT[:, d, :].unsqueeze(1).broadcast_to([128, 4, 64]), ALU.mult)
                A_ = lambda pr, h4: AM[pr, h4, 2, :]
                B_ = lambda pr, h4: N0[pr, h4, :]
                Pc = PI[0]
                P.tt("vector", Pc[:], id4[:], AM[:, :, 2, :], ALU.subtract)
                for lv in range(1, 6):
                    ab = AB[lv % 2]
                    for (hh, h4) in HH:
                        pr = HP[hh]
                        o_ = bA[pr, h4 * 128:(h4 + 1) * 128]
                        if lv < 5:
                            P.mm(o_[:, 0:64], B_(pr, h4), A_(pr, h4))
                        P.mm(o_[:, 64:128], A_(pr, h4), B_(pr, h4))
                    src = bA[:].rearrange("p (h a t) -> p h a t", h=4, a=2)
                    bi = BI[lv % 2]
                    P.tt("vector", bi[:], src[:, :, 1, :], id4[:], ALU.add)
                    if lv < 5:
                        P.copy("scalar", ab[:], src)
                    A_ = (lambda ab: (lambda pr, h4: ab[pr, h4, 0, :]))(ab)
                    B_ = (lambda ab: (lambda pr, h4: ab[pr, h4, 1, :]))(ab)
                    for (hh, h4) in HH:
                        pr = HP[hh]
                        P.mm(bP[pr, h4 * 64:(h4 + 1) * 64], bi[pr, h4, :], Pc[pr, h4, :])
                    Pn = PI[lv % 2]
                    P.copy("vector", Pn[:].rearrange("p h t -> p (h t)"), bP[:])
                    Pc = Pn
                for (hh, h4) in HH:
                    pr = HP[hh]
                    o_ = bZ[0][pr, h4 * 64:(h4 + 1) * 64]
                    P.mm(o_, KR[pr, h4, 0, :], TS[pr, d, h4, :], start=True, stop=False)
                    P.mm(o_, AM[pr, h4, 0, :], V[pr, h4, :], start=False, stop=True)
                P.copy("scalar", RH[:].rearrange("p h t -> p (h t)"), bZ[0][:])
                for (hh, h4) in HH:
                    pr = HP[hh]
                    P.mm(bZ[1][pr, h4 * 64:(h4 + 1) * 64], Pc[pr, h4, :], RH[pr, h4, :])
                P.act(Un[:].rearrange("p h t -> p (h t)"), bZ[1][:], AF.Copy, scale=-1.0)
                Y_ = Yo[j % 2]
                for (hh, h4) in HH:
                    pr = HP[hh]
                    o_ = bZ[0][pr, h4 * 64:(h4 + 1) * 64]
                    P.mm(o_, KR[pr, h4, 1, :], TS[pr, d, h4, :], start=True, stop=False)
                    P.mm(o_, AM[pr, h4, 1, :], V[pr, h4, :], start=False, stop=False)
                    P.mm(o_, AM[pr, h4, 3, :], Un[pr, h4, :], start=False, stop=True)
                P.copy("scalar", Y_[:].rearrange("p h t -> p (h t)"), bZ[0][:])
                for hh in range(2):
                    P.dma(YS[d, c * 64:(c + 1) * 64, hh * 256:(hh + 1) * 256], Y_[HP[hh]].rearrange("p h t -> p (h t)"), wk=[(YS.name, d, c, hh)], eng=STQ)
                for (hh, h4) in HH:
                    pr = HP[hh]
                    for a_ in range(2):
                        trm(bX[0][pr, (h4 * 2 + a_) * 64:(h4 * 2 + a_ + 1) * 64], KBe[pr, h4, a_, :], hh)
                P.copy("vector", KBt[:].rearrange("p h a t -> p (h a t)"), bX[0][:])
                for (hh, h4) in HH:
                    pr = HP[hh]
                    o_ = bZ[1][pr, h4 * 64:(h4 + 1) * 64]
                    P.mm(o_, KBt[pr, h4, 0, :], V[pr, h4, :], start=True, stop=False)
                    P.mm(o_, KBt[pr, h4, 1, :], Un[pr, h4, :], start=False, stop=True)
                Td = TS[:, d, :, :]
                P.tt("vector", Td, Td, gl[:].unsqueeze(2).broadcast_to([128, 4, 64]), ALU.mult)
                P.tt("vector", Td, Td, bZ[1][:].rearrange("p (h t) -> p h t", h=4), ALU.add)

            for j in range(NCH):
                for d in range(2):
                    dpass(j, d)
            if not jb["ctx"]:
                for d in range(2):
                    for (hh, h4) in HH:
                        trm(bX[0][HP[hh], (d * 4 + h4) * 64:(d * 4 + h4 + 1) * 64], TS[HP[hh], d, h4, :], hh)
                P.copy("vector", s0t[:].rearrange("p d h k -> p (d h k)"), bX[0][:])
                for hh in range(2):
                    for d in range(2):
                        P.dma(O["o_rw"][si, l][d, hh * 4:(hh + 1) * 4].rearrange("h v k -> v h k"), s0t[HP[hh], d], wk=[("o_rw", si, l, hh, d)], eng=STQ, final=True)
            P.pop()
            P.push()
            gng = P.sb("gng", [128, 4]); gnb = P.sb("gnb", [128, 4])
            P.dma(gng[:], I["rwkv_gn_gT"][l]); P.dma(gnb[:], I["rwkv_gn_bT"][l])
            y0 = [P.sb(f"y0{i}", [128, 8, 64]) for i in range(2)]; y1 = [P.sb(f"y1{i}", [128, 8, 64]) for i in range(2)]
            bg = [P.sb(f"bg{i}", [128, 2, 4, 128]) for i in range(2)]
            junk = P.sb("junk", [128, 8, 64]); st8 = P.sb("st8", [128, 8]); yT = [P.sb(f"yT{i}", [128, 4, 128], FDT) for i in range(2)]
            ytmp = P.sb("ytmp", [128, 4, 128])
            pyt = P.ps("pyt", [128, 4, 128])
            for i in range(S // 128):
                a_ = y0[i % 2]; b_ = y1[i % 2]; g_ = bg[i % 2]; y_ = yT[i % 2]
                P.dma(a_[:], YS[0, i * 128:(i + 1) * 128, :].rearrange("p (h e) -> p h e", h=8), rk=[YS.name + "*"])
                P.dma(b_[:], YS[1, i * 128:(i + 1) * 128, :].rearrange("p (h e) -> p h e", h=8), rk=[YS.name + "*"])
                P.dma(g_[:], BG[:, :, i * 128:(i + 1) * 128].rearrange("a (c p) t -> p a c t", p=128), rk=[BG.name + "*"])
                P.tt("vector", a_[:], a_[:], b_[:], ALU.add)
                P.red("vector", st8[:], a_[:])
                P.ts("vector", st8[:], st8[:], -1.0 / 64, None, op0=ALU.mult)
                P.tt("vector", a_[:], a_[:], st8[:].unsqueeze(2).broadcast_to([128, 8, 64]), ALU.add)
                P.act(junk[:], a_[:], AF.Square)
                P.red("vector", st8[:], junk[:])
                rstd_of(st8[:], st8[:], 64, RW_GN_EPS)
                P.tt("vector", a_[:], a_[:], st8[:].unsqueeze(2).broadcast_to([128, 8, 64]), ALU.mult)
                for c in range(4):
                    P.tr(pyt[:, c, :], a_[:, 2 * c:2 * c + 2, :].rearrange("p h e -> p (h e)"), ident[:])
                for c in range(4):
                    P.act(ytmp[:, c, :], pyt[:, c, :], AF.Identity, bias=gnb[:, c:c + 1], scale=gng[:, c:c + 1])
                P.tt("vector", ytmp[:], ytmp[:], g_[:, 0, :, :], ALU.add)
                P.tt("vector", y_[:], ytmp[:], g_[:, 1, :, :], ALU.mult)
                c0 = t_off + i * 128
                P.dma(jb["YT"][3, :, c0:c0 + 128].rearrange("(c p) t -> p c t", p=128), y_[:], wk=[(jb["YT"].name, 3, c0)], eng=STQ)
            P.pop()


    def phaseM1(l, jb):
        TOK, j = jb["TOK"], jb["j"]
        ST = 512
        P.push()
        Yb = P.sb("Yb", [128, 4, 4, ST], FDT)
        Wo = P.sb("Wo", [128, 4, 4, D], FDT)
        wo2 = P.sb("wo2", [128, 8, D], FDT)
        Gt = [P.sb(f"Gt{i}", [128, 4, ST]) for i in range(2)]
        mg = P.sb("mg", [128, 8, ST], FDT); tmp = [P.sb(f"tmp{i}", [128, ST]) for i in range(3)]
        xT = P.sb("xT", [128, 8, ST])
        pp = [P.ps(f"pp{i}", [128, 512]) for i in range(6)]
        XTv = jb["XT"].rearrange("(k p) t -> p k t", p=128)
        wnames = ["mla_w_o", "mlstm_w_o", "swa_w_o", "rwkv_w_o"]
        for b in range(4):
            P.dma(Wo[:, b, :, :], WR[wnames[b]][l].rearrange("(c p) n -> p c n", p=128), wk=[Wo])
        for k2 in range(2):
            P.dma(wo2[:, k2 * 4:(k2 + 1) * 4, :], WR["w_out"][l][k2 * 512:(k2 + 1) * 512, :].rearrange("(k p) n -> p k n", p=128), wk=[wo2])
        ip = 0; io = 0
        for s_ in range(TOK // ST):
            t0 = s_ * ST
            Y_ = Yb; x_ = xT
            for b in range(4):
                P.dma(Y_[:, b, :, :], jb["YT"][b, :, t0:t0 + ST].rearrange("(c p) t -> p c t", p=128), rk=[jb["YT"].name + "*"], wk=[Y_])
            P.dma(x_[:], XTv[:, :, t0:t0 + ST], rk=[(jb["XT"].name, s_)])
            for oc in range(8):
                G_ = Gt[io % 2]; io += 1
                P.dma(G_[:], jb["UF"][C_GATE:C_GATE + 4096, t0:t0 + ST].rearrange("(b o p) t -> p b o t", b=4, o=8)[:, :, oc, :], rk=[jb["UF"].name + "*"])
                for b in range(4):
                    ps_ = pp[ip % 6]; ip += 1
                    for c in range(4):
                        P.mm(ps_[:, :ST], Wo[:, b, c, oc * 128:(oc + 1) * 128], Y_[:, b, c, :], start=(c == 0), stop=(c == 3), fast=True)
                    if b == 0:
                        P.tt("vector", tmp[2][:], ps_[:, :ST], G_[:, b, :], ALU.mult)
                    else:
                        t_ = tmp[b % 2]
                        P.tt("vector", t_[:], ps_[:, :ST], G_[:, b, :], ALU.mult)
                        P.tt(PENG, mg[:, oc, :] if b == 3 else tmp[2][:], tmp[2][:], t_[:], ALU.add)
            for oc in range(8):
                ps_ = pp[ip % 6]; ip += 1
                for k in range(8):
                    P.mm(ps_[:, :ST], wo2[:, k, oc * 128:(oc + 1) * 128], mg[:, k, :], start=(k == 0), stop=(k == 7), fast=True)
                P.stt("vector", x_[:, oc, :], ps_[:, :ST], modT[l][:, 16 + oc, j:j + 1], x_[:, oc, :], ALU.mult, ALU.add)
            P.dma(XTv[:, :, t0:t0 + ST], x_[:], wk=[(jb["XT"].name, s_)], eng=STQ)
        P.pop()

    def phaseM2(l, jb, last):
        TOK, j = jb["TOK"], jb["j"]
        ST = 512
        P.push()
        x1 = P.sb("x1", [128, 8, ST]); sq = P.sb("sq", [128, 8, ST]); h2 = P.sb("h2", [128, 8, ST], FDT); rstd = P.sb("rstd", [128, ST])
        w1c = [P.sb(f"w1c{i}", [128, 8, 512], FDT) for i in range(2)]
        hid = P.sb("hid", [128, 32, ST], FDT); rl = [P.sb(f"rl{i}", [128, ST]) for i in range(2)]
        w2r = [P.sb(f"w2r{i}", [128, D], FDT) for i in range(3)]
        ytok = [P.sb(f"ytok{i}", [128, D]) for i in range(2)]
        bank = [P.ps(f"bk{i}", [128, 512]) for i in range(8)]
        pst = bank[7]
        XTv = jb["XT"].rearrange("(k p) t -> p k t", p=128)
        ip = 0; i1 = 0; i2 = 0
        for s_ in range(TOK // ST):
            t0 = s_ * ST
            P.dma(x1[:], XTv[:, :, t0:t0 + ST], rk=[(jb["XT"].name, s_)])
            rms_rstd_featmajor(x1, sq, pst, rstd, ST)
            P.tt("vector", sq[:], x1[:], rstd[:].unsqueeze(1).broadcast_to([128, 8, ST]), ALU.mult)
            for k in range(8):
                P.act(h2[:, k, :], sq[:, k, :], AF.Identity, bias=modT[l][:, 24 + k, j:j + 1], scale=A2[l][:, k, j:j + 1])
            for fc in range(32):
                if fc % 4 == 0:
                    w_ = w1c[i1 % 2]; i1 += 1
                    P.dma(w_[:], WR["mlp_w1"][l][:, fc * 128:(fc + 4) * 128].rearrange("(k p) n -> p k n", p=128))
                ps_ = bank[ip % 6]; ip += 1
                for k in range(8):
                    P.mm(ps_[:, :ST], w_[:, k, (fc % 4) * 128:(fc % 4 + 1) * 128], h2[:, k, :], start=(k == 0), stop=(k == 7), fast=True)
                r_ = rl[fc % 2]
                P.act(r_[:], ps_[:, :ST], AF.Relu)
                P.tt(PENG, hid[:, fc, :], r_[:], r_[:], ALU.mult)
            x2 = sq
            for fc in range(32):
                w_ = w2r[i2 % 3]; i2 += 1
                P.dma(w_[:], WR["mlp_w2"][l][fc * 128:(fc + 1) * 128, :])
                for oc in range(8):
                    P.mm(bank[oc][:, :ST], w_[:, oc * 128:(oc + 1) * 128], hid[:, fc, :], start=(fc == 0), stop=(fc == 31), fast=True)
            for oc in range(8):
                P.stt("vector", x2[:, oc, :], bank[oc][:, :ST], modT[l][:, 40 + oc, j:j + 1], x1[:, oc, :], ALU.mult, ALU.add)
            if not last:
                P.dma(XTv[:, :, t0:t0 + ST], x2[:], wk=[(jb["XT"].name, s_)], eng=STQ)
            else:
                for tt in range(ST // 128):
                    yt = ytok[tt % 2]
                    for kk in range(2):
                        ps_ = bank[ip % 6]; ip += 1
                        for k4 in range(4):
                            P.tr(ps_[:, k4 * 128:(k4 + 1) * 128], x2[:, kk * 4 + k4, tt * 128:(tt + 1) * 128], ident[:])
                        P.evac(yt[:, kk * 512:(kk + 1) * 512], ps_[:])
                    P.dma(jb["y"][t0 + tt * 128:t0 + (tt + 1) * 128, :], yt[:], wk=[("y", j, t0, tt)], eng=STQ, final=True)
        P.pop()

    WR = {}
    fast_w = ["w_in", "mla_w_o", "mlstm_w_o", "swa_w_o", "rwkv_w_o", "w_out", "mlp_w1", "mlp_w2"]
    if FAST_MM:
        P.push()
        CH = 2048
        raw = [P.sb(f"wraw{i}", [128, CH]) for i in range(3)]
        rnd = [P.sb(f"wrnd{i}", [128, CH], F32R) for i in range(3)]
        engs = ["gpsimd", "vector", "scalar"]
        iw = 0
        for name in fast_w:
            src = I[name]
            _, Rr, Cc = src.shape
            dst = nc.dram_tensor(name + "_r", [L, Rr, Cc], F32R, kind="Internal").ap()
            WR[name] = dst
            for l in range(L):
                for rb in range(Rr // 128):
                    for c0 in range(0, Cc, CH):
                        w = min(CH, Cc - c0)
                        a_ = raw[iw % 3]; b_ = rnd[iw % 3]
                        P.dma(a_[:, :w], src[l, rb * 128:(rb + 1) * 128, c0:c0 + w])
                        P.copy(engs[iw % 3], b_[:, :w], a_[:, :w])
                        P.dma(dst[l, rb * 128:(rb + 1) * 128, c0:c0 + w], b_[:, :w], wk=[(dst.name, l, rb, c0)], eng=STQ)
                        iw += 1
        P.pop()
    else:
        for name in fast_w:
            WR[name] = I[name]

    only = os.environ.get("KONLY", "")
    for l in range(L):
        phase0(l)
        for jb in jobs:
            phase1(l, jb, first=(l == 0))
        for nm, fn in (("A", phaseA), ("C", phaseC), ("B", phaseB), ("D", phaseD)):
            if only and nm not in only:
                continue
            for jb in jobs:
                fn(l, jb)
        if only and "M" not in only:
            break
        for jb in jobs:
            phaseM1(l, jb)
        for jb in jobs:
            phaseM2(l, jb, last=(l == L - 1))
    print("NREC", getattr(P, "nrec", 0), flush=True)
    P.emit()
    return nc, I, O, SCR


def _fm(v, width=8):
    v = np.asarray(v, np.float32)
    return np.ascontiguousarray(np.swapaxes(v.reshape(v.shape[:-1] + (width, 128)), -1, -2))


def _rope_table(S, R):
    q = R // 4
    t = np.arange(S)
    pr = (t // 64).astype(np.float32); pc = (t % 64).astype(np.float32)
    inv = (10000.0 ** (-np.arange(q, dtype=np.float32) / q)).astype(np.float32)
    ar = pr[:, None] * inv; ac = pc[:, None] * inv
    ang = np.concatenate([ar, ar, ac, ac], -1).astype(np.float32)
    sign = np.concatenate([-np.ones(q), np.ones(q), -np.ones(q), np.ones(q)]).astype(np.float32)
    return np.ascontiguousarray(np.stack([np.cos(ang), np.sin(ang) * sign], 1).astype(np.float32))


def make_in_map(inp, cfg, core):
    L = cfg.depth
    f = lambda a: np.ascontiguousarray(np.asarray(a, np.float32))
    n_p = cfg.n_p
    m = {}
    m["xs"] = f(inp["x_sample"][core])
    m["xp"] = f(inp["x_prompt"][core * n_p:(core + 1) * n_p].reshape(n_p * SP, D))
    cc = np.stack([np.asarray(inp["c"][core]), np.asarray(inp["c_ctx"])], 0)
    m["cT"] = f(cc.reshape(2, 8, 128).transpose(2, 1, 0))
    m["ckv"] = f(inp["cache_mla_ckv"][core]); m["ckr"] = f(inp["cache_mla_krope"][core])
    m["cswk"] = f(np.asarray(inp["cache_swa_k"][core]).reshape(L, CTX, 128))
    m["cswv"] = f(np.asarray(inp["cache_swa_v"][core]).reshape(L, CTX, 128))
    m["mC"] = f(inp["state_mlstm_C"][core]); m["mn"] = f(inp["state_mlstm_n"][core])
    m["mm"] = f(inp["state_mlstm_m"][core]); m["rw"] = f(inp["state_rwkv"][core])
    m["ada_w"] = f(inp["ada_w"]); m["ada_bT"] = _fm(inp["ada_b"], 48)
    m["norm1T"] = _fm(inp["norm1"]); m["norm2T"] = _fm(inp["norm2"])
    for k in ["w_in", "mla_q_a_norm", "mla_kv_a_norm", "mla_w_uq", "mla_w_ukv", "mla_q_norm", "mla_k_norm", "mla_w_o",
              "mlstm_norm", "mlstm_w_o", "swa_q_norm", "swa_k_norm", "swa_sink", "swa_w_o", "rwkv_w2", "rwkv_a2",
              "rwkv_g2", "rwkv_w_o", "w_out", "mlp_w1", "mlp_w2"]:
        m[k] = f(inp[k])
    m["mlstm_i_bias"] = f(np.asarray(inp["mlstm_i_bias"]).reshape(L, 8))
    m["mlstm_f_bias"] = f(np.asarray(inp["mlstm_f_bias"]).reshape(L, 8))
    for k in ["rwkv_gn_g", "rwkv_gn_b"]:
        m[k + "T"] = _fm(inp[k], 4)
    m["rwkv_w0"] = f(inp["rwkv_w0"])
    for k in ["rwkv_kk", "rwkv_ka"]:
        m[k + "64"] = f(np.asarray(inp[k]).reshape(L, 8, 64).transpose(0, 2, 1))
    for k in ["rwkv_a0", "rwkv_u"]:
        m[k + "64"] = f(np.asarray(inp[k]).reshape(L, 2, 8, 64).transpose(0, 3, 1, 2))
    m["k_ident"] = np.eye(128, dtype=np.float32)
    m["k_ones"] = np.ones((128, 128), np.float32)
    kq = np.arange(128)
    m["k_tri"] = np.stack([(kq[:, None] >= kq[None, :]), (kq[:, None] <= kq[None, :])], 0).astype(np.float32)
    m["k_tris"] = np.stack([(kq[:, None] > kq[None, :]), (kq[:, None] < kq[None, :])], 0).astype(np.float32)
    sel = np.zeros((2, 128, 128), np.float32); sel[0, 127, :] = 1.0; sel[1, 0, :] = 1.0
    m["k_sel"] = sel
    m["k_rope32"] = _rope_table(cfg.S_s, 32); m["k_rope64"] = _rope_table(cfg.S_s, 64)
    return m


_CACHE = {}


def kernel(**inputs):
    cfg = Cfg(S_s=4096, n_p=2, depth=2)
    n_cores = 8
    if "nc" not in _CACHE:
        _CACHE["nc"] = build(cfg)
    nc, I, O, SCR = _CACHE["nc"]
    in_maps = []
    for c in range(n_cores):
        m = make_in_map(inputs, cfg, c)
        in_maps.append({k: v for k, v in m.items() if k in I})
    res = run_bass_kernel_spmd(nc, in_maps, core_ids=list(range(n_cores)))
    R = res.results
    L = cfg.depth
    cat = lambda k: np.concatenate([np.asarray(r[k], np.float32) for r in R], 0)
    y_prompt = cat("y_p").reshape(16, SP, D)
    y_sample = np.stack([np.asarray(r["y_s"], np.float32) for r in R], 0)
    return (y_prompt, y_sample,
            cat("o_ckv"), cat("o_ckr"),
            cat("o_swk").reshape(16, L, SP, 2, 64), cat("o_swv").reshape(16, L, SP, 2, 64),
            cat("o_mC"), cat("o_mn"), cat("o_mm"), cat("o_rw"))
```

```python
import os
import numpy as np
import concourse.bass as bass
import concourse.mybir as mybir
from concourse.bass_utils import run_bass_kernel_spmd
from contextlib import ExitStack

F32 = mybir.dt.float32
F32R = mybir.dt.float32r
FAST_MM = os.environ.get("FAST_MM", "1") == "1"
FDT = F32R if FAST_MM else F32


def fr(ap):
    return ap.bitcast(F32R) if FAST_MM else ap
AF = mybir.ActivationFunctionType
ALU = mybir.AluOpType
AX = mybir.AxisListType

ENGS = ("sync", "scalar", "vector", "gpsimd", "tensor")
SEM_LIMIT = int(os.environ.get("SEM_LIMIT", 30000))
N_DMA_SEMS = 16
import os
STQ = os.environ.get("STQ", "gpsimd")
PENG = os.environ.get("PENG", "vector")

D = 1024
NORM_EPS = 1e-6
CTX = 256
SP = 256
D_IN = 8752
D_FF = 4096
MLA_SCALE = 96 ** -0.5
SW_SCALE = 64 ** -0.5
RW_DECAY = 0.6065306597126334
RW_GN_EPS = 64e-5
C_QA, C_KVA, C_KR = 0, 256, 384
C_MQ, C_MK, C_MV, C_MI, C_MF, C_MO = 416, 672, 928, 1440, 1448, 1456
C_SQ, C_SK, C_SV = 1968, 2480, 2608
C_RR, C_RK, C_RV, C_RW, C_RA, C_RG = 2736, 3248, 3760, 4272, 4400, 4528
C_GATE = 4656
NTOKC = 4656


class Op:
    __slots__ = ("eng", "fn", "deps", "is_dma", "signal", "sem", "cnt", "dsem_prev", "barriered")

    def __init__(self, eng, fn, is_dma):
        self.eng = eng
        self.fn = fn
        self.deps = []
        self.is_dma = is_dma
        self.signal = False
        self.sem = None
        self.cnt = 0
        self.dsem_prev = None
        self.barriered = False


class Prog:
    def __init__(self, nc):
        self.nc = nc
        self.ops = {e: [] for e in ENGS}
        self.lastw = {}
        self.readers = {}
        self.stacks = [ExitStack()]
        self.out_dmas = []
        self.uid = 0
        self.rr = 0
        self.psum_names = set()

    def sb(self, name, shape, dt=F32):
        self.uid += 1
        return self.stacks[-1].enter_context(self.nc.sbuf_tensor(f"{name}_{self.uid}", list(shape), dt))

    def ps(self, name, shape, dt=F32):
        self.uid += 1
        n = 1
        for d in shape[1:]:
            n *= d
        nb = (n * 4 + 2047) // 2048
        t = self.stacks[-1].enter_context(self.nc.psum_tensor(f"{name}_{self.uid}", [128, nb * 512], dt))
        self.psum_names.add(t.name)
        v = t[:shape[0], :n]
        if len(shape) == 3:
            v = v.rearrange("p (a b) -> p a b", a=shape[1])
        elif len(shape) == 4:
            v = v.rearrange("p (a b c) -> p a b c", a=shape[1], b=shape[2])
        return v

    def push(self):
        self.stacks.append(ExitStack())

    def pop(self):
        self.barrier()
        self.stacks.pop().close()

    @staticmethod
    def _key(k):
        if isinstance(k, (str, tuple)):
            return k
        return k.name

    def op(self, eng, fn, reads=(), writes=(), is_dma=False):
        self.nrec = getattr(self, "nrec", 0) + 1
        if self.nrec > int(os.environ.get("KMAXOPS", 10 ** 9)):
            return None
        o = Op(eng, fn, is_dma)
        if os.environ.get("KTRACE") and abs(self.nrec - int(os.environ["KTRACE"])) <= 6:
            print("OP", self.nrec, eng, [self._key(k) for k in writes], flush=True)
        rk = [self._key(k) for k in reads if k is not None and not isinstance(k, (int, float))]
        wk = [self._key(k) for k in writes]
        if eng != "tensor":
            wk = wk + [k for k in rk if k in self.psum_names and k not in wk]
        raw = set()
        deps = set()
        for k in rk:
            w = self.lastw.get(k)
            if w is not None:
                raw.add(w)
                deps.add(w)
        for k in wk:
            w = self.lastw.get(k)
            if w is not None:
                deps.add(w)
            last_by_eng = {}
            for r in self.readers.get(k, ()):
                if r.is_dma:
                    deps.add(r)
                else:
                    last_by_eng[r.eng] = r
            deps.update(last_by_eng.values())
        for d in deps:
            if d.eng == eng and not d.is_dma and not is_dma:
                if eng == "tensor":
                    continue
            o.deps.append(d)
        for k in rk:
            self.readers.setdefault(k, []).append(o)
        for k in wk:
            self.lastw[k] = o
            self.readers[k] = []
        self.ops[eng].append(o)
        return o

    def barrier(self):
        lasts = [self.ops[e][-1] for e in ENGS if self.ops[e]]
        pend = [o for e in ("sync", "scalar", "gpsimd") for o in self.ops[e] if o.is_dma and not o.barriered]
        for o in pend:
            o.barriered = True
        for e in ENGS:
            b = Op(e, None, False)
            b.deps = lasts + pend
            self.ops[e].append(b)
        self.lastw.clear()
        self.readers.clear()

    def dma(self, out, in_, rk=None, wk=None, eng="sync", final=False, **kw):
        r = [in_] if rk is None else rk
        w = [out] if wk is None else wk
        o = self.op(eng, lambda e: e.dma_start(out=out, in_=in_, **kw), r, w, is_dma=True)
        if final and o is not None:
            self.out_dmas.append(o)
        return o

    def mm(self, out, lhsT, rhs, start=True, stop=True, rk=None, wk=None, fast=False):
        r = [lhsT, rhs] if rk is None else rk
        w = [out] if wk is None else wk
        return self.op("tensor", lambda e: e.matmul(out, lhsT=lhsT, rhs=rhs, start=start, stop=stop), r, w)

    def tr(self, out, in_, ident, rk=None, wk=None):
        r = [in_, ident] if rk is None else rk
        w = [out] if wk is None else wk
        return self.op("tensor", lambda e: e.transpose(out, in_, ident), r, w)

    def act(self, out, in_, func, bias=None, scale=None, accum=None, rk=None, wk=None, extra_r=()):
        kw = {}
        if bias is not None:
            kw["bias"] = bias
        if scale is not None:
            kw["scale"] = scale
        if accum is not None:
            kw["accum_out"] = accum
        r = ([in_, bias, scale] if rk is None else list(rk)) + list(extra_r)
        w = ([out] + ([accum] if accum is not None else [])) if wk is None else wk
        return self.op("scalar", lambda e: e.activation(out=out, in_=in_, func=func, **kw), r, w)

    def tt(self, eng, out, in0, in1, op, rk=None, wk=None):
        r = [in0, in1] if rk is None else rk
        w = [out] if wk is None else wk
        return self.op(eng, lambda e: e.tensor_tensor(out=out, in0=in0, in1=in1, op=op), r, w)

    def ts(self, eng, out, in0, s1, s2=None, op0=ALU.mult, op1=None, rk=None, wk=None):
        r = [in0, s1, s2] if rk is None else rk
        w = [out] if wk is None else wk
        if op1 is None:
            return self.op(eng, lambda e: e.tensor_scalar(out=out, in0=in0, scalar1=s1, scalar2=None, op0=op0), r, w)
        return self.op(eng, lambda e: e.tensor_scalar(out=out, in0=in0, scalar1=s1, scalar2=s2, op0=op0, op1=op1), r, w)

    def stt(self, eng, out, in0, scalar, in1, op0, op1, rk=None, wk=None):
        r = [in0, scalar, in1] if rk is None else rk
        w = [out] if wk is None else wk
        return self.op(eng, lambda e: e.scalar_tensor_tensor(out=out, in0=in0, scalar=scalar, in1=in1, op0=op0, op1=op1), r, w)

    def copy(self, eng, out, in_, rk=None, wk=None):
        r = [in_] if rk is None else rk
        w = [out] if wk is None else wk
        if eng == "scalar":
            return self.op(eng, lambda e: e.copy(out=out, in_=in_), r, w)
        return self.op(eng, lambda e: e.tensor_copy(out=out, in_=in_), r, w)

    def red(self, eng, out, in_, op=ALU.add, rk=None, wk=None):
        r = [in_] if rk is None else rk
        w = [out] if wk is None else wk
        return self.op(eng, lambda e: e.tensor_reduce(out=out, in_=in_, axis=AX.X, op=op), r, w)

    def recip(self, out, in_, rk=None, wk=None):
        r = [in_] if rk is None else rk
        w = [out] if wk is None else wk
        return self.op("vector", lambda e: e.reciprocal(out=out, in_=in_), r, w)

    def memset(self, eng, out, val, wk=None):
        w = [out] if wk is None else wk
        return self.op(eng, lambda e: e.memset(out, val), [], w)

    def evac(self, out, in_, rk=None, wk=None):
        self.rr += 1
        ev = os.environ.get("KEVAC", "alt")
        if ev == "alt":
            ev = "scalar" if self.rr % 2 else "vector"
        return self.copy(ev, out, in_, rk, wk)

    def emit(self):
        nc = self.nc
        if self.out_dmas:
            b = Op("sync", None, False)
            b.deps = list(self.out_dmas)
            self.ops["sync"].append(b)
        for e in ENGS:
            for o in self.ops[e]:
                for d in o.deps:
                    d.signal = True
        with ExitStack() as st:
            def newsem(nm):
                return st.enter_context(nc.semaphore(nm))
            for e in ENGS:
                cur, c, ep = None, 0, 0
                for o in self.ops[e]:
                    if o.is_dma or not o.signal or o.fn is None:
                        continue
                    if cur is None or c >= SEM_LIMIT:
                        cur = newsem(f"s_{e}_{ep}")
                        ep += 1
                        c = 0
                    c += 1
                    o.sem = cur
                    o.cnt = c
            for e in ENGS:
                dl = [o for o in self.ops[e] if o.is_dma]
                if not dl:
                    continue
                dsems = [newsem(f"dq_{e}_{i}") for i in range(N_DMA_SEMS)]
                dcnt = [0] * N_DMA_SEMS
                dlast = [None] * N_DMA_SEMS
                for di, o in enumerate(dl):
                    s_ = di % N_DMA_SEMS
                    if dcnt[s_] + 16 > SEM_LIMIT:
                        dsems[s_] = newsem(f"dq_{e}_{s_}_{di}")
                        dcnt[s_] = 0
                    dcnt[s_] += 16
                    o.sem = dsems[s_]
                    o.cnt = dcnt[s_]
                    o.dsem_prev = dlast[s_]
                    dlast[s_] = o
                    o.signal = True
            if os.environ.get("KSTATS"):
                for e in ENGS:
                    sig = [o for o in self.ops[e] if o.signal and not o.is_dma and o.fn is not None]
                    print("ENG", e, "ops", len(self.ops[e]), "signals", len(sig), "maxcnt", max([o.cnt for o in self.ops[e]] + [0]), flush=True)
            with nc.Block() as block:
                def run(e):
                    def body(eng):
                        seen = {}
                        for o in self.ops[e]:
                            need = {}
                            deps = o.deps
                            if o.is_dma and o.dsem_prev is not None:
                                deps = deps + [o.dsem_prev]
                            for d in deps:
                                if d.fn is None or d.sem is None:
                                    continue
                                nm = d.sem.name
                                if need.get(nm, (None, 0))[1] < d.cnt:
                                    need[nm] = (d.sem, d.cnt)
                            for nm, (s, c) in need.items():
                                if seen.get(nm, 0) >= c:
                                    continue
                                eng.wait_ge(s, c)
                                seen[nm] = c
                            if o.fn is None:
                                continue
                            ins = o.fn(eng)
                            if o.signal:
                                ins.then_inc(o.sem, 16 if o.is_dma else 1)
                    return body
                block.sync(run("sync"))
                block.scalar(run("scalar"))
                block.vector(run("vector"))
                block.gpsimd(run("gpsimd"))
                block.tensor(run("tensor"))
        while self.stacks:
            self.stacks.pop().close()


class Cfg:
    def __init__(self, S_s=4096, n_p=2, depth=2, debug=()):
        self.S_s = S_s
        self.n_p = n_p
        self.depth = depth
        self.debug = tuple(debug)


W_NAMES = ["ada_w", "w_in", "mla_w_uq", "mla_w_ukv", "mla_w_o", "mlstm_w_o", "swa_w_o", "rwkv_w2", "rwkv_a2",
           "rwkv_g2", "rwkv_w_o", "w_out", "mlp_w1", "mlp_w2"]

P1_BLOCKS = [
    (0, 416, "T"), (416, 672, "F"), (672, 928, "TF"), (928, 1440, "T"), (1440, 1456, "T"), (1456, 1968, "T"),
    (1968, 2480, "T"), (2480, 2736, "T"), (2736, 3248, "F"), (3248, 3760, "F"), (3760, 4272, "TF"),
    (4272, 4656, "F"),
] + [(4656 + 512 * i, 4656 + 512 * (i + 1), "G") for i in range(8)]


def build(cfg):
    nc = bass.Bass("TRN2", target_bir_lowering=False)
    if FAST_MM:
        nc.dge_precook = False
    L = cfg.depth
    S_s, n_p = cfg.S_s, cfg.n_p
    TOKP = n_p * SP
    P = Prog(nc)
    I = {}
    O = {}
    SCR = {}

    def din(name, shape):
        I[name] = nc.dram_tensor(name, list(shape), F32, kind="ExternalInput").ap()
        return I[name]

    def dout(name, shape):
        O[name] = nc.dram_tensor(name, list(shape), F32, kind="ExternalOutput").ap()
        return O[name]

    def scr(name, shape, dt=F32):
        kind = "ExternalOutput" if name in cfg.debug else "Internal"
        SCR[name] = nc.dram_tensor(name, list(shape), dt, kind=kind).ap()
        return SCR[name]

    din("xs", [S_s, D]); din("xp", [TOKP, D])
    din("cT", [128, 8, 2])
    din("ckv", [L, CTX, 128]); din("ckr", [L, CTX, 32]); din("cswk", [L, CTX, 128]); din("cswv", [L, CTX, 128])
    din("mC", [L, 2, 4, 64, 128]); din("mn", [L, 2, 4, 64]); din("mm", [L, 2, 4]); din("rw", [L, 2, 8, 64, 64])
    din("ada_w", [L, D, 6 * D]); din("ada_bT", [L, 128, 48]); din("norm1T", [L, 128, 8]); din("norm2T", [L, 128, 8])
    din("w_in", [L, D, D_IN])
    din("mla_q_a_norm", [L, 256]); din("mla_kv_a_norm", [L, 128]); din("mla_w_uq", [L, 256, 768])
    din("mla_w_ukv", [L, 128, 1024]); din("mla_q_norm", [L, 96]); din("mla_k_norm", [L, 96]); din("mla_w_o", [L, 512, D])
    din("mlstm_i_bias", [L, 8]); din("mlstm_f_bias", [L, 8]); din("mlstm_norm", [L, 128]); din("mlstm_w_o", [L, 512, D])
    din("swa_q_norm", [L, 64]); din("swa_k_norm", [L, 64]); din("swa_sink", [L, 8]); din("swa_w_o", [L, 512, D])
    din("rwkv_w0", [L, 2, 512]); din("rwkv_w2", [L, 2, 64, 512]); din("rwkv_a064", [L, 64, 2, 8])
    din("rwkv_a2", [L, 2, 64, 512]); din("rwkv_g2", [L, 128, 512]); din("rwkv_kk64", [L, 64, 8]); din("rwkv_ka64", [L, 64, 8])
    din("rwkv_u64", [L, 64, 2, 8]); din("rwkv_gn_gT", [L, 128, 4]); din("rwkv_gn_bT", [L, 128, 4]); din("rwkv_w_o", [L, 512, D])
    din("w_out", [L, D, D]); din("mlp_w1", [L, D, D_FF]); din("mlp_w2", [L, D_FF, D])
    din("k_ident", [128, 128]); din("k_ones", [128, 128])
    din("k_rope32", [S_s, 2, 32]); din("k_rope64", [S_s, 2, 64]); din("k_tri", [2, 128, 128]); din("k_sel", [2, 128, 128]); din("k_tris", [2, 128, 128])
    dout("y_p", [TOKP, D]); dout("y_s", [S_s, D])
    dout("o_ckv", [n_p, L, SP, 128]); dout("o_ckr", [n_p, L, SP, 32]); dout("o_swk", [n_p, L, SP, 128]); dout("o_swv", [n_p, L, SP, 128])
    dout("o_mC", [n_p, L, 2, 4, 64, 128]); dout("o_mn", [n_p, L, 2, 4, 64]); dout("o_mm", [n_p, L, 2, 4]); dout("o_rw", [n_p, L, 2, 8, 64, 64])

    jobs = [dict(name="s", TOK=S_s, seqs=[(0, S_s)], ctx=True, j=0, x=I["xs"], y=O["y_s"]),
            dict(name="p", TOK=TOKP, seqs=[(i * SP, SP) for i in range(n_p)], ctx=False, j=1, x=I["xp"], y=O["y_p"])]
    for jb in jobs:
        n = jb["name"]
        jb["XT"] = scr(f"XT_{n}", [D, jb["TOK"]])
        jb["UT"] = scr(f"UTOK_{n}", [jb["TOK"], NTOKC])
        jb["UF"] = scr(f"UFEAT_{n}", [D_IN, jb["TOK"]])
        jb["YT"] = scr(f"YT_{n}", [4, 512, jb["TOK"]], FDT)

    ident = P.sb("ident", [128, 128]); ones = P.sb("ones", [128, 128])
    P.dma(ident[:], I["k_ident"][:, :], wk=[ident]); P.dma(ones[:], I["k_ones"][:, :], wk=[ones])
    cT = P.sb("cT", [128, 8, 2]); sT = P.sb("sT", [128, 8, 2])
    P.dma(cT[:], I["cT"][:, :, :])
    P.act(sT[:], cT[:], AF.Silu)
    modT = [P.sb(f"modT{l}", [128, 48, 2]) for l in range(L)]
    A1 = [P.sb(f"A1_{l}", [128, 8, 2]) for l in range(L)]
    A2 = [P.sb(f"A2_{l}", [128, 8, 2]) for l in range(L)]

    def phase0(l):
        P.push()
        wb = [P.sb(f"adaw{i}", [128, 8, 512]) for i in range(2)]
        pm = P.ps("pm", [128, 48, 2])
        bT = P.sb("bT", [128, 48]); n1 = P.sb("n1", [128, 8]); n2 = P.sb("n2", [128, 8])
        P.dma(bT[:], I["ada_bT"][l]); P.dma(n1[:], I["norm1T"][l]); P.dma(n2[:], I["norm2T"][l])
        wv = I["ada_w"][l].rearrange("(k p) c -> p k c", p=128)
        for g in range(12):
            w = wb[g % 2]
            P.dma(w[:], wv[:, :, g * 512:(g + 1) * 512])
            for jj in range(4):
                jc = g * 4 + jj
                for k in range(8):
                    P.mm(pm[:, jc, :], w[:, k, jj * 128:(jj + 1) * 128], sT[:, k, :], start=(k == 0), stop=(k == 7))
        P.tt("vector", modT[l][:], pm[:], bT[:].unsqueeze(2).broadcast_to([128, 48, 2]), ALU.add)
        P.stt("vector", A1[l][:], modT[l][:, 8:16, :], 1.0, n1[:].unsqueeze(2).broadcast_to([128, 8, 2]), ALU.add, ALU.mult)
        P.stt("vector", A2[l][:], modT[l][:, 32:40, :], 1.0, n2[:].unsqueeze(2).broadcast_to([128, 8, 2]), ALU.add, ALU.mult)
        P.pop()

    def rms_rstd_featmajor(xT, sq, pst, rstd, n):
        P.act(sq[:, :, :n], xT[:, :, :n], AF.Square)
        for k in range(8):
            P.mm(pst[:, :n], ones[:], sq[:, k, :n], start=(k == 0), stop=(k == 7))
        P.act(rstd[:, :n], pst[:, :n], AF.Sqrt, bias=NORM_EPS, scale=1.0 / D)
        P.recip(rstd[:, :n], rstd[:, :n])

    def phase1(l, jb, first):
        TOK, j = jb["TOK"], jb["j"]
        ST = 512
        P.push()
        xT = [P.sb(f"xT{i}", [128, 8, ST]) for i in range(2)]
        hT = [P.sb(f"hT{i}", [128, 8, ST], FDT) for i in range(2)]
        sq = P.sb("sq", [128, 8, ST]); rstd = P.sb("rstd", [128, ST])
        wt = [P.sb(f"wt{i}", [128, 8, 512], FDT) for i in range(3)]
        stg = [P.sb(f"stg{i}", [128, 4, 512]) for i in range(3)]
        xin = [P.sb(f"xin{i}", [128, D]) for i in range(2)] if first else None
        pst = P.ps("pst", [128, ST])
        pp = [P.ps(f"pp{i}", [128, 512]) for i in range(6)]
        XTv = jb["XT"].rearrange("(k p) t -> p k t", p=128)
        wv = WR["w_in"][l].rearrange("(k p) c -> p k c", p=128)
        UTv = jb["UT"].rearrange("(tt p) c -> p tt c", p=128)
        ip = 0
        ist = 0
        iw = 0
        for s in range(TOK // ST):
            x_ = xT[s % 2]; h_ = hT[s % 2]
            t0 = s * ST
            if first:
                for tt in range(4):
                    xi = xin[tt % 2]
                    P.dma(xi[:], jb["x"][t0 + tt * 128:t0 + (tt + 1) * 128, :])
                    for kk in range(2):
                        pq = pp[ip % 6]; ip += 1
                        for k4 in range(4):
                            P.tr(pq[:, k4 * 128:(k4 + 1) * 128], xi[:, (kk * 4 + k4) * 128:(kk * 4 + k4 + 1) * 128], ident[:])
                        P.evac(x_[:, kk * 4:(kk + 1) * 4, tt * 128:(tt + 1) * 128], pq[:].rearrange("p (a b) -> p a b", a=4))
                P.dma(XTv[:, :, t0:t0 + ST], x_[:], wk=[(jb["XT"].name, s)], eng=STQ)
            else:
                P.dma(x_[:], XTv[:, :, t0:t0 + ST], rk=[(jb["XT"].name, s)])
            rms_rstd_featmajor(x_, sq, pst, rstd, ST)
            P.tt("vector", sq[:], x_[:], rstd[:].unsqueeze(1).broadcast_to([128, 8, ST]), ALU.mult)
            for k in range(8):
                P.act(h_[:, k, :], sq[:, k, :], AF.Identity, bias=modT[l][:, k, j:j + 1], scale=A1[l][:, k, j:j + 1])
            for (cs, ce, lay) in P1_BLOCKS:
                wd = ce - cs
                w = wt[iw % 3]; iw += 1
                P.dma(w[:, :, :wd], wv[:, :, cs:ce])
                if "T" in lay:
                    sg = stg[ist % 3]; ist += 1
                    for tt in range(4):
                        pq = pp[ip % 6]; ip += 1
                        for k in range(8):
                            P.mm(pq[:, :wd], h_[:, k, tt * 128:(tt + 1) * 128], w[:, k, :wd], start=(k == 0), stop=(k == 7), fast=(wd >= 256))
                        P.evac(sg[:, tt, :wd], pq[:, :wd])
                    P.dma(UTv[:, s * 4:(s + 1) * 4, cs:ce], sg[:, :, :wd], wk=[(jb["UT"].name, s, cs)], eng=STQ)
                if "F" in lay or "G" in lay:
                    sg = stg[ist % 3]; ist += 1
                    nb = wd // 128
                    for cb in range(nb):
                        pq = pp[ip % 6]; ip += 1
                        for k in range(8):
                            P.mm(pq[:, :ST], w[:, k, cb * 128:(cb + 1) * 128], h_[:, k, :], start=(k == 0), stop=(k == 7), fast=True)
                        if lay == "G":
                            P.act(sg[:, cb, :], pq[:, :ST], AF.Sigmoid)
                        else:
                            P.evac(sg[:, cb, :], pq[:, :ST])
                    P.dma(jb["UF"][cs:ce, t0:t0 + ST].rearrange("(cb p) t -> p cb t", p=128), sg[:, :nb, :],
                          wk=[(jb["UF"].name, s, cs)], eng=STQ)
        P.pop()


    def rope(x, cos, sinS, H, q, t1, t2):
        R4 = 4 * q
        t1v = t1[:, :H * R4].rearrange("p (h r) -> p h r", h=H)
        t2v = t2[:, :H * R4].rearrange("p (h a b c) -> p h a b c", h=H, a=2, b=2)
        xv = x.rearrange("p h (a b c) -> p h a b c", a=2, b=2)
        sv = sinS.rearrange("p (a b c) -> p a b c", a=2, b=2)
        P.tt("vector", t1v, x, cos.unsqueeze(1).broadcast_to([128, H, R4]), ALU.mult)
        for b in range(2):
            P.tt(PENG, t2v[:, :, :, b, :], xv[:, :, :, 1 - b, :],
                 sv[:, :, b, :].unsqueeze(1).broadcast_to([128, H, 2, q]), ALU.mult)
        P.tt("vector", x, t1v, t2[:, :H * R4].rearrange("p (h r) -> p h r", h=H), ALU.add)

    def rstd_of(out, ssq, n, eps):
        P.act(out, ssq, AF.Sqrt, bias=eps, scale=1.0 / n)
        P.recip(out, out)

    def bcast_row(name, src_row, n):
        t = P.sb(name, [128, n])
        P.dma(t[:], src_row.partition_broadcast(128))
        return t

    def phaseA(l, jb):
        TOK = jb["TOK"]
        for si, (t_off, S) in enumerate(jb["seqs"]):
            NK = S + (CTX if jb["ctx"] else 0)
            NKT = NK // 128
            n = jb["name"]
            KTs = scr(f"A_KT_{n}{l}_{si}", [8, 96, NK], FDT); VPs = scr(f"A_VP_{n}{l}_{si}", [NK, 8, 65], FDT); QTs = scr(f"A_QT_{n}{l}_{si}", [8, 96, S], FDT)
            P.push()
            gqa = bcast_row("gqa", I["mla_q_a_norm"][l], 256); gkv = bcast_row("gkv", I["mla_kv_a_norm"][l], 128)
            gqn = bcast_row("gqn", I["mla_q_norm"][l], 96); gkn = bcast_row("gkn", I["mla_k_norm"][l], 96)
            wuq = P.sb("wuq", [128, 2, 768]); wukv = P.sb("wukv", [128, 1024])
            P.dma(wuq[:], I["mla_w_uq"][l].rearrange("(k p) c -> p k c", p=128)); P.dma(wukv[:], I["mla_w_ukv"][l])
            ua = [P.sb(f"ua{i}", [128, 416]) for i in range(2)]
            rt = [P.sb(f"rt{i}", [128, 2, 32]) for i in range(2)]
            junk = P.sb("junk", [128, 768]); junk2 = P.sb("junk2", [128, 768])
            st = P.sb("st", [128, 4]); ss16 = P.sb("ss16", [128, 16]); ssr = P.sb("ssr", [128, 1])
            qlat = P.sb("qlat", [128, 256]); ckv = [P.sb(f"ckv{i}", [128, 128]) for i in range(2)]
            qlT = P.sb("qlT", [128, 2, 128]); ckT = P.sb("ckT", [128, 128])
            qf = P.sb("qf", [128, 8, 96]); kvf = P.sb("kvf", [128, 8, 128]); kn = P.sb("kn", [128, 8, 96])
            kr = P.sb("kr", [128, 32])
            vp = [P.sb(f"vp{i}", [128, 8, 65], FDT) for i in range(2)]
            qT = [P.sb(f"qT{i}", [96, 8, 128], FDT) for i in range(2)]; kT = [P.sb(f"kT{i}", [96, 8, 128], FDT) for i in range(2)]
            for v in vp:
                P.copy("vector", v[:, :, 64:65], ones[:, 0:8].unsqueeze(2), wk=[v])
            ptr = P.ps("ptr", [128, 3, 128]); pq1 = P.ps("pq1", [128, 512]); pq2 = P.ps("pq2", [128, 256])
            pk1 = P.ps("pk1", [128, 512]); pk2 = P.ps("pk2", [128, 512])
            pT = [P.ps(f"pT{i}", [96, 4, 128]) for i in range(2)]
            tiles = [("new", i) for i in range(S // 128)] + ([("ctx", i) for i in range(CTX // 128)] if jb["ctx"] else [])
            for it, (kind, i) in enumerate(tiles):
                if it >= int(os.environ.get("KA1T", 999)):
                    break
                u = ua[it % 2]; ck = ckv[it % 2]; v_ = vp[it % 2]; r_ = rt[it % 2]
                if os.environ.get("KSTATS"):
                    print("A1 tile start", jb["name"], si, it, getattr(P, "nrec", 0))
                new = kind == "new"
                rows = slice(t_off + i * 128, t_off + (i + 1) * 128)
                krow = i * 128 if new else S + i * 128
                if new:
                    P.dma(u[:], jb["UT"][rows, 0:416], rk=[(jb["UT"].name, (t_off + i * 128) // 512, 0)])
                    if jb["ctx"]:
                        P.dma(r_[:], I["k_rope32"][i * 128:(i + 1) * 128])
                    P.act(junk[:, :256], u[:, 0:256], AF.Square, accum=st[:, 0:1])
                    P.act(junk[:, :128], u[:, 256:384], AF.Square, accum=st[:, 1:2])
                    P.act(st[:, 2:3], st[:, 0:1], AF.Sqrt, bias=NORM_EPS, scale=1.0 / 256)
                    P.act(st[:, 3:4], st[:, 1:2], AF.Sqrt, bias=NORM_EPS, scale=1.0 / 128)
                    P.recip(st[:, 2:4], st[:, 2:4])
                    P.stt("vector", qlat[:], u[:, 0:256], st[:, 2:3], gqa[:], ALU.mult, ALU.mult)
                    P.stt("vector", ck[:], u[:, 256:384], st[:, 3:4], gkv[:], ALU.mult, ALU.mult)
                    if not jb["ctx"]:
                        P.dma(O["o_ckv"][si, l, i * 128:(i + 1) * 128, :], ck[:], wk=[("o_ckv", si, l, i)], eng=STQ, final=True)
                        P.dma(O["o_ckr"][si, l, i * 128:(i + 1) * 128, :], u[:, 384:416], wk=[("o_ckr", si, l, i)], eng=STQ, final=True)
                    for k in range(2):
                        P.tr(ptr[:, k, :], qlat[:, k * 128:(k + 1) * 128], ident[:])
                    P.tr(ptr[:, 2, :], ck[:], ident[:])
                    P.evac(qlT[:], ptr[:, 0:2, :]); P.evac(ckT[:], ptr[:, 2, :])
                    for k in range(2):
                        P.mm(pq1[:], qlT[:, k, :], wuq[:, k, 0:512], start=(k == 0), stop=(k == 1))
                    for k in range(2):
                        P.mm(pq2[:], qlT[:, k, :], wuq[:, k, 512:768], start=(k == 0), stop=(k == 1))
                    qff = qf[:].rearrange("p h d -> p (h d)")
                    P.evac(qff[:, 0:512], pq1[:]); P.evac(qff[:, 512:768], pq2[:])
                    krope = u[:, 384:416]
                else:
                    P.dma(ck[:], I["ckv"][l, i * 128:(i + 1) * 128, :])
                    P.dma(u[:, 384:416], I["ckr"][l, i * 128:(i + 1) * 128, :])
                    P.tr(ptr[:, 2, :], ck[:], ident[:])
                    P.evac(ckT[:], ptr[:, 2, :])
                    krope = u[:, 384:416]
                P.mm(pk1[:], ckT[:], wukv[:, 0:512]); P.mm(pk2[:], ckT[:], wukv[:, 512:1024])
                kvff = kvf[:].rearrange("p h d -> p (h d)")
                P.evac(kvff[:, 0:512], pk1[:]); P.evac(kvff[:, 512:1024], pk2[:])
                j2 = junk2[:, :512].rearrange("p (h d) -> p h d", h=8)
                P.act(j2, kvf[:, :, 0:64], AF.Square)
                P.red("vector", ss16[:, 8:16], j2)
                P.act(junk[:, :32], krope, AF.Square, accum=ssr[:, 0:1])
                P.ts("vector", ss16[:, 8:16], ss16[:, 8:16], ssr[:, 0:1], None, op0=ALU.add)
                if new:
                    P.act(junk[:].rearrange("p (h d) -> p h d", h=8), qf[:], AF.Square)
                    P.red("vector", ss16[:, 0:8], junk[:].rearrange("p (h d) -> p h d", h=8))
                else:
                    P.memset("vector", ss16[:, 0:8], 1.0)
                rstd_of(ss16[:], ss16[:], 96, NORM_EPS)
                P.tt("vector", kn[:, :, 0:64], kvf[:, :, 0:64], ss16[:, 8:16].unsqueeze(2).broadcast_to([128, 8, 64]), ALU.mult)
                P.tt(PENG, kn[:, :, 0:64], kn[:, :, 0:64], gkn[:, 0:64].unsqueeze(1).broadcast_to([128, 8, 64]), ALU.mult)
                P.tt("vector", kr[:], krope, gkn[:, 64:96], ALU.mult)
                if new and jb["ctx"]:
                    rope(kr[:].unsqueeze(1), r_[:, 0, :], r_[:, 1, :], 1, 8, junk, junk2)
                P.tt("vector", kn[:, :, 64:96], kr[:].unsqueeze(1).broadcast_to([128, 8, 32]),
                     ss16[:, 8:16].unsqueeze(2).broadcast_to([128, 8, 32]), ALU.mult)
                P.copy(PENG, v_[:, :, 0:64], kvf[:, :, 64:128])
                P.dma(VPs[krow:krow + 128], v_[:], wk=[(VPs.name, krow)], eng=STQ)
                kt_ = kT[it % 2]
                for hh in range(2):
                    for h4 in range(4):
                        P.tr(pT[hh][:, h4, :], kn[:, hh * 4 + h4, :], ident[:])
                    P.evac(kt_[:, hh * 4:(hh + 1) * 4, :], pT[hh][:])
                P.dma(KTs[:, :, krow:krow + 128].rearrange("h d t -> d h t"), kt_[:], wk=[(KTs.name, krow)], eng=STQ)
                if new:
                    P.tt("vector", qf[:], qf[:], ss16[:, 0:8].unsqueeze(2).broadcast_to([128, 8, 96]), ALU.mult)
                    P.tt(PENG, qf[:], qf[:], gqn[:].unsqueeze(1).broadcast_to([128, 8, 96]), ALU.mult)
                    if jb["ctx"]:
                        rope(qf[:, :, 64:96], r_[:, 0, :], r_[:, 1, :], 8, 8, junk, junk2)
                    qt_ = qT[it % 2]
                    for hh in range(2):
                        for h4 in range(4):
                            P.tr(pT[hh][:, h4, :], qf[:, hh * 4 + h4, :], ident[:])
                        P.evac(qt_[:, hh * 4:(hh + 1) * 4, :], pT[hh][:])
                    P.dma(QTs[:, :, i * 128:(i + 1) * 128].rearrange("h d t -> d h t"), qt_[:], wk=[(QTs.name, i)], eng=STQ)
            P.pop()
            if os.environ.get("KSTOP", "") == "a1":
                continue
            P.push()
            QC = 256
            KT = [P.sb(f"KT{i}", [96, NK], FDT) for i in range(2)]
            VP = [P.sb(f"VP{i}", [128, NKT, 65], FDT) for i in range(2)]
            QT = [P.sb(f"QT{i}", [96, QC], FDT) for i in range(3)]
            pTs = [P.sb(f"pTs{i}", [128, NKT, QC], FDT) for i in range(2)]
            oT = [P.sb(f"oT{i}", [65, QC], FDT) for i in range(2)]; rec = P.sb("rec", [64, QC]); yT = [P.sb(f"yT{i}", [64, QC], FDT) for i in range(2)]
            sel65 = P.sb("sel65", [65, 64], FDT)
            sel65f = P.sb("sel65f", [65, 64])
            P.memset("vector", sel65f[:], 0.0); P.memset("vector", sel65f[64:65, :], 1.0)
            P.copy("vector", sel65[:], sel65f[:])
            pss = [P.ps(f"pss{i}", [128, 512]) for i in range(4)]
            po = [P.ps(f"po{i}", [65, QC]) for i in range(2)]
            pb = P.ps("pb", [64, QC])
            units = [(h, qc) for h in range(8) for qc in range(S // QC)]
            ik = [0]

            def qk(iu):
                h, qc = units[iu]
                K_ = KT[h % 2]; V_ = VP[h % 2]
                if qc == 0:
                    P.dma(K_[:], KTs[h], rk=[KTs.name + "*"])
                    P.dma(V_[:], VPs[:, h, :].rearrange("(kt p) c -> p kt c", p=128), rk=[VPs.name + "*"])
                Q_ = QT[iu % 3]; p_ = pTs[iu % 2]
                P.dma(Q_[:], QTs[h, :, qc * QC:(qc + 1) * QC], rk=[QTs.name + "*"])
                for kt in range(NKT):
                    ps_ = pss[ik[0] % 4]; ik[0] += 1
                    P.mm(ps_[:, :QC], K_[:, kt * 128:(kt + 1) * 128], Q_[:], fast=True)
                    P.act(p_[:, kt, :], ps_[:, :QC], AF.Exp, scale=MLA_SCALE)

            def pv(iu):
                h, qc = units[iu]
                V_ = VP[h % 2]; p_ = pTs[iu % 2]; po_ = po[iu % 2]; o_ = oT[iu % 2]; yT_ = yT[iu % 2]
                for kt in range(NKT):
                    P.mm(po_[:], V_[:, kt, :], p_[:, kt, :], start=(kt == 0), stop=(kt == NKT - 1), fast=True)
                P.copy("scalar", o_[:], po_[:])
                P.mm(pb[:], sel65[:], o_[:], fast=True)
                P.recip(rec[:], pb[:])
                P.tt("vector", yT_[:], o_[0:64, :], rec[:], ALU.mult)
                c0 = t_off + qc * QC
                P.dma(jb["YT"][0, h * 64:(h + 1) * 64, c0:c0 + QC], yT_[:], wk=[(jb["YT"].name, 0, h, c0)], eng=STQ)

            qk(0)
            for iu in range(len(units)):
                if iu + 1 < len(units):
                    qk(iu + 1)
                pv(iu)
            P.pop()


    def phaseC(l, jb):
        for si, (t_off, S) in enumerate(jb["seqs"]):
            NT = S // 128
            NCT = (CTX // 128) if jb["ctx"] else 0
            NKT = NT + NCT
            P.push()
            gq = bcast_row("gq", I["swa_q_norm"][l], 64); gk = bcast_row("gk", I["swa_k_norm"][l], 64)
            esink = bcast_row("esink", I["swa_sink"][l], 8)
            P.act(esink[:], esink[:], AF.Exp)
            tri = P.sb("tri", [128, 2, 128])
            P.dma(tri[:], I["k_tri"].rearrange("a k q -> k a q"))
            KTa = P.sb("KTa", [128, NKT * 128]); VPa = P.sb("VPa", [128, NKT, 2, 65])
            P.memset("vector", VPa[:, :, :, 64:65], 1.0, wk=[VPa])
            uk = [P.sb(f"uk{i}", [128, 256]) for i in range(2)]
            uq = [P.sb(f"uq{i}", [128, 512]) for i in range(2)]
            rt = [P.sb(f"rt{i}", [128, 2, 64]) for i in range(2)]
            junk = P.sb("junk", [128, 512]); junk2 = P.sb("junk2", [128, 512]); ss = P.sb("ss", [128, 8])
            kk = [P.sb(f"kk{i}", [128, 2, 64]) for i in range(2)]
            qp = P.sb("qp", [128, 4, 2, 64]); QT = [P.sb(f"QT{i}", [128, 4, 128]) for i in range(2)]
            pTa = [P.sb(f"pTa{i}", [128, 5, 4, 128]) for i in range(2)]
            yc = P.sb("yc", [128, 8, 64]); den = P.sb("den", [128, 4, 1]); ycT = [P.sb(f"ycT{i}", [128, 4, 128], FDT) for i in range(2)]
            ptk = P.ps("ptk", [128, 128]); ptq = P.ps("ptq", [128, 4, 128])
            pss = [P.ps(f"pss{i}", [128, 512]) for i in range(3)]
            po = [P.ps(f"po{i}", [128, 4, 65]) for i in range(2)]
            pyt = P.ps("pyt", [128, 4, 128])
            for i in range(NKT):
                new = i < NT
                u = uk[i % 2]; k_ = kk[i % 2]; r_ = rt[i % 2]
                if new:
                    rows = slice(t_off + i * 128, t_off + (i + 1) * 128)
                    P.dma(u[:], jb["UT"][rows, C_SK:C_SK + 256], rk=[(jb["UT"].name, (t_off + i * 128) // 512, 2480)])
                    kv = u[:, 0:128].rearrange("p (g d) -> p g d", g=2)
                    P.act(junk[:, :128].rearrange("p (g d) -> p g d", g=2), kv, AF.Square)
                    P.red("vector", ss[:, 0:2], junk[:, :128].rearrange("p (g d) -> p g d", g=2))
                    rstd_of(ss[:, 0:2], ss[:, 0:2], 64, NORM_EPS)
                    P.tt("vector", k_[:], kv, ss[:, 0:2].unsqueeze(2).broadcast_to([128, 2, 64]), ALU.mult)
                    P.tt("vector", k_[:], k_[:], gk[:].unsqueeze(1).broadcast_to([128, 2, 64]), ALU.mult)
                    if not jb["ctx"]:
                        P.dma(O["o_swk"][si, l, i * 128:(i + 1) * 128, :], k_[:].rearrange("p g d -> p (g d)"), wk=[("o_swk", si, l, i)], eng=STQ, final=True)
                        P.dma(O["o_swv"][si, l, i * 128:(i + 1) * 128, :], u[:, 128:256], wk=[("o_swv", si, l, i)], eng=STQ, final=True)
                    else:
                        P.dma(r_[:], I["k_rope64"][i * 128:(i + 1) * 128])
                        rope(k_[:], r_[:, 0, :], r_[:, 1, :], 2, 16, junk, junk2)
                    ksrc = k_[:].rearrange("p g d -> p (g d)")
                    vsrc = u[:, 128:256]
                else:
                    c = i - NT
                    P.dma(u[:, 0:128], I["cswk"][l, c * 128:(c + 1) * 128, :]); P.dma(u[:, 128:256], I["cswv"][l, c * 128:(c + 1) * 128, :])
                    ksrc = u[:, 0:128]; vsrc = u[:, 128:256]
                P.tr(ptk[:], ksrc, ident[:])
                P.copy("vector", KTa[:, i * 128:(i + 1) * 128], ptk[:])
                P.copy("vector", VPa[:, i, :, 0:64], vsrc.rearrange("p (g d) -> p g d", g=2))
            units = [(b, g) for b in range(NT) for g in range(2)]
            ik = [0]

            def ktiles(b):
                if not jb["ctx"]:
                    return [(kt, None) for kt in range(NT)]
                lst = []
                if b > 0:
                    lst.append((b - 1, 0))
                lst.append((b, None))
                if b < NT - 1:
                    lst.append((b + 1, 1))
                return lst + [(NT + c, None) for c in range(NCT)]

            def qprep(b):
                u = uq[b % 2]; r_ = rt[b % 2]; Q_ = QT[b % 2]
                rows = slice(t_off + b * 128, t_off + (b + 1) * 128)
                P.dma(u[:], jb["UT"][rows, C_SQ:C_SQ + 512], rk=[(jb["UT"].name, (t_off + b * 128) // 512, 1968)])
                qv = u[:].rearrange("p (h d) -> p h d", h=8)
                P.act(junk[:].rearrange("p (h d) -> p h d", h=8), qv, AF.Square)
                P.red("vector", ss[:], junk[:].rearrange("p (h d) -> p h d", h=8))
                rstd_of(ss[:], ss[:], 64, NORM_EPS)
                P.tt("vector", qv, qv, ss[:].unsqueeze(2).broadcast_to([128, 8, 64]), ALU.mult)
                P.tt("vector", qv, qv, gq[:].unsqueeze(1).broadcast_to([128, 8, 64]), ALU.mult)
                if jb["ctx"]:
                    P.dma(r_[:], I["k_rope64"][b * 128:(b + 1) * 128])
                    rope(qv, r_[:, 0, :], r_[:, 1, :], 8, 16, junk, junk2)
                P.copy("vector", qp[:].rearrange("p r g d -> p g r d"), u[:].rearrange("p (g r d) -> p g r d", g=2, r=4))
                for r in range(4):
                    P.tr(ptq[:, r, :], qp[:, r, :, :].rearrange("p g d -> p (g d)"), ident[:])
                P.copy("scalar", Q_[:], ptq[:])

            def qk(iu):
                b, g = units[iu]
                if g == 0:
                    qprep(b)
                Q_ = QT[b % 2]; p_ = pTa[iu % 2]
                pr = slice(g * 64, (g + 1) * 64)
                for j, (kt, mk) in enumerate(ktiles(b)):
                    ps_ = pss[ik[0] % 3]; ik[0] += 1
                    P.mm(ps_[:], KTa[pr, kt * 128:(kt + 1) * 128], Q_[pr, :, :].rearrange("p r q -> p (r q)"))
                    P.act(p_[:, j, :, :].rearrange("p r q -> p (r q)"), ps_[:], AF.Exp, scale=SW_SCALE)
                    if mk is not None:
                        P.tt("vector", p_[:, j, :, :], p_[:, j, :, :], tri[:, mk, :].unsqueeze(1).broadcast_to([128, 4, 128]), ALU.mult)

            def pv(iu):
                b, g = units[iu]
                p_ = pTa[iu % 2]; po_ = po[iu % 2]
                kts = ktiles(b)
                for r in range(4):
                    for j, (kt, mk) in enumerate(kts):
                        P.mm(po_[:, r, :], p_[:, j, r, :], VPa[:, kt, g, :], start=(j == 0), stop=(j == len(kts) - 1))
                P.tt("vector", den[:], po_[:, :, 64:65], esink[:, g * 4:(g + 1) * 4].unsqueeze(2), ALU.add)
                P.recip(den[:], den[:])
                P.tt("vector", yc[:, g * 4:(g + 1) * 4, :], po_[:, :, 0:64], den[:].broadcast_to([128, 4, 64]), ALU.mult)
                if g == 1:
                    yT_ = ycT[b % 2]
                    for c in range(4):
                        P.tr(pyt[:, c, :], yc[:, 2 * c:2 * c + 2, :].rearrange("p h d -> p (h d)"), ident[:])
                    P.copy("scalar", yT_[:], pyt[:])
                    c0 = t_off + b * 128
                    P.dma(jb["YT"][2, :, c0:c0 + 128].rearrange("(c p) t -> p c t", p=128), yT_[:], wk=[(jb["YT"].name, 2, c0)], eng=STQ)

            qk(0)
            for iu in range(len(units)):
                if iu + 1 < len(units):
                    qk(iu + 1)
                pv(iu)
            P.pop()


    def phaseB(l, jb):
        n = jb["name"]
        for si, (t_off, S) in enumerate(jb["seqs"]):
            NC = S // 128
            HS = scr(f"B_HS_{n}{l}_{si}", [2, S, 512])
            P.push()
            tri = P.sb("tri", [128, 2, 128]); neg = P.sb("neg", [128, 2, 128]); sel = P.sb("sel", [128, 2, 128])
            P.dma(tri[:], I["k_tri"].rearrange("a k q -> k a q")); P.dma(sel[:], I["k_sel"].rearrange("a k q -> k a q"))
            P.ts("vector", neg[:], tri[:], -1.0, 1e30, op0=ALU.add, op1=ALU.mult)
            TRI = [tri[:, 1, :], tri[:, 0, :]]
            NEG = [neg[:, 1, :], neg[:, 0, :]]
            NEGts = [neg[:, 0, :], neg[:, 1, :]]
            bias16 = P.sb("bias16", [128, 2, 8])
            P.dma(bias16[:, 0, :], I["mlstm_i_bias"][l].partition_broadcast(128), wk=[bias16])
            P.dma(bias16[:, 1, :], I["mlstm_f_bias"][l].partition_broadcast(128), wk=[bias16])
            Cst = P.sb("Cst", [64, 8, 129]); mprev = P.sb("mprev", [128, 8])
            if jb["ctx"]:
                P.dma(Cst[:, :, 0:128], I["mC"][l].rearrange("d h k v -> k (d h) v"), wk=[Cst])
                P.dma(Cst[:, :, 128:129], I["mn"][l].rearrange("d h (k o) -> k (d h) o", o=1), wk=[Cst], allow_slow_non_contiguous=True)
                P.dma(mprev[:], I["mm"][l].rearrange("d h -> (d h)").partition_broadcast(128))
            else:
                P.memset("vector", Cst[:], 0.0); P.memset("vector", mprev[:], 0.0)
            G = [P.sb(f"G{i}", [128, 2, 8]) for i in range(2)]
            QTd = [P.sb(f"QTd{i}", [64, 2, 4, 128]) for i in range(2)]; KTd = [P.sb(f"KTd{i}", [64, 2, 4, 128]) for i in range(2)]
            Kt = [P.sb(f"Kt{i}", [128, 2, 4, 64]) for i in range(2)]; VPd = [P.sb(f"VPd{i}", [128, 2, 4, 129]) for i in range(2)]
            for v in VPd:
                P.memset("vector", v[:, :, :, 128:129], 1.0, wk=[v])
            sp = P.sb("sp", [128, 8]); b = P.sb("b", [128, 8]); li = P.sb("li", [128, 8]); c = P.sb("c", [128, 8])
            MB = P.sb("MB", [128, 2, 2, 4]); cmax = P.sb("cmax", [128, 8]); bm = P.sb("bm", [128, 8]); ain = P.sb("ain", [128, 8]); en = P.sb("en", [128, 8])
            DG = P.sb("DG", [128, 8, 128]); DG2 = P.sb("DG2", [128, 8, 128]); Rm = P.sb("Rm", [128, 8, 128])
            ET = P.sb("ET", [128, 8, 128]); WT = P.sb("WT", [128, 8, 128])
            tI = P.sb("tI", [128, 8, 129]); numS = P.sb("numS", [128, 8, 129]); dab = P.sb("dab", [128, 8]); hh = P.sb("hh", [128, 8, 128])
            mbl = P.sb("mbl", [128, 2, 2, 4]); wk_ = P.sb("wk", [128, 8]); dec = P.sb("dec", [128, 8]); KW = P.sb("KW", [128, 8, 64])
            pA = [P.ps(f"pA{i}", [128, 4, 128]) for i in range(2)]
            pB = [P.ps(f"pB{i}", [128, 4, 128]) for i in range(2)]
            pC = [P.ps(f"pC{i}", [128, 3, 129]) for i in range(3)]
            pD = P.ps("pD", [128, 2, 8])
            grp3 = [(0, 0, 3), (1, 3, 6), (2, 6, 8)]

            def pc_slot(dh):
                return pC[dh // 3], dh % 3

            for j in range(NC):
                g_ = G[j % 2]; qt = QTd[j % 2]; kt = KTd[j % 2]; ktok = Kt[j % 2]; vp = VPd[j % 2]
                cd = [j, NC - 1 - j]
                for d in range(2):
                    r0 = t_off + cd[d] * 128
                    rows = slice(r0, r0 + 128)
                    sk = r0 // 512
                    P.dma(g_[:, :, d * 4:(d + 1) * 4], jb["UT"][rows, C_MI:C_MI + 16].rearrange("p (a e) -> p a e", a=2)[:, :, d * 4:(d + 1) * 4],
                          rk=[(jb["UT"].name, sk, 1440)], wk=[g_])
                    P.dma(qt[:, d, :, :], jb["UF"][C_MQ:C_MQ + 256, r0:r0 + 128].rearrange("(h p) t -> p h t", p=64), rk=[(jb["UF"].name, sk, 416)], wk=[qt])
                    P.dma(kt[:, d, :, :], jb["UF"][C_MK:C_MK + 256, r0:r0 + 128].rearrange("(h p) t -> p h t", p=64), rk=[(jb["UF"].name, sk, 672)], wk=[kt])
                    P.dma(ktok[:, d, :, :], jb["UT"][rows, C_MK:C_MK + 256].rearrange("p (h e) -> p h e", h=4), rk=[(jb["UT"].name, sk, 672)], wk=[ktok])
                    P.dma(vp[:, d, :, 0:128], jb["UT"][rows, C_MV:C_MV + 512].rearrange("p (h e) -> p h e", h=4), rk=[(jb["UT"].name, sk, 928)], wk=[vp])
                P.act(qt[:], qt[:], AF.Copy, scale=0.125)
                P.tt("vector", g_[:], g_[:], bias16[:], ALU.add)
                P.copy("vector", li[:], g_[:, 0, :])
                P.act(sp[:], g_[:, 1, :], AF.Exp, scale=-1.0)
                P.act(sp[:], sp[:], AF.Ln, bias=1.0)
                for d in range(2):
                    P.mm(pD[:, 0, d * 4:(d + 1) * 4], TRI[d], sp[:, d * 4:(d + 1) * 4])
                P.act(b[:], pD[:, 0, :], AF.Copy, scale=-1.0)
                P.tt("vector", c[:], li[:], b[:], ALU.subtract)
                P.tt("vector", DG[:], ident[:].unsqueeze(1).broadcast_to([128, 8, 128]), c[:].unsqueeze(2).broadcast_to([128, 8, 128]), ALU.mult)
                for dh in range(8):
                    P.mm(pA[dh // 4][:, dh % 4, :], ones[:], DG[:, dh, :])
                for d in range(2):
                    P.tt("vector", Rm[:, d * 4:(d + 1) * 4, :], pA[d][:], NEGts[d].unsqueeze(1).broadcast_to([128, 4, 128]), ALU.add)
                P.red("vector", cmax[:], Rm[:], op=ALU.max)
                v24 = lambda t: t.rearrange("p (d h) -> p d h", d=2)
                mt = MB[:, :, 0, :]
                P.tt("vector", mt, v24(mprev[:]), v24(cmax[:]), ALU.max)
                P.tt("vector", mt, mt, v24(b[:]), ALU.add)
                P.copy("vector", MB[:, :, 1, :], v24(b[:]))
                P.tt("vector", v24(bm[:]), v24(b[:]), mt, ALU.subtract)
                P.tt("vector", ain[:], bm[:], mprev[:], ALU.add)
                P.act(ain[:], ain[:], AF.Exp)
                P.act(v24(en[:]), mt, AF.Exp, scale=-1.0)
                P.tt("vector", DG2[:], ident[:].unsqueeze(1).broadcast_to([128, 8, 128]), bm[:].unsqueeze(2).broadcast_to([128, 8, 128]), ALU.mult)
                for dh in range(8):
                    o_ = pA[dh // 4][:, dh % 4, :]
                    P.mm(o_, ones[:], DG2[:, dh, :], start=True, stop=False)
                    P.mm(o_, DG[:, dh, :], ones[:], start=False, stop=False)
                    P.mm(o_, ident[:], NEG[dh // 4], start=False, stop=True)
                for d in range(2):
                    P.act(ET[:, d * 4:(d + 1) * 4, :], pA[d][:], AF.Exp)
                for dh in range(8):
                    d, h = dh // 4, dh % 4
                    P.mm(pB[d][:, h, :], kt[:, d, h, :], qt[:, d, h, :])
                for d in range(2):
                    P.tt("vector", WT[:, d * 4:(d + 1) * 4, :], ET[:, d * 4:(d + 1) * 4, :], pB[d][:], ALU.mult)
                for dh in range(8):
                    d, h = dh // 4, dh % 4
                    pc, sl = pc_slot(dh)
                    P.mm(pc[:, sl, :], qt[:, d, h, :], Cst[:, dh, :])
                for (bk, lo, hi) in grp3:
                    P.tt("vector", tI[:, lo:hi, :], pC[bk][:, 0:hi - lo, :], ain[:, lo:hi].unsqueeze(2).broadcast_to([128, hi - lo, 129]), ALU.mult)
                for dh in range(8):
                    d, h = dh // 4, dh % 4
                    pc, sl = pc_slot(dh)
                    P.mm(pc[:, sl, :], WT[:, dh, :], vp[:, d, h, :])
                for (bk, lo, hi) in grp3:
                    P.tt("vector", numS[:, lo:hi, :], pC[bk][:, 0:hi - lo, :], tI[:, lo:hi, :], ALU.add)
                P.act(dab[:].unsqueeze(2), numS[:, :, 128:129], AF.Abs)
                P.tt("vector", dab[:], dab[:], en[:], ALU.max)
                P.recip(dab[:], dab[:])
                P.tt("vector", hh[:], numS[:, :, 0:128], dab[:].unsqueeze(2).broadcast_to([128, 8, 128]), ALU.mult)
                for d in range(2):
                    r0 = cd[d] * 128
                    P.dma(HS[d, r0:r0 + 128, :].rearrange("p (h e) -> p h e", h=4), hh[:, d * 4:(d + 1) * 4, :], wk=[(HS.name, d, cd[d])], eng=STQ)
                for d in range(2):
                    P.mm(pD[:, d, :].rearrange("p (a h) -> p a h", a=2).rearrange("p a h -> p (a h)"), sel[:, d, :], MB[:, d, :, :].rearrange("p a h -> p (a h)"))
                P.copy("vector", mbl[:].rearrange("p d a h -> p (d a h)"), pD[:].rearrange("p a e -> p (a e)"))
                P.tt("vector", v24(wk_[:]), mbl[:, :, 1, :], mbl[:, :, 0, :], ALU.subtract)
                P.tt("vector", dec[:], wk_[:], mprev[:], ALU.add)
                P.act(dec[:], dec[:], AF.Exp)
                P.tt("vector", wk_[:], wk_[:], c[:], ALU.add)
                P.act(wk_[:], wk_[:], AF.Exp)
                for d in range(2):
                    P.tt("vector", KW[:, d * 4:(d + 1) * 4, :], ktok[:, d, :, :], wk_[:, d * 4:(d + 1) * 4].unsqueeze(2).broadcast_to([128, 4, 64]), ALU.mult)
                for dh in range(8):
                    d, h = dh // 4, dh % 4
                    pc, sl = pc_slot(dh)
                    P.mm(pc[0:64, sl, :], KW[:, dh, :], vp[:, d, h, :])
                P.tt("vector", Cst[:], Cst[:], dec[0:64, :].unsqueeze(2).broadcast_to([64, 8, 129]), ALU.mult)
                for (bk, lo, hi) in grp3:
                    P.tt("vector", Cst[:, lo:hi, :], Cst[:, lo:hi, :], pC[bk][0:64, 0:hi - lo, :], ALU.add)
                P.copy("vector", v24(mprev[:]), mbl[:, :, 0, :])
            if not jb["ctx"]:
                P.dma(O["o_mC"][si, l].rearrange("d h k v -> k (d h) v"), Cst[:, :, 0:128], wk=[("o_mC", si, l)], eng=STQ, final=True)
                P.dma(O["o_mn"][si, l].rearrange("d h (k o) -> k (d h) o", o=1), Cst[:, :, 128:129], wk=[("o_mn", si, l)], eng=STQ, final=True, allow_slow_non_contiguous=True)
                P.dma(O["o_mm"][si, l].rearrange("d (h o) -> o (d h)", o=1), mprev[0:1, :], wk=[("o_mm", si, l)], eng=STQ, final=True, allow_slow_non_contiguous=True)
            P.pop()
            P.push()
            gm = bcast_row("gm", I["mlstm_norm"][l], 128)
            h0 = [P.sb(f"h0{i}", [128, 4, 128]) for i in range(2)]; h1 = [P.sb(f"h1{i}", [128, 4, 128]) for i in range(2)]
            og = [P.sb(f"og{i}", [128, 512]) for i in range(2)]
            junk = P.sb("junk", [128, 4, 128]); ss = P.sb("ss", [128, 4]); yT = [P.sb(f"yT{i}", [128, 4, 128], FDT) for i in range(2)]
            pyt = P.ps("pyt", [128, 4, 128])
            for i in range(NC):
                a_ = h0[i % 2]; b_ = h1[i % 2]; o_ = og[i % 2]; y_ = yT[i % 2]
                rows = slice(t_off + i * 128, t_off + (i + 1) * 128)
                P.dma(a_[:], HS[0, i * 128:(i + 1) * 128, :].rearrange("p (h e) -> p h e", h=4), rk=[HS.name + "*"])
                P.dma(b_[:], HS[1, i * 128:(i + 1) * 128, :].rearrange("p (h e) -> p h e", h=4), rk=[HS.name + "*"])
                P.dma(o_[:], jb["UT"][rows, C_MO:C_MO + 512], rk=[(jb["UT"].name, (t_off + i * 128) // 512, 1456)])
                P.tt("vector", a_[:], a_[:], b_[:], ALU.add)
                P.act(junk[:], a_[:], AF.Square)
                P.red("vector", ss[:], junk[:])
                rstd_of(ss[:], ss[:], 128, NORM_EPS)
                P.act(o_[:], o_[:], AF.Sigmoid)
                P.tt("vector", a_[:], a_[:], ss[:].unsqueeze(2).broadcast_to([128, 4, 128]), ALU.mult)
                P.tt("vector", a_[:], a_[:], gm[:].unsqueeze(1).broadcast_to([128, 4, 128]), ALU.mult)
                P.tt("vector", a_[:], a_[:], o_[:].rearrange("p (h e) -> p h e", h=4), ALU.mult)
                for h in range(4):
                    P.tr(pyt[:, h, :], a_[:, h, :], ident[:])
                P.copy("scalar", y_[:], pyt[:])
                c0 = t_off + i * 128
                P.dma(jb["YT"][1, :, c0:c0 + 128].rearrange("(c p) t -> p c t", p=128), y_[:], wk=[(jb["YT"].name, 1, c0)], eng=STQ)
            P.pop()


    def phaseD(l, jb):
        n = jb["name"]
        CW = RW_DECAY
        for si, (t_off, S) in enumerate(jb["seqs"]):
            NCH = S // 64
            DF = scr(f"D_F_{n}{l}_{si}", [2, NCH, 2, 64, 4, 4, 64])
            LW = scr(f"D_LW_{n}{l}_{si}", [S, 2, 512])
            BG = scr(f"D_BG_{n}{l}_{si}", [2, 512, S])
            YS = scr(f"D_YS_{n}{l}_{si}", [2, S, 512])
            P.push()
            ST = min(512, S)
            kk = P.sb("kk", [64, 8]); ka = P.sb("ka", [64, 8]); omka = P.sb("omka", [64, 8]); uu = P.sb("uu", [64, 2, 8]); a0 = P.sb("a0", [64, 2, 8])
            P.dma(kk[:], I["rwkv_kk64"][l]); P.dma(ka[:], I["rwkv_ka64"][l]); P.dma(uu[:], I["rwkv_u64"][l]); P.dma(a0[:], I["rwkv_a064"][l])
            P.ts("vector", omka[:], ka[:], -1.0, 1.0, op0=ALU.mult, op1=ALU.add)
            w2 = P.sb("w2", [64, 2, 512]); a2 = P.sb("a2", [64, 2, 512]); g2 = P.sb("g2", [128, 512])
            P.dma(w2[:], I["rwkv_w2"][l].rearrange("d r c -> r d c")); P.dma(a2[:], I["rwkv_a2"][l].rearrange("d r c -> r d c")); P.dma(g2[:], I["rwkv_g2"][l])
            w0row = bcast_row("w0row", I["rwkv_w0"][l].rearrange("d c -> (d c)"), 1024)
            rT = P.sb("rT", [64, 8, ST]); kT = P.sb("kT", [64, 8, ST]); vT = P.sb("vT", [64, 8, ST])
            w1T = P.sb("w1T", [64, 2, ST]); a1T = P.sb("a1T", [64, 2, ST]); g1T = P.sb("g1T", [128, ST])
            kap = P.sb("kap", [64, 8, ST]); kh = P.sb("kh", [64, 8, ST]); tA = P.sb("tA", [64, 8, ST]); tB = P.sb("tB", [64, 8, ST])
            ktt = P.sb("ktt", [64, 8, ST]); rku = P.sb("rku", [64, 8, ST]); lw = [P.sb(f"lw{i}", [128, 2, 512]) for i in range(2)]
            pp = [P.ps(f"pp{i}", [128, 512]) for i in range(4)]
            ip = [0]

            def nps():
                ip[0] += 1
                return pp[ip[0] % 4]

            def store_df(d, arr, tile, s0):
                for cc in range(ST // 64):
                    for hh in range(2):
                        P.dma(DF[d, s0 // 64 + cc, hh, :, arr, :, :], tile[:, hh * 4:(hh + 1) * 4, cc * 64:(cc + 1) * 64], wk=[(DF.name, d, arr, s0, cc, hh)], eng=STQ)

            for s_ in range(S // ST):
                s0 = s_ * ST
                c0 = t_off + s0
                sk = c0 // 512
                UF = jb["UF"]
                P.dma(rT[:], UF[C_RR:C_RR + 512, c0:c0 + ST].rearrange("(h p) t -> p h t", p=64), rk=[(UF.name, sk, 2736)])
                P.dma(kT[:], UF[C_RK:C_RK + 512, c0:c0 + ST].rearrange("(h p) t -> p h t", p=64), rk=[(UF.name, sk, 3248)])
                P.dma(vT[:], UF[C_RV:C_RV + 512, c0:c0 + ST].rearrange("(h p) t -> p h t", p=64), rk=[(UF.name, sk, 3760)])
                P.dma(w1T[:], UF[C_RW:C_RW + 128, c0:c0 + ST].rearrange("(d p) t -> p d t", p=64), rk=[(UF.name, sk, 4272)])
                P.dma(a1T[:], UF[C_RA:C_RA + 128, c0:c0 + ST].rearrange("(d p) t -> p d t", p=64), rk=[(UF.name, sk, 4272)])
                P.dma(g1T[:], UF[C_RG:C_RG + 128, c0:c0 + ST], rk=[(UF.name, sk, 4272)])
                P.act(w1T[:], w1T[:], AF.Tanh)
                P.act(g1T[:], g1T[:], AF.Sigmoid)
                P.tt("vector", kap[:], kT[:], kk[:].unsqueeze(2).broadcast_to([64, 8, ST]), ALU.mult)
                P.act(tA[:], kap[:], AF.Square)
                for h in range(8):
                    ps_ = nps()
                    P.mm(ps_[0:64, :ST], ones[0:64, 0:64], tA[:, h, :])
                    P.act(kh[:, h, :], ps_[0:64, :ST], AF.Sqrt, bias=1e-12)
                P.recip(kh[:], kh[:])
                P.tt("vector", kh[:], kh[:], kap[:], ALU.mult)
                for d in range(2):
                    store_df(d, 0, rT, s0); store_df(d, 1, kh, s0)
                for d in range(2):
                    for h in range(8):
                        ps_ = nps()
                        P.mm(ps_[0:64, :ST], a2[:, d, h * 64:(h + 1) * 64], a1T[:, d, :])
                        P.act(tA[:, h, :], ps_[0:64, :ST], AF.Sigmoid, bias=a0[:, d, h:h + 1])
                    P.tt("vector", tB[:], tA[:], kh[:], ALU.mult)
                    store_df(d, 3, tB, s0)
                    P.tt("vector", tA[:], tA[:], ka[:].unsqueeze(2).broadcast_to([64, 8, ST]), ALU.mult)
                    P.tt("vector", tA[:], tA[:], omka[:].unsqueeze(2).broadcast_to([64, 8, ST]), ALU.add)
                    P.tt("vector", ktt[:], kT[:], tA[:], ALU.mult)
                    store_df(d, 2, ktt, s0)
                    P.tt("vector", tA[:], ktt[:], rT[:], ALU.mult)
                    if d == 0:
                        P.tt("vector", rku[:], tA[:], uu[:, d, :].unsqueeze(2).broadcast_to([64, 8, ST]), ALU.mult)
                    else:
                        P.tt("vector", tA[:], tA[:], uu[:, d, :].unsqueeze(2).broadcast_to([64, 8, ST]), ALU.mult)
                        P.tt("vector", rku[:], rku[:], tA[:], ALU.add)
                for h in range(8):
                    ps_ = nps()
                    P.mm(ps_[0:64, :ST], ones[0:64, 0:64], rku[:, h, :])
                    P.tt("vector", tB[:, h, :], ps_[0:64, :ST], vT[:, h, :], ALU.mult)
                P.dma(BG[0, :, s0:s0 + ST].rearrange("(h p) t -> p h t", p=64), tB[:], wk=[(BG.name, 0, s0)], eng=STQ)
                for h in range(8):
                    ps_ = nps()
                    P.mm(ps_[0:64, :ST], g2[:, h * 64:(h + 1) * 64], g1T[:])
                    P.copy("scalar", kap[:, h, :], ps_[0:64, :ST])
                P.dma(BG[1, :, s0:s0 + ST].rearrange("(h p) t -> p h t", p=64), kap[:], wk=[(BG.name, 1, s0)], eng=STQ)
                for tt in range(ST // 128):
                    lw_ = lw[tt % 2]
                    for d in range(2):
                        ps_ = nps()
                        P.mm(ps_[:], w1T[:, d, tt * 128:(tt + 1) * 128], w2[:, d, :])
                        P.tt("vector", lw_[:, d, :], ps_[:], w0row[:, d * 512:(d + 1) * 512], ALU.add)
                    P.act(lw_[:], lw_[:], AF.Sigmoid)
                    P.dma(LW[s0 + tt * 128:s0 + (tt + 1) * 128], lw_[:], wk=[(LW.name, s0, tt)], eng=STQ)
            P.pop()
            P.push()
            tri = P.sb("tri", [128, 2, 128]); P.dma(tri[:], I["k_tri"].rearrange("a k q -> k a q"))
            trs = P.sb("trs", [128, 2, 128]); P.dma(trs[:], I["k_tris"].rearrange("a k q -> k a q"))
            HP = [slice(0, 64), slice(64, 128)]
            cum = P.sb("cum", [128, 2, 2, 64]); mask4 = P.sb("mask4", [128, 2, 4, 64]); maskT = P.sb("maskT", [128, 2, 64])
            for hh in range(2):
                pr = HP[hh]
                INC = [tri[pr, 1, pr], tri[pr, 0, pr]]; STR = [trs[pr, 1, pr], trs[pr, 0, pr]]
                for d in range(2):
                    P.copy("vector", cum[pr, d, 0, :], INC[d], wk=[cum]); P.copy("vector", cum[pr, d, 1, :], STR[d], wk=[cum])
                    for a_, m_ in enumerate([STR[d], INC[d], STR[d], INC[d]]):
                        P.copy("vector", mask4[pr, d, a_, :], m_, wk=[mask4])
                    P.copy("vector", maskT[pr, d, :], STR[1 - d], wk=[maskT])
            idh = [ident[HP[0], HP[0]], ident[HP[1], HP[1]]]
            id4 = P.sb("id4", [128, 4, 64])
            for hh in range(2):
                for h4 in range(4):
                    P.copy("vector", id4[HP[hh], h4, :], idh[hh], wk=[id4])
            TS = P.sb("TS", [128, 2, 4, 64])
            bG = P.ps("bG", [128, 512]); bX = [P.ps(f"bX{i}", [128, 512]) for i in range(2)]; bY = P.ps("bY", [128, 256])
            bA = P.ps("bA", [128, 512]); bP = P.ps("bP", [128, 256]); bZ = [P.ps(f"bZ{i}", [128, 256]) for i in range(2)]
            s0t = P.sb("s0t", [128, 2, 4, 64])

            def trm(out, in_, hh):
                P.mm(out, in_, idh[hh])

            if jb["ctx"]:
                for hh in range(2):
                    for d in range(2):
                        P.dma(s0t[HP[hh], d], I["rw"][l][d, hh * 4:(hh + 1) * 4].rearrange("h v k -> v h k"), wk=[s0t])
                for d in range(2):
                    for h in range(8):
                        hh, h4 = h // 4, h % 4
                        trm(bX[0][HP[hh], (d * 4 + h4) * 64:(d * 4 + h4 + 1) * 64], s0t[HP[hh], d, h4, :], hh)
                P.copy("vector", TS[:].rearrange("p d h v -> p (d h v)"), bX[0][:])
            else:
                P.memset("vector", TS[:], 0.0)
            Xd = [[P.sb(f"Xd{d}{i}", [128, 4, 4, 64]) for i in range(2)] for d in range(2)]
            Vd = [[P.sb(f"Vd{d}{i}", [128, 4, 64]) for i in range(2)] for d in range(2)]
            LWd = [[P.sb(f"LWd{d}{i}", [128, 256]) for i in range(2)] for d in range(2)]
            EI = P.sb("EI", [128, 4, 64]); EX = P.sb("EX", [128, 4, 64]); EN = P.sb("EN", [128, 4, 64]); gl = P.sb("gl", [128, 4])
            KR = P.sb("KR", [128, 4, 2, 64]); KtM = P.sb("KtM", [128, 4, 64]); BM = P.sb("BM", [128, 4, 64]); KBe = P.sb("KBe", [128, 4, 2, 64])
            AM = P.sb("AM", [128, 4, 4, 64]); N0 = P.sb("N0", [128, 4, 64])
            AB = [P.sb(f"AB{i}", [128, 4, 2, 64]) for i in range(2)]; PI = [P.sb(f"PI{i}", [128, 4, 64]) for i in range(2)]
            BI = [P.sb(f"BI{i}", [128, 4, 64]) for i in range(2)]
            RH = P.sb("RH", [128, 4, 64]); Un = P.sb("Un", [128, 4, 64]); Yo = [P.sb(f"Yo{i}", [128, 4, 64]) for i in range(2)]
            KBt = P.sb("KBt", [128, 4, 2, 64])
            HH = [(h // 4, h % 4) for h in range(8)]

            def dpass(j, d):
                c = j if d == 0 else NCH - 1 - j
                X = Xd[d][j % 2]; V = Vd[d][j % 2]; LWc = LWd[d][j % 2]
                r0 = t_off + c * 64
                for hh in range(2):
                    pr = HP[hh]
                    P.dma(X[pr], DF[d, c, hh], rk=[DF.name + "*"], wk=[X])
                    P.dma(V[pr], jb["UT"][r0:r0 + 64, C_RV + hh * 256:C_RV + (hh + 1) * 256].rearrange("p (h e) -> p h e", h=4),
                          rk=[(jb["UT"].name, r0 // 512, 3760)], wk=[V])
                    P.dma(LWc[pr], LW[c * 64:(c + 1) * 64, d, hh * 256:(hh + 1) * 256], rk=[LW.name + "*"], wk=[LWc])
                Rr = X[:, 0, :, :]; Kh = X[:, 1, :, :]; Kt = X[:, 2, :, :]; Bb = X[:, 3, :, :]
                for (hh, h4) in HH:
                    pr = HP[hh]
                    P.mm(bG[pr, h4 * 128:(h4 + 1) * 128], LWc[pr, h4 * 64:(h4 + 1) * 64], cum[pr, d, :, :].rearrange("p a t -> p (a t)"))
                gv = bG[:].rearrange("p (h a t) -> p h a t", h=4, a=2)
                P.act(EI[:], gv[:, :, 0, :], AF.Exp, scale=-CW)
                P.act(EN[:], gv[:, :, 0, :], AF.Exp, scale=CW)
                P.act(EX[:], gv[:, :, 1, :], AF.Exp, scale=-CW)
                last = 63 if d == 0 else 0
                P.copy("vector", gl[:].unsqueeze(2), EI[:, :, last:last + 1])
                P.tt("vector", KR[:, :, 1, :], Rr, EI[:], ALU.mult)
                P.tt("vector", KR[:, :, 0, :], Kh, EX[:], ALU.mult)
                P.tt("vector", KtM[:], Kt, EN[:], ALU.mult)
                P.tt("vector", BM[:], Bb, EN[:], ALU.mult)
                P.tt("vector", KBe[:, :, 0, :], KtM[:], gl[:].unsqueeze(2).broadcast_to([128, 4, 64]), ALU.mult)
                P.tt("vector", KBe[:, :, 1, :], BM[:], gl[:].unsqueeze(2).broadcast_to([128, 4, 64]), ALU.mult)
                for (hh, h4) in HH:
                    pr = HP[hh]
                    o_ = bX[h4 // 2][pr, (h4 % 2) * 256:(h4 % 2 + 1) * 256]
                    rhs = KR[pr, h4, :, :].rearrange("p a t -> p (a t)")
                    P.mm(o_[:, 0:128], KtM[pr, h4, :], rhs)
                    P.mm(o_[:, 128:256], BM[pr, h4, :], rhs)
                for q in range(2):
                    P.tt("vector", AM[:, q * 2:(q + 1) * 2, :, :], bX[q][:].rearrange("p (h a t) -> p h a t", h=2, a=4),
                         mask4[:, d, :, :].unsqueeze(1).broadcast_to([128, 2, 4, 64]), ALU.mult)
                for (hh, h4) in HH:
                    pr = HP[hh]
                    P.mm(bY[pr, h4 * 64:(h4 + 1) * 64], KR[pr, h4, 0, :], BM[pr, h4, :])
                P.tt("vector", N0[:], bY[:].rearrange("p (h s) -> p h s", h=4), maskT[:, d, :].unsqueeze(1).broadcast_to([128, 4, 64]), ALU.mult)
                A_ = lambda pr, h4: AM[pr, h4, 2, :]
                B_ = lambda pr, h4: N0[pr, h4, :]
                Pc = PI[0]
                P.tt("vector", Pc[:], id4[:], AM[:, :, 2, :], ALU.subtract)
                for lv in range(1, 6):
                    ab = AB[lv % 2]
                    for (hh, h4) in HH:
                        pr = HP[hh]
                        o_ = bA[pr, h4 * 128:(h4 + 1) * 128]
                        if lv < 5:
                            P.mm(o_[:, 0:64], B_(pr, h4), A_(pr, h4))
                        P.mm(o_[:, 64:128], A_(pr, h4), B_(pr, h4))
                    src = bA[:].rearrange("p (h a t) -> p h a t", h=4, a=2)
                    bi = BI[lv % 2]
                    P.tt("vector", bi[:], src[:, :, 1, :], id4[:], ALU.add)
                    if lv < 5:
                        P.copy("scalar", ab[:], src)
                    A_ = (lambda ab: (lambda pr, h4: ab[pr, h4, 0, :]))(ab)
                    B_ = (lambda ab: (lambda pr, h4: ab[pr, h4, 1, :]))(ab)
                    for (hh, h4) in HH:
                        pr = HP[hh]
                        P.mm(bP[pr, h4 * 64:(h4 + 1) * 64], bi[pr, h4, :], Pc[pr, h4, :])
                    Pn = PI[lv % 2]
                    P.copy("vector", Pn[:].rearrange("p h t -> p (h t)"), bP[:])
                    Pc = Pn
                for (hh, h4) in HH:
                    pr = HP[hh]
                    o_ = bZ[0][pr, h4 * 64:(h4 + 1) * 64]
                    P.mm(o_, KR[pr, h4, 0, :], TS[pr, d, h4, :], start=True, stop=False)
                    P.mm(o_, AM[pr, h4, 0, :], V[pr, h4, :], start=False, stop=True)
                P.copy("scalar", RH[:].rearrange("p h t -> p (h t)"), bZ[0][:])
                for (hh, h4) in HH:
                    pr = HP[hh]
                    P.mm(bZ[1][pr, h4 * 64:(h4 + 1) * 64], Pc[pr, h4, :], RH[pr, h4, :])
                P.act(Un[:].rearrange("p h t -> p (h t)"), bZ[1][:], AF.Copy, scale=-1.0)
                Y_ = Yo[j % 2]
                for (hh, h4) in HH:
                    pr = HP[hh]
                    o_ = bZ[0][pr, h4 * 64:(h4 + 1) * 64]
                    P.mm(o_, KR[pr, h4, 1, :], TS[pr, d, h4, :], start=True, stop=False)
                    P.mm(o_, AM[pr, h4, 1, :], V[pr, h4, :], start=False, stop=False)
                    P.mm(o_, AM[pr, h4, 3, :], Un[pr, h4, :], start=False, stop=True)
                P.copy("scalar", Y_[:].rearrange("p h t -> p (h t)"), bZ[0][:])
                for hh in range(2):
                    P.dma(YS[d, c * 64:(c + 1) * 64, hh * 256:(hh + 1) * 256], Y_[HP[hh]].rearrange("p h t -> p (h t)"), wk=[(YS.name, d, c, hh)], eng=STQ)
                for (hh, h4) in HH:
                    pr = HP[hh]
                    for a_ in range(2):
                        trm(bX[0][pr, (h4 * 2 + a_) * 64:(h4 * 2 + a_ + 1) * 64], KBe[pr, h4, a_, :], hh)
                P.copy("vector", KBt[:].rearrange("p h a t -> p (h a t)"), bX[0][:])
                for (hh, h4) in HH:
                    pr = HP[hh]
                    o_ = bZ[1][pr, h4 * 64:(h4 + 1) * 64]
                    P.mm(o_, KBt[pr, h4, 0, :], V[pr, h4, :], start=True, stop=False)
                    P.mm(o_, KBt[pr, h4, 1, :], Un[pr, h4, :], start=False, stop=True)
                Td = TS[:, d, :, :]
                P.tt("vector", Td, Td, gl[:].unsqueeze(2).broadcast_to([128, 4, 64]), ALU.mult)
                P.tt("vector", Td, Td, bZ[1][:].rearrange("p (h t) -> p h t", h=4), ALU.add)

            for j in range(NCH):
                for d in range(2):
                    dpass(j, d)
            if not jb["ctx"]:
                for d in range(2):
                    for (hh, h4) in HH:
                        trm(bX[0][HP[hh], (d * 4 + h4) * 64:(d * 4 + h4 + 1) * 64], TS[HP[hh], d, h4, :], hh)
                P.copy("vector", s0t[:].rearrange("p d h k -> p (d h k)"), bX[0][:])
                for hh in range(2):
                    for d in range(2):
                        P.dma(O["o_rw"][si, l][d, hh * 4:(hh + 1) * 4].rearrange("h v k -> v h k"), s0t[HP[hh], d], wk=[("o_rw", si, l, hh, d)], eng=STQ, final=True)
            P.pop()
            P.push()
            gng = P.sb("gng", [128, 4]); gnb = P.sb("gnb", [128, 4])
            P.dma(gng[:], I["rwkv_gn_gT"][l]); P.dma(gnb[:], I["rwkv_gn_bT"][l])
            y0 = [P.sb(f"y0{i}", [128, 8, 64]) for i in range(2)]; y1 = [P.sb(f"y1{i}", [128, 8, 64]) for i in range(2)]
            bg = [P.sb(f"bg{i}", [128, 2, 4, 128]) for i in range(2)]
            junk = P.sb("junk", [128, 8, 64]); st8 = P.sb("st8", [128, 8]); yT = [P.sb(f"yT{i}", [128, 4, 128], FDT) for i in range(2)]
            ytmp = P.sb("ytmp", [128, 4, 128])
            pyt = P.ps("pyt", [128, 4, 128])
            for i in range(S // 128):
                a_ = y0[i % 2]; b_ = y1[i % 2]; g_ = bg[i % 2]; y_ = yT[i % 2]
                P.dma(a_[:], YS[0, i * 128:(i + 1) * 128, :].rearrange("p (h e) -> p h e", h=8), rk=[YS.name + "*"])
                P.dma(b_[:], YS[1, i * 128:(i + 1) * 128, :].rearrange("p (h e) -> p h e", h=8), rk=[YS.name + "*"])
                P.dma(g_[:], BG[:, :, i * 128:(i + 1) * 128].rearrange("a (c p) t -> p a c t", p=128), rk=[BG.name + "*"])
                P.tt("vector", a_[:], a_[:], b_[:], ALU.add)
                P.red("vector", st8[:], a_[:])
                P.ts("vector", st8[:], st8[:], -1.0 / 64, None, op0=ALU.mult)
                P.tt("vector", a_[:], a_[:], st8[:].unsqueeze(2).broadcast_to([128, 8, 64]), ALU.add)
                P.act(junk[:], a_[:], AF.Square)
                P.red("vector", st8[:], junk[:])
                rstd_of(st8[:], st8[:], 64, RW_GN_EPS)
                P.tt("vector", a_[:], a_[:], st8[:].unsqueeze(2).broadcast_to([128, 8, 64]), ALU.mult)
                for c in range(4):
                    P.tr(pyt[:, c, :], a_[:, 2 * c:2 * c + 2, :].rearrange("p h e -> p (h e)"), ident[:])
                for c in range(4):
                    P.act(ytmp[:, c, :], pyt[:, c, :], AF.Identity, bias=gnb[:, c:c + 1], scale=gng[:, c:c + 1])
                P.tt("vector", ytmp[:], ytmp[:], g_[:, 0, :, :], ALU.add)
                P.tt("vector", y_[:], ytmp[:], g_[:, 1, :, :], ALU.mult)
                c0 = t_off + i * 128
                P.dma(jb["YT"][3, :, c0:c0 + 128].rearrange("(c p) t -> p c t", p=128), y_[:], wk=[(jb["YT"].name, 3, c0)], eng=STQ)
            P.pop()


    def phaseM1(l, jb):
        TOK, j = jb["TOK"], jb["j"]
        ST = 512
        P.push()
        Yb = P.sb("Yb", [128, 4, 4, ST], FDT)
        Wo = P.sb("Wo", [128, 4, 4, D], FDT)
        wo2 = P.sb("wo2", [128, 8, D], FDT)
        Gt = [P.sb(f"Gt{i}", [128, 4, ST]) for i in range(2)]
        mg = P.sb("mg", [128, 8, ST], FDT); tmp = [P.sb(f"tmp{i}", [128, ST]) for i in range(3)]
        xT = P.sb("xT", [128, 8, ST])
        pp = [P.ps(f"pp{i}", [128, 512]) for i in range(6)]
        XTv = jb["XT"].rearrange("(k p) t -> p k t", p=128)
        wnames = ["mla_w_o", "mlstm_w_o", "swa_w_o", "rwkv_w_o"]
        for b in range(4):
            P.dma(Wo[:, b, :, :], WR[wnames[b]][l].rearrange("(c p) n -> p c n", p=128), wk=[Wo])
        for k2 in range(2):
            P.dma(wo2[:, k2 * 4:(k2 + 1) * 4, :], WR["w_out"][l][k2 * 512:(k2 + 1) * 512, :].rearrange("(k p) n -> p k n", p=128), wk=[wo2])
        ip = 0; io = 0
        for s_ in range(TOK // ST):
            t0 = s_ * ST
            Y_ = Yb; x_ = xT
            for b in range(4):
                P.dma(Y_[:, b, :, :], jb["YT"][b, :, t0:t0 + ST].rearrange("(c p) t -> p c t", p=128), rk=[jb["YT"].name + "*"], wk=[Y_])
            P.dma(x_[:], XTv[:, :, t0:t0 + ST], rk=[(jb["XT"].name, s_)])
            for oc in range(8):
                G_ = Gt[io % 2]; io += 1
                P.dma(G_[:], jb["UF"][C_GATE:C_GATE + 4096, t0:t0 + ST].rearrange("(b o p) t -> p b o t", b=4, o=8)[:, :, oc, :], rk=[jb["UF"].name + "*"])
                for b in range(4):
                    ps_ = pp[ip % 6]; ip += 1
                    for c in range(4):
                        P.mm(ps_[:, :ST], Wo[:, b, c, oc * 128:(oc + 1) * 128], Y_[:, b, c, :], start=(c == 0), stop=(c == 3), fast=True)
                    if b == 0:
                        P.tt("vector", tmp[2][:], ps_[:, :ST], G_[:, b, :], ALU.mult)
                    else:
                        t_ = tmp[b % 2]
                        P.tt("vector", t_[:], ps_[:, :ST], G_[:, b, :], ALU.mult)
                        P.tt(PENG, mg[:, oc, :] if b == 3 else tmp[2][:], tmp[2][:], t_[:], ALU.add)
            for oc in range(8):
                ps_ = pp[ip % 6]; ip += 1
                for k in range(8):
                    P.mm(ps_[:, :ST], wo2[:, k, oc * 128:(oc + 1) * 128], mg[:, k, :], start=(k == 0), stop=(k == 7), fast=True)
                P.stt("vector", x_[:, oc, :], ps_[:, :ST], modT[l][:, 16 + oc, j:j + 1], x_[:, oc, :], ALU.mult, ALU.add)
            P.dma(XTv[:, :, t0:t0 + ST], x_[:], wk=[(jb["XT"].name, s_)], eng=STQ)
        P.pop()

    def phaseM2(l, jb, last):
        TOK, j = jb["TOK"], jb["j"]
        ST = 512
        P.push()
        x1 = P.sb("x1", [128, 8, ST]); sq = P.sb("sq", [128, 8, ST]); h2 = P.sb("h2", [128, 8, ST], FDT); rstd = P.sb("rstd", [128, ST])
        w1c = [P.sb(f"w1c{i}", [128, 8, 512], FDT) for i in range(2)]
        hid = P.sb("hid", [128, 32, ST], FDT); rl = [P.sb(f"rl{i}", [128, ST]) for i in range(2)]
        w2r = [P.sb(f"w2r{i}", [128, D], FDT) for i in range(3)]
        ytok = [P.sb(f"ytok{i}", [128, D]) for i in range(2)]
        bank = [P.ps(f"bk{i}", [128, 512]) for i in range(8)]
        pst = bank[7]
        XTv = jb["XT"].rearrange("(k p) t -> p k t", p=128)
        ip = 0; i1 = 0; i2 = 0
        for s_ in range(TOK // ST):
            t0 = s_ * ST
            P.dma(x1[:], XTv[:, :, t0:t0 + ST], rk=[(jb["XT"].name, s_)])
            rms_rstd_featmajor(x1, sq, pst, rstd, ST)
            P.tt("vector", sq[:], x1[:], rstd[:].unsqueeze(1).broadcast_to([128, 8, ST]), ALU.mult)
            for k in range(8):
                P.act(h2[:, k, :], sq[:, k, :], AF.Identity, bias=modT[l][:, 24 + k, j:j + 1], scale=A2[l][:, k, j:j + 1])
            for fc in range(32):
                if fc % 4 == 0:
                    w_ = w1c[i1 % 2]; i1 += 1
                    P.dma(w_[:], WR["mlp_w1"][l][:, fc * 128:(fc + 4) * 128].rearrange("(k p) n -> p k n", p=128))
                ps_ = bank[ip % 6]; ip += 1
                for k in range(8):
                    P.mm(ps_[:, :ST], w_[:, k, (fc % 4) * 128:(fc % 4 + 1) * 128], h2[:, k, :], start=(k == 0), stop=(k == 7), fast=True)
                r_ = rl[fc % 2]
                P.act(r_[:], ps_[:, :ST], AF.Relu)
                P.tt(PENG, hid[:, fc, :], r_[:], r_[:], ALU.mult)
            x2 = sq
            for fc in range(32):
                w_ = w2r[i2 % 3]; i2 += 1
                P.dma(w_[:], WR["mlp_w2"][l][fc * 128:(fc + 1) * 128, :])
                for oc in range(8):
                    P.mm(bank[oc][:, :ST], w_[:, oc * 128:(oc + 1) * 128], hid[:, fc, :], start=(fc == 0), stop=(fc == 31), fast=True)
            for oc in range(8):
                P.stt("vector", x2[:, oc, :], bank[oc][:, :ST], modT[l][:, 40 + oc, j:j + 1], x1[:, oc, :], ALU.mult, ALU.add)
            if not last:
                P.dma(XTv[:, :, t0:t0 + ST], x2[:], wk=[(jb["XT"].name, s_)], eng=STQ)
            else:
                for tt in range(ST // 128):
                    yt = ytok[tt % 2]
                    for kk in range(2):
                        ps_ = bank[ip % 6]; ip += 1
                        for k4 in range(4):
                            P.tr(ps_[:, k4 * 128:(k4 + 1) * 128], x2[:, kk * 4 + k4, tt * 128:(tt + 1) * 128], ident[:])
                        P.evac(yt[:, kk * 512:(kk + 1) * 512], ps_[:])
                    P.dma(jb["y"][t0 + tt * 128:t0 + (tt + 1) * 128, :], yt[:], wk=[("y", j, t0, tt)], eng=STQ, final=True)
        P.pop()

    WR = {}
    fast_w = ["w_in", "mla_w_o", "mlstm_w_o", "swa_w_o", "rwkv_w_o", "w_out", "mlp_w1", "mlp_w2"]
    if FAST_MM:
        P.push()
        CH = 2048
        raw = [P.sb(f"wraw{i}", [128, CH]) for i in range(3)]
        rnd = [P.sb(f"wrnd{i}", [128, CH], F32R) for i in range(3)]
        engs = ["gpsimd", "vector", "scalar"]
        iw = 0
        for name in fast_w:
            src = I[name]
            _, Rr, Cc = src.shape
            dst = nc.dram_tensor(name + "_r", [L, Rr, Cc], F32R, kind="Internal").ap()
            WR[name] = dst
            for l in range(L):
                for rb in range(Rr // 128):
                    for c0 in range(0, Cc, CH):
                        w = min(CH, Cc - c0)
                        a_ = raw[iw % 3]; b_ = rnd[iw % 3]
                        P.dma(a_[:, :w], src[l, rb * 128:(rb + 1) * 128, c0:c0 + w])
                        P.copy(engs[iw % 3], b_[:, :w], a_[:, :w])
                        P.dma(dst[l, rb * 128:(rb + 1) * 128, c0:c0 + w], b_[:, :w], wk=[(dst.name, l, rb, c0)], eng=STQ)
                        iw += 1
        P.pop()
    else:
        for name in fast_w:
            WR[name] = I[name]

    only = os.environ.get("KONLY", "")
    for l in range(L):
        phase0(l)
        for jb in jobs:
            phase1(l, jb, first=(l == 0))
        for nm, fn in (("A", phaseA), ("C", phaseC), ("B", phaseB), ("D", phaseD)):
            if only and nm not in only:
                continue
            for jb in jobs:
                fn(l, jb)
        if only and "M" not in only:
            break
        for jb in jobs:
            phaseM1(l, jb)
        for jb in jobs:
            phaseM2(l, jb, last=(l == L - 1))
    print("NREC", getattr(P, "nrec", 0), flush=True)
    P.emit()
    return nc, I, O, SCR


def _fm(v, width=8):
    v = np.asarray(v, np.float32)
    return np.ascontiguousarray(np.swapaxes(v.reshape(v.shape[:-1] + (width, 128)), -1, -2))


def _rope_table(S, R):
    q = R // 4
    t = np.arange(S)
    pr = (t // 64).astype(np.float32); pc = (t % 64).astype(np.float32)
    inv = (10000.0 ** (-np.arange(q, dtype=np.float32) / q)).astype(np.float32)
    ar = pr[:, None] * inv; ac = pc[:, None] * inv
    ang = np.concatenate([ar, ar, ac, ac], -1).astype(np.float32)
    sign = np.concatenate([-np.ones(q), np.ones(q), -np.ones(q), np.ones(q)]).astype(np.float32)
    return np.ascontiguousarray(np.stack([np.cos(ang), np.sin(ang) * sign], 1).astype(np.float32))


def make_in_map(inp, cfg, core):
    L = cfg.depth
    f = lambda a: np.ascontiguousarray(np.asarray(a, np.float32))
    n_p = cfg.n_p
    m = {}
    m["xs"] = f(inp["x_sample"][core])
    m["xp"] = f(inp["x_prompt"][core * n_p:(core + 1) * n_p].reshape(n_p * SP, D))
    cc = np.stack([np.asarray(inp["c"][core]), np.asarray(inp["c_ctx"])], 0)
    m["cT"] = f(cc.reshape(2, 8, 128).transpose(2, 1, 0))
    m["ckv"] = f(inp["cache_mla_ckv"][core]); m["ckr"] = f(inp["cache_mla_krope"][core])
    m["cswk"] = f(np.asarray(inp["cache_swa_k"][core]).reshape(L, CTX, 128))
    m["cswv"] = f(np.asarray(inp["cache_swa_v"][core]).reshape(L, CTX, 128))
    m["mC"] = f(inp["state_mlstm_C"][core]); m["mn"] = f(inp["state_mlstm_n"][core])
    m["mm"] = f(inp["state_mlstm_m"][core]); m["rw"] = f(inp["state_rwkv"][core])
    m["ada_w"] = f(inp["ada_w"]); m["ada_bT"] = _fm(inp["ada_b"], 48)
    m["norm1T"] = _fm(inp["norm1"]); m["norm2T"] = _fm(inp["norm2"])
    for k in ["w_in", "mla_q_a_norm", "mla_kv_a_norm", "mla_w_uq", "mla_w_ukv", "mla_q_norm", "mla_k_norm", "mla_w_o",
              "mlstm_norm", "mlstm_w_o", "swa_q_norm", "swa_k_norm", "swa_sink", "swa_w_o", "rwkv_w2", "rwkv_a2",
              "rwkv_g2", "rwkv_w_o", "w_out", "mlp_w1", "mlp_w2"]:
        m[k] = f(inp[k])
    m["mlstm_i_bias"] = f(np.asarray(inp["mlstm_i_bias"]).reshape(L, 8))
    m["mlstm_f_bias"] = f(np.asarray(inp["mlstm_f_bias"]).reshape(L, 8))
    for k in ["rwkv_gn_g", "rwkv_gn_b"]:
        m[k + "T"] = _fm(inp[k], 4)
    m["rwkv_w0"] = f(inp["rwkv_w0"])
    for k in ["rwkv_kk", "rwkv_ka"]:
        m[k + "64"] = f(np.asarray(inp[k]).reshape(L, 8, 64).transpose(0, 2, 1))
    for k in ["rwkv_a0", "rwkv_u"]:
        m[k + "64"] = f(np.asarray(inp[k]).reshape(L, 2, 8, 64).transpose(0, 3, 1, 2))
    m["k_ident"] = np.eye(128, dtype=np.float32)
    m["k_ones"] = np.ones((128, 128), np.float32)
    kq = np.arange(128)
    m["k_tri"] = np.stack([(kq[:, None] >= kq[None, :]), (kq[:, None] <= kq[None, :])], 0).astype(np.float32)
    m["k_tris"] = np.stack([(kq[:, None] > kq[None, :]), (kq[:, None] < kq[None, :])], 0).astype(np.float32)
    sel = np.zeros((2, 128, 128), np.float32); sel[0, 127, :] = 1.0; sel[1, 0, :] = 1.0
    m["k_sel"] = sel
    m["k_rope32"] = _rope_table(cfg.S_s, 32); m["k_rope64"] = _rope_table(cfg.S_s, 64)
    return m


_CACHE = {}


def kernel(**inputs):
    cfg = Cfg(S_s=4096, n_p=2, depth=2)
    n_cores = 8
    if "nc" not in _CACHE:
        _CACHE["nc"] = build(cfg)
    nc, I, O, SCR = _CACHE["nc"]
    in_maps = []
    for c in range(n_cores):
        m = make_in_map(inputs, cfg, c)
        in_maps.append({k: v for k, v in m.items() if k in I})
    res = run_bass_kernel_spmd(nc, in_maps, core_ids=list(range(n_cores)))
    R = res.results
    L = cfg.depth
    cat = lambda k: np.concatenate([np.asarray(r[k], np.float32) for r in R], 0)
    y_prompt = cat("y_p").reshape(16, SP, D)
    y_sample = np.stack([np.asarray(r["y_s"], np.float32) for r in R], 0)
    return (y_prompt, y_sample,
            cat("o_ckv"), cat("o_ckr"),
            cat("o_swk").reshape(16, L, SP, 2, 64), cat("o_swv").reshape(16, L, SP, 2, 64),
            cat("o_mC"), cat("o_mn"), cat("o_mm"), cat("o_rw"))
```

```python
import os
import numpy as np
import concourse.bass as bass
import concourse.mybir as mybir
from concourse.bass_utils import run_bass_kernel_spmd
from contextlib import ExitStack

F32 = mybir.dt.float32
F32R = mybir.dt.float32r
FAST_MM = os.environ.get("FAST_MM", "1") == "1"
FDT = F32R if FAST_MM else F32


def fr(ap):
    return ap.bitcast(F32R) if FAST_MM else ap
AF = mybir.ActivationFunctionType
ALU = mybir.AluOpType
AX = mybir.AxisListType

ENGS = ("sync", "scalar", "vector", "gpsimd", "tensor")
SEM_LIMIT = int(os.environ.get("SEM_LIMIT", 30000))
N_DMA_SEMS = 16
import os
STQ = os.environ.get("STQ", "gpsimd")
PENG = os.environ.get("PENG", "vector")

D = 1024
NORM_EPS = 1e-6
CTX = 256
SP = 256
D_IN = 8752
D_FF = 4096
MLA_SCALE = 96 ** -0.5
SW_SCALE = 64 ** -0.5
RW_DECAY = 0.6065306597126334
RW_GN_EPS = 64e-5
C_QA, C_KVA, C_KR = 0, 256, 384
C_MQ, C_MK, C_MV, C_MI, C_MF, C_MO = 416, 672, 928, 1440, 1448, 1456
C_SQ, C_SK, C_SV = 1968, 2480, 2608
C_RR, C_RK, C_RV, C_RW, C_RA, C_RG = 2736, 3248, 3760, 4272, 4400, 4528
C_GATE = 4656
NTOKC = 4656


class Op:
    __slots__ = ("eng", "fn", "deps", "is_dma", "signal", "sem", "cnt", "dsem_prev", "barriered")

    def __init__(self, eng, fn, is_dma):
        self.eng = eng
        self.fn = fn
        self.deps = []
        self.is_dma = is_dma
        self.signal = False
        self.sem = None
        self.cnt = 0
        self.dsem_prev = None
        self.barriered = False


class Prog:
    def __init__(self, nc):
        self.nc = nc
        self.ops = {e: [] for e in ENGS}
        self.lastw = {}
        self.readers = {}
        self.stacks = [ExitStack()]
        self.out_dmas = []
        self.uid = 0
        self.rr = 0
        self.psum_names = set()

    def sb(self, name, shape, dt=F32):
        self.uid += 1
        return self.stacks[-1].enter_context(self.nc.sbuf_tensor(f"{name}_{self.uid}", list(shape), dt))

    def ps(self, name, shape, dt=F32):
        self.uid += 1
        n = 1
        for d in shape[1:]:
            n *= d
        nb = (n * 4 + 2047) // 2048
        t = self.stacks[-1].enter_context(self.nc.psum_tensor(f"{name}_{self.uid}", [128, nb * 512], dt))
        self.psum_names.add(t.name)
        v = t[:shape[0], :n]
        if len(shape) == 3:
            v = v.rearrange("p (a b) -> p a b", a=shape[1])
        elif len(shape) == 4:
            v = v.rearrange("p (a b c) -> p a b c", a=shape[1], b=shape[2])
        return v

    def push(self):
        self.stacks.append(ExitStack())

    def pop(self):
        self.barrier()
        self.stacks.pop().close()

    @staticmethod
    def _key(k):
        if isinstance(k, (str, tuple)):
            return k
        return k.name

    def op(self, eng, fn, reads=(), writes=(), is_dma=False):
        self.nrec = getattr(self, "nrec", 0) + 1
        if self.nrec > int(os.environ.get("KMAXOPS", 10 ** 9)):
            return None
        o = Op(eng, fn, is_dma)
        if os.environ.get("KTRACE") and abs(self.nrec - int(os.environ["KTRACE"])) <= 6:
            print("OP", self.nrec, eng, [self._key(k) for k in writes], flush=True)
        rk = [self._key(k) for k in reads if k is not None and not isinstance(k, (int, float))]
        wk = [self._key(k) for k in writes]
        if eng != "tensor":
            wk = wk + [k for k in rk if k in self.psum_names and k not in wk]
        raw = set()
        deps = set()
        for k in rk:
            w = self.lastw.get(k)
            if w is not None:
                raw.add(w)
                deps.add(w)
        for k in wk:
            w = self.lastw.get(k)
            if w is not None:
                deps.add(w)
            last_by_eng = {}
            for r in self.readers.get(k, ()):
                if r.is_dma:
                    deps.add(r)
                else:
                    last_by_eng[r.eng] = r
            deps.update(last_by_eng.values())
        for d in deps:
            if d.eng == eng and not d.is_dma and not is_dma:
                if eng == "tensor":
                    continue
            o.deps.append(d)
        for k in rk:
            self.readers.setdefault(k, []).append(o)
        for k in wk:
            self.lastw[k] = o
            self.readers[k] = []
        self.ops[eng].append(o)
        return o

    def barrier(self):
        lasts = [self.ops[e][-1] for e in ENGS if self.ops[e]]
        pend = [o for e in ("sync", "scalar", "gpsimd") for o in self.ops[e] if o.is_dma and not o.barriered]
        for o in pend:
            o.barriered = True
        for e in ENGS:
            b = Op(e, None, False)
            b.deps = lasts + pend
            self.ops[e].append(b)
        self.lastw.clear()
        self.readers.clear()

    def dma(self, out, in_, rk=None, wk=None, eng="sync", final=False, **kw):
        r = [in_] if rk is None else rk
        w = [out] if wk is None else wk
        o = self.op(eng, lambda e: e.dma_start(out=out, in_=in_, **kw), r, w, is_dma=True)
        if final and o is not None:
            self.out_dmas.append(o)
        return o

    def mm(self, out, lhsT, rhs, start=True, stop=True, rk=None, wk=None, fast=False):
        r = [lhsT, rhs] if rk is None else rk
        w = [out] if wk is None else wk
        return self.op("tensor", lambda e: e.matmul(out, lhsT=lhsT, rhs=rhs, start=start, stop=stop), r, w)

    def tr(self, out, in_, ident, rk=None, wk=None):
        r = [in_, ident] if rk is None else rk
        w = [out] if wk is None else wk
        return self.op("tensor", lambda e: e.transpose(out, in_, ident), r, w)

    def act(self, out, in_, func, bias=None, scale=None, accum=None, rk=None, wk=None, extra_r=()):
        kw = {}
        if bias is not None:
            kw["bias"] = bias
        if scale is not None:
            kw["scale"] = scale
        if accum is not None:
            kw["accum_out"] = accum
        r = ([in_, bias, scale] if rk is None else list(rk)) + list(extra_r)
        w = ([out] + ([accum] if accum is not None else [])) if wk is None else wk
        return self.op("scalar", lambda e: e.activation(out=out, in_=in_, func=func, **kw), r, w)

    def tt(self, eng, out, in0, in1, op, rk=None, wk=None):
        r = [in0, in1] if rk is None else rk
        w = [out] if wk is None else wk
        return self.op(eng, lambda e: e.tensor_tensor(out=out, in0=in0, in1=in1, op=op), r, w)

    def ts(self, eng, out, in0, s1, s2=None, op0=ALU.mult, op1=None, rk=None, wk=None):
        r = [in0, s1, s2] if rk is None else rk
        w = [out] if wk is None else wk
        if op1 is None:
            return self.op(eng, lambda e: e.tensor_scalar(out=out, in0=in0, scalar1=s1, scalar2=None, op0=op0), r, w)
        return self.op(eng, lambda e: e.tensor_scalar(out=out, in0=in0, scalar1=s1, scalar2=s2, op0=op0, op1=op1), r, w)

    def stt(self, eng, out, in0, scalar, in1, op0, op1, rk=None, wk=None):
        r = [in0, scalar, in1] if rk is None else rk
        w = [out] if wk is None else wk
        return self.op(eng, lambda e: e.scalar_tensor_tensor(out=out, in0=in0, scalar=scalar, in1=in1, op0=op0, op1=op1), r, w)

    def copy(self, eng, out, in_, rk=None, wk=None):
        r = [in_] if rk is None else rk
        w = [out] if wk is None else wk
        if eng == "scalar":
            return self.op(eng, lambda e: e.copy(out=out, in_=in_), r, w)
        return self.op(eng, lambda e: e.tensor_copy(out=out, in_=in_), r, w)

    def red(self, eng, out, in_, op=ALU.add, rk=None, wk=None):
        r = [in_] if rk is None else rk
        w = [out] if wk is None else wk
        return self.op(eng, lambda e: e.tensor_reduce(out=out, in_=in_, axis=AX.X, op=op), r, w)

    def recip(self, out, in_, rk=None, wk=None):
        r = [in_] if rk is None else rk
        w = [out] if wk is None else wk
        return self.op("vector", lambda e: e.reciprocal(out=out, in_=in_), r, w)

    def memset(self, eng, out, val, wk=None):
        w = [out] if wk is None else wk
        return self.op(eng, lambda e: e.memset(out, val), [], w)

    def evac(self, out, in_, rk=None, wk=None):
        self.rr += 1
        ev = os.environ.get("KEVAC", "alt")
        if ev == "alt":
            ev = "scalar" if self.rr % 2 else "vector"
        return self.copy(ev, out, in_, rk, wk)

    def emit(self):
        nc = self.nc
        if self.out_dmas:
            b = Op("sync", None, False)
            b.deps = list(self.out_dmas)
            self.ops["sync"].append(b)
        for e in ENGS:
            for o in self.ops[e]:
                for d in o.deps:
                    d.signal = True
        with ExitStack() as st:
            def newsem(nm):
                return st.enter_context(nc.semaphore(nm))
            for e in ENGS:
                cur, c, ep = None, 0, 0
                for o in self.ops[e]:
                    if o.is_dma or not o.signal or o.fn is None:
                        continue
                    if cur is None or c >= SEM_LIMIT:
                        cur = newsem(f"s_{e}_{ep}")
                        ep += 1
                        c = 0
                    c += 1
                    o.sem = cur
                    o.cnt = c
            for e in ENGS:
                dl = [o for o in self.ops[e] if o.is_dma]
                if not dl:
                    continue
                dsems = [newsem(f"dq_{e}_{i}") for i in range(N_DMA_SEMS)]
                dcnt = [0] * N_DMA_SEMS
                dlast = [None] * N_DMA_SEMS
                for di, o in enumerate(dl):
                    s_ = di % N_DMA_SEMS
                    if dcnt[s_] + 16 > SEM_LIMIT:
                        dsems[s_] = newsem(f"dq_{e}_{s_}_{di}")
                        dcnt[s_] = 0
                    dcnt[s_] += 16
                    o.sem = dsems[s_]
                    o.cnt = dcnt[s_]
                    o.dsem_prev = dlast[s_]
                    dlast[s_] = o
                    o.signal = True
            if os.environ.get("KSTATS"):
                for e in ENGS:
                    sig = [o for o in self.ops[e] if o.signal and not o.is_dma and o.fn is not None]
                    print("ENG", e, "ops", len(self.ops[e]), "signals", len(sig), "maxcnt", max([o.cnt for o in self.ops[e]] + [0]), flush=True)
            with nc.Block() as block:
                def run(e):
                    def body(eng):
                        seen = {}
                        for o in self.ops[e]:
                            need = {}
                            deps = o.deps
                            if o.is_dma and o.dsem_prev is not None:
                                deps = deps + [o.dsem_prev]
                            for d in deps:
                                if d.fn is None or d.sem is None:
                                    continue
                                nm = d.sem.name
                                if need.get(nm, (None, 0))[1] < d.cnt:
                                    need[nm] = (d.sem, d.cnt)
                            for nm, (s, c) in need.items():
                                if seen.get(nm, 0) >= c:
                                    continue
                                eng.wait_ge(s, c)
                                seen[nm] = c
                            if o.fn is None:
                                continue
                            ins = o.fn(eng)
                            if o.signal:
                                ins.then_inc(o.sem, 16 if o.is_dma else 1)
                    return body
                block.sync(run("sync"))
                block.scalar(run("scalar"))
                block.vector(run("vector"))
                block.gpsimd(run("gpsimd"))
                block.tensor(run("tensor"))
        while self.stacks:
            self.stacks.pop().close()


class Cfg:
    def __init__(self, S_s=4096, n_p=2, depth=2, debug=()):
        self.S_s = S_s
        self.n_p = n_p
        self.depth = depth
        self.debug = tuple(debug)


W_NAMES = ["ada_w", "w_in", "mla_w_uq", "mla_w_ukv", "mla_w_o", "mlstm_w_o", "swa_w_o", "rwkv_w2", "rwkv_a2",
           "rwkv_g2", "rwkv_w_o", "w_out", "mlp_w1", "mlp_w2"]

P1_BLOCKS = [
    (0, 416, "T"), (416, 672, "F"), (672, 928, "TF"), (928, 1440, "T"), (1440, 1456, "T"), (1456, 1968, "T"),
    (1968, 2480, "T"), (2480, 2736, "T"), (2736, 3248, "F"), (3248, 3760, "F"), (3760, 4272, "TF"),
    (4272, 4656, "F"),
] + [(4656 + 512 * i, 4656 + 512 * (i + 1), "G") for i in range(8)]


def build(cfg):
    nc = bass.Bass("TRN2", target_bir_lowering=False)
    if FAST_MM:
        nc.dge_precook = False
    L = cfg.depth
    S_s, n_p = cfg.S_s, cfg.n_p
    TOKP = n_p * SP
    P = Prog(nc)
    I = {}
    O = {}
    SCR = {}

    def din(name, shape):
        I[name] = nc.dram_tensor(name, list(shape), F32, kind="ExternalInput").ap()
        return I[name]

    def dout(name, shape):
        O[name] = nc.dram_tensor(name, list(shape), F32, kind="ExternalOutput").ap()
        return O[name]

    def scr(name, shape, dt=F32):
        kind = "ExternalOutput" if name in cfg.debug else "Internal"
        SCR[name] = nc.dram_tensor(name, list(shape), dt, kind=kind).ap()
        return SCR[name]

    din("xs", [S_s, D]); din("xp", [TOKP, D])
    din("cT", [128, 8, 2])
    din("ckv", [L, CTX, 128]); din("ckr", [L, CTX, 32]); din("cswk", [L, CTX, 128]); din("cswv", [L, CTX, 128])
    din("mC", [L, 2, 4, 64, 128]); din("mn", [L, 2, 4, 64]); din("mm", [L, 2, 4]); din("rw", [L, 2, 8, 64, 64])
    din("ada_w", [L, D, 6 * D]); din("ada_bT", [L, 128, 48]); din("norm1T", [L, 128, 8]); din("norm2T", [L, 128, 8])
    din("w_in", [L, D, D_IN])
    din("mla_q_a_norm", [L, 256]); din("mla_kv_a_norm", [L, 128]); din("mla_w_uq", [L, 256, 768])
    din("mla_w_ukv", [L, 128, 1024]); din("mla_q_norm", [L, 96]); din("mla_k_norm", [L, 96]); din("mla_w_o", [L, 512, D])
    din("mlstm_i_bias", [L, 8]); din("mlstm_f_bias", [L, 8]); din("mlstm_norm", [L, 128]); din("mlstm_w_o", [L, 512, D])
    din("swa_q_norm", [L, 64]); din("swa_k_norm", [L, 64]); din("swa_sink", [L, 8]); din("swa_w_o", [L, 512, D])
    din("rwkv_w0", [L, 2, 512]); din("rwkv_w2", [L, 2, 64, 512]); din("rwkv_a064", [L, 64, 2, 8])
    din("rwkv_a2", [L, 2, 64, 512]); din("rwkv_g2", [L, 128, 512]); din("rwkv_kk64", [L, 64, 8]); din("rwkv_ka64", [L, 64, 8])
    din("rwkv_u64", [L, 64, 2, 8]); din("rwkv_gn_gT", [L, 128, 4]); din("rwkv_gn_bT", [L, 128, 4]); din("rwkv_w_o", [L, 512, D])
    din("w_out", [L, D, D]); din("mlp_w1", [L, D, D_FF]); din("mlp_w2", [L, D_FF, D])
    din("k_ident", [128, 128]); din("k_ones", [128, 128])
    din("k_rope32", [S_s, 2, 32]); din("k_rope64", [S_s, 2, 64]); din("k_tri", [2, 128, 128]); din("k_sel", [2, 128, 128]); din("k_tris", [2, 128, 128])
    dout("y_p", [TOKP, D]); dout("y_s", [S_s, D])
    dout("o_ckv", [n_p, L, SP, 128]); dout("o_ckr", [n_p, L, SP, 32]); dout("o_swk", [n_p, L, SP, 128]); dout("o_swv", [n_p, L, SP, 128])
    dout("o_mC", [n_p, L, 2, 4, 64, 128]); dout("o_mn", [n_p, L, 2, 4, 64]); dout("o_mm", [n_p, L, 2, 4]); dout("o_rw", [n_p, L, 2, 8, 64, 64])

    jobs = [dict(name="s", TOK=S_s, seqs=[(0, S_s)], ctx=True, j=0, x=I["xs"], y=O["y_s"]),
            dict(name="p", TOK=TOKP, seqs=[(i * SP, SP) for i in range(n_p)], ctx=False, j=1, x=I["xp"], y=O["y_p"])]
    for jb in jobs:
        n = jb["name"]
        jb["XT"] = scr(f"XT_{n}", [D, jb["TOK"]])
        jb["UT"] = scr(f"UTOK_{n}", [jb["TOK"], NTOKC])
        jb["UF"] = scr(f"UFEAT_{n}", [D_IN, jb["TOK"]])
        jb["YT"] = scr(f"YT_{n}", [4, 512, jb["TOK"]], FDT)

    ident = P.sb("ident", [128, 128]); ones = P.sb("ones", [128, 128])
    P.dma(ident[:], I["k_ident"][:, :], wk=[ident]); P.dma(ones[:], I["k_ones"][:, :], wk=[ones])
    cT = P.sb("cT", [128, 8, 2]); sT = P.sb("sT", [128, 8, 2])
    P.dma(cT[:], I["cT"][:, :, :])
    P.act(sT[:], cT[:], AF.Silu)
    modT = [P.sb(f"modT{l}", [128, 48, 2]) for l in range(L)]
    A1 = [P.sb(f"A1_{l}", [128, 8, 2]) for l in range(L)]
    A2 = [P.sb(f"A2_{l}", [128, 8, 2]) for l in range(L)]

    def phase0(l):
        P.push()
        wb = [P.sb(f"adaw{i}", [128, 8, 512]) for i in range(2)]
        pm = P.ps("pm", [128, 48, 2])
        bT = P.sb("bT", [128, 48]); n1 = P.sb("n1", [128, 8]); n2 = P.sb("n2", [128, 8])
        P.dma(bT[:], I["ada_bT"][l]); P.dma(n1[:], I["norm1T"][l]); P.dma(n2[:], I["norm2T"][l])
        wv = I["ada_w"][l].rearrange("(k p) c -> p k c", p=128)
        for g in range(12):
            w = wb[g % 2]
            P.dma(w[:], wv[:, :, g * 512:(g + 1) * 512])
            for jj in range(4):
                jc = g * 4 + jj
                for k in range(8):
                    P.mm(pm[:, jc, :], w[:, k, jj * 128:(jj + 1) * 128], sT[:, k, :], start=(k == 0), stop=(k == 7))
        P.tt("vector", modT[l][:], pm[:], bT[:].unsqueeze(2).broadcast_to([128, 48, 2]), ALU.add)
        P.stt("vector", A1[l][:], modT[l][:, 8:16, :], 1.0, n1[:].unsqueeze(2).broadcast_to([128, 8, 2]), ALU.add, ALU.mult)
        P.stt("vector", A2[l][:], modT[l][:, 32:40, :], 1.0, n2[:].unsqueeze(2).broadcast_to([128, 8, 2]), ALU.add, ALU.mult)
        P.pop()

    def rms_rstd_featmajor(xT, sq, pst, rstd, n):
        P.act(sq[:, :, :n], xT[:, :, :n], AF.Square)
        for k in range(8):
            P.mm(pst[:, :n], ones[:], sq[:, k, :n], start=(k == 0), stop=(k == 7))
        P.act(rstd[:, :n], pst[:, :n], AF.Sqrt, bias=NORM_EPS, scale=1.0 / D)
        P.recip(rstd[:, :n], rstd[:, :n])

    def phase1(l, jb, first):
        TOK, j = jb["TOK"], jb["j"]
        ST = 512
        P.push()
        xT = [P.sb(f"xT{i}", [128, 8, ST]) for i in range(2)]
        hT = [P.sb(f"hT{i}", [128, 8, ST], FDT) for i in range(2)]
        sq = P.sb("sq", [128, 8, ST]); rstd = P.sb("rstd", [128, ST])
        wt = [P.sb(f"wt{i}", [128, 8, 512], FDT) for i in range(4)]
        stg = [P.sb(f"stg{i}", [128, 4, 512]) for i in range(3)]
        xin = [P.sb(f"xin{i}", [128, D]) for i in range(2)] if first else None
        pst = P.ps("pst", [128, ST])
        pp = [P.ps(f"pp{i}", [128, 512]) for i in range(6)]
        XTv = jb["XT"].rearrange("(k p) t -> p k t", p=128)
        wv = WR["w_in"][l].rearrange("(k p) c -> p k c", p=128)
        UTv = jb["UT"].rearrange("(tt p) c -> p tt c", p=128)
        ip = 0
        ist = 0
        iw = 0
        for s in range(TOK // ST):
            x_ = xT[s % 2]; h_ = hT[s % 2]
            t0 = s * ST
            if first:
                for tt in range(4):
                    xi = xin[tt % 2]
                    P.dma(xi[:], jb["x"][t0 + tt * 128:t0 + (tt + 1) * 128, :])
                    for kk in range(2):
                        pq = pp[ip % 6]; ip += 1
                        for k4 in range(4):
                            P.tr(pq[:, k4 * 128:(k4 + 1) * 128], xi[:, (kk * 4 + k4) * 128:(kk * 4 + k4 + 1) * 128], ident[:])
                        P.evac(x_[:, kk * 4:(kk + 1) * 4, tt * 128:(tt + 1) * 128], pq[:].rearrange("p (a b) -> p a b", a=4))
                P.dma(XTv[:, :, t0:t0 + ST], x_[:], wk=[(jb["XT"].name, s)], eng=STQ)
            else:
                P.dma(x_[:], XTv[:, :, t0:t0 + ST], rk=[(jb["XT"].name, s)])
            rms_rstd_featmajor(x_, sq, pst, rstd, ST)
            P.tt("vector", sq[:], x_[:], rstd[:].unsqueeze(1).broadcast_to([128, 8, ST]), ALU.mult)
            for k in range(8):
                P.act(h_[:, k, :], sq[:, k, :], AF.Identity, bias=modT[l][:, k, j:j + 1], scale=A1[l][:, k, j:j + 1])
            for (cs, ce, lay) in P1_BLOCKS:
                wd = ce - cs
                w = wt[iw % 4]; iw += 1
                P.dma(w[:, :, :wd], wv[:, :, cs:ce])
                if "T" in lay:
                    sg = stg[ist % 3]; ist += 1
                    for tt in range(4):
                        pq = pp[ip % 6]; ip += 1
                        for k in range(8):
                            P.mm(pq[:, :wd], h_[:, k, tt * 128:(tt + 1) * 128], w[:, k, :wd], start=(k == 0), stop=(k == 7), fast=(wd >= 256))
                        P.evac(sg[:, tt, :wd], pq[:, :wd])
                    P.dma(UTv[:, s * 4:(s + 1) * 4, cs:ce], sg[:, :, :wd], wk=[(jb["UT"].name, s, cs)], eng=STQ)
                if "F" in lay or "G" in lay:
                    sg = stg[ist % 3]; ist += 1
                    nb = wd // 128
                    for cb in range(nb):
                        pq = pp[ip % 6]; ip += 1
                        for k in range(8):
                            P.mm(pq[:, :ST], w[:, k, cb * 128:(cb + 1) * 128], h_[:, k, :], start=(k == 0), stop=(k == 7), fast=True)
                        if lay == "G":
                            P.act(sg[:, cb, :], pq[:, :ST], AF.Sigmoid)
                        else:
                            P.evac(sg[:, cb, :], pq[:, :ST])
                    P.dma(jb["UF"][cs:ce, t0:t0 + ST].rearrange("(cb p) t -> p cb t", p=128), sg[:, :nb, :],
                          wk=[(jb["UF"].name, s, cs)], eng=STQ)
        P.pop()


    def rope(x, cos, sinS, H, q, t1, t2):
        R4 = 4 * q
        t1v = t1[:, :H * R4].rearrange("p (h r) -> p h r", h=H)
        t2v = t2[:, :H * R4].rearrange("p (h a b c) -> p h a b c", h=H, a=2, b=2)
        xv = x.rearrange("p h (a b c) -> p h a b c", a=2, b=2)
        sv = sinS.rearrange("p (a b c) -> p a b c", a=2, b=2)
        P.tt("vector", t1v, x, cos.unsqueeze(1).broadcast_to([128, H, R4]), ALU.mult)
        for b in range(2):
            P.tt(PENG, t2v[:, :, :, b, :], xv[:, :, :, 1 - b, :],
                 sv[:, :, b, :].unsqueeze(1).broadcast_to([128, H, 2, q]), ALU.mult)
        P.tt("vector", x, t1v, t2[:, :H * R4].rearrange("p (h r) -> p h r", h=H), ALU.add)

    def rstd_of(out, ssq, n, eps):
        P.act(out, ssq, AF.Sqrt, bias=eps, scale=1.0 / n)
        P.recip(out, out)

    def bcast_row(name, src_row, n):
        t = P.sb(name, [128, n])
        P.dma(t[:], src_row.partition_broadcast(128))
        return t

    def phaseA(l, jb):
        TOK = jb["TOK"]
        for si, (t_off, S) in enumerate(jb["seqs"]):
            NK = S + (CTX if jb["ctx"] else 0)
            NKT = NK // 128
            n = jb["name"]
            KTs = scr(f"A_KT_{n}{l}_{si}", [8, 96, NK], FDT); VPs = scr(f"A_VP_{n}{l}_{si}", [NK, 8, 65], FDT); QTs = scr(f"A_QT_{n}{l}_{si}", [8, 96, S], FDT)
            P.push()
            gqa = bcast_row("gqa", I["mla_q_a_norm"][l], 256); gkv = bcast_row("gkv", I["mla_kv_a_norm"][l], 128)
            gqn = bcast_row("gqn", I["mla_q_norm"][l], 96); gkn = bcast_row("gkn", I["mla_k_norm"][l], 96)
            wuq = P.sb("wuq", [128, 2, 768]); wukv = P.sb("wukv", [128, 1024])
            P.dma(wuq[:], I["mla_w_uq"][l].rearrange("(k p) c -> p k c", p=128)); P.dma(wukv[:], I["mla_w_ukv"][l])
            ua = [P.sb(f"ua{i}", [128, 416]) for i in range(2)]
            rt = [P.sb(f"rt{i}", [128, 2, 32]) for i in range(2)]
            junk = P.sb("junk", [128, 768]); junk2 = P.sb("junk2", [128, 768])
            st = P.sb("st", [128, 4]); ss16 = P.sb("ss16", [128, 16]); ssr = P.sb("ssr", [128, 1])
            qlat = P.sb("qlat", [128, 256]); ckv = [P.sb(f"ckv{i}", [128, 128]) for i in range(2)]
            qlT = P.sb("qlT", [128, 2, 128]); ckT = P.sb("ckT", [128, 128])
            qf = P.sb("qf", [128, 8, 96]); kvf = P.sb("kvf", [128, 8, 128]); kn = P.sb("kn", [128, 8, 96])
            kr = P.sb("kr", [128, 32])
            vp = [P.sb(f"vp{i}", [128, 8, 65], FDT) for i in range(2)]
            qT = [P.sb(f"qT{i}", [96, 8, 128], FDT) for i in range(2)]; kT = [P.sb(f"kT{i}", [96, 8, 128], FDT) for i in range(2)]
            for v in vp:
                P.copy("vector", v[:, :, 64:65], ones[:, 0:8].unsqueeze(2), wk=[v])
            ptr = P.ps("ptr", [128, 3, 128]); pq1 = P.ps("pq1", [128, 512]); pq2 = P.ps("pq2", [128, 256])
            pk1 = P.ps("pk1", [128, 512]); pk2 = P.ps("pk2", [128, 512])
            pT = [P.ps(f"pT{i}", [96, 4, 128]) for i in range(2)]
            tiles = [("new", i) for i in range(S // 128)] + ([("ctx", i) for i in range(CTX // 128)] if jb["ctx"] else [])
            for it, (kind, i) in enumerate(tiles):
                if it >= int(os.environ.get("KA1T", 999)):
                    break
                u = ua[it % 2]; ck = ckv[it % 2]; v_ = vp[it % 2]; r_ = rt[it % 2]
                if os.environ.get("KSTATS"):
                    print("A1 tile start", jb["name"], si, it, getattr(P, "nrec", 0))
                new = kind == "new"
                rows = slice(t_off + i * 128, t_off + (i + 1) * 128)
                krow = i * 128 if new else S + i * 128
                if new:
                    P.dma(u[:], jb["UT"][rows, 0:416], rk=[(jb["UT"].name, (t_off + i * 128) // 512, 0)])
                    if jb["ctx"]:
                        P.dma(r_[:], I["k_rope32"][i * 128:(i + 1) * 128])
                    P.act(junk[:, :256], u[:, 0:256], AF.Square, accum=st[:, 0:1])
                    P.act(junk[:, :128], u[:, 256:384], AF.Square, accum=st[:, 1:2])
                    P.act(st[:, 2:3], st[:, 0:1], AF.Sqrt, bias=NORM_EPS, scale=1.0 / 256)
                    P.act(st[:, 3:4], st[:, 1:2], AF.Sqrt, bias=NORM_EPS, scale=1.0 / 128)
                    P.recip(st[:, 2:4], st[:, 2:4])
                    P.stt("vector", qlat[:], u[:, 0:256], st[:, 2:3], gqa[:], ALU.mult, ALU.mult)
                    P.stt("vector", ck[:], u[:, 256:384], st[:, 3:4], gkv[:], ALU.mult, ALU.mult)
                    if not jb["ctx"]:
                        P.dma(O["o_ckv"][si, l, i * 128:(i + 1) * 128, :], ck[:], wk=[("o_ckv", si, l, i)], eng=STQ, final=True)
                        P.dma(O["o_ckr"][si, l, i * 128:(i + 1) * 128, :], u[:, 384:416], wk=[("o_ckr", si, l, i)], eng=STQ, final=True)
                    for k in range(2):
                        P.tr(ptr[:, k, :], qlat[:, k * 128:(k + 1) * 128], ident[:])
                    P.tr(ptr[:, 2, :], ck[:], ident[:])
                    P.evac(qlT[:], ptr[:, 0:2, :]); P.evac(ckT[:], ptr[:, 2, :])
                    for k in range(2):
                        P.mm(pq1[:], qlT[:, k, :], wuq[:, k, 0:512], start=(k == 0), stop=(k == 1))
                    for k in range(2):
                        P.mm(pq2[:], qlT[:, k, :], wuq[:, k, 512:768], start=(k == 0), stop=(k == 1))
                    qff = qf[:].rearrange("p h d -> p (h d)")
                    P.evac(qff[:, 0:512], pq1[:]); P.evac(qff[:, 512:768], pq2[:])
                    krope = u[:, 384:416]
                else:
                    P.dma(ck[:], I["ckv"][l, i * 128:(i + 1) * 128, :])
                    P.dma(u[:, 384:416], I["ckr"][l, i * 128:(i + 1) * 128, :])
                    P.tr(ptr[:, 2, :], ck[:], ident[:])
                    P.evac(ckT[:], ptr[:, 2, :])
                    krope = u[:, 384:416]
                P.mm(pk1[:], ckT[:], wukv[:, 0:512]); P.mm(pk2[:], ckT[:], wukv[:, 512:1024])
                kvff = kvf[:].rearrange("p h d -> p (h d)")
                P.evac(kvff[:, 0:512], pk1[:]); P.evac(kvff[:, 512:1024], pk2[:])
                j2 = junk2[:, :512].rearrange("p (h d) -> p h d", h=8)
                P.act(j2, kvf[:, :, 0:64], AF.Square)
                P.red("vector", ss16[:, 8:16], j2)
                P.act(junk[:, :32], krope, AF.Square, accum=ssr[:, 0:1])
                P.ts("vector", ss16[:, 8:16], ss16[:, 8:16], ssr[:, 0:1], None, op0=ALU.add)
                if new:
                    P.act(junk[:].rearrange("p (h d) -> p h d", h=8), qf[:], AF.Square)
                    P.red("vector", ss16[:, 0:8], junk[:].rearrange("p (h d) -> p h d", h=8))
                else:
                    P.memset("vector", ss16[:, 0:8], 1.0)
                rstd_of(ss16[:], ss16[:], 96, NORM_EPS)
                P.tt("vector", kn[:, :, 0:64], kvf[:, :, 0:64], ss16[:, 8:16].unsqueeze(2).broadcast_to([128, 8, 64]), ALU.mult)
                P.tt(PENG, kn[:, :, 0:64], kn[:, :, 0:64], gkn[:, 0:64].unsqueeze(1).broadcast_to([128, 8, 64]), ALU.mult)
                P.tt("vector", kr[:], krope, gkn[:, 64:96], ALU.mult)
                if new and jb["ctx"]:
                    rope(kr[:].unsqueeze(1), r_[:, 0, :], r_[:, 1, :], 1, 8, junk, junk2)
                P.tt("vector", kn[:, :, 64:96], kr[:].unsqueeze(1).broadcast_to([128, 8, 32]),
                     ss16[:, 8:16].unsqueeze(2).broadcast_to([128, 8, 32]), ALU.mult)
                P.copy(PENG, v_[:, :, 0:64], kvf[:, :, 64:128])
                P.dma(VPs[krow:krow + 128], v_[:], wk=[(VPs.name, krow)], eng=STQ)
                kt_ = kT[it % 2]
                for hh in range(2):
                    for h4 in range(4):
                        P.tr(pT[hh][:, h4, :], kn[:, hh * 4 + h4, :], ident[:])
                    P.evac(kt_[:, hh * 4:(hh + 1) * 4, :], pT[hh][:])
                P.dma(KTs[:, :, krow:krow + 128].rearrange("h d t -> d h t"), kt_[:], wk=[(KTs.name, krow)], eng=STQ)
                if new:
                    P.tt("vector", qf[:], qf[:], ss16[:, 0:8].unsqueeze(2).broadcast_to([128, 8, 96]), ALU.mult)
                    P.tt(PENG, qf[:], qf[:], gqn[:].unsqueeze(1).broadcast_to([128, 8, 96]), ALU.mult)
                    if jb["ctx"]:
                        rope(qf[:, :, 64:96], r_[:, 0, :], r_[:, 1, :], 8, 8, junk, junk2)
                    qt_ = qT[it % 2]
                    for hh in range(2):
                        for h4 in range(4):
                            P.tr(pT[hh][:, h4, :], qf[:, hh * 4 + h4, :], ident[:])
                        P.evac(qt_[:, hh * 4:(hh + 1) * 4, :], pT[hh][:])
                    P.dma(QTs[:, :, i * 128:(i + 1) * 128].rearrange("h d t -> d h t"), qt_[:], wk=[(QTs.name, i)], eng=STQ)
            P.pop()
            if os.environ.get("KSTOP", "") == "a1":
                continue
            P.push()
            QC = 256
            KT = [P.sb(f"KT{i}", [96, NK], FDT) for i in range(2)]
            VP = [P.sb(f"VP{i}", [128, NKT, 65], FDT) for i in range(2)]
            QT = [P.sb(f"QT{i}", [96, QC], FDT) for i in range(3)]
            pTs = [P.sb(f"pTs{i}", [128, NKT, QC], FDT) for i in range(2)]
            oT = [P.sb(f"oT{i}", [65, QC], FDT) for i in range(2)]; rec = P.sb("rec", [64, QC]); yT = [P.sb(f"yT{i}", [64, QC], FDT) for i in range(2)]
            sel65 = P.sb("sel65", [65, 64], FDT)
            sel65f = P.sb("sel65f", [65, 64])
            P.memset("vector", sel65f[:], 0.0); P.memset("vector", sel65f[64:65, :], 1.0)
            P.copy("vector", sel65[:], sel65f[:])
            pss = [P.ps(f"pss{i}", [128, 512]) for i in range(4)]
            po = [P.ps(f"po{i}", [65, QC]) for i in range(2)]
            pb = P.ps("pb", [64, QC])
            units = [(h, qc) for h in range(8) for qc in range(S // QC)]
            ik = [0]

            def qk(iu):
                h, qc = units[iu]
                K_ = KT[h % 2]; V_ = VP[h % 2]
                if qc == 0:
                    P.dma(K_[:], KTs[h], rk=[KTs.name + "*"])
                    P.dma(V_[:], VPs[:, h, :].rearrange("(kt p) c -> p kt c", p=128), rk=[VPs.name + "*"])
                Q_ = QT[iu % 3]; p_ = pTs[iu % 2]
                P.dma(Q_[:], QTs[h, :, qc * QC:(qc + 1) * QC], rk=[QTs.name + "*"])
                for kt in range(NKT):
                    ps_ = pss[ik[0] % 4]; ik[0] += 1
                    P.mm(ps_[:, :QC], K_[:, kt * 128:(kt + 1) * 128], Q_[:], fast=True)
                    P.act(p_[:, kt, :], ps_[:, :QC], AF.Exp, scale=MLA_SCALE)

            def pv(iu):
                h, qc = units[iu]
                V_ = VP[h % 2]; p_ = pTs[iu % 2]; po_ = po[iu % 2]; o_ = oT[iu % 2]; yT_ = yT[iu % 2]
                for kt in range(NKT):
                    P.mm(po_[:], V_[:, kt, :], p_[:, kt, :], start=(kt == 0), stop=(kt == NKT - 1), fast=True)
                P.copy("scalar", o_[:], po_[:])
                P.mm(pb[:], sel65[:], o_[:], fast=True)
                P.recip(rec[:], pb[:])
                P.tt("vector", yT_[:], o_[0:64, :], rec[:], ALU.mult)
                c0 = t_off + qc * QC
                P.dma(jb["YT"][0, h * 64:(h + 1) * 64, c0:c0 + QC], yT_[:], wk=[(jb["YT"].name, 0, h, c0)], eng=STQ)

            qk(0)
            for iu in range(len(units)):
                if iu + 1 < len(units):
                    qk(iu + 1)
                pv(iu)
            P.pop()


    def phaseC(l, jb):
        for si, (t_off, S) in enumerate(jb["seqs"]):
            NT = S // 128
            NCT = (CTX // 128) if jb["ctx"] else 0
            NKT = NT + NCT
            P.push()
            gq = bcast_row("gq", I["swa_q_norm"][l], 64); gk = bcast_row("gk", I["swa_k_norm"][l], 64)
            esink = bcast_row("esink", I["swa_sink"][l], 8)
            P.act(esink[:], esink[:], AF.Exp)
            tri = P.sb("tri", [128, 2, 128])
            P.dma(tri[:], I["k_tri"].rearrange("a k q -> k a q"))
            KTa = P.sb("KTa", [128, NKT * 128]); VPa = P.sb("VPa", [128, NKT, 2, 65])
            P.memset("vector", VPa[:, :, :, 64:65], 1.0, wk=[VPa])
            uk = [P.sb(f"uk{i}", [128, 256]) for i in range(2)]
            uq = [P.sb(f"uq{i}", [128, 512]) for i in range(2)]
            rt = [P.sb(f"rt{i}", [128, 2, 64]) for i in range(2)]
            junk = P.sb("junk", [128, 512]); junk2 = P.sb("junk2", [128, 512]); ss = P.sb("ss", [128, 8])
            kk = [P.sb(f"kk{i}", [128, 2, 64]) for i in range(2)]
            qp = P.sb("qp", [128, 4, 2, 64]); QT = [P.sb(f"QT{i}", [128, 4, 128]) for i in range(2)]
            pTa = [P.sb(f"pTa{i}", [128, 5, 4, 128]) for i in range(2)]
            yc = P.sb("yc", [128, 8, 64]); den = P.sb("den", [128, 4, 1]); ycT = [P.sb(f"ycT{i}", [128, 4, 128], FDT) for i in range(2)]
            ptk = P.ps("ptk", [128, 128]); ptq = P.ps("ptq", [128, 4, 128])
            pss = [P.ps(f"pss{i}", [128, 512]) for i in range(3)]
            po = [P.ps(f"po{i}", [128, 4, 65]) for i in range(2)]
            pyt = P.ps("pyt", [128, 4, 128])
            for i in range(NKT):
                new = i < NT
                u = uk[i % 2]; k_ = kk[i % 2]; r_ = rt[i % 2]
                if new:
                    rows = slice(t_off + i * 128, t_off + (i + 1) * 128)
                    P.dma(u[:], jb["UT"][rows, C_SK:C_SK + 256], rk=[(jb["UT"].name, (t_off + i * 128) // 512, 2480)])
                    kv = u[:, 0:128].rearrange("p (g d) -> p g d", g=2)
                    P.act(junk[:, :128].rearrange("p (g d) -> p g d", g=2), kv, AF.Square)
                    P.red("vector", ss[:, 0:2], junk[:, :128].rearrange("p (g d) -> p g d", g=2))
                    rstd_of(ss[:, 0:2], ss[:, 0:2], 64, NORM_EPS)
                    P.tt("vector", k_[:], kv, ss[:, 0:2].unsqueeze(2).broadcast_to([128, 2, 64]), ALU.mult)
                    P.tt("vector", k_[:], k_[:], gk[:].unsqueeze(1).broadcast_to([128, 2, 64]), ALU.mult)
                    if not jb["ctx"]:
                        P.dma(O["o_swk"][si, l, i * 128:(i + 1) * 128, :], k_[:].rearrange("p g d -> p (g d)"), wk=[("o_swk", si, l, i)], eng=STQ, final=True)
                        P.dma(O["o_swv"][si, l, i * 128:(i + 1) * 128, :], u[:, 128:256], wk=[("o_swv", si, l, i)], eng=STQ, final=True)
                    else:
                        P.dma(r_[:], I["k_rope64"][i * 128:(i + 1) * 128])
                        rope(k_[:], r_[:, 0, :], r_[:, 1, :], 2, 16, junk, junk2)
                    ksrc = k_[:].rearrange("p g d -> p (g d)")
                    vsrc = u[:, 128:256]
                else:
                    c = i - NT
                    P.dma(u[:, 0:128], I["cswk"][l, c * 128:(c + 1) * 128, :]); P.dma(u[:, 128:256], I["cswv"][l, c * 128:(c + 1) * 128, :])
                    ksrc = u[:, 0:128]; vsrc = u[:, 128:256]
                P.tr(ptk[:], ksrc, ident[:])
                P.copy("vector", KTa[:, i * 128:(i + 1) * 128], ptk[:])
                P.copy("vector", VPa[:, i, :, 0:64], vsrc.rearrange("p (g d) -> p g d", g=2))
            units = [(b, g) for b in range(NT) for g in range(2)]
            ik = [0]

            def ktiles(b):
                if not jb["ctx"]:
                    return [(kt, None) for kt in range(NT)]
                lst = []
                if b > 0:
                    lst.append((b - 1, 0))
                lst.append((b, None))
                if b < NT - 1:
                    lst.append((b + 1, 1))
                return lst + [(NT + c, None) for c in range(NCT)]

            def qprep(b):
                u = uq[b % 2]; r_ = rt[b % 2]; Q_ = QT[b % 2]
                rows = slice(t_off + b * 128, t_off + (b + 1) * 128)
                P.dma(u[:], jb["UT"][rows, C_SQ:C_SQ + 512], rk=[(jb["UT"].name, (t_off + b * 128) // 512, 1968)])
                qv = u[:].rearrange("p (h d) -> p h d", h=8)
                P.act(junk[:].rearrange("p (h d) -> p h d", h=8), qv, AF.Square)
                P.red("vector", ss[:], junk[:].rearrange("p (h d) -> p h d", h=8))
                rstd_of(ss[:], ss[:], 64, NORM_EPS)
                P.tt("vector", qv, qv, ss[:].unsqueeze(2).broadcast_to([128, 8, 64]), ALU.mult)
                P.tt("vector", qv, qv, gq[:].unsqueeze(1).broadcast_to([128, 8, 64]), ALU.mult)
                if jb["ctx"]:
                    P.dma(r_[:], I["k_rope64"][b * 128:(b + 1) * 128])
                    rope(qv, r_[:, 0, :], r_[:, 1, :], 8, 16, junk, junk2)
                P.copy("vector", qp[:].rearrange("p r g d -> p g r d"), u[:].rearrange("p (g r d) -> p g r d", g=2, r=4))
                for r in range(4):
                    P.tr(ptq[:, r, :], qp[:, r, :, :].rearrange("p g d -> p (g d)"), ident[:])
                P.copy("scalar", Q_[:], ptq[:])

            def qk(iu):
                b, g = units[iu]
                if g == 0:
                    qprep(b)
                Q_ = QT[b % 2]; p_ = pTa[iu % 2]
                pr = slice(g * 64, (g + 1) * 64)
                for j, (kt, mk) in enumerate(ktiles(b)):
                    ps_ = pss[ik[0] % 3]; ik[0] += 1
                    P.mm(ps_[:], KTa[pr, kt * 128:(kt + 1) * 128], Q_[pr, :, :].rearrange("p r q -> p (r q)"))
                    P.act(p_[:, j, :, :].rearrange("p r q -> p (r q)"), ps_[:], AF.Exp, scale=SW_SCALE)
                    if mk is not None:
                        P.tt("vector", p_[:, j, :, :], p_[:, j, :, :], tri[:, mk, :].unsqueeze(1).broadcast_to([128, 4, 128]), ALU.mult)

            def pv(iu):
                b, g = units[iu]
                p_ = pTa[iu % 2]; po_ = po[iu % 2]
                kts = ktiles(b)
                for r in range(4):
                    for j, (kt, mk) in enumerate(kts):
                        P.mm(po_[:, r, :], p_[:, j, r, :], VPa[:, kt, g, :], start=(j == 0), stop=(j == len(kts) - 1))
                P.tt("vector", den[:], po_[:, :, 64:65], esink[:, g * 4:(g + 1) * 4].unsqueeze(2), ALU.add)
                P.recip(den[:], den[:])
                P.tt("vector", yc[:, g * 4:(g + 1) * 4, :], po_[:, :, 0:64], den[:].broadcast_to([128, 4, 64]), ALU.mult)
                if g == 1:
                    yT_ = ycT[b % 2]
                    for c in range(4):
                        P.tr(pyt[:, c, :], yc[:, 2 * c:2 * c + 2, :].rearrange("p h d -> p (h d)"), ident[:])
                    P.copy("scalar", yT_[:], pyt[:])
                    c0 = t_off + b * 128
                    P.dma(jb["YT"][2, :, c0:c0 + 128].rearrange("(c p) t -> p c t", p=128), yT_[:], wk=[(jb["YT"].name, 2, c0)], eng=STQ)

            qk(0)
            for iu in range(len(units)):
                if iu + 1 < len(units):
                    qk(iu + 1)
                pv(iu)
            P.pop()


    def phaseB(l, jb):
        n = jb["name"]
        for si, (t_off, S) in enumerate(jb["seqs"]):
            NC = S // 128
            HS = scr(f"B_HS_{n}{l}_{si}", [2, S, 512])
            P.push()
            tri = P.sb("tri", [128, 2, 128]); neg = P.sb("neg", [128, 2, 128]); sel = P.sb("sel", [128, 2, 128])
            P.dma(tri[:], I["k_tri"].rearrange("a k q -> k a q")); P.dma(sel[:], I["k_sel"].rearrange("a k q -> k a q"))
            P.ts("vector", neg[:], tri[:], -1.0, 1e30, op0=ALU.add, op1=ALU.mult)
            TRI = [tri[:, 1, :], tri[:, 0, :]]
            NEG = [neg[:, 1, :], neg[:, 0, :]]
            NEGts = [neg[:, 0, :], neg[:, 1, :]]
            bias16 = P.sb("bias16", [128, 2, 8])
            P.dma(bias16[:, 0, :], I["mlstm_i_bias"][l].partition_broadcast(128), wk=[bias16])
            P.dma(bias16[:, 1, :], I["mlstm_f_bias"][l].partition_broadcast(128), wk=[bias16])
            Cst = P.sb("Cst", [64, 8, 129]); mprev = P.sb("mprev", [128, 8])
            if jb["ctx"]:
                P.dma(Cst[:, :, 0:128], I["mC"][l].rearrange("d h k v -> k (d h) v"), wk=[Cst])
                P.dma(Cst[:, :, 128:129], I["mn"][l].rearrange("d h (k o) -> k (d h) o", o=1), wk=[Cst], allow_slow_non_contiguous=True)
                P.dma(mprev[:], I["mm"][l].rearrange("d h -> (d h)").partition_broadcast(128))
            else:
                P.memset("vector", Cst[:], 0.0); P.memset("vector", mprev[:], 0.0)
            G = [P.sb(f"G{i}", [128, 2, 8]) for i in range(2)]
            QTd = [P.sb(f"QTd{i}", [64, 2, 4, 128]) for i in range(2)]; KTd = [P.sb(f"KTd{i}", [64, 2, 4, 128]) for i in range(2)]
            Kt = [P.sb(f"Kt{i}", [128, 2, 4, 64]) for i in range(2)]; VPd = [P.sb(f"VPd{i}", [128, 2, 4, 129]) for i in range(2)]
            for v in VPd:
                P.memset("vector", v[:, :, :, 128:129], 1.0, wk=[v])
            sp = P.sb("sp", [128, 8]); b = P.sb("b", [128, 8]); li = P.sb("li", [128, 8]); c = P.sb("c", [128, 8])
            MB = P.sb("MB", [128, 2, 2, 4]); cmax = P.sb("cmax", [128, 8]); bm = P.sb("bm", [128, 8]); ain = P.sb("ain", [128, 8]); en = P.sb("en", [128, 8])
            DG = P.sb("DG", [128, 8, 128]); DG2 = P.sb("DG2", [128, 8, 128]); Rm = P.sb("Rm", [128, 8, 128])
            ET = P.sb("ET", [128, 8, 128]); WT = P.sb("WT", [128, 8, 128])
            tI = P.sb("tI", [128, 8, 129]); numS = P.sb("numS", [128, 8, 129]); dab = P.sb("dab", [128, 8]); hh = P.sb("hh", [128, 8, 128])
            mbl = P.sb("mbl", [128, 2, 2, 4]); wk_ = P.sb("wk", [128, 8]); dec = P.sb("dec", [128, 8]); KW = P.sb("KW", [128, 8, 64])
            pA = [P.ps(f"pA{i}", [128, 4, 128]) for i in range(2)]
            pB = [P.ps(f"pB{i}", [128, 4, 128]) for i in range(2)]
            pC = [P.ps(f"pC{i}", [128, 3, 129]) for i in range(3)]
            pD = P.ps("pD", [128, 2, 8])
            grp3 = [(0, 0, 3), (1, 3, 6), (2, 6, 8)]

            def pc_slot(dh):
                return pC[dh // 3], dh % 3

            for j in range(NC):
                g_ = G[j % 2]; qt = QTd[j % 2]; kt = KTd[j % 2]; ktok = Kt[j % 2]; vp = VPd[j % 2]
                cd = [j, NC - 1 - j]
                for d in range(2):
                    r0 = t_off + cd[d] * 128
                    rows = slice(r0, r0 + 128)
                    sk = r0 // 512
                    P.dma(g_[:, :, d * 4:(d + 1) * 4], jb["UT"][rows, C_MI:C_MI + 16].rearrange("p (a e) -> p a e", a=2)[:, :, d * 4:(d + 1) * 4],
                          rk=[(jb["UT"].name, sk, 1440)], wk=[g_])
                    P.dma(qt[:, d, :, :], jb["UF"][C_MQ:C_MQ + 256, r0:r0 + 128].rearrange("(h p) t -> p h t", p=64), rk=[(jb["UF"].name, sk, 416)], wk=[qt])
                    P.dma(kt[:, d, :, :], jb["UF"][C_MK:C_MK + 256, r0:r0 + 128].rearrange("(h p) t -> p h t", p=64), rk=[(jb["UF"].name, sk, 672)], wk=[kt])
                    P.dma(ktok[:, d, :, :], jb["UT"][rows, C_MK:C_MK + 256].rearrange("p (h e) -> p h e", h=4), rk=[(jb["UT"].name, sk, 672)], wk=[ktok])
                    P.dma(vp[:, d, :, 0:128], jb["UT"][rows, C_MV:C_MV + 512].rearrange("p (h e) -> p h e", h=4), rk=[(jb["UT"].name, sk, 928)], wk=[vp])
                P.act(qt[:], qt[:], AF.Copy, scale=0.125)
                P.tt("vector", g_[:], g_[:], bias16[:], ALU.add)
                P.copy("vector", li[:], g_[:, 0, :])
                P.act(sp[:], g_[:, 1, :], AF.Exp, scale=-1.0)
                P.act(sp[:], sp[:], AF.Ln, bias=1.0)
                for d in range(2):
                    P.mm(pD[:, 0, d * 4:(d + 1) * 4], TRI[d], sp[:, d * 4:(d + 1) * 4])
                P.act(b[:], pD[:, 0, :], AF.Copy, scale=-1.0)
                P.tt("vector", c[:], li[:], b[:], ALU.subtract)
                P.tt("vector", DG[:], ident[:].unsqueeze(1).broadcast_to([128, 8, 128]), c[:].unsqueeze(2).broadcast_to([128, 8, 128]), ALU.mult)
                for dh in range(8):
                    P.mm(pA[dh // 4][:, dh % 4, :], ones[:], DG[:, dh, :])
                for d in range(2):
                    P.tt("vector", Rm[:, d * 4:(d + 1) * 4, :], pA[d][:], NEGts[d].unsqueeze(1).broadcast_to([128, 4, 128]), ALU.add)
                P.red("vector", cmax[:], Rm[:], op=ALU.max)
                v24 = lambda t: t.rearrange("p (d h) -> p d h", d=2)
                mt = MB[:, :, 0, :]
                P.tt("vector", mt, v24(mprev[:]), v24(cmax[:]), ALU.max)
                P.tt("vector", mt, mt, v24(b[:]), ALU.add)
                P.copy("vector", MB[:, :, 1, :], v24(b[:]))
                P.tt("vector", v24(bm[:]), v24(b[:]), mt, ALU.subtract)
                P.tt("vector", ain[:], bm[:], mprev[:], ALU.add)
                P.act(ain[:], ain[:], AF.Exp)
                P.act(v24(en[:]), mt, AF.Exp, scale=-1.0)
                P.tt("vector", DG2[:], ident[:].unsqueeze(1).broadcast_to([128, 8, 128]), bm[:].unsqueeze(2).broadcast_to([128, 8, 128]), ALU.mult)
                for dh in range(8):
                    o_ = pA[dh // 4][:, dh % 4, :]
                    P.mm(o_, ones[:], DG2[:, dh, :], start=True, stop=False)
                    P.mm(o_, DG[:, dh, :], ones[:], start=False, stop=False)
                    P.mm(o_, ident[:], NEG[dh // 4], start=False, stop=True)
                for d in range(2):
                    P.act(ET[:, d * 4:(d + 1) * 4, :], pA[d][:], AF.Exp)
                for dh in range(8):
                    d, h = dh // 4, dh % 4
                    P.mm(pB[d][:, h, :], kt[:, d, h, :], qt[:, d, h, :])
                for d in range(2):
                    P.tt("vector", WT[:, d * 4:(d + 1) * 4, :], ET[:, d * 4:(d + 1) * 4, :], pB[d][:], ALU.mult)
                for dh in range(8):
                    d, h = dh // 4, dh % 4
                    pc, sl = pc_slot(dh)
                    P.mm(pc[:, sl, :], qt[:, d, h, :], Cst[:, dh, :])
                for (bk, lo, hi) in grp3:
                    P.tt("vector", tI[:, lo:hi, :], pC[bk][:, 0:hi - lo, :], ain[:, lo:hi].unsqueeze(2).broadcast_to([128, hi - lo, 129]), ALU.mult)
                for dh in range(8):
                    d, h = dh // 4, dh % 4
                    pc, sl = pc_slot(dh)
                    P.mm(pc[:, sl, :], WT[:, dh, :], vp[:, d, h, :])
                for (bk, lo, hi) in grp3:
                    P.tt("vector", numS[:, lo:hi, :], pC[bk][:, 0:hi - lo, :], tI[:, lo:hi, :], ALU.add)
                P.act(dab[:].unsqueeze(2), numS[:, :, 128:129], AF.Abs)
                P.tt("vector", dab[:], dab[:], en[:], ALU.max)
                P.recip(dab[:], dab[:])
                P.tt("vector", hh[:], numS[:, :, 0:128], dab[:].unsqueeze(2).broadcast_to([128, 8, 128]), ALU.mult)
                for d in range(2):
                    r0 = cd[d] * 128
                    P.dma(HS[d, r0:r0 + 128, :].rearrange("p (h e) -> p h e", h=4), hh[:, d * 4:(d + 1) * 4, :], wk=[(HS.name, d, cd[d])], eng=STQ)
                for d in range(2):
                    P.mm(pD[:, d, :].rearrange("p (a h) -> p a h", a=2).rearrange("p a h -> p (a h)"), sel[:, d, :], MB[:, d, :, :].rearrange("p a h -> p (a h)"))
                P.copy("vector", mbl[:].rearrange("p d a h -> p (d a h)"), pD[:].rearrange("p a e -> p (a e)"))
                P.tt("vector", v24(wk_[:]), mbl[:, :, 1, :], mbl[:, :, 0, :], ALU.subtract)
                P.tt("vector", dec[:], wk_[:], mprev[:], ALU.add)
                P.act(dec[:], dec[:], AF.Exp)
                P.tt("vector", wk_[:], wk_[:], c[:], ALU.add)
                P.act(wk_[:], wk_[:], AF.Exp)
                for d in range(2):
                    P.tt("vector", KW[:, d * 4:(d + 1) * 4, :], ktok[:, d, :, :], wk_[:, d * 4:(d + 1) * 4].unsqueeze(2).broadcast_to([128, 4, 64]), ALU.mult)
                for dh in range(8):
                    d, h = dh // 4, dh % 4
                    pc, sl = pc_slot(dh)
                    P.mm(pc[0:64, sl, :], KW[:, dh, :], vp[:, d, h, :])
                P.tt("vector", Cst[:], Cst[:], dec[0:64, :].unsqueeze(2).broadcast_to([64, 8, 129]), ALU.mult)
                for (bk, lo, hi) in grp3:
                    P.tt("vector", Cst[:, lo:hi, :], Cst[:, lo:hi, :], pC[bk][0:64, 0:hi - lo, :], ALU.add)
                P.copy("vector", v24(mprev[:]), mbl[:, :, 0, :])
            if not jb["ctx"]:
                P.dma(O["o_mC"][si, l].rearrange("d h k v -> k (d h) v"), Cst[:, :, 0:128], wk=[("o_mC", si, l)], eng=STQ, final=True)
                P.dma(O["o_mn"][si, l].rearrange("d h (k o) -> k (d h) o", o=1), Cst[:, :, 128:129], wk=[("o_mn", si, l)], eng=STQ, final=True, allow_slow_non_contiguous=True)
                P.dma(O["o_mm"][si, l].rearrange("d (h o) -> o (d h)", o=1), mprev[0:1, :], wk=[("o_mm", si, l)], eng=STQ, final=True, allow_slow_non_contiguous=True)
            P.pop()
            P.push()
            gm = bcast_row("gm", I["mlstm_norm"][l], 128)
            h0 = [P.sb(f"h0{i}", [128, 4, 128]) for i in range(2)]; h1 = [P.sb(f"h1{i}", [128, 4, 128]) for i in range(2)]
            og = [P.sb(f"og{i}", [128, 512]) for i in range(2)]
            junk = P.sb("junk", [128, 4, 128]); ss = P.sb("ss", [128, 4]); yT = [P.sb(f"yT{i}", [128, 4, 128], FDT) for i in range(2)]
            pyt = P.ps("pyt", [128, 4, 128])
            for i in range(NC):
                a_ = h0[i % 2]; b_ = h1[i % 2]; o_ = og[i % 2]; y_ = yT[i % 2]
                rows = slice(t_off + i * 128, t_off + (i + 1) * 128)
                P.dma(a_[:], HS[0, i * 128:(i + 1) * 128, :].rearrange("p (h e) -> p h e", h=4), rk=[HS.name + "*"])
                P.dma(b_[:], HS[1, i * 128:(i + 1) * 128, :].rearrange("p (h e) -> p h e", h=4), rk=[HS.name + "*"])
                P.dma(o_[:], jb["UT"][rows, C_MO:C_MO + 512], rk=[(jb["UT"].name, (t_off + i * 128) // 512, 1456)])
                P.tt("vector", a_[:], a_[:], b_[:], ALU.add)
                P.act(junk[:], a_[:], AF.Square)
                P.red("vector", ss[:], junk[:])
                rstd_of(ss[:], ss[:], 128, NORM_EPS)
                P.act(o_[:], o_[:], AF.Sigmoid)
                P.tt("vector", a_[:], a_[:], ss[:].unsqueeze(2).broadcast_to([128, 4, 128]), ALU.mult)
                P.tt("vector", a_[:], a_[:], gm[:].unsqueeze(1).broadcast_to([128, 4, 128]), ALU.mult)
                P.tt("vector", a_[:], a_[:], o_[:].rearrange("p (h e) -> p h e", h=4), ALU.mult)
                for h in range(4):
                    P.tr(pyt[:, h, :], a_[:, h, :], ident[:])
                P.copy("scalar", y_[:], pyt[:])
                c0 = t_off + i * 128
                P.dma(jb["YT"][1, :, c0:c0 + 128].rearrange("(c p) t -> p c t", p=128), y_[:], wk=[(jb["YT"].name, 1, c0)], eng=STQ)
            P.pop()


    def phaseD(l, jb):
        n = jb["name"]
        CW = RW_DECAY
        for si, (t_off, S) in enumerate(jb["seqs"]):
            NCH = S // 64
            DF = scr(f"D_F_{n}{l}_{si}", [2, NCH, 2, 64, 4, 4, 64])
            LW = scr(f"D_LW_{n}{l}_{si}", [S, 2, 512])
            BG = scr(f"D_BG_{n}{l}_{si}", [2, 512, S])
            YS = scr(f"D_YS_{n}{l}_{si}", [2, S, 512])
            P.push()
            ST = min(512, S)
            kk = P.sb("kk", [64, 8]); ka = P.sb("ka", [64, 8]); omka = P.sb("omka", [64, 8]); uu = P.sb("uu", [64, 2, 8]); a0 = P.sb("a0", [64, 2, 8])
            P.dma(kk[:], I["rwkv_kk64"][l]); P.dma(ka[:], I["rwkv_ka64"][l]); P.dma(uu[:], I["rwkv_u64"][l]); P.dma(a0[:], I["rwkv_a064"][l])
            P.ts("vector", omka[:], ka[:], -1.0, 1.0, op0=ALU.mult, op1=ALU.add)
            w2 = P.sb("w2", [64, 2, 512]); a2 = P.sb("a2", [64, 2, 512]); g2 = P.sb("g2", [128, 512])
            P.dma(w2[:], I["rwkv_w2"][l].rearrange("d r c -> r d c")); P.dma(a2[:], I["rwkv_a2"][l].rearrange("d r c -> r d c")); P.dma(g2[:], I["rwkv_g2"][l])
            w0row = bcast_row("w0row", I["rwkv_w0"][l].rearrange("d c -> (d c)"), 1024)
            rT = P.sb("rT", [64, 8, ST]); kT = P.sb("kT", [64, 8, ST]); vT = P.sb("vT", [64, 8, ST])
            w1T = P.sb("w1T", [64, 2, ST]); a1T = P.sb("a1T", [64, 2, ST]); g1T = P.sb("g1T", [128, ST])
            kap = P.sb("kap", [64, 8, ST]); kh = P.sb("kh", [64, 8, ST]); tA = P.sb("tA", [64, 8, ST]); tB = P.sb("tB", [64, 8, ST])
            ktt = P.sb("ktt", [64, 8, ST]); rku = P.sb("rku", [64, 8, ST]); lw = [P.sb(f"lw{i}", [128, 2, 512]) for i in range(2)]
            pp = [P.ps(f"pp{i}", [128, 512]) for i in range(4)]
            ip = [0]

            def nps():
                ip[0] += 1
                return pp[ip[0] % 4]

            def store_df(d, arr, tile, s0):
                for cc in range(ST // 64):
                    for hh in range(2):
                        P.dma(DF[d, s0 // 64 + cc, hh, :, arr, :, :], tile[:, hh * 4:(hh + 1) * 4, cc * 64:(cc + 1) * 64], wk=[(DF.name, d, arr, s0, cc, hh)], eng=STQ)

            for s_ in range(S // ST):
                s0 = s_ * ST
                c0 = t_off + s0
                sk = c0 // 512
                UF = jb["UF"]
                P.dma(rT[:], UF[C_RR:C_RR + 512, c0:c0 + ST].rearrange("(h p) t -> p h t", p=64), rk=[(UF.name, sk, 2736)])
                P.dma(kT[:], UF[C_RK:C_RK + 512, c0:c0 + ST].rearrange("(h p) t -> p h t", p=64), rk=[(UF.name, sk, 3248)])
                P.dma(vT[:], UF[C_RV:C_RV + 512, c0:c0 + ST].rearrange("(h p) t -> p h t", p=64), rk=[(UF.name, sk, 3760)])
                P.dma(w1T[:], UF[C_RW:C_RW + 128, c0:c0 + ST].rearrange("(d p) t -> p d t", p=64), rk=[(UF.name, sk, 4272)])
                P.dma(a1T[:], UF[C_RA:C_RA + 128, c0:c0 + ST].rearrange("(d p) t -> p d t", p=64), rk=[(UF.name, sk, 4272)])
                P.dma(g1T[:], UF[C_RG:C_RG + 128, c0:c0 + ST], rk=[(UF.name, sk, 4272)])
                P.act(w1T[:], w1T[:], AF.Tanh)
                P.act(g1T[:], g1T[:], AF.Sigmoid)
                P.tt("vector", kap[:], kT[:], kk[:].unsqueeze(2).broadcast_to([64, 8, ST]), ALU.mult)
                P.act(tA[:], kap[:], AF.Square)
                for h in range(8):
                    ps_ = nps()
                    P.mm(ps_[0:64, :ST], ones[0:64, 0:64], tA[:, h, :])
                    P.act(kh[:, h, :], ps_[0:64, :ST], AF.Sqrt, bias=1e-12)
                P.recip(kh[:], kh[:])
                P.tt("vector", kh[:], kh[:], kap[:], ALU.mult)
                for d in range(2):
                    store_df(d, 0, rT, s0); store_df(d, 1, kh, s0)
                for d in range(2):
                    for h in range(8):
                        ps_ = nps()
                        P.mm(ps_[0:64, :ST], a2[:, d, h * 64:(h + 1) * 64], a1T[:, d, :])
                        P.act(tA[:, h, :], ps_[0:64, :ST], AF.Sigmoid, bias=a0[:, d, h:h + 1])
                    P.tt("vector", tB[:], tA[:], kh[:], ALU.mult)
                    store_df(d, 3, tB, s0)
                    P.tt("vector", tA[:], tA[:], ka[:].unsqueeze(2).broadcast_to([64, 8, ST]), ALU.mult)
                    P.tt("vector", tA[:], tA[:], omka[:].unsqueeze(2).broadcast_to([64, 8, ST]), ALU.add)
                    P.tt("vector", ktt[:], kT[:], tA[:], ALU.mult)
                    store_df(d, 2, ktt, s0)
                    P.tt("vector", tA[:], ktt[:], rT[:], ALU.mult)
                    if d == 0:
                        P.tt("vector", rku[:], tA[:], uu[:, d, :].unsqueeze(2).broadcast_to([64, 8, ST]), ALU.mult)
                    else:
                        P.tt("vector", tA[:], tA[:], uu[:, d, :].unsqueeze(2).broadcast_to([64, 8, ST]), ALU.mult)
                        P.tt("vector", rku[:], rku[:], tA[:], ALU.add)
                for h in range(8):
                    ps_ = nps()
                    P.mm(ps_[0:64, :ST], ones[0:64, 0:64], rku[:, h, :])
                    P.tt("vector", tB[:, h, :], ps_[0:64, :ST], vT[:, h, :], ALU.mult)
                P.dma(BG[0, :, s0:s0 + ST].rearrange("(h p) t -> p h t", p=64), tB[:], wk=[(BG.name, 0, s0)], eng=STQ)
                for h in range(8):
                    ps_ = nps()
                    P.mm(ps_[0:64, :ST], g2[:, h * 64:(h + 1) * 64], g1T[:])
                    P.copy("scalar", kap[:, h, :], ps_[0:64, :ST])
                P.dma(BG[1, :, s0:s0 + ST].rearrange("(h p) t -> p h t", p=64), kap[:], wk=[(BG.name, 1, s0)], eng=STQ)
                for tt in range(ST // 128):
                    lw_ = lw[tt % 2]
                    for d in range(2):
                        ps_ = nps()
                        P.mm(ps_[:], w1T[:, d, tt * 128:(tt + 1) * 128], w2[:, d, :])
                        P.tt("vector", lw_[:, d, :], ps_[:], w0row[:, d * 512:(d + 1) * 512], ALU.add)
                    P.act(lw_[:], lw_[:], AF.Sigmoid)
                    P.dma(LW[s0 + tt * 128:s0 + (tt + 1) * 128], lw_[:], wk=[(LW.name, s0, tt)], eng=STQ)
            P.pop()
            P.push()
            tri = P.sb("tri", [128, 2, 128]); P.dma(tri[:], I["k_tri"].rearrange("a k q -> k a q"))
            trs = P.sb("trs", [128, 2, 128]); P.dma(trs[:], I["k_tris"].rearrange("a k q -> k a q"))
            HP = [slice(0, 64), slice(64, 128)]
            cum = P.sb("cum", [128, 2, 2, 64]); mask4 = P.sb("mask4", [128, 2, 4, 64]); maskT = P.sb("maskT", [128, 2, 64])
            for hh in range(2):
                pr = HP[hh]
                INC = [tri[pr, 1, pr], tri[pr, 0, pr]]; STR = [trs[pr, 1, pr], trs[pr, 0, pr]]
                for d in range(2):
                    P.copy("vector", cum[pr, d, 0, :], INC[d], wk=[cum]); P.copy("vector", cum[pr, d, 1, :], STR[d], wk=[cum])
                    for a_, m_ in enumerate([STR[d], INC[d], STR[d], INC[d]]):
                        P.copy("vector", mask4[pr, d, a_, :], m_, wk=[mask4])
                    P.copy("vector", maskT[pr, d, :], STR[1 - d], wk=[maskT])
            idh = [ident[HP[0], HP[0]], ident[HP[1], HP[1]]]
            id4 = P.sb("id4", [128, 4, 64])
            for hh in range(2):
                for h4 in range(4):
                    P.copy("vector", id4[HP[hh], h4, :], idh[hh], wk=[id4])
            TS = P.sb("TS", [128, 2, 4, 64])
            bG = P.ps("bG", [128, 512]); bX = [P.ps(f"bX{i}", [128, 512]) for i in range(2)]; bY = P.ps("bY", [128, 256])
            bA = P.ps("bA", [128, 512]); bP = P.ps("bP", [128, 256]); bZ = [P.ps(f"bZ{i}", [128, 256]) for i in range(2)]
            s0t = P.sb("s0t", [128, 2, 4, 64])

            def trm(out, in_, hh):
                P.mm(out, in_, idh[hh])

            if jb["ctx"]:
                for hh in range(2):
                    for d in range(2):
                        P.dma(s0t[HP[hh], d], I["rw"][l][d, hh * 4:(hh + 1) * 4].rearrange("h v k -> v h k"), wk=[s0t])
                for d in range(2):
                    for h in range(8):
                        hh, h4 = h // 4, h % 4
                        trm(bX[0][HP[hh], (d * 4 + h4) * 64:(d * 4 + h4 + 1) * 64], s0t[HP[hh], d, h4, :], hh)
                P.copy("vector", TS[:].rearrange("p d h v -> p (d h v)"), bX[0][:])
            else:
                P.memset("vector", TS[:], 0.0)
            Xd = [[P.sb(f"Xd{d}{i}", [128, 4, 4, 64]) for i in range(2)] for d in range(2)]
            Vd = [[P.sb(f"Vd{d}{i}", [128, 4, 64]) for i in range(2)] for d in range(2)]
            LWd = [[P.sb(f"LWd{d}{i}", [128, 256]) for i in range(2)] for d in range(2)]
            EI = P.sb("EI", [128, 4, 64]); EX = P.sb("EX", [128, 4, 64]); EN = P.sb("EN", [128, 4, 64]); gl = P.sb("gl", [128, 4])
            KR = P.sb("KR", [128, 4, 2, 64]); KtM = P.sb("KtM", [128, 4, 64]); BM = P.sb("BM", [128, 4, 64]); KBe = P.sb("KBe", [128, 4, 2, 64])
            AM = P.sb("AM", [128, 4, 4, 64]); N0 = P.sb("N0", [128, 4, 64])
            AB = [P.sb(f"AB{i}", [128, 4, 2, 64]) for i in range(2)]; PI = [P.sb(f"PI{i}", [128, 4, 64]) for i in range(2)]
            BI = [P.sb(f"BI{i}", [128, 4, 64]) for i in range(2)]
            RH = P.sb("RH", [128, 4, 64]); Un = P.sb("Un", [128, 4, 64]); Yo = [P.sb(f"Yo{i}", [128, 4, 64]) for i in range(2)]
            KBt = P.sb("KBt", [128, 4, 2, 64])
            HH = [(h // 4, h % 4) for h in range(8)]

            def dpass(j, d):
                c = j if d == 0 else NCH - 1 - j
                X = Xd[d][j % 2]; V = Vd[d][j % 2]; LWc = LWd[d][j % 2]
                r0 = t_off + c * 64
                for hh in range(2):
                    pr = HP[hh]
                    P.dma(X[pr], DF[d, c, hh], rk=[DF.name + "*"], wk=[X])
                    P.dma(V[pr], jb["UT"][r0:r0 + 64, C_RV + hh * 256:C_RV + (hh + 1) * 256].rearrange("p (h e) -> p h e", h=4),
                          rk=[(jb["UT"].name, r0 // 512, 3760)], wk=[V])
                    P.dma(LWc[pr], LW[c * 64:(c + 1) * 64, d, hh * 256:(hh + 1) * 256], rk=[LW.name + "*"], wk=[LWc])
                Rr = X[:, 0, :, :]; Kh = X[:, 1, :, :]; Kt = X[:, 2, :, :]; Bb = X[:, 3, :, :]
                for (hh, h4) in HH:
                    pr = HP[hh]
                    P.mm(bG[pr, h4 * 128:(h4 + 1) * 128], LWc[pr, h4 * 64:(h4 + 1) * 64], cum[pr, d, :, :].rearrange("p a t -> p (a t)"))
                gv = bG[:].rearrange("p (h a t) -> p h a t", h=4, a=2)
                P.act(EI[:], gv[:, :, 0, :], AF.Exp, scale=-CW)
                P.act(EN[:], gv[:, :, 0, :], AF.Exp, scale=CW)
                P.act(EX[:], gv[:, :, 1, :], AF.Exp, scale=-CW)
                last = 63 if d == 0 else 0
                P.copy("vector", gl[:].unsqueeze(2), EI[:, :, last:last + 1])
                P.tt("vector", KR[:, :, 1, :], Rr, EI[:], ALU.mult)
                P.tt("vector", KR[:, :, 0, :], Kh, EX[:], ALU.mult)
                P.tt("vector", KtM[:], Kt, EN[:], ALU.mult)
                P.tt("vector", BM[:], Bb, EN[:], ALU.mult)
                P.tt("vector", KBe[:, :, 0, :], KtM[:], gl[:].unsqueeze(2).broadcast_to([128, 4, 64]), ALU.mult)
                P.tt("vector", KBe[:, :, 1, :], BM[:], gl[:].unsqueeze(2).broadcast_to([128, 4, 64]), ALU.mult)
                for (hh, h4) in HH:
                    pr = HP[hh]
                    o_ = bX[h4 // 2][pr, (h4 % 2) * 256:(h4 % 2 + 1) * 256]
                    rhs = KR[pr, h4, :, :].rearrange("p a t -> p (a t)")
                    P.mm(o_[:, 0:128], KtM[pr, h4, :], rhs)
                    P.mm(o_[:, 128:256], BM[pr, h4, :], rhs)
                for q in range(2):
                    P.tt("vector", AM[:, q * 2:(q + 1) * 2, :, :], bX[q][:].rearrange("p (h a t) -> p h a t", h=2, a=4),
                         mask4[:, d, :, :].unsqueeze(1).broadcast_to([128, 2, 4, 64]), ALU.mult)
                for (hh, h4) in HH:
                    pr = HP[hh]
                    P.mm(bY[pr, h4 * 64:(h4 + 1) * 64], KR[pr, h4, 0, :], BM[pr, h4, :])
                P.tt("vector", N0[:], bY[:].rearrange("p (h s) -> p h s", h=4), maskT[:, d, :].unsqueeze(1).broadcast_to([128, 4, 64]), ALU.mult)
                A_ = lambda pr, h4: AM[pr, h4, 2, :]
                B_ = lambda pr, h4: N0[pr, h4, :]
                Pc = PI[0]
                P.tt("vector", Pc[:], id4[:], AM[:, :, 2, :], ALU.subtract)
                for lv in range(1, 6):
                    ab = AB[lv % 2]
                    for (hh, h4) in HH:
                        pr = HP[hh]
                        o_ = bA[pr, h4 * 128:(h4 + 1) * 128]
                        if lv < 5:
                            P.mm(o_[:, 0:64], B_(pr, h4), A_(pr, h4))
                        P.mm(o_[:, 64:128], A_(pr, h4), B_(pr, h4))
                    src = bA[:].rearrange("p (h a t) -> p h a t", h=4, a=2)
                    bi = BI[lv % 2]
                    P.tt("vector", bi[:], src[:, :, 1, :], id4[:], ALU.add)
                    if lv < 5:
                        P.copy("scalar", ab[:], src)
                    A_ = (lambda ab: (lambda pr, h4: ab[pr, h4, 0, :]))(ab)
                    B_ = (lambda ab: (lambda pr, h4: ab[pr, h4, 1, :]))(ab)
                    for (hh, h4) in HH:
                        pr = HP[hh]
                        P.mm(bP[pr, h4 * 64:(h4 + 1) * 64], bi[pr, h4, :], Pc[pr, h4, :])
                    Pn = PI[lv % 2]
                    P.copy("vector", Pn[:].rearrange("p h t -> p (h t)"), bP[:])
                    Pc = Pn
                for (hh, h4) in HH:
                    pr = HP[hh]
                    o_ = bZ[0][pr, h4 * 64:(h4 + 1) * 64]
                    P.mm(o_, KR[pr, h4, 0, :], TS[pr, d, h4, :], start=True, stop=False)
                    P.mm(o_, AM[pr, h4, 0, :], V[pr, h4, :], start=False, stop=True)
                P.copy("scalar", RH[:].rearrange("p h t -> p (h t)"), bZ[0][:])
                for (hh, h4) in HH:
                    pr = HP[hh]
                    P.mm(bZ[1][pr, h4 * 64:(h4 + 1) * 64], Pc[pr, h4, :], RH[pr, h4, :])
                P.act(Un[:].rearrange("p h t -> p (h t)"), bZ[1][:], AF.Copy, scale=-1.0)
                Y_ = Yo[j % 2]
                for (hh, h4) in HH:
                    pr = HP[hh]
                    o_ = bZ[0][pr, h4 * 64:(h4 + 1) * 64]
                    P.mm(o_, KR[pr, h4, 1, :], TS[pr, d, h4, :], start=True, stop=False)
                    P.mm(o_, AM[pr, h4, 1, :], V[pr, h4, :], start=False, stop=False)
                    P.mm(o_, AM[pr, h4, 3, :], Un[pr, h4, :], start=False, stop=True)
                P.copy("scalar", Y_[:].rearrange("p h t -> p (h t)"), bZ[0][:])
                for hh in range(2):
                    P.dma(YS[d, c * 64:(c + 1) * 64, hh * 256:(hh + 1) * 256], Y_[HP[hh]].rearrange("p h t -> p (h t)"), wk=[(YS.name, d, c, hh)], eng=STQ)
                for (hh, h4) in HH:
                    pr = HP[hh]
                    for a_ in range(2):
                        trm(bX[0][pr, (h4 * 2 + a_) * 64:(h4 * 2 + a_ + 1) * 64], KBe[pr, h4, a_, :], hh)
                P.copy("vector", KBt[:].rearrange("p h a t -> p (h a t)"), bX[0][:])
                for (hh, h4) in HH:
                    pr = HP[hh]
                    o_ = bZ[1][pr, h4 * 64:(h4 + 1) * 64]
                    P.mm(o_, KBt[pr, h4, 0, :], V[pr, h4, :], start=True, stop=False)
                    P.mm(o_, KBt[pr, h4, 1, :], Un[pr, h4, :], start=False, stop=True)
                Td = TS[:, d, :, :]
                P.tt("vector", Td, Td, gl[:].unsqueeze(2).broadcast_to([128, 4, 64]), ALU.mult)
                P.tt("vector", Td, Td, bZ[1][:].rearrange("p (h t) -> p h t", h=4), ALU.add)

            for j in range(NCH):
                for d in range(2):
                    dpass(j, d)
            if not jb["ctx"]:
                for d in range(2):
                    for (hh, h4) in HH:
                        trm(bX[0][HP[hh], (d * 4 + h4) * 64:(d * 4 + h4 + 1) * 64], TS[HP[hh], d, h4, :], hh)
                P.copy("vector", s0t[:].rearrange("p d h k -> p (d h k)"), bX[0][:])
                for hh in range(2):
                    for d in range(2):
                        P.dma(O["o_rw"][si, l][d, hh * 4:(hh + 1) * 4].rearrange("h v k -> v h k"), s0t[HP[hh], d], wk=[("o_rw", si, l, hh, d)], eng=STQ, final=True)
            P.pop()
            P.push()
            gng = P.sb("gng", [128, 4]); gnb = P.sb("gnb", [128, 4])
            P.dma(gng[:], I["rwkv_gn_gT"][l]); P.dma(gnb[:], I["rwkv_gn_bT"][l])
            y0 = [P.sb(f"y0{i}", [128, 8, 64]) for i in range(2)]; y1 = [P.sb(f"y1{i}", [128, 8, 64]) for i in range(2)]
            bg = [P.sb(f"bg{i}", [128, 2, 4, 128]) for i in range(2)]
            junk = P.sb("junk", [128, 8, 64]); st8 = P.sb("st8", [128, 8]); yT = [P.sb(f"yT{i}", [128, 4, 128], FDT) for i in range(2)]
            ytmp = P.sb("ytmp", [128, 4, 128])
            pyt = P.ps("pyt", [128, 4, 128])
            for i in range(S // 128):
                a_ = y0[i % 2]; b_ = y1[i % 2]; g_ = bg[i % 2]; y_ = yT[i % 2]
                P.dma(a_[:], YS[0, i * 128:(i + 1) * 128, :].rearrange("p (h e) -> p h e", h=8), rk=[YS.name + "*"])
                P.dma(b_[:], YS[1, i * 128:(i + 1) * 128, :].rearrange("p (h e) -> p h e", h=8), rk=[YS.name + "*"])
                P.dma(g_[:], BG[:, :, i * 128:(i + 1) * 128].rearrange("a (c p) t -> p a c t", p=128), rk=[BG.name + "*"])
                P.tt("vector", a_[:], a_[:], b_[:], ALU.add)
                P.red("vector", st8[:], a_[:])
                P.ts("vector", st8[:], st8[:], -1.0 / 64, None, op0=ALU.mult)
                P.tt("vector", a_[:], a_[:], st8[:].unsqueeze(2).broadcast_to([128, 8, 64]), ALU.add)
                P.act(junk[:], a_[:], AF.Square)
                P.red("vector", st8[:], junk[:])
                rstd_of(st8[:], st8[:], 64, RW_GN_EPS)
                P.tt("vector", a_[:], a_[:], st8[:].unsqueeze(2).broadcast_to([128, 8, 64]), ALU.mult)
                for c in range(4):
                    P.tr(pyt[:, c, :], a_[:, 2 * c:2 * c + 2, :].rearrange("p h e -> p (h e)"), ident[:])
                for c in range(4):
                    P.act(ytmp[:, c, :], pyt[:, c, :], AF.Identity, bias=gnb[:, c:c + 1], scale=gng[:, c:c + 1])
                P.tt("vector", ytmp[:], ytmp[:], g_[:, 0, :, :], ALU.add)
                P.tt("vector", y_[:], ytmp[:], g_[:, 1, :, :], ALU.mult)
                c0 = t_off + i * 128
                P.dma(jb["YT"][3, :, c0:c0 + 128].rearrange("(c p) t -> p c t", p=128), y_[:], wk=[(jb["YT"].name, 3, c0)], eng=STQ)
            P.pop()


    def phaseM1(l, jb):
        TOK, j = jb["TOK"], jb["j"]
        ST = 512
        P.push()
        Yb = P.sb("Yb", [128, 4, 4, ST], FDT)
        Wo = P.sb("Wo", [128, 4, 4, D], FDT)
        wo2 = P.sb("wo2", [128, 8, D], FDT)
        Gt = [P.sb(f"Gt{i}", [128, 4, ST]) for i in range(2)]
        mg = P.sb("mg", [128, 8, ST], FDT); tmp = [P.sb(f"tmp{i}", [128, ST]) for i in range(3)]
        xT = P.sb("xT", [128, 8, ST])
        pp = [P.ps(f"pp{i}", [128, 512]) for i in range(6)]
        XTv = jb["XT"].rearrange("(k p) t -> p k t", p=128)
        wnames = ["mla_w_o", "mlstm_w_o", "swa_w_o", "rwkv_w_o"]
        for b in range(4):
            P.dma(Wo[:, b, :, :], WR[wnames[b]][l].rearrange("(c p) n -> p c n", p=128), wk=[Wo])
        for k2 in range(2):
            P.dma(wo2[:, k2 * 4:(k2 + 1) * 4, :], WR["w_out"][l][k2 * 512:(k2 + 1) * 512, :].rearrange("(k p) n -> p k n", p=128), wk=[wo2])
        ip = 0; io = 0
        for s_ in range(TOK // ST):
            t0 = s_ * ST
            Y_ = Yb; x_ = xT
            for b in range(4):
                P.dma(Y_[:, b, :, :], jb["YT"][b, :, t0:t0 + ST].rearrange("(c p) t -> p c t", p=128), rk=[jb["YT"].name + "*"], wk=[Y_])
            P.dma(x_[:], XTv[:, :, t0:t0 + ST], rk=[(jb["XT"].name, s_)])
            for oc in range(8):
                G_ = Gt[io % 2]; io += 1
                P.dma(G_[:], jb["UF"][C_GATE:C_GATE + 4096, t0:t0 + ST].rearrange("(b o p) t -> p b o t", b=4, o=8)[:, :, oc, :], rk=[jb["UF"].name + "*"])
                for b in range(4):
                    ps_ = pp[ip % 6]; ip += 1
                    for c in range(4):
                        P.mm(ps_[:, :ST], Wo[:, b, c, oc * 128:(oc + 1) * 128], Y_[:, b, c, :], start=(c == 0), stop=(c == 3), fast=True)
                    if b == 0:
                        P.tt("vector", tmp[2][:], ps_[:, :ST], G_[:, b, :], ALU.mult)
                    else:
                        t_ = tmp[b % 2]
                        P.tt("vector", t_[:], ps_[:, :ST], G_[:, b, :], ALU.mult)
                        P.tt(PENG, mg[:, oc, :] if b == 3 else tmp[2][:], tmp[2][:], t_[:], ALU.add)
            for oc in range(8):
                ps_ = pp[ip % 6]; ip += 1
                for k in range(8):
                    P.mm(ps_[:, :ST], wo2[:, k, oc * 128:(oc + 1) * 128], mg[:, k, :], start=(k == 0), stop=(k == 7), fast=True)
                P.stt("vector", x_[:, oc, :], ps_[:, :ST], modT[l][:, 16 + oc, j:j + 1], x_[:, oc, :], ALU.mult, ALU.add)
            P.dma(XTv[:, :, t0:t0 + ST], x_[:], wk=[(jb["XT"].name, s_)], eng=STQ)
        P.pop()

    def phaseM2(l, jb, last):
        TOK, j = jb["TOK"], jb["j"]
        ST = 512
        P.push()
        x1 = P.sb("x1", [128, 8, ST]); sq = P.sb("sq", [128, 8, ST]); h2 = P.sb("h2", [128, 8, ST], FDT); rstd = P.sb("rstd", [128, ST])
        w1c = [P.sb(f"w1c{i}", [128, 8, 512], FDT) for i in range(2)]
        hid = P.sb("hid", [128, 32, ST], FDT); rl = [P.sb(f"rl{i}", [128, ST]) for i in range(2)]
        w2r = [P.sb(f"w2r{i}", [128, D], FDT) for i in range(6)]
        ytok = [P.sb(f"ytok{i}", [128, D]) for i in range(2)]
        bank = [P.ps(f"bk{i}", [128, 512]) for i in range(8)]
        pst = bank[7]
        XTv = jb["XT"].rearrange("(k p) t -> p k t", p=128)
        ip = 0; i1 = 0; i2 = 0
        for s_ in range(TOK // ST):
            t0 = s_ * ST
            P.dma(x1[:], XTv[:, :, t0:t0 + ST], rk=[(jb["XT"].name, s_)])
            rms_rstd_featmajor(x1, sq, pst, rstd, ST)
            P.tt("vector", sq[:], x1[:], rstd[:].unsqueeze(1).broadcast_to([128, 8, ST]), ALU.mult)
            for k in range(8):
                P.act(h2[:, k, :], sq[:, k, :], AF.Identity, bias=modT[l][:, 24 + k, j:j + 1], scale=A2[l][:, k, j:j + 1])
            for fc in range(32):
                if fc % 4 == 0:
                    w_ = w1c[i1 % 2]; i1 += 1
                    P.dma(w_[:], WR["mlp_w1"][l][:, fc * 128:(fc + 4) * 128].rearrange("(k p) n -> p k n", p=128))
                ps_ = bank[ip % 6]; ip += 1
                for k in range(8):
                    P.mm(ps_[:, :ST], w_[:, k, (fc % 4) * 128:(fc % 4 + 1) * 128], h2[:, k, :], start=(k == 0), stop=(k == 7), fast=True)
                r_ = rl[fc % 2]
                P.act(r_[:], ps_[:, :ST], AF.Relu)
                P.tt(PENG, hid[:, fc, :], r_[:], r_[:], ALU.mult)
            x2 = sq
            for fc in range(32):
                w_ = w2r[i2 % 6]; i2 += 1
                P.dma(w_[:], WR["mlp_w2"][l][fc * 128:(fc + 1) * 128, :])
                for oc in range(8):
                    P.mm(bank[oc][:, :ST], w_[:, oc * 128:(oc + 1) * 128], hid[:, fc, :], start=(fc == 0), stop=(fc == 31), fast=True)
            for oc in range(8):
                P.stt("vector", x2[:, oc, :], bank[oc][:, :ST], modT[l][:, 40 + oc, j:j + 1], x1[:, oc, :], ALU.mult, ALU.add)
            if not last:
                P.dma(XTv[:, :, t0:t0 + ST], x2[:], wk=[(jb["XT"].name, s_)], eng=STQ)
            else:
                for tt in range(ST // 128):
                    yt = ytok[tt % 2]
                    for kk in range(2):
                        ps_ = bank[ip % 6]; ip += 1
                        for k4 in range(4):
                            P.tr(ps_[:, k4 * 128:(k4 + 1) * 128], x2[:, kk * 4 + k4, tt * 128:(tt + 1) * 128], ident[:])
                        P.evac(yt[:, kk * 512:(kk + 1) * 512], ps_[:])
                    P.dma(jb["y"][t0 + tt * 128:t0 + (tt + 1) * 128, :], yt[:], wk=[("y", j, t0, tt)], eng=STQ, final=True)
        P.pop()

    WR = {}
    fast_w = ["w_in", "mla_w_o", "mlstm_w_o", "swa_w_o", "rwkv_w_o", "w_out", "mlp_w1", "mlp_w2"]
    if FAST_MM:
        P.push()
        CH = 2048
        raw = [P.sb(f"wraw{i}", [128, CH]) for i in range(3)]
        rnd = [P.sb(f"wrnd{i}", [128, CH], F32R) for i in range(3)]
        engs = ["gpsimd", "vector", "scalar"]
        iw = 0
        for name in fast_w:
            src = I[name]
            _, Rr, Cc = src.shape
            dst = nc.dram_tensor(name + "_r", [L, Rr, Cc], F32R, kind="Internal").ap()
            WR[name] = dst
            for l in range(L):
                for rb in range(Rr // 128):
                    for c0 in range(0, Cc, CH):
                        w = min(CH, Cc - c0)
                        a_ = raw[iw % 3]; b_ = rnd[iw % 3]
                        P.dma(a_[:, :w], src[l, rb * 128:(rb + 1) * 128, c0:c0 + w])
                        P.copy(engs[iw % 3], b_[:, :w], a_[:, :w])
                        P.dma(dst[l, rb * 128:(rb + 1) * 128, c0:c0 + w], b_[:, :w], wk=[(dst.name, l, rb, c0)], eng=STQ)
                        iw += 1
        P.pop()
    else:
        for name in fast_w:
            WR[name] = I[name]

    only = os.environ.get("KONLY", "")
    for l in range(L):
        phase0(l)
        for jb in jobs:
            phase1(l, jb, first=(l == 0))
        for nm, fn in (("A", phaseA), ("C", phaseC), ("B", phaseB), ("D", phaseD)):
            if only and nm not in only:
                continue
            for jb in jobs:
                fn(l, jb)
        if only and "M" not in only:
            break
        for jb in jobs:
            phaseM1(l, jb)
        for jb in jobs:
            phaseM2(l, jb, last=(l == L - 1))
    print("NREC", getattr(P, "nrec", 0), flush=True)
    P.emit()
    return nc, I, O, SCR


def _fm(v, width=8):
    v = np.asarray(v, np.float32)
    return np.ascontiguousarray(np.swapaxes(v.reshape(v.shape[:-1] + (width, 128)), -1, -2))


def _rope_table(S, R):
    q = R // 4
    t = np.arange(S)
    pr = (t // 64).astype(np.float32); pc = (t % 64).astype(np.float32)
    inv = (10000.0 ** (-np.arange(q, dtype=np.float32) / q)).astype(np.float32)
    ar = pr[:, None] * inv; ac = pc[:, None] * inv
    ang = np.concatenate([ar, ar, ac, ac], -1).astype(np.float32)
    sign = np.concatenate([-np.ones(q), np.ones(q), -np.ones(q), np.ones(q)]).astype(np.float32)
    return np.ascontiguousarray(np.stack([np.cos(ang), np.sin(ang) * sign], 1).astype(np.float32))


def make_in_map(inp, cfg, core):
    L = cfg.depth
    f = lambda a: np.ascontiguousarray(np.asarray(a, np.float32))
    n_p = cfg.n_p
    m = {}
    m["xs"] = f(inp["x_sample"][core])
    m["xp"] = f(inp["x_prompt"][core * n_p:(core + 1) * n_p].reshape(n_p * SP, D))
    cc = np.stack([np.asarray(inp["c"][core]), np.asarray(inp["c_ctx"])], 0)
    m["cT"] = f(cc.reshape(2, 8, 128).transpose(2, 1, 0))
    m["ckv"] = f(inp["cache_mla_ckv"][core]); m["ckr"] = f(inp["cache_mla_krope"][core])
    m["cswk"] = f(np.asarray(inp["cache_swa_k"][core]).reshape(L, CTX, 128))
    m["cswv"] = f(np.asarray(inp["cache_swa_v"][core]).reshape(L, CTX, 128))
    m["mC"] = f(inp["state_mlstm_C"][core]); m["mn"] = f(inp["state_mlstm_n"][core])
    m["mm"] = f(inp["state_mlstm_m"][core]); m["rw"] = f(inp["state_rwkv"][core])
    m["ada_w"] = f(inp["ada_w"]); m["ada_bT"] = _fm(inp["ada_b"], 48)
    m["norm1T"] = _fm(inp["norm1"]); m["norm2T"] = _fm(inp["norm2"])
    for k in ["w_in", "mla_q_a_norm", "mla_kv_a_norm", "mla_w_uq", "mla_w_ukv", "mla_q_norm", "mla_k_norm", "mla_w_o",
              "mlstm_norm", "mlstm_w_o", "swa_q_norm", "swa_k_norm", "swa_sink", "swa_w_o", "rwkv_w2", "rwkv_a2",
              "rwkv_g2", "rwkv_w_o", "w_out", "mlp_w1", "mlp_w2"]:
        m[k] = f(inp[k])
    m["mlstm_i_bias"] = f(np.asarray(inp["mlstm_i_bias"]).reshape(L, 8))
    m["mlstm_f_bias"] = f(np.asarray(inp["mlstm_f_bias"]).reshape(L, 8))
    for k in ["rwkv_gn_g", "rwkv_gn_b"]:
        m[k + "T"] = _fm(inp[k], 4)
    m["rwkv_w0"] = f(inp["rwkv_w0"])
    for k in ["rwkv_kk", "rwkv_ka"]:
        m[k + "64"] = f(np.asarray(inp[k]).reshape(L, 8, 64).transpose(0, 2, 1))
    for k in ["rwkv_a0", "rwkv_u"]:
        m[k + "64"] = f(np.asarray(inp[k]).reshape(L, 2, 8, 64).transpose(0, 3, 1, 2))
    m["k_ident"] = np.eye(128, dtype=np.float32)
    m["k_ones"] = np.ones((128, 128), np.float32)
    kq = np.arange(128)
    m["k_tri"] = np.stack([(kq[:, None] >= kq[None, :]), (kq[:, None] <= kq[None, :])], 0).astype(np.float32)
    m["k_tris"] = np.stack([(kq[:, None] > kq[None, :]), (kq[:, None] < kq[None, :])], 0).astype(np.float32)
    sel = np.zeros((2, 128, 128), np.float32); sel[0, 127, :] = 1.0; sel[1, 0, :] = 1.0
    m["k_sel"] = sel
    m["k_rope32"] = _rope_table(cfg.S_s, 32); m["k_rope64"] = _rope_table(cfg.S_s, 64)
    return m


_CACHE = {}


def kernel(**inputs):
    cfg = Cfg(S_s=4096, n_p=2, depth=2)
    n_cores = 8
    if "nc" not in _CACHE:
        _CACHE["nc"] = build(cfg)
    nc, I, O, SCR = _CACHE["nc"]
    in_maps = []
    for c in range(n_cores):
        m = make_in_map(inputs, cfg, c)
        in_maps.append({k: v for k, v in m.items() if k in I})
    res = run_bass_kernel_spmd(nc, in_maps, core_ids=list(range(n_cores)))
    R = res.results
    L = cfg.depth
    cat = lambda k: np.concatenate([np.asarray(r[k], np.float32) for r in R], 0)
    y_prompt = cat("y_p").reshape(16, SP, D)
    y_sample = np.stack([np.asarray(r["y_s"], np.float32) for r in R], 0)
    return (y_prompt, y_sample,
            cat("o_ckv"), cat("o_ckr"),
            cat("o_swk").reshape(16, L, SP, 2, 64), cat("o_swv").reshape(16, L, SP, 2, 64),
            cat("o_mC"), cat("o_mn"), cat("o_mm"), cat("o_rw"))
```
